# Optimizing a Trainium2 kernel written in Bass

```python
import jax
import jax.numpy as jnp
from jax import lax
import numpy as np

D_MODEL = 1024
BATCH = 16
SEQ = 2048
DEPTH = 2

CONV_WIDTH = 512
CONV_GROUPS = 8
CONV_K = 3
MLA_HEADS = 8
MLA_Q_LORA = 256
MLA_KV_LORA = 128
MLA_NOPE = 64
MLA_ROPE = 32
MLA_V = 64
MLA_QK = MLA_NOPE + MLA_ROPE
ROPE_THETA = 10000.0
DIL_PATTERNS = ((128, 1), (512, 4), (2048, 16))
DIL_GROUPS = len(DIL_PATTERNS)
DIL_HEADS = 8
DIL_HEAD_DIM = 64
DIL_WIDTH = DIL_HEADS * DIL_HEAD_DIM
N_BRANCH = 3
Q_BLOCK = 128
EPS = 1e-6

SPLIT_SIZES = ((CONV_WIDTH,) * 4
               + (MLA_Q_LORA, MLA_KV_LORA, MLA_ROPE, MLA_HEADS * MLA_V)
               + (DIL_GROUPS * DIL_WIDTH,) * 3 + (DIL_WIDTH,)
               + (N_BRANCH * D_MODEL,))
SPLIT_POINTS = tuple(int(v) for v in np.cumsum(SPLIT_SIZES)[:-1])
N_IN = int(sum(SPLIT_SIZES))

kernel_name = 'hybrid_gatedconv_mla_dilated_layer'


def rmsnorm(x, g):
    xf = x.astype(jnp.float32)
    y = xf * lax.rsqrt(jnp.mean(xf * xf, axis=-1, keepdims=True) + EPS)
    return (y * g.astype(jnp.float32)).astype(x.dtype)


def rope_tables(S):
    inv = ROPE_THETA ** (-jnp.arange(0, MLA_ROPE, 2, dtype=jnp.float32) / MLA_ROPE)
    ang = jnp.arange(S, dtype=jnp.float32)[:, None] * inv[None, :]
    return jnp.cos(ang), jnp.sin(ang)


def apply_rope(t, cos, sin):
    tf = t.astype(jnp.float32)
    t1, t2 = jnp.split(tf, 2, axis=-1)
    c = cos[None, :, None, :]
    s = sin[None, :, None, :]
    return jnp.concatenate([t1 * c - t2 * s, t2 * c + t1 * s], axis=-1).astype(t.dtype)


def alibi_slopes():
    n = DIL_GROUPS * DIL_HEADS
    m = 2.0 ** (-8.0 * jnp.arange(1, n + 1, dtype=jnp.float32) / n)
    return m.reshape(DIL_GROUPS, DIL_HEADS)


def causal_block_attention(q, k, v, scale):
    B, H, S, Dk = q.shape
    Dv = v.shape[-1]
    nqb = S // Q_BLOCK
    qb = q.reshape(B, H, nqb, Q_BLOCK, Dk).transpose(2, 0, 1, 3, 4)
    kf = k.astype(jnp.float32)
    vf = v.astype(jnp.float32)
    kpos = jnp.arange(S)

    def one_block(args):
        qblk, i = args
        s = jnp.einsum('bhqc,bhkc->bhqk', qblk.astype(jnp.float32), kf) * scale
        qpos = i * Q_BLOCK + jnp.arange(Q_BLOCK)
        s = jnp.where(kpos[None, :] <= qpos[:, None], s, -jnp.inf)
        p = jax.nn.softmax(s, axis=-1)
        return jnp.einsum('bhqk,bhkc->bhqc', p, vf)

    o = lax.map(one_block, (qb, jnp.arange(nqb)))
    return o.transpose(1, 2, 0, 3, 4).reshape(B, H, S, Dv)


def dilated_group_attention(q, k, v, slopes, dilation, n_back):
    B, S, H, hd = q.shape
    L = S // dilation
    nb = -(-L // Q_BLOCK)
    Lp = nb * Q_BLOCK

    def to_sub(t):
        return t.reshape(B, L, dilation, H, hd).transpose(0, 2, 3, 1, 4)

    qs = jnp.pad(to_sub(q).astype(jnp.float32), ((0, 0), (0, 0), (0, 0), (0, Lp - L), (0, 0)))
    qs = qs.reshape(B, dilation, H, nb, Q_BLOCK, hd)

    def windows(t):
        tp = jnp.pad(to_sub(t).astype(jnp.float32), ((0, 0), (0, 0), (0, 0), (Q_BLOCK, Lp - L), (0, 0)))
        tb = tp.reshape(B, dilation, H, nb + 1, Q_BLOCK, hd)
        return jnp.concatenate([tb[:, :, :, :-1], tb[:, :, :, 1:]], axis=4)

    kw = windows(k)
    vw = windows(v)
    qq = jnp.arange(Q_BLOCK)[:, None]
    kk = jnp.arange(2 * Q_BLOCK)[None, :]
    j = Q_BLOCK + qq - kk
    key_idx = (jnp.arange(nb)[:, None, None] - 1) * Q_BLOCK + kk[None]
    valid = (j >= 0) & (j <= n_back) & (key_idx >= 0)
    dist = (dilation * j).astype(jnp.float32)
    bias = -slopes.reshape(1, 1, H, 1, 1, 1) * dist
    scale = DIL_HEAD_DIM ** -0.5
    s = jnp.einsum('bdhnqc,bdhnkc->bdhnqk', qs, kw) * scale + bias
    s = jnp.where(valid, s, -jnp.inf)
    lse = jax.nn.logsumexp(s, axis=-1)
    p = jnp.exp(s - lse[..., None])
    o = jnp.einsum('bdhnqk,bdhnkc->bdhnqc', p, vw)
    o = o.reshape(B, dilation, H, Lp, hd)[:, :, :, :L]
    lse = lse.reshape(B, dilation, H, Lp)[:, :, :, :L]
    o = o.transpose(0, 3, 1, 2, 4).reshape(B, S, H, hd)
    lse = lse.transpose(0, 3, 1, 2).reshape(B, S, H)
    return o, lse


def hybrid_layer(x, norm_g, w_in, b_gate, conv_w, conv_b, q_a_norm_g, w_uq, kv_a_norm_g, w_ukv,
                 mla_q_norm_g, mla_k_norm_g, dil_q_norm_g, dil_k_norm_g,
                 w_out_a, w_out_b, w_out_c, w_o, cos, sin, slopes):
    B, S, _ = x.shape
    h = rmsnorm(x, norm_g)
    proj = h @ w_in
    (a_b, a_c, a_x, a_z, c_q, c_kv, k_pe, b_z, dq, dk, dv, c_z, gate_pre) = jnp.split(proj, SPLIT_POINTS, axis=-1)

    u = a_c * a_x
    up = jnp.pad(u, ((0, 0), (CONV_K - 1, 0), (0, 0)))
    conv = conv_b
    for tap in range(CONV_K):
        conv = conv + up[:, tap:tap + S] * conv_w[tap]
    y_a = a_b * conv * jax.nn.silu(a_z)

    q = (rmsnorm(c_q, q_a_norm_g) @ w_uq).reshape(B, S, MLA_HEADS, MLA_QK)
    kv = (rmsnorm(c_kv, kv_a_norm_g) @ w_ukv).reshape(B, S, MLA_HEADS, MLA_NOPE + MLA_V)
    k_nope, v = jnp.split(kv, [MLA_NOPE], axis=-1)
    k = jnp.concatenate([k_nope, jnp.broadcast_to(k_pe[:, :, None, :], (B, S, MLA_HEADS, MLA_ROPE))], axis=-1)
    q = rmsnorm(q, mla_q_norm_g)
    k = rmsnorm(k, mla_k_norm_g)
    q = jnp.concatenate([q[..., :MLA_NOPE], apply_rope(q[..., MLA_NOPE:], cos, sin)], axis=-1)
    k = jnp.concatenate([k[..., :MLA_NOPE], apply_rope(k[..., MLA_NOPE:], cos, sin)], axis=-1)
    o_b = causal_block_attention(q.transpose(0, 2, 1, 3), k.transpose(0, 2, 1, 3), v.transpose(0, 2, 1, 3),
                                 MLA_QK ** -0.5)
    o_b = o_b.transpose(0, 2, 1, 3).reshape(B, S, MLA_HEADS * MLA_V).astype(x.dtype)
    y_b = o_b * jax.nn.silu(b_z)

    dq = rmsnorm(dq.reshape(B, S, DIL_GROUPS, DIL_HEADS, DIL_HEAD_DIM), dil_q_norm_g[:, None, :])
    dk = rmsnorm(dk.reshape(B, S, DIL_GROUPS, DIL_HEADS, DIL_HEAD_DIM), dil_k_norm_g[:, None, :])
    dv = dv.reshape(B, S, DIL_GROUPS, DIL_HEADS, DIL_HEAD_DIM)
    outs = []
    lses = []
    for gi, (window, dilation) in enumerate(DIL_PATTERNS):
        o_g, lse_g = dilated_group_attention(dq[:, :, gi], dk[:, :, gi], dv[:, :, gi], slopes[gi],
                                             dilation, window // dilation)
        outs.append(o_g)
        lses.append(lse_g)
    alpha = jax.nn.softmax(jnp.stack(lses, axis=0), axis=0)
    o_c = jnp.sum(alpha[..., None] * jnp.stack(outs, axis=0), axis=0)
    o_c = o_c.reshape(B, S, DIL_WIDTH).astype(x.dtype)
    y_c = o_c * jax.nn.silu(c_z)

    g_a, g_b, g_c = jnp.split(jax.nn.sigmoid(gate_pre + b_gate), N_BRANCH, axis=-1)
    merged = g_a * (y_a @ w_out_a) + g_b * (y_b @ w_out_b) + g_c * (y_c @ w_out_c)
    return x + merged @ w_o


def setup_inputs(seed: int = 0) -> dict:
    key = jax.random.key(seed)
    ks = jax.random.split(key, 18)
    f32 = jnp.float32

    def nrm(k, shape, scale):
        return jax.random.normal(k, shape, f32) * scale

    def gain(k, shape):
        return 1.0 + 0.02 * jax.random.normal(k, shape, f32)

    Ld = DEPTH
    return {
        'x': jax.random.normal(ks[0], (BATCH, SEQ, D_MODEL), f32),
        'norm_g': gain(ks[1], (Ld, D_MODEL)),
        'w_in': nrm(ks[2], (Ld, D_MODEL, N_IN), D_MODEL ** -0.5),
        'b_gate': nrm(ks[3], (Ld, N_BRANCH * D_MODEL), 0.1),
        'conv_w': nrm(ks[4], (Ld, CONV_K, CONV_WIDTH), CONV_K ** -0.5),
        'conv_b': nrm(ks[5], (Ld, CONV_WIDTH), 0.02),
        'q_a_norm_g': gain(ks[6], (Ld, MLA_Q_LORA)),
        'w_uq': nrm(ks[7], (Ld, MLA_Q_LORA, MLA_HEADS * MLA_QK), MLA_Q_LORA ** -0.5),
        'kv_a_norm_g': gain(ks[8], (Ld, MLA_KV_LORA)),
        'w_ukv': nrm(ks[9], (Ld, MLA_KV_LORA, MLA_HEADS * (MLA_NOPE + MLA_V)), MLA_KV_LORA ** -0.5),
        'mla_q_norm_g': gain(ks[10], (Ld, MLA_QK)),
        'mla_k_norm_g': gain(ks[11], (Ld, MLA_QK)),
        'dil_q_norm_g': gain(ks[12], (Ld, DIL_GROUPS, DIL_HEAD_DIM)),
        'dil_k_norm_g': gain(ks[13], (Ld, DIL_GROUPS, DIL_HEAD_DIM)),
        'w_out_a': nrm(ks[14], (Ld, CONV_WIDTH, D_MODEL), CONV_WIDTH ** -0.5),
        'w_out_b': nrm(ks[15], (Ld, MLA_HEADS * MLA_V, D_MODEL), (MLA_HEADS * MLA_V) ** -0.5),
        'w_out_c': nrm(ks[16], (Ld, DIL_WIDTH, D_MODEL), DIL_WIDTH ** -0.5),
        'w_o': nrm(ks[17], (Ld, D_MODEL, D_MODEL), D_MODEL ** -0.5),
    }


def reference(x, norm_g, w_in, b_gate, conv_w, conv_b, q_a_norm_g, w_uq, kv_a_norm_g, w_ukv,
              mla_q_norm_g, mla_k_norm_g, dil_q_norm_g, dil_k_norm_g, w_out_a, w_out_b, w_out_c, w_o):
    cos, sin = rope_tables(x.shape[1])
    slopes = alibi_slopes()
    for l in range(DEPTH):
        x = hybrid_layer(x, norm_g[l], w_in[l], b_gate[l], conv_w[l], conv_b[l], q_a_norm_g[l], w_uq[l],
                         kv_a_norm_g[l], w_ukv[l], mla_q_norm_g[l], mla_k_norm_g[l], dil_q_norm_g[l],
                         dil_k_norm_g[l], w_out_a[l], w_out_b[l], w_out_c[l], w_o[l], cos, sin, slopes)
    return x
```

```python
import numpy as np
import concourse.bass as bass
import concourse.mybir as mybir
from concourse.bass_utils import run_bass_kernel_spmd

F32 = mybir.dt.float32
BF16 = mybir.dt.bfloat16
AF = mybir.ActivationFunctionType
ALU = mybir.AluOpType

ENGS = ("pe", "act", "dve", "pool", "sp")

D = 1024
S = 2048
DEPTH = 2
NB = 16
NT = 4
TT = 512
EPS = 1e-6
N_IN = 11168
C_AB, C_AC, C_AX, C_AZ = 0, 512, 1024, 1536
C_CQ, C_CKV, C_KPE, C_BZ = 2048, 2304, 2432, 2464
C_DQ, C_DK, C_DV, C_CZ, C_GATE = 2976, 4512, 6048, 7584, 8096
DIL = (1, 4, 16)
CV_BG, CV_CW, CV_CB, CV_GQ, CV_GKV, CV_GQM, CV_GQMS, CV_GKM, CV_GKMS, CV_GDQ, CV_GDK, NCV = 0, 24, 36, 40, 42, 43, 44, 45, 46, 47, 50, 53


class Chan:
    def __init__(self, sem, name):
        self.sem = sem
        self.name = name
        self.count = 0


class Op:
    __slots__ = ("eng", "fn", "idx", "waits", "signal", "sigcount", "chan", "dma_waits")

    def __init__(self, eng, fn):
        self.eng = eng
        self.fn = fn
        self.idx = -1
        self.waits = {}
        self.dma_waits = {}
        self.signal = False
        self.sigcount = 0
        self.chan = None


class Prog:
    def __init__(self, nc):
        self.nc = nc
        self.eng_ops = {e: [] for e in ENGS}
        self.last_w = {}
        self.last_r = {}
        self.sems = {e: nc.alloc_semaphore("sem_" + e) for e in ENGS}
        self.nops = 0
        self.chans = []

    def chan(self, name):
        c = Chan(self.nc.alloc_semaphore("dsem_" + name), name)
        self.chans.append(c)
        return c

    def barrier(self):
        drains = {}
        for e in ENGS:
            if e == "sp":
                continue
            op = Op(e, lambda h: h.drain())
            op.idx = len(self.eng_ops[e])
            self.eng_ops[e].append(op)
            drains[e] = op
        for e in ENGS:
            op = Op(e, lambda h: h.nop())
            op.idx = len(self.eng_ops[e])
            self.eng_ops[e].append(op)
            for e2, d in drains.items():
                if e2 != e:
                    op.waits[e2] = d
            for c in self.chans:
                if c.count > 0:
                    op.dma_waits[c] = c.count

    def _dep(self, op, prev):
        if prev is None or prev is op:
            return
        if prev.chan is not None:
            c = prev.chan
            op.dma_waits[c] = max(op.dma_waits.get(c, 0), c.count)
            return
        cur = op.waits.get(prev.eng)
        if cur is None or cur.idx < prev.idx:
            op.waits[prev.eng] = prev

    def add(self, eng, fn, reads=(), writes=(), chan=None):
        op = Op(eng, fn)
        op.idx = len(self.eng_ops[eng])
        self.eng_ops[eng].append(op)
        self.nops += 1
        isdma = chan is not None
        for k in reads:
            for w in self.last_w.get(k, {}).values():
                self._dep(op, w)
        for k in writes:
            for w in self.last_w.get(k, {}).values():
                if isdma or w.chan is not None or w.eng != eng:
                    self._dep(op, w)
            for r in self.last_r.get(k, {}).values():
                if isdma or r.chan is not None or r.eng != eng:
                    self._dep(op, r)
        if isdma:
            chan.count += 1
            op.chan = chan
        me = chan if isdma else eng
        for k in reads:
            self.last_r.setdefault(k, {})[me] = op
        for k in writes:
            self.last_w.setdefault(k, {})[me] = op
            self.last_r[k] = {}
        return op

    def emit(self, final_chans=()):
        nc = self.nc
        for e in ENGS:
            for op in self.eng_ops[e]:
                for p in op.waits.values():
                    p.signal = True
        for e in ENGS:
            c = 0
            for op in self.eng_ops[e]:
                if op.signal:
                    c += 1
                op.sigcount = c

        def run_engine(e, h):
            seen = {}
            for op in self.eng_ops[e]:
                for pe_, p in op.waits.items():
                    need = p.sigcount
                    if seen.get(pe_, 0) < need:
                        h.wait_ge(self.sems[pe_], need)
                        seen[pe_] = need
                for c, cnt in op.dma_waits.items():
                    if seen.get(c, 0) < cnt:
                        h.wait_ge(c.sem, 16 * cnt)
                        seen[c] = cnt
                ins = op.fn(h)
                if op.chan is not None:
                    ins.then_inc(op.chan.sem, 16)
                elif op.signal:
                    ins.then_inc(self.sems[e], 1)
            if e == "sp":
                for c in final_chans:
                    h.wait_ge(c.sem, 16 * c.count)

        with nc.Block() as block:
            @block.tensor
            def _(h):
                run_engine("pe", h)

            @block.scalar
            def _(h):
                run_engine("act", h)

            @block.vector
            def _(h):
                run_engine("dve", h)

            @block.gpsimd
            def _(h):
                run_engine("pool", h)

            @block.sync
            def _(h):
                run_engine("sp", h)


class Arena:
    def __init__(self, nc, nbytes):
        self.cap = nbytes // 2
        self.t = nc.alloc_sbuf_tensor("arena", [128, self.cap], BF16).ap()
        self.top = 0

    def alloc(self, free_shape, dtype):
        n = 1
        for v in free_shape:
            n *= v
        ne = n * (2 if dtype == F32 else 1)
        ne_al = (ne + 31) // 32 * 32
        off = self.top
        self.top += ne_al
        assert self.top <= self.cap, ("arena overflow", self.top * 2, self.cap * 2)
        v = self.t[:, off:off + ne]
        if dtype == F32:
            v = v.bitcast(F32)
        if len(free_shape) > 1:
            names = " ".join("a%d" % i for i in range(len(free_shape)))
            kw = {"a%d" % i: free_shape[i] for i in range(len(free_shape))}
            v = v.rearrange("p (%s) -> p %s" % (names, names), **kw)
        return v


class Builder:
    def __init__(self, nseq, nlayers, layer0=0, dbg=False):
        self.nseq = nseq
        self.nlayers = nlayers
        self.dbg = dbg
        nc = bass.Bass("TRN2", target_bir_lowering=False)
        self.nc = nc
        P = Prog(nc)
        self.P = P
        L = nlayers
        dt = nc.dram_tensor
        self.x = dt("x", [nseq * S, D], F32, kind="ExternalInput").ap()
        self.out = dt("out", [nseq * S, D], F32, kind="ExternalOutput").ap()
        self.w_in = dt("w_in", [L, D, N_IN], F32, kind="ExternalInput").ap()
        self.w_uq = dt("w_uq", [L, 256, 768], F32, kind="ExternalInput").ap()
        self.w_uq_sw = dt("w_uq_sw", [L, 256, 768], F32, kind="ExternalInput").ap()
        self.w_ukv_k = dt("w_ukv_k", [L, 128, 512], F32, kind="ExternalInput").ap()
        self.w_ukv_v = dt("w_ukv_v", [L, 128, 512], F32, kind="ExternalInput").ap()
        self.w_kpe = dt("w_kpe", [L, D, 96], F32, kind="ExternalInput").ap()
        self.w_kpe_sw = dt("w_kpe_sw", [L, D, 96], F32, kind="ExternalInput").ap()
        self.w_out = [dt("w_out_%s" % n, [L, 512, D], F32, kind="ExternalInput").ap() for n in "abc"]
        self.w_o = dt("w_o", [L, D, D], F32, kind="ExternalInput").ap()
        self.colv_d = dt("colv", [L, 128, NCV], F32, kind="ExternalInput").ap()
        self.gbc_d = dt("gbc", [L, 128, D], F32, kind="ExternalInput").ap()
        self.ident_d = dt("ident", [128, 128], F32, kind="ExternalInput").ap()
        self.ind2_d = dt("ind2", [128, 128], F32, kind="ExternalInput").ap()
        self.emask_d = dt("emask", [128, 25 * 256], F32, kind="ExternalInput").ap()
        self.ct_d = dt("ctab", [96, S], F32, kind="ExternalInput").ap()
        self.st_d = dt("stab", [96, S], F32, kind="ExternalInput").ap()
        if nlayers > 1:
            self.scr = dt("scr", [nseq * S, D], F32).ap()
        if dbg:
            self.dbg_h = dt("dbg_h", [128, 8 * S], BF16, kind="ExternalOutput").ap()
            self.dbg_y = dt("dbg_y", [128, 12 * S], BF16, kind="ExternalOutput").ap()
        self.ps = [nc.alloc_psum_tensor("ps%d" % i, [128, 512], F32).ap() for i in range(8)]
        A = Arena(nc, 206 * 1024)
        self.A = A
        self.hT = A.alloc([8, S], BF16)
        self.yT = A.alloc([12, S], BF16)
        self.ident = A.alloc([128], BF16)
        self.ones = A.alloc([128], BF16)
        self.ind2 = A.alloc([128], BF16)
        self.onesf = A.alloc([64], F32)
        self.emask = A.alloc([25, 256], BF16)
        self.colv = A.alloc([NCV], F32)
        self.epsc = A.alloc([1], F32)
        self.NWR = 8
        self.wr = [A.alloc([8, 128], BF16) for _ in range(self.NWR)]
        self.wrc = [P.chan("wr%d" % i) for i in range(self.NWR)]
        self.wri = 0
        self.stage_base = A.top
        self.cst = P.chan("const")
        self.cout = P.chan("out")
        self.cmisc = {}
        self.build()

    def misc_chan(self, key):
        if key not in self.cmisc:
            self.cmisc[key] = self.P.chan("m%d" % len(self.cmisc))
        return self.cmisc[key]

    def dma(self, q, out, in_, key_w=None, key_r=None, chan=None):
        if chan is None:
            chan = self.misc_chan(key_w if key_w is not None else key_r)
        self.P.add(q, lambda h: h.dma_start(out=out, in_=in_),
                   reads=[key_r] if key_r else [], writes=[key_w] if key_w else [], chan=chan)

    def mm(self, out, lhsT, rhs, start, stop, r, w):
        self.P.add("pe", lambda h: h.matmul(out, lhsT=lhsT, rhs=rhs, start=start, stop=stop, skip_group_check=True), r, w)

    def tr(self, out, in_, r, w):
        ident = self.ident
        self.P.add("pe", lambda h: h.transpose(out, in_, ident), list(r) + ["const"], w)

    def act(self, out, in_, func, r, w, scale=None, bias=None, accum=None):
        kw = {}
        if scale is not None:
            kw["scale"] = scale
        if bias is not None:
            kw["bias"] = bias
        if accum is not None:
            kw["accum_out"] = accum
        self.P.add("act", lambda h: h.activation(out=out, in_=in_, func=func, **kw), r, w)

    def tt(self, out, in0, in1, op, r, w, eng="dve"):
        self.P.add(eng, lambda h: h.tensor_tensor(out=out, in0=in0, in1=in1, op=op), r, w)

    def stt(self, out, in0, scalar, in1, op0, op1, r, w, eng="dve"):
        self.P.add(eng, lambda h: h.scalar_tensor_tensor(out=out, in0=in0, scalar=scalar, in1=in1, op0=op0, op1=op1), r, w)

    def ts(self, out, in0, s1, s2, op0, op1, r, w, eng="dve"):
        if s2 is None:
            self.P.add(eng, lambda h: h.tensor_scalar(out=out, in0=in0, scalar1=s1, scalar2=None, op0=op0), r, w)
        else:
            self.P.add(eng, lambda h: h.tensor_scalar(out=out, in0=in0, scalar1=s1, scalar2=s2, op0=op0, op1=op1), r, w)

    def recip(self, out, in_, r, w):
        self.P.add("dve", lambda h: h.reciprocal(out=out, in_=in_), r, w)

    def cpy(self, out, in_, r, w, eng="dve"):
        self.P.add(eng, lambda h: h.tensor_copy(out=out, in_=in_), r, w)

    def memset(self, ap, val, w, eng="dve"):
        self.P.add(eng, lambda h: h.memset(ap, val), [], w)

    def ldw(self, src):
        i = self.wri % self.NWR
        self.wri += 1
        self.dma("pool", self.wr[i], src, key_w="wr%d" % i, chan=self.wrc[i])
        return self.wr[i], "wr%d" % i

    def win_cols(self, l, c0, n):
        return self.w_in[l, :, c0:c0 + n].rearrange("(k p) n -> p k n", p=128)

    def rstd_from(self, ssq_ps, pkey, rows, inv_n, sd, rs):
        self.act(sd[0:rows], ssq_ps[0:rows], AF.Sqrt, ["const"], ["sd", pkey], scale=inv_n, bias=self.epsc[0:rows, 0:1])
        self.recip(rs[0:rows], sd[0:rows], ["sd"], ["rs"])

    def build(self):
        P = self.P
        self.dma("pool", self.ident, self.ident_d, key_w="const", chan=self.cst)
        self.dma("pool", self.ind2, self.ind2_d, key_w="const", chan=self.cst)
        self.dma("pool", self.emask, self.emask_d.rearrange("p (a b) -> p a b", a=25), key_w="const", chan=self.cst)
        self.memset(self.ones, 1.0, ["const"])
        self.memset(self.onesf, 1.0, ["const"])
        self.memset(self.epsc, EPS, ["const"])
        for l in range(self.nlayers):
            src = self.x if l == 0 else self.scr
            dst = self.out if l == self.nlayers - 1 else self.scr
            self.dma("sp", self.colv, self.colv_d[l], key_w="colv")
            for seq in range(self.nseq):
                self.A.top = self.stage_base
                self.P.barrier()
                self.stage0(l, seq, src)
                if self.dbg and l == 0 and seq == 0:
                    self.dma("sp", self.dbg_h.rearrange("p (a b) -> p a b", a=8), self.hT, key_r="hT", chan=self.cout)
                self.A.top = self.stage_base
                self.P.barrier()
                self.stage_mla(l, seq)
                self.A.top = self.stage_base
                self.P.barrier()
                self.stage_dil(l, seq)
                self.A.top = self.stage_base
                self.P.barrier()
                self.stage_conv(l, seq)
                if self.dbg and l == 0 and seq == 0:
                    self.P.add("sp", lambda h: h.dma_start(out=self.dbg_y.rearrange("p (a b) -> p a b", a=12), in_=self.yT), reads=["yT_a", "yT_b", "yT_c"], chan=self.cout)
                self.A.top = self.stage_base
                self.P.barrier()
                self.stage_out(l, seq, src, dst, final=(l == self.nlayers - 1))
        P.emit(final_chans=[self.cout] + ([self.cmisc["scrw"]] if "scrw" in self.cmisc else []))

    def stage0(self, l, seq, src):
        A = self.A
        xb = [A.alloc([D], F32) for _ in range(2)]
        hn = [A.alloc([D], BF16) for _ in range(2)]
        junk = A.alloc([D], BF16)
        gbc = A.alloc([D], F32)
        ss = A.alloc([1], F32)
        sd1 = A.alloc([1], F32)
        rs1 = A.alloc([1], F32)
        self.dma("sp", gbc, self.gbc_d[l], key_w="s0gbc")
        for b in range(NB):
            i = b % 2
            r0 = seq * S + b * 128
            self.dma("sp", xb[i], src[r0:r0 + 128, :], key_w="xb%d" % i, key_r="scr")
            self.memset(ss, 0.0, ["ss"])
            self.act(junk, xb[i], AF.Square, ["xb%d" % i], ["junk", "ss"], accum=ss)
            self.act(sd1, ss, AF.Sqrt, ["ss", "const"], ["sd1"], scale=1.0 / D, bias=self.epsc[:, 0:1])
            self.recip(rs1, sd1, ["sd1"], ["rs1"])
            self.stt(hn[i], xb[i], rs1[:, 0:1], gbc, ALU.mult, ALU.mult, ["xb%d" % i, "rs1", "s0gbc"], ["hn%d" % i])
            pst = self.ps[i].bitcast(BF16)
            for k in range(8):
                self.tr(pst[:, k * 128:(k + 1) * 128], hn[i][:, k * 128:(k + 1) * 128], ["hn%d" % i], ["ps%d" % i])
            self.act(self.hT[:, :, b * 128:(b + 1) * 128], pst.rearrange("p (k n) -> p k n", k=8), AF.Copy,
                     [], ["ps%d" % i, "hT"])

    def stage_mla(self, l, seq):
        A = self.A
        ps = self.ps
        cv = self.colv
        Kc = self.yT[:, 4:12, :]
        Va = A.alloc([NB, 8, 65], BF16)
        qT = A.alloc([8, TT], BF16)
        cqn = A.alloc([2, TT], BF16)
        ckvn = A.alloc([TT], BF16)
        sqpe = A.alloc([TT], BF16)
        kr = A.alloc([TT], F32)
        ctt = A.alloc([TT], F32)
        stt_ = A.alloc([TT], F32)
        cqf = A.alloc([TT], F32)
        sqf = A.alloc([TT], F32)
        ckf = A.alloc([TT], F32)
        skf = A.alloc([TT], F32)
        sqA = A.alloc([TT], BF16)
        sqB = A.alloc([TT], BF16)
        sd = A.alloc([TT], F32)
        rs = A.alloc([TT], F32)
        ta = A.alloc([TT], F32)
        tb = A.alloc([TT], F32)
        PT = [A.alloc([TT], BF16) for _ in range(3)]
        osb = A.alloc([TT], F32)
        rden = A.alloc([TT], F32)
        sbz = A.alloc([8, TT], BF16)
        Wlat = A.alloc([8, 384], BF16)
        Wkpe = A.alloc([8, 96], BF16)
        Wkpes = A.alloc([8, 96], BF16)
        Wuq = A.alloc([2, 768], BF16)
        Wuqs = A.alloc([2, 768], BF16)
        Wk = A.alloc([512], BF16)
        Wv = A.alloc([512], BF16)
        hT = self.hT
        self.dma("pool", Wlat, self.win_cols(l, C_CQ, 384), key_w="Wlat")
        self.dma("pool", Wkpe, self.w_kpe[l].rearrange("(k p) n -> p k n", p=128), key_w="Wkpe")
        self.dma("pool", Wkpes, self.w_kpe_sw[l].rearrange("(k p) n -> p k n", p=128), key_w="Wkpes")
        self.dma("pool", Wuq, self.w_uq[l].rearrange("(k p) n -> p k n", p=128), key_w="Wuq")
        self.dma("pool", Wuqs, self.w_uq_sw[l].rearrange("(k p) n -> p k n", p=128), key_w="Wuqs")
        self.dma("pool", Wk, self.w_ukv_k[l], key_w="Wk")
        self.dma("pool", Wv, self.w_ukv_v[l], key_w="Wv")
        self.memset(Va[:, :, :, 64:65], 1.0, ["Va"])
        sc_mla = 96.0 ** -0.5
        for t in range(NT):
            T0 = t * TT
            hs = lambda k: hT[:, k, T0:T0 + TT]
            wbz = [self.ldw(self.win_cols(l, C_BZ + c * 128, 128)) for c in range(4)]
            self.dma("sp", ctt[0:96], self.ct_d[:, T0:T0 + TT], key_w="ctt")
            self.dma("sp", stt_[0:96], self.st_d[:, T0:T0 + TT], key_w="stt")
            self.ts(cqf[0:96], ctt[0:96], cv[0:96, CV_GQM:CV_GQM + 1], None, ALU.mult, None, ["ctt", "colv"], ["cqf"])
            self.ts(sqf[0:96], stt_[0:96], cv[0:96, CV_GQMS:CV_GQMS + 1], None, ALU.mult, None, ["stt", "colv"], ["sqf"])
            self.ts(ckf[0:96], ctt[0:96], cv[0:96, CV_GKM:CV_GKM + 1], None, ALU.mult, None, ["ctt", "colv"], ["ckf"])
            self.ts(skf[0:96], stt_[0:96], cv[0:96, CV_GKMS:CV_GKMS + 1], None, ALU.mult, None, ["stt", "colv"], ["skf"])
            for j in range(3):
                for k in range(8):
                    self.mm(ps[j], Wlat[:, k, j * 128:(j + 1) * 128], hs(k), k == 0, k == 7, ["Wlat", "hT"], ["ps%d" % j])
            for k in range(8):
                self.mm(ps[3][0:96], Wkpe[:, k, :], hs(k), k == 0, k == 7, ["Wkpe", "hT"], ["ps3"])
            for k in range(8):
                self.mm(ps[4][0:96], Wkpes[:, k, :], hs(k), k == 0, k == 7, ["Wkpes", "hT"], ["ps4"])
            self.act(sqA, ps[0], AF.Square, [], ["sqA", "ps0"])
            self.act(sqB, ps[1], AF.Square, [], ["sqB", "ps1"])
            self.mm(ps[5], self.ones, sqA, True, False, ["const", "sqA"], ["ps5"])
            self.mm(ps[5], self.ones, sqB, False, True, ["const", "sqB"], ["ps5"])
            self.rstd_from(ps[5], "ps5", 128, 1.0 / 256, sd, rs)
            for j in range(2):
                self.stt(cqn[:, j, :], ps[j], cv[:, CV_GQ + j:CV_GQ + j + 1], rs, ALU.mult, ALU.mult,
                         ["colv", "rs"], ["cqn", "ps%d" % j])
            self.act(sqA, ps[2], AF.Square, [], ["sqA", "ps2"])
            self.mm(ps[5], self.ones, sqA, True, True, ["const", "sqA"], ["ps5"])
            self.rstd_from(ps[5], "ps5", 128, 1.0 / 128, sd, rs)
            self.stt(ckvn, ps[2], cv[:, CV_GKV:CV_GKV + 1], rs, ALU.mult, ALU.mult, ["colv", "rs"], ["ckvn", "ps2"])
            R = slice(64, 96)
            self.act(sqpe[R], ps[3][R], AF.Square, [], ["sqpe", "ps3"])
            self.tt(kr[R], ps[3][R], ckf[R], ALU.mult, ["ckf"], ["kr", "ps3"])
            self.tt(tb[R], ps[4][R], skf[R], ALU.mult, ["skf"], ["tb", "ps4"])
            self.tt(kr[R], kr[R], tb[R], ALU.add, ["tb", "kr"], ["kr"])
            for bb in range(4):
                blk = t * 4 + bb
                pv = ps[6 + (bb % 2)]
                pk = "ps%d" % (6 + (bb % 2))
                self.mm(pv, ckvn[:, bb * 128:(bb + 1) * 128], Wv, True, True, ["ckvn", "Wv"], [pk])
                self.act(Va[:, blk, :, 0:64], pv.rearrange("p (h c) -> p h c", h=8), AF.Copy, [], ["Va", pk])
            for h in range(8):
                pk_i = 6 + (h % 2)
                pK = ps[pk_i]
                pk = "ps%d" % pk_i
                self.mm(pK[0:64], Wk[:, h * 64:(h + 1) * 64], ckvn, True, True, ["Wk", "ckvn"], [pk])
                self.act(sqA[0:64], pK[0:64], AF.Square, [], ["sqA", pk])
                self.mm(ps[5][0:96], self.ones[0:64, 0:96], sqA[0:64], True, False, ["const", "sqA"], ["ps5"])
                self.mm(ps[5][0:96], self.ones[64:96, 0:96], sqpe[R], False, True, ["const", "sqpe"], ["ps5"])
                self.rstd_from(ps[5], "ps5", 96, 1.0 / 96, sd, rs)
                self.stt(Kc[0:64, h, T0:T0 + TT], pK[0:64], cv[0:64, CV_GKM:CV_GKM + 1], rs[0:64], ALU.mult, ALU.mult,
                         ["colv", "rs"], ["yT_c", "yT_a", pk])
                self.tt(Kc[R, h, T0:T0 + TT], kr[R], rs[R], ALU.mult, ["kr", "rs"], ["yT_c", "yT_a"])
            for h in range(8):
                wt, wk = wbz[h // 2]
                pb_i = 6 + (h % 2)
                pb = ps[pb_i]
                pk = "ps%d" % pb_i
                c0 = (h % 2) * 64
                for k in range(8):
                    self.mm(pb[0:64], wt[:, k, c0:c0 + 64], hs(k), k == 0, k == 7, [wk, "hT"], [pk])
                self.act(sbz[0:64, h, :], pb[0:64], AF.Silu, [], ["sbz", pk])
            for h in range(8):
                qi = h % 2
                pQ, pQs = ps[qi], ps[2 + qi]
                kq, kqs = "ps%d" % qi, "ps%d" % (2 + qi)
                for j in range(2):
                    self.mm(pQ[0:96], Wuq[:, j, h * 96:(h + 1) * 96], cqn[:, j, :], j == 0, j == 1, ["Wuq", "cqn"], [kq])
                for j in range(2):
                    self.mm(pQs[0:96], Wuqs[:, j, h * 96:(h + 1) * 96], cqn[:, j, :], j == 0, j == 1, ["Wuqs", "cqn"], [kqs])
                self.act(sqA[0:96], pQ[0:96], AF.Square, [], ["sqA", kq])
                self.mm(ps[5][0:96], self.ones[0:96, 0:96], sqA[0:96], True, True, ["const", "sqA"], ["ps5"])
                self.rstd_from(ps[5], "ps5", 96, 1.0 / 96, sd, rs)
                self.tt(ta[0:96], pQ[0:96], cqf[0:96], ALU.mult, ["cqf"], ["ta", kq])
                self.tt(tb[R], pQs[R], sqf[R], ALU.mult, ["sqf"], ["tb", kqs])
                self.tt(ta[R], ta[R], tb[R], ALU.add, ["tb", "ta"], ["ta"])
                self.tt(qT[0:96, h, :], ta[0:96], rs[0:96], ALU.mult, ["ta", "rs"], ["qT"])
                nkb = 4 * t + 4
                for kb in range(nkb):
                    c0 = 0 if kb < 4 * t else (kb - 4 * t) * 128
                    si = 6 + (kb % 2)
                    pS = ps[si]
                    ks = "ps%d" % si
                    pt = PT[kb % 3]
                    kpt = "PT%d" % (kb % 3)
                    self.mm(pS[:, c0:TT], Kc[0:96, h, kb * 128:(kb + 1) * 128], qT[0:96, h, c0:TT], True, True, ["yT_c", "yT_a", "qT"], [ks])
                    self.act(pt[:, c0:TT], pS[:, c0:TT], AF.Exp, [], [kpt, ks], scale=sc_mla)
                    if kb >= 4 * t:
                        self.tt(pt[:, c0:c0 + 128], pt[:, c0:c0 + 128], self.emask[:, 24, 0:128], ALU.mult, ["const", kpt], [kpt])
                    self.mm(ps[4][0:65, c0:TT], Va[:, kb, h, :], pt[:, c0:TT], kb == 0, kb == nkb - 1, ["Va", kpt], ["ps4"])
                self.act(osb[0:65], ps[4][0:65], AF.Copy, [], ["osb", "ps4"])
                self.recip(rden[64:65], osb[64:65], ["osb"], ["rden"])
                self.mm(ps[5][0:64], self.onesf[64:65, 0:64], rden[64:65], True, True, ["const", "rden"], ["ps5"])
                self.tt(ta[0:64], osb[0:64], ps[5][0:64], ALU.mult, ["osb"], ["ta", "ps5"])
                if h % 2 == 0:
                    self.tt(self.yT[0:64, h // 2, T0:T0 + TT], ta[0:64], sbz[0:64, h, :], ALU.mult, ["ta", "sbz"], ["yT_b"])
                else:
                    self.tt(tb[0:64], ta[0:64], sbz[0:64, h, :], ALU.mult, ["ta", "sbz"], ["tb"])
                    self.act(self.yT[64:128, h // 2, T0:T0 + TT], tb[0:64], AF.Copy, ["tb"], ["yT_b"])

    def stage_dil(self, l, seq):
        A = self.A
        ps = self.ps
        cv = self.colv
        hT = self.hT
        dqT = A.alloc([S], BF16)
        dkT = A.alloc([S], BF16)
        dvT = A.alloc([S], BF16)
        dva = A.alloc([NB, 2, 65], BF16)
        acc = A.alloc([2, S], F32)
        scz = A.alloc([2, S], BF16)
        sqA = A.alloc([TT], BF16)
        sd = A.alloc([TT], F32)
        rs = A.alloc([TT], F32)
        ta = A.alloc([TT], F32)
        tb = A.alloc([TT], F32)
        rden = A.alloc([S], F32)
        PT = [A.alloc([256], BF16) for _ in range(3)]
        self.memset(dva[:, :, :, 64:65], 1.0, ["dva"])

        def perm_out(buf, g, t):
            if g == 0:
                return buf[:, t * TT:(t + 1) * TT], None
            if g == 1:
                return buf.rearrange("p (r m i) -> p r m i", r=4, m=4)[:, :, t, :], ("p (i r) -> p r i", 4)
            return buf.rearrange("p (r i) -> p r i", r=16)[:, :, 32 * t:32 * t + 32], ("p (i r) -> p r i", 16)

        for j in range(4):
            wcz, kcz = self.ldw(self.win_cols(l, C_CZ + j * 128, 128))
            for hh in range(2):
                for t in range(NT):
                    pi = (hh * NT + t) % 2
                    for k in range(8):
                        self.mm(ps[pi][0:64], wcz[:, k, hh * 64:(hh + 1) * 64], hT[:, k, t * TT:(t + 1) * TT], k == 0, k == 7,
                                [kcz, "hT"], ["ps%d" % pi])
                    self.act(scz[0:64, hh, t * TT:(t + 1) * TT], ps[pi][0:64], AF.Silu, [], ["scz", "ps%d" % pi])
            for g in range(3):
                d = DIL[g]
                c_off = g * 512 + j * 128
                wq, kwq = self.ldw(self.win_cols(l, C_DQ + c_off, 128))
                wk_, kwk = self.ldw(self.win_cols(l, C_DK + c_off, 128))
                wv, kwv = self.ldw(self.win_cols(l, C_DV + c_off, 128))
                for t in range(NT):
                    hs = lambda k: hT[:, k, t * TT:(t + 1) * TT]
                    for k in range(8):
                        self.mm(ps[0], wq[:, k, :], hs(k), k == 0, k == 7, [kwq, "hT"], ["ps0"])
                    for k in range(8):
                        self.mm(ps[1], wk_[:, k, :], hs(k), k == 0, k == 7, [kwk, "hT"], ["ps1"])
                    for k in range(8):
                        self.mm(ps[2], wv[:, k, :], hs(k), k == 0, k == 7, [kwv, "hT"], ["ps2"])
                    for (pi, dst, gcol, dkey) in ((0, dqT, CV_GDQ + g, "dqT"), (1, dkT, CV_GDK + g, "dkT")):
                        pk = "ps%d" % pi
                        self.act(sqA, ps[pi], AF.Square, [], ["sqA", pk])
                        self.mm(ps[3], self.ind2, sqA, True, True, ["const", "sqA"], ["ps3"])
                        self.rstd_from(ps[3], "ps3", 128, 1.0 / 64, sd, rs)
                        ov, pr = perm_out(dst, g, t)
                        if pr is None:
                            self.stt(ov, ps[pi], cv[:, gcol:gcol + 1], rs, ALU.mult, ALU.mult, ["colv", "rs"], [dkey, pk])
                        else:
                            self.stt(ov, ps[pi].rearrange(pr[0], r=pr[1]), cv[:, gcol:gcol + 1], rs.rearrange(pr[0], r=pr[1]),
                                     ALU.mult, ALU.mult, ["colv", "rs"], [dkey, pk])
                    ov, pr = perm_out(dvT, g, t)
                    if pr is None:
                        self.act(ov, ps[2], AF.Copy, [], ["dvT", "ps2"])
                    else:
                        self.act(ov, ps[2].rearrange(pr[0], r=pr[1]), AF.Copy, [], ["dvT", "ps2"])
                for half in range(2):
                    pst = ps[4 + half].bitcast(BF16)
                    pk = "ps%d" % (4 + half)
                    for bb in range(8):
                        blk = half * 8 + bb
                        self.tr(pst[:, bb * 128:(bb + 1) * 128], dvT[:, blk * 128:(blk + 1) * 128], ["dvT"], [pk])
                    self.act(dva[:, half * 8:(half + 1) * 8, :, 0:64],
                             pst.rearrange("p (b h c) -> p b h c", b=8, h=2), AF.Copy, [], ["dva", pk])
                nch = (1, 4, 16)[g]
                nblk = NB // nch
                for hh in range(2):
                    R = slice(hh * 64, hh * 64 + 64)
                    em = self.emask[:, g * 8 + 2 * j + hh, :]
                    opened = [False] * 4
                    u = 0
                    for c in range(nch):
                        for kb in range(nblk):
                            blk = c * nblk + kb
                            K0 = blk * 128
                            nq = 256 if kb + 1 < nblk else 128
                            si = 6 + (u % 2)
                            pS = ps[si]
                            ks = "ps%d" % si
                            pt = PT[u % 3]
                            kpt = "dPT%d" % (u % 3)
                            u += 1
                            self.mm(pS[:, 0:nq], dkT[R, K0:K0 + 128], dqT[R, K0:K0 + nq], True, True, ["dkT", "dqT"], [ks])
                            self.act(pt[:, 0:nq], pS[:, 0:nq], AF.Exp, [], [kpt, ks], scale=0.125)
                            self.tt(pt[:, 0:nq], pt[:, 0:nq], em[:, 0:nq], ALU.mult, ["const", kpt], [kpt])
                            q0 = K0
                            while q0 < K0 + nq:
                                bank = q0 // 512
                                q1 = min(K0 + nq, (bank + 1) * 512)
                                self.mm(ps[bank][0:65, q0 - bank * 512:q1 - bank * 512], dva[:, blk, hh, :],
                                        pt[:, q0 - K0:q1 - K0], not opened[bank], False, ["dva", kpt], ["ps%d" % bank])
                                opened[bank] = True
                                q0 = q1
                    for b in range(4):
                        pk = "ps%d" % b
                        if g == 0:
                            self.act(acc[0:65, hh, b * 512:(b + 1) * 512], ps[b][0:65], AF.Copy, [], ["acc", pk])
                        elif g == 1:
                            av = acc[0:65, hh, :].rearrange("p (m i r) -> p r m i", m=4, r=4)[:, b]
                            self.tt(av, av, ps[b][0:65].rearrange("p (m i) -> p m i", m=4), ALU.add, ["acc"], ["acc", pk])
                        else:
                            av = acc[0:65, hh, :].rearrange("p (i r) -> p r i", r=16)[:, 4 * b:4 * b + 4, :]
                            self.tt(av, av, ps[b][0:65].rearrange("p (r i) -> p r i", r=4), ALU.add, ["acc"], ["acc", pk])
            for hh in range(2):
                self.recip(rden[64:65], acc[64:65, hh, :], ["acc"], ["drden"])
                for t in range(NT):
                    cs = slice(t * TT, (t + 1) * TT)
                    self.mm(ps[5][0:64], self.onesf[64:65, 0:64], rden[64:65, cs], True, True, ["const", "drden"], ["ps5"])
                    self.tt(ta[0:64], acc[0:64, hh, cs], ps[5][0:64], ALU.mult, ["acc"], ["ta", "ps5"])
                    if hh == 0:
                        self.tt(self.yT[0:64, 4 + j, cs], ta[0:64], scz[0:64, hh, cs], ALU.mult, ["ta", "scz"], ["yT_c"])
                    else:
                        self.tt(tb[0:64], ta[0:64], scz[0:64, hh, cs], ALU.mult, ["ta", "scz"], ["tb"])
                        self.act(self.yT[64:128, 4 + j, cs], tb[0:64], AF.Copy, ["tb"], ["yT_c"])

    def stage_conv(self, l, seq):
        A = self.A
        ps = self.ps
        cv = self.colv
        hT = self.hT
        upad = A.alloc([S + 2], F32)
        acs = A.alloc([TT], F32)
        sz = A.alloc([TT], F32)
        c1 = A.alloc([TT], F32)
        c2 = A.alloc([TT], F32)
        for c in range(4):
            ws = [self.ldw(self.win_cols(l, base + c * 128, 128)) for base in (C_AB, C_AC, C_AX, C_AZ)]
            self.memset(upad[:, 0:2], 0.0, ["upad"])
            for t in range(NT):
                T0 = t * TT
                o = 4 * (t % 2)
                for i in range(4):
                    wt, wk = ws[i]
                    for k in range(8):
                        self.mm(ps[o + i], wt[:, k, :], hT[:, k, T0:T0 + TT], k == 0, k == 7, [wk, "hT"], ["ps%d" % (o + i)])
                kb_, kc_, kx_, kz_ = ["ps%d" % (o + i) for i in range(4)]
                self.act(acs, ps[o + 1], AF.Copy, [], ["acs", kc_])
                self.tt(upad[:, 2 + T0:2 + T0 + TT], acs, ps[o + 2], ALU.mult, ["acs"], ["upad", kx_])
                self.act(sz, ps[o + 3], AF.Silu, [], ["sz", kz_])
                self.ts(c1, upad[:, T0:T0 + TT], cv[:, CV_CW + c:CV_CW + c + 1], cv[:, CV_CB + c:CV_CB + c + 1], ALU.mult, ALU.add,
                        ["upad", "colv"], ["c1"])
                self.stt(c2, upad[:, T0 + 1:T0 + 1 + TT], cv[:, CV_CW + 4 + c:CV_CW + 5 + c], c1, ALU.mult, ALU.add,
                         ["upad", "colv", "c1"], ["c2"])
                self.stt(c1, upad[:, T0 + 2:T0 + 2 + TT], cv[:, CV_CW + 8 + c:CV_CW + 9 + c], c2, ALU.mult, ALU.add,
                         ["upad", "colv", "c2"], ["c1"])
                self.tt(c2, c1, ps[o + 0], ALU.mult, ["c1"], ["c2", kb_])
                self.tt(self.yT[:, 8 + c, T0:T0 + TT], c2, sz, ALU.mult, ["c2", "sz"], ["yT_a"])

    def stage_out(self, l, seq, src, dst, final):
        A = self.A
        ps = self.ps
        cv = self.colv
        hT = self.hT
        yT = self.yT
        Wo3 = [A.alloc([4, D], BF16) for _ in range(3)]
        Wo = A.alloc([8, D], BF16)
        G = A.alloc([24, TT], BF16)
        mT = A.alloc([8, TT], BF16)
        m1 = A.alloc([TT], F32)
        m2 = A.alloc([TT], F32)
        xb = [A.alloc([D], F32) for _ in range(2)]
        ob = [A.alloc([D], F32) for _ in range(2)]
        for i in range(3):
            self.dma("pool", Wo3[i], self.w_out[i][l].rearrange("(k p) n -> p k n", p=128), key_w="Wo3_%d" % i)
        self.dma("pool", Wo, self.w_o[l].rearrange("(k p) n -> p k n", p=128), key_w="Wo")
        och = self.cout if final else self.misc_chan("scrw")
        for t in range(NT):
            T0 = t * TT
            DEPTHW = 4
            wq = [self.ldw(self.win_cols(l, C_GATE + jj * 128, 128)) for jj in range(DEPTHW)]
            for jj in range(24):
                wt, wk = wq[jj]
                pi = jj % 2
                for k in range(8):
                    self.mm(ps[pi], wt[:, k, :], hT[:, k, T0:T0 + TT], k == 0, k == 7, [wk, "hT"], ["ps%d" % pi])
                self.act(G[:, jj, :], ps[pi], AF.Sigmoid, ["colv"], ["G", "ps%d" % pi], bias=cv[:, CV_BG + jj:CV_BG + jj + 1])
                if jj + DEPTHW < 24:
                    wq.append(self.ldw(self.win_cols(l, C_GATE + (jj + DEPTHW) * 128, 128)))
            for j in range(8):
                o = 2 + 3 * (j % 2)
                for i in range(3):
                    for k in range(4):
                        self.mm(ps[o + i], Wo3[i][:, k, j * 128:(j + 1) * 128], yT[:, (8, 0, 4)[i] + k, T0:T0 + TT], k == 0, k == 3,
                                ["Wo3_%d" % i, ("yT_a", "yT_b", "yT_c")[i]], ["ps%d" % (o + i)])
                self.tt(m1, ps[o], G[:, j, :], ALU.mult, ["G"], ["m1", "ps%d" % o])
                self.tt(m2, ps[o + 1], G[:, 8 + j, :], ALU.mult, ["G"], ["m2", "ps%d" % (o + 1)])
                self.tt(m1, m1, m2, ALU.add, ["m1", "m2"], ["m1"])
                self.tt(m2, ps[o + 2], G[:, 16 + j, :], ALU.mult, ["G"], ["m2", "ps%d" % (o + 2)])
                self.tt(mT[:, j, :], m1, m2, ALU.add, ["m1", "m2"], ["mT"])
            for bb in range(4):
                b = t * 4 + bb
                i = b % 2
                r0 = seq * S + b * 128
                self.dma("sp", xb[i], src[r0:r0 + 128, :], key_w="oxb%d" % i, key_r="scr")
                for half in range(2):
                    pi = half
                    for k in range(8):
                        self.mm(ps[pi], mT[:, k, bb * 128:(bb + 1) * 128], Wo[:, k, half * 512:(half + 1) * 512], k == 0, k == 7,
                                ["mT", "Wo"], ["ps%d" % pi])
                    self.tt(ob[i][:, half * 512:(half + 1) * 512], ps[pi], xb[i][:, half * 512:(half + 1) * 512], ALU.add,
                            ["oxb%d" % i], ["ob%d" % i, "ps%d" % pi])
                self.dma("sp", dst[r0:r0 + 128, :], ob[i], key_r="ob%d" % i, key_w=(None if final else "scr"), chan=och)


def _rope_tables():
    inv = (10000.0 ** (-np.arange(0, 32, 2, dtype=np.float32) / 32)).astype(np.float32)
    ang = np.arange(S, dtype=np.float32)[:, None] * inv[None, :]
    cos, sin = np.cos(ang).astype(np.float32), np.sin(ang).astype(np.float32)
    ct = np.ones((96, S), np.float32)
    st = np.zeros((96, S), np.float32)
    ct[64:80] = cos.T
    ct[80:96] = cos.T
    st[64:80] = -sin.T
    st[80:96] = sin.T
    return ct, st


def _emask():
    n = 24
    slopes = (2.0 ** (-8.0 * np.arange(1, n + 1, dtype=np.float32) / n)).reshape(3, 8)
    k = np.arange(128)[:, None].astype(np.float32)
    q = np.arange(128)[None, :].astype(np.float32)
    em = np.zeros((128, 25, 256), np.float32)
    for g in range(3):
        for h in range(8):
            sl = slopes[g, h] * DIL[g]
            em[:, g * 8 + h, 0:128] = np.where(k <= q, np.exp(-sl * (q - k)), 0.0)
            if g < 2:
                em[:, g * 8 + h, 128:256] = np.where(k >= q, np.exp(-sl * (128.0 + q - k)), 0.0)
    em[:, 24, 0:128] = (k <= q).astype(np.float32)
    return em.reshape(128, 25 * 256)


def _host_layout(inp, layers):
    f = lambda a: np.ascontiguousarray(a, dtype=np.float32)
    L = len(layers)
    w_uq = f(inp["w_uq"][layers])
    w_uq_sw = np.zeros_like(w_uq)
    for h in range(8):
        b = h * 96
        w_uq_sw[:, :, b + 64:b + 80] = w_uq[:, :, b + 80:b + 96]
        w_uq_sw[:, :, b + 80:b + 96] = w_uq[:, :, b + 64:b + 80]
    w_ukv = f(inp["w_ukv"][layers]).reshape(L, 128, 8, 128)
    w_ukv_k = f(w_ukv[:, :, :, 0:64].reshape(L, 128, 512))
    w_ukv_v = f(w_ukv[:, :, :, 64:128].reshape(L, 128, 512))
    w_in = inp["w_in"]
    w_kpe = np.zeros((L, D, 96), np.float32)
    w_kpe_sw = np.zeros((L, D, 96), np.float32)
    for i, l in enumerate(layers):
        kp = w_in[l][:, C_KPE:C_KPE + 32]
        w_kpe[i, :, 64:96] = kp
        w_kpe_sw[i, :, 64:80] = kp[:, 16:32]
        w_kpe_sw[i, :, 80:96] = kp[:, 0:16]
    colv = np.zeros((L, 128, NCV), np.float32)
    gbc = np.zeros((L, 128, D), np.float32)
    for i, l in enumerate(layers):
        colv[i, :, CV_BG:CV_BG + 24] = inp["b_gate"][l].reshape(24, 128).T
        for tap in range(3):
            colv[i, :, CV_CW + tap * 4:CV_CW + tap * 4 + 4] = inp["conv_w"][l][tap].reshape(4, 128).T
        colv[i, :, CV_CB:CV_CB + 4] = inp["conv_b"][l].reshape(4, 128).T
        colv[i, :, CV_GQ:CV_GQ + 2] = inp["q_a_norm_g"][l].reshape(2, 128).T
        colv[i, :, CV_GKV] = inp["kv_a_norm_g"][l]
        for (col, src) in ((CV_GQM, inp["mla_q_norm_g"][l]), (CV_GKM, inp["mla_k_norm_g"][l])):
            colv[i, 0:96, col] = src
            colv[i, 64:80, col + 1] = src[80:96]
            colv[i, 80:96, col + 1] = src[64:80]
        for g in range(3):
            colv[i, :, CV_GDQ + g] = np.tile(inp["dil_q_norm_g"][l][g], 2)
            colv[i, :, CV_GDK + g] = np.tile(inp["dil_k_norm_g"][l][g], 2)
        gbc[i] = np.broadcast_to(inp["norm_g"][l][None, :], (128, D))
    ind2 = np.zeros((128, 128), np.float32)
    ind2[0:64, 0:64] = 1.0
    ind2[64:128, 64:128] = 1.0
    ct, st = _rope_tables()
    shared = {
        "w_in": f(w_in[layers]), "w_uq": w_uq, "w_uq_sw": w_uq_sw, "w_ukv_k": w_ukv_k, "w_ukv_v": w_ukv_v,
        "w_kpe": w_kpe, "w_kpe_sw": w_kpe_sw,
        "w_out_a": f(inp["w_out_a"][layers]), "w_out_b": f(inp["w_out_b"][layers]), "w_out_c": f(inp["w_out_c"][layers]),
        "w_o": f(inp["w_o"][layers]), "colv": colv, "gbc": gbc,
        "ident": np.eye(128, dtype=np.float32), "ind2": ind2, "emask": _emask(), "ctab": ct, "stab": st,
    }
    return shared


_CACHE = {}


def _get_builder(nseq, nlayers, dbg=False):
    key = (nseq, nlayers, dbg)
    if key not in _CACHE:
        _CACHE[key] = Builder(nseq, nlayers, dbg=dbg)
    return _CACHE[key]


def kernel(**inputs):
    inp = {k: np.asarray(v) for k, v in inputs.items()}
    x = np.ascontiguousarray(inp["x"], dtype=np.float32)
    B = x.shape[0]
    ncores = 8
    nseq = B // ncores
    shared = _host_layout(inp, list(range(DEPTH)))
    bld = Builder(nseq, DEPTH)
    in_maps = []
    for c in range(ncores):
        m = dict(shared)
        m["x"] = x[c * nseq:(c + 1) * nseq].reshape(nseq * S, D)
        in_maps.append(m)
    res = run_bass_kernel_spmd(bld.nc, in_maps, core_ids=list(range(ncores)))
    outs = [np.asarray(r["out"]).reshape(nseq, S, D) for r in res.results]
    return np.concatenate(outs, axis=0).astype(np.float32)
```

```python
import numpy as np
import concourse.bass as bass
import concourse.mybir as mybir
from concourse.bass_utils import run_bass_kernel_spmd

F32 = mybir.dt.float32
BF16 = mybir.dt.bfloat16
AF = mybir.ActivationFunctionType
ALU = mybir.AluOpType

ENGS = ("pe", "act", "dve", "pool", "sp")

D = 1024
S = 2048
DEPTH = 2
NB = 16
NT = 4
TT = 512
EPS = 1e-6
N_IN = 11168
C_AB, C_AC, C_AX, C_AZ = 0, 512, 1024, 1536
C_CQ, C_CKV, C_KPE, C_BZ = 2048, 2304, 2432, 2464
C_DQ, C_DK, C_DV, C_CZ, C_GATE = 2976, 4512, 6048, 7584, 8096
DIL = (1, 4, 16)
CV_BG, CV_CW, CV_CB, CV_GQ, CV_GKV, CV_GQM, CV_GQMS, CV_GKM, CV_GKMS, CV_GDQ, CV_GDK, NCV = 0, 24, 36, 40, 42, 43, 44, 45, 46, 47, 50, 53


import heapq

RAW, WAR, WAW = 0, 1, 2
ACT_SET = {"Exp": "exp", "Sqrt": "sqrt", "Silu": "silu", "Sigmoid": "sigmoid"}


class Chan:
    def __init__(self, sem, name):
        self.sem = sem
        self.name = name
        self.count = 0


class Op:
    __slots__ = ("eng", "fn", "idx", "pidx", "deps", "succ", "ndeps", "waits", "signal", "sigcount", "chan",
                 "dma_deps", "dur", "tset", "ready", "finish", "seg", "isbar")

    def __init__(self, eng, fn):
        self.eng = eng
        self.fn = fn
        self.idx = -1
        self.pidx = -1
        self.deps = []
        self.succ = []
        self.ndeps = 0
        self.waits = {}
        self.dma_deps = []
        self.signal = False
        self.sigcount = 0
        self.chan = None
        self.dur = 0.5
        self.tset = None
        self.ready = 0.0
        self.finish = 0.0
        self.seg = 0
        self.isbar = False


class Prog:
    def __init__(self, nc):
        self.nc = nc
        self.ops = []
        self.kw = {}
        self.kr = {}
        self.sems = {e: nc.alloc_semaphore("sem_" + e) for e in ENGS}
        self.nops = 0
        self.chans = []
        self.seg = 0
        self.sched = True

    def chan(self, name):
        c = Chan(self.nc.alloc_semaphore("dsem_" + name), name)
        self.chans.append(c)
        return c

    def barrier(self):
        self.seg += 1

    def add(self, eng, fn, reads=(), writes=(), chan=None, dur=0.5, tset=None):
        op = Op(eng, fn)
        op.pidx = len(self.ops)
        op.seg = self.seg
        op.chan = chan
        op.dur = dur
        op.tset = tset
        self.ops.append(op)
        self.nops += 1
        deps = {}
        for k in reads:
            w = self.kw.get(k)
            if w is not None:
                deps[w] = RAW
        for k in writes:
            w = self.kw.get(k)
            if w is not None and w not in deps:
                deps[w] = WAW
            for r in self.kr.get(k, ()):
                if r is not op and r not in deps:
                    deps[r] = WAR
        for k in reads:
            self.kr.setdefault(k, []).append(op)
        for k in writes:
            self.kw[k] = op
            self.kr[k] = []
        op.deps = [(d, kind) for d, kind in deps.items() if d.seg == op.seg]
        return op

    def _schedule(self, ops):
        if not self.sched:
            return list(ops)
        for op in ops:
            op.ndeps = len(op.deps)
            op.succ = []
            op.ready = 0.0
        for op in ops:
            for d, _ in op.deps:
                d.succ.append(op)
        free = {e: 0.0 for e in ENGS}
        pending = {e: [] for e in ENGS}
        avail = {e: {} for e in ENGS}
        curset = [None]
        for op in ops:
            if op.ndeps == 0:
                heapq.heappush(pending[op.eng], (0.0, op.pidx, op))
        order = []
        LAT = 0.3
        n = len(ops)

        def cand(e):
            t = free[e]
            pend = pending[e]
            av = avail[e]
            while pend and pend[0][0] <= t:
                _, pi, o = heapq.heappop(pend)
                heapq.heappush(av.setdefault(o.tset if e == "act" else None, []), (pi, o))
            best = None
            if e == "act":
                for ts_ in (None, curset[0]):
                    h = av.get(ts_)
                    if h and (best is None or h[0][0] < best[0]):
                        best = (h[0][0], ts_)
                if best is None:
                    for ts_, h in av.items():
                        if h and (best is None or h[0][0] < best[0]):
                            best = (h[0][0], ts_)
            else:
                h = av.get(None)
                if h:
                    best = (h[0][0], None)
            if best is not None:
                return (t, best[0], best[1], False)
            if pend:
                return (pend[0][0], pend[0][1], None, True)
            return None

        while len(order) < n:
            bc = None
            be = None
            for e in ENGS:
                c = cand(e)
                if c is not None and (bc is None or (c[0], c[1]) < (bc[0], bc[1])):
                    bc, be = c, e
            assert bc is not None, "scheduler stuck"
            if bc[3]:
                _, pi, o = heapq.heappop(pending[be])
            else:
                pi, o = heapq.heappop(avail[be][bc[2]])
            start = max(bc[0], free[be])
            dur = o.dur
            if be == "act" and o.tset is not None and o.tset != curset[0]:
                dur += 2.7
                curset[0] = o.tset
            if o.chan is not None:
                free[be] = start + 0.15
                o.finish = start + 2.0 + dur
            else:
                free[be] = start + dur
                o.finish = start + dur
            order.append(o)
            for s_ in o.succ:
                s_.ndeps -= 1
                rt = o.finish + (LAT if s_.eng != o.eng else 0.05)
                if rt > s_.ready:
                    s_.ready = rt
                if s_.ndeps == 0:
                    heapq.heappush(pending[s_.eng], (s_.ready, s_.pidx, s_))
        return order

    def emit(self, final_chans=()):
        nc = self.nc
        nseg = self.seg + 1
        segs = [[] for _ in range(nseg)]
        for op in self.ops:
            segs[op.seg].append(op)
        glob = []
        for si in range(nseg):
            if si > 0:
                drains = []
                for e in ENGS:
                    if e == "sp":
                        continue
                    d = Op(e, lambda h: h.drain())
                    d.isbar = True
                    drains.append(d)
                    glob.append(d)
                for e in ENGS:
                    w = Op(e, lambda h: h.nop())
                    w.isbar = True
                    w.deps = [(d, RAW) for d in drains if d.eng != e]
                    w.dma_deps = "all"
                    glob.append(w)
            glob.extend(self._schedule(segs[si]))
        self.eng_ops = {e: [] for e in ENGS}
        for op in glob:
            op.idx = len(self.eng_ops[op.eng])
            self.eng_ops[op.eng].append(op)
        chan_cnt = {c: 0 for c in self.chans}
        dma_wait_vals = {}
        for op in glob:
            isdma = op.chan is not None
            dw = {}
            if op.dma_deps == "all":
                for c in self.chans:
                    if chan_cnt[c] > 0:
                        dw[c] = chan_cnt[c]
            for d, kind in op.deps:
                if d.chan is not None:
                    dw[d.chan] = chan_cnt[d.chan]
                    continue
                if d.eng == op.eng and not isdma and kind != RAW:
                    continue
                if d.eng == op.eng and d.eng == "pe":
                    continue
                cur = op.waits.get(d.eng)
                if cur is None or cur.idx < d.idx:
                    op.waits[d.eng] = d
            if isdma:
                chan_cnt[op.chan] += 1
            dma_wait_vals[op] = dw
        for op in glob:
            for p in op.waits.values():
                p.signal = True
        for e in ENGS:
            c = 0
            for op in self.eng_ops[e]:
                if op.signal:
                    c += 1
                op.sigcount = c
        final_cnt = {c: chan_cnt[c] for c in final_chans}

        def run_engine(e, h):
            seen = {}
            for op in self.eng_ops[e]:
                for pe_, p in op.waits.items():
                    need = p.sigcount
                    if seen.get(pe_, 0) < need:
                        h.wait_ge(self.sems[pe_], need)
                        seen[pe_] = need
                for c, cnt in dma_wait_vals[op].items():
                    if seen.get(c, 0) < cnt:
                        h.wait_ge(c.sem, 16 * cnt)
                        seen[c] = cnt
                ins = op.fn(h)
                if op.chan is not None:
                    ins.then_inc(op.chan.sem, 16)
                elif op.signal:
                    ins.then_inc(self.sems[e], 1)
            if e == "sp":
                for c, cnt in final_cnt.items():
                    h.wait_ge(c.sem, 16 * cnt)

        with nc.Block() as block:
            @block.tensor
            def _(h):
                run_engine("pe", h)

            @block.scalar
            def _(h):
                run_engine("act", h)

            @block.vector
            def _(h):
                run_engine("dve", h)

            @block.gpsimd
            def _(h):
                run_engine("pool", h)

            @block.sync
            def _(h):
                run_engine("sp", h)


class Arena:
    def __init__(self, nc, nbytes):
        self.cap = nbytes // 2
        self.t = nc.alloc_sbuf_tensor("arena", [128, self.cap], BF16).ap()
        self.top = 0

    def alloc(self, free_shape, dtype):
        n = 1
        for v in free_shape:
            n *= v
        ne = n * (2 if dtype == F32 else 1)
        ne_al = (ne + 31) // 32 * 32
        off = self.top
        self.top += ne_al
        assert self.top <= self.cap, ("arena overflow", self.top * 2, self.cap * 2)
        v = self.t[:, off:off + ne]
        if dtype == F32:
            v = v.bitcast(F32)
        if len(free_shape) > 1:
            names = " ".join("a%d" % i for i in range(len(free_shape)))
            kw = {"a%d" % i: free_shape[i] for i in range(len(free_shape))}
            v = v.rearrange("p (%s) -> p %s" % (names, names), **kw)
        return v


class Builder:
    def __init__(self, nseq, nlayers, layer0=0, dbg=False):
        self.nseq = nseq
        self.nlayers = nlayers
        self.dbg = dbg
        nc = bass.Bass("TRN2", target_bir_lowering=False)
        self.nc = nc
        P = Prog(nc)
        self.P = P
        L = nlayers
        dt = nc.dram_tensor
        self.x = dt("x", [nseq * S, D], F32, kind="ExternalInput").ap()
        self.out = dt("out", [nseq * S, D], F32, kind="ExternalOutput").ap()
        self.w_in = dt("w_in", [L, D, N_IN], F32, kind="ExternalInput").ap()
        self.w_uq = dt("w_uq", [L, 256, 768], F32, kind="ExternalInput").ap()
        self.w_uq_sw = dt("w_uq_sw", [L, 256, 768], F32, kind="ExternalInput").ap()
        self.w_ukv_k = dt("w_ukv_k", [L, 128, 512], F32, kind="ExternalInput").ap()
        self.w_ukv_v = dt("w_ukv_v", [L, 128, 512], F32, kind="ExternalInput").ap()
        self.w_kpe = dt("w_kpe", [L, D, 96], F32, kind="ExternalInput").ap()
        self.w_kpe_sw = dt("w_kpe_sw", [L, D, 96], F32, kind="ExternalInput").ap()
        self.w_out = [dt("w_out_%s" % n, [L, 512, D], F32, kind="ExternalInput").ap() for n in "abc"]
        self.w_o = dt("w_o", [L, D, D], F32, kind="ExternalInput").ap()
        self.colv_d = dt("colv", [L, 128, NCV], F32, kind="ExternalInput").ap()
        self.gbc_d = dt("gbc", [L, 128, D], F32, kind="ExternalInput").ap()
        self.ident_d = dt("ident", [128, 128], F32, kind="ExternalInput").ap()
        self.ind2_d = dt("ind2", [128, 128], F32, kind="ExternalInput").ap()
        self.emask_d = dt("emask", [128, 25 * 256], F32, kind="ExternalInput").ap()
        self.ct_d = dt("ctab", [96, S], F32, kind="ExternalInput").ap()
        self.st_d = dt("stab", [96, S], F32, kind="ExternalInput").ap()
        if nlayers > 1:
            self.scr = dt("scr", [nseq * S, D], F32).ap()
        if dbg:
            self.dbg_h = dt("dbg_h", [128, 8 * S], BF16, kind="ExternalOutput").ap()
            self.dbg_y = dt("dbg_y", [128, 12 * S], BF16, kind="ExternalOutput").ap()
        self.ps = [nc.alloc_psum_tensor("ps%d" % i, [128, 512], F32).ap() for i in range(8)]
        A = Arena(nc, 206 * 1024)
        self.A = A
        self.hT = A.alloc([8, S], BF16)
        self.yT = A.alloc([12, S], BF16)
        self.ident = A.alloc([128], BF16)
        self.ones = A.alloc([128], BF16)
        self.ind2 = A.alloc([128], BF16)
        self.onesf = A.alloc([64], F32)
        self.emask = A.alloc([25, 256], BF16)
        self.colv = A.alloc([NCV], F32)
        self.epsc = A.alloc([1], F32)
        self.NWR = 8
        self.wr = [A.alloc([8, 128], BF16) for _ in range(self.NWR)]
        self.wrc = [P.chan("wr%d" % i) for i in range(self.NWR)]
        self.wri = 0
        self.stage_base = A.top
        self.cst = P.chan("const")
        self.cout = P.chan("out")
        self.cmisc = {}
        self.build()

    def misc_chan(self, key):
        if key not in self.cmisc:
            self.cmisc[key] = self.P.chan("m%d" % len(self.cmisc))
        return self.cmisc[key]

    def dma(self, q, out, in_, key_w=None, key_r=None, chan=None):
        if chan is None:
            chan = self.misc_chan(key_w if key_w is not None else key_r)
        nbytes = out.free_size() * out.partition_size() * 4
        self.P.add(q, lambda h: h.dma_start(out=out, in_=in_),
                   reads=[key_r] if key_r else [], writes=[key_w] if key_w else [], chan=chan, dur=nbytes / 150e3)

    def mm(self, out, lhsT, rhs, start, stop, r, w):
        self.P.add("pe", lambda h: h.matmul(out, lhsT=lhsT, rhs=rhs, start=start, stop=stop, skip_group_check=True), r, w,
                   dur=0.064 + out.free_size() * 0.00045)

    def tr(self, out, in_, r, w):
        ident = self.ident
        self.P.add("pe", lambda h: h.transpose(out, in_, ident), list(r) + ["const"], w, dur=0.1)

    def act(self, out, in_, func, r, w, scale=None, bias=None, accum=None):
        kw = {}
        if scale is not None:
            kw["scale"] = scale
        if bias is not None:
            kw["bias"] = bias
        if accum is not None:
            kw["accum_out"] = accum
        self.P.add("act", lambda h: h.activation(out=out, in_=in_, func=func, **kw), r, w,
                   dur=0.22 + out.free_size() * 0.00075, tset=ACT_SET.get(func.name))

    def _dd(self, out):
        return 0.15 + out.free_size() * 0.0011

    def tt(self, out, in0, in1, op, r, w, eng="dve"):
        self.P.add(eng, lambda h: h.tensor_tensor(out=out, in0=in0, in1=in1, op=op), r, w, dur=self._dd(out))

    def stt(self, out, in0, scalar, in1, op0, op1, r, w, eng="dve"):
        self.P.add(eng, lambda h: h.scalar_tensor_tensor(out=out, in0=in0, scalar=scalar, in1=in1, op0=op0, op1=op1), r, w,
                   dur=self._dd(out))

    def ts(self, out, in0, s1, s2, op0, op1, r, w, eng="dve"):
        if s2 is None:
            self.P.add(eng, lambda h: h.tensor_scalar(out=out, in0=in0, scalar1=s1, scalar2=None, op0=op0), r, w, dur=self._dd(out))
        else:
            self.P.add(eng, lambda h: h.tensor_scalar(out=out, in0=in0, scalar1=s1, scalar2=s2, op0=op0, op1=op1), r, w,
                       dur=self._dd(out))

    def recip(self, out, in_, r, w):
        self.P.add("dve", lambda h: h.reciprocal(out=out, in_=in_), r, w, dur=self._dd(out))

    def cpy(self, out, in_, r, w, eng="dve"):
        self.P.add(eng, lambda h: h.tensor_copy(out=out, in_=in_), r, w, dur=self._dd(out))

    def memset(self, ap, val, w, eng="dve"):
        self.P.add(eng, lambda h: h.memset(ap, val), [], w, dur=self._dd(ap))

    def ldw(self, src):
        i = self.wri % self.NWR
        self.wri += 1
        self.dma("pool", self.wr[i], src, key_w="wr%d" % i, chan=self.wrc[i])
        return self.wr[i], "wr%d" % i

    def win_cols(self, l, c0, n):
        return self.w_in[l, :, c0:c0 + n].rearrange("(k p) n -> p k n", p=128)

    def rstd_from(self, ssq_ps, pkey, rows, inv_n, sd, rs):
        self.act(sd[0:rows], ssq_ps[0:rows], AF.Sqrt, ["const"], ["sd", pkey], scale=inv_n, bias=self.epsc[0:rows, 0:1])
        self.recip(rs[0:rows], sd[0:rows], ["sd"], ["rs"])

    def build(self):
        P = self.P
        self.dma("pool", self.ident, self.ident_d, key_w="const", chan=self.cst)
        self.dma("pool", self.ind2, self.ind2_d, key_w="const", chan=self.cst)
        self.dma("pool", self.emask, self.emask_d.rearrange("p (a b) -> p a b", a=25), key_w="const", chan=self.cst)
        self.memset(self.ones, 1.0, ["const"])
        self.memset(self.onesf, 1.0, ["const"])
        self.memset(self.epsc, EPS, ["const"])
        for l in range(self.nlayers):
            src = self.x if l == 0 else self.scr
            dst = self.out if l == self.nlayers - 1 else self.scr
            self.dma("sp", self.colv, self.colv_d[l], key_w="colv")
            for seq in range(self.nseq):
                self.A.top = self.stage_base
                self.P.barrier()
                self.stage0(l, seq, src)
                if self.dbg and l == 0 and seq == 0:
                    self.dma("sp", self.dbg_h.rearrange("p (a b) -> p a b", a=8), self.hT, key_r="hT", chan=self.cout)
                self.A.top = self.stage_base
                self.P.barrier()
                self.stage_mla(l, seq)
                self.A.top = self.stage_base
                self.P.barrier()
                self.stage_dil(l, seq)
                self.A.top = self.stage_base
                self.P.barrier()
                self.stage_conv(l, seq)
                if self.dbg and l == 0 and seq == 0:
                    self.P.add("sp", lambda h: h.dma_start(out=self.dbg_y.rearrange("p (a b) -> p a b", a=12), in_=self.yT), reads=["yT_a", "yT_b", "yT_c"], chan=self.cout)
                self.A.top = self.stage_base
                self.P.barrier()
                self.stage_out(l, seq, src, dst, final=(l == self.nlayers - 1))
        P.emit(final_chans=[self.cout] + ([self.cmisc["scrw"]] if "scrw" in self.cmisc else []))

    def stage0(self, l, seq, src):
        A = self.A
        xb = [A.alloc([D], F32) for _ in range(2)]
        hn = [A.alloc([D], BF16) for _ in range(2)]
        junk = A.alloc([D], BF16)
        gbc = A.alloc([D], F32)
        ss = A.alloc([1], F32)
        sd1 = A.alloc([1], F32)
        rs1 = A.alloc([1], F32)
        self.dma("sp", gbc, self.gbc_d[l], key_w="s0gbc")
        for b in range(NB):
            i = b % 2
            r0 = seq * S + b * 128
            self.dma("sp", xb[i], src[r0:r0 + 128, :], key_w="xb%d" % i, key_r=("scr" if l > 0 else None))
            self.memset(ss, 0.0, ["ss"])
            self.act(junk, xb[i], AF.Square, ["xb%d" % i], ["junk", "ss"], accum=ss)
            self.act(sd1, ss, AF.Sqrt, ["ss", "const"], ["sd1"], scale=1.0 / D, bias=self.epsc[:, 0:1])
            self.recip(rs1, sd1, ["sd1"], ["rs1"])
            self.stt(hn[i], xb[i], rs1[:, 0:1], gbc, ALU.mult, ALU.mult, ["xb%d" % i, "rs1", "s0gbc"], ["hn%d" % i])
            pst = self.ps[i].bitcast(BF16)
            for k in range(8):
                self.tr(pst[:, k * 128:(k + 1) * 128], hn[i][:, k * 128:(k + 1) * 128], ["hn%d" % i], ["ps%d" % i])
            self.act(self.hT[:, :, b * 128:(b + 1) * 128], pst.rearrange("p (k n) -> p k n", k=8), AF.Copy,
                     [], ["ps%d" % i, "hT"])

    def ring(self, n, shape, dtype, name):
        aps = [self.A.alloc(shape, dtype) for _ in range(n)]
        st = {"i": 0}

        def nxt():
            i = st["i"] % n
            st["i"] += 1
            return aps[i], "%s%d" % (name, i)
        return nxt

    def psring(self, banks):
        st = {"i": 0}

        def nxt():
            b = banks[st["i"] % len(banks)]
            st["i"] += 1
            return self.ps[b], "ps%d" % b
        return nxt

    def stage_mla(self, l, seq):
        A = self.A
        cv = self.colv
        Kc = self.yT[:, 4:12, :]
        KK = ["yT_c", "yT_a"]
        Va = A.alloc([NB, 8, 65], BF16)
        qT = A.alloc([8, TT], BF16)
        cqn = A.alloc([2, TT], BF16)
        ckvn = A.alloc([TT], BF16)
        sqpe = A.alloc([TT], BF16)
        kr = A.alloc([TT], F32)
        ctt = A.alloc([TT], F32)
        stt_ = A.alloc([TT], F32)
        cqf = A.alloc([TT], F32)
        sqf = A.alloc([TT], F32)
        ckf = A.alloc([TT], F32)
        skf = A.alloc([TT], F32)
        sbz = A.alloc([4, TT], BF16)
        r_sq = self.ring(2, [TT], BF16, "sq")
        r_sd = self.ring(2, [TT], F32, "sd")
        r_rs = self.ring(2, [TT], F32, "rs")
        r_ta = self.ring(2, [TT], F32, "ta")
        r_tb = self.ring(2, [TT], F32, "tb")
        r_pt = self.ring(3, [TT], BF16, "PT")
        r_osb = self.ring(2, [TT], F32, "osb")
        Wlat = A.alloc([8, 384], BF16)
        Wkpe = A.alloc([8, 96], BF16)
        Wkpes = A.alloc([8, 96], BF16)
        Wuq = A.alloc([2, 768], BF16)
        Wuqs = A.alloc([2, 768], BF16)
        Wk = A.alloc([512], BF16)
        Wv = A.alloc([512], BF16)
        pP = self.psring([0, 1, 2])
        pSS = self.psring([3, 4])
        pS_ = self.psring([6, 7])
        pO_ = self.psring([5])
        hT = self.hT
        self.dma("pool", Wlat, self.win_cols(l, C_CQ, 384), key_w="Wlat")
        self.dma("pool", Wkpe, self.w_kpe[l].rearrange("(k p) n -> p k n", p=128), key_w="Wkpe")
        self.dma("pool", Wkpes, self.w_kpe_sw[l].rearrange("(k p) n -> p k n", p=128), key_w="Wkpes")
        self.dma("pool", Wuq, self.w_uq[l].rearrange("(k p) n -> p k n", p=128), key_w="Wuq")
        self.dma("pool", Wuqs, self.w_uq_sw[l].rearrange("(k p) n -> p k n", p=128), key_w="Wuqs")
        self.dma("pool", Wk, self.w_ukv_k[l], key_w="Wk")
        self.dma("pool", Wv, self.w_ukv_v[l], key_w="Wv")
        self.memset(Va[:, :, :, 64:65], 1.0, ["Va"])
        sc_mla = 96.0 ** -0.5
        R = slice(64, 96)

        def rstd(ssq, kss, rows, inv_n):
            sd, ksd = r_sd()
            rs, krs = r_rs()
            self.act(sd[0:rows], ssq[0:rows], AF.Sqrt, ["const"], [ksd, kss], scale=inv_n, bias=self.epsc[0:rows, 0:1])
            self.recip(rs[0:rows], sd[0:rows], [ksd], [krs])
            return rs, krs

        for t in range(NT):
            T0 = t * TT
            hs = lambda k: hT[:, k, T0:T0 + TT]
            wbz = [self.ldw(self.win_cols(l, C_BZ + c * 128, 128)) for c in range(4)]
            self.dma("sp", ctt[0:96], self.ct_d[:, T0:T0 + TT], key_w="ctt")
            self.dma("sp", stt_[0:96], self.st_d[:, T0:T0 + TT], key_w="stt")
            self.ts(cqf[0:96], ctt[0:96], cv[0:96, CV_GQM:CV_GQM + 1], None, ALU.mult, None, ["ctt", "colv"], ["cqf"])
            self.ts(sqf[0:96], stt_[0:96], cv[0:96, CV_GQMS:CV_GQMS + 1], None, ALU.mult, None, ["stt", "colv"], ["sqf"])
            self.ts(ckf[0:96], ctt[0:96], cv[0:96, CV_GKM:CV_GKM + 1], None, ALU.mult, None, ["ctt", "colv"], ["ckf"])
            self.ts(skf[0:96], stt_[0:96], cv[0:96, CV_GKMS:CV_GKMS + 1], None, ALU.mult, None, ["stt", "colv"], ["skf"])
            pcq = [pP(), pP()]
            sqs = []
            for j in range(2):
                p_, k_ = pcq[j]
                for k in range(8):
                    self.mm(p_, Wlat[:, k, j * 128:(j + 1) * 128], hs(k), k == 0, k == 7, ["Wlat", "hT"], [k_])
                sq, ksq = r_sq()
                self.act(sq, p_, AF.Square, [], [ksq, k_])
                sqs.append((sq, ksq))
            ss, kss = pSS()
            self.mm(ss, self.ones, sqs[0][0], True, False, ["const", sqs[0][1]], [kss])
            self.mm(ss, self.ones, sqs[1][0], False, True, ["const", sqs[1][1]], [kss])
            rs, krs = rstd(ss, kss, 128, 1.0 / 256)
            for j in range(2):
                p_, k_ = pcq[j]
                self.stt(cqn[:, j, :], p_, cv[:, CV_GQ + j:CV_GQ + j + 1], rs, ALU.mult, ALU.mult, ["colv", krs], ["cqn", k_])
            p_, k_ = pP()
            for k in range(8):
                self.mm(p_, Wlat[:, k, 256:384], hs(k), k == 0, k == 7, ["Wlat", "hT"], [k_])
            sq, ksq = r_sq()
            self.act(sq, p_, AF.Square, [], [ksq, k_])
            ss, kss = pSS()
            self.mm(ss, self.ones, sq, True, True, ["const", ksq], [kss])
            rs, krs = rstd(ss, kss, 128, 1.0 / 128)
            self.stt(ckvn, p_, cv[:, CV_GKV:CV_GKV + 1], rs, ALU.mult, ALU.mult, ["colv", krs], ["ckvn", k_])
            p3, k3 = pP()
            for k in range(8):
                self.mm(p3[0:96], Wkpe[:, k, :], hs(k), k == 0, k == 7, ["Wkpe", "hT"], [k3])
            self.act(sqpe[R], p3[R], AF.Square, [], ["sqpe", k3])
            self.tt(kr[R], p3[R], ckf[R], ALU.mult, ["ckf"], ["kr", k3])
            p4, k4 = pP()
            for k in range(8):
                self.mm(p4[0:96], Wkpes[:, k, :], hs(k), k == 0, k == 7, ["Wkpes", "hT"], [k4])
            tb, ktb = r_tb()
            self.tt(tb[R], p4[R], skf[R], ALU.mult, ["skf"], [ktb, k4])
            self.tt(kr[R], kr[R], tb[R], ALU.add, [ktb, "kr"], ["kr"])
            for bb in range(4):
                blk = t * 4 + bb
                pv, pk = pP()
                self.mm(pv, ckvn[:, bb * 128:(bb + 1) * 128], Wv, True, True, ["ckvn", "Wv"], [pk])
                self.act(Va[:, blk, :, 0:64], pv.rearrange("p (h c) -> p h c", h=8), AF.Copy, [], ["Va", pk])
            for h in range(8):
                pK, pk = pP()
                self.mm(pK[0:64], Wk[:, h * 64:(h + 1) * 64], ckvn, True, True, ["Wk", "ckvn"], [pk])
                sq, ksq = r_sq()
                self.act(sq[0:64], pK[0:64], AF.Square, [], [ksq, pk])
                ss, kss = pSS()
                self.mm(ss[0:96], self.ones[0:64, 0:96], sq[0:64], True, False, ["const", ksq], [kss])
                self.mm(ss[0:96], self.ones[64:96, 0:96], sqpe[R], False, True, ["const", "sqpe"], [kss])
                rs, krs = rstd(ss, kss, 96, 1.0 / 96)
                self.stt(Kc[0:64, h, T0:T0 + TT], pK[0:64], cv[0:64, CV_GKM:CV_GKM + 1], rs[0:64], ALU.mult, ALU.mult,
                         ["colv", krs], KK + [pk])
                self.tt(Kc[R, h, T0:T0 + TT], kr[R], rs[R], ALU.mult, ["kr", krs], KK)
            for c in range(4):
                wt, wk = wbz[c]
                pb, pk = pP()
                for k in range(8):
                    self.mm(pb, wt[:, k, :], hs(k), k == 0, k == 7, [wk, "hT"], [pk])
                self.act(sbz[:, c, :], pb, AF.Silu, [], ["sbz", pk])
            for h in range(8):
                pQ, kq = pP()
                for j in range(2):
                    self.mm(pQ[0:96], Wuq[:, j, h * 96:(h + 1) * 96], cqn[:, j, :], j == 0, j == 1, ["Wuq", "cqn"], [kq])
                pQs, kqs = pP()
                for j in range(2):
                    self.mm(pQs[0:96], Wuqs[:, j, h * 96:(h + 1) * 96], cqn[:, j, :], j == 0, j == 1, ["Wuqs", "cqn"], [kqs])
                sq, ksq = r_sq()
                self.act(sq[0:96], pQ[0:96], AF.Square, [], [ksq, kq])
                ss, kss = pSS()
                self.mm(ss[0:96], self.ones[0:96, 0:96], sq[0:96], True, True, ["const", ksq], [kss])
                rs, krs = rstd(ss, kss, 96, 1.0 / 96)
                ta, kta = r_ta()
                tb, ktb = r_tb()
                self.tt(ta[0:96], pQ[0:96], cqf[0:96], ALU.mult, ["cqf"], [kta, kq])
                self.tt(tb[R], pQs[R], sqf[R], ALU.mult, ["sqf"], [ktb, kqs])
                self.tt(ta[R], ta[R], tb[R], ALU.add, [ktb, kta], [kta])
                self.tt(qT[0:96, h, :], ta[0:96], rs[0:96], ALU.mult, [kta, krs], ["qT%d" % h])
            for h in range(8):
                nkb = 4 * t + 4
                pO, ko = pO_()
                for kb in range(nkb):
                    c0 = 0 if kb < 4 * t else (kb - 4 * t) * 128
                    pS, ks = pS_()
                    pt, kpt = r_pt()
                    self.mm(pS[:, c0:TT], Kc[0:96, h, kb * 128:(kb + 1) * 128], qT[0:96, h, c0:TT], True, True,
                            KK + ["qT%d" % h], [ks])
                    self.act(pt[:, c0:TT], pS[:, c0:TT], AF.Exp, [], [kpt, ks], scale=sc_mla)
                    if kb >= 4 * t:
                        self.tt(pt[:, c0:c0 + 128], pt[:, c0:c0 + 128], self.emask[:, 24, 0:128], ALU.mult, ["const", kpt], [kpt])
                    self.mm(pO[0:65, c0:TT], Va[:, kb, h, :], pt[:, c0:TT], kb == 0, kb == nkb - 1, ["Va", kpt], [ko])
                osb, kos = r_osb()
                self.act(osb[0:65], pO[0:65], AF.Copy, [], [kos, ko])
                self.recip(osb[64:65], osb[64:65], [kos], [kos])
                pb, kb_ = pSS()
                self.mm(pb[0:64], self.onesf[64:65, 0:64], osb[64:65], True, True, ["const", kos], [kb_])
                ta, kta = r_ta()
                Rh = slice(0, 64) if h % 2 == 0 else slice(64, 128)
                self.tt(ta[Rh], osb[0:64], pb[0:64], ALU.mult, [kos], [kta, kb_])
                self.tt(self.yT[Rh, h // 2, T0:T0 + TT], ta[Rh], sbz[Rh, h // 2, :], ALU.mult, [kta, "sbz"], ["yT_b"])

    def stage_dil(self, l, seq):
        A = self.A
        cv = self.colv
        hT = self.hT
        r_dq = self.ring(2, [S], BF16, "dqT")
        r_dk = self.ring(2, [S], BF16, "dkT")
        r_dv = self.ring(2, [S], BF16, "dvT")
        r_dva = self.ring(2, [NB, 2, 65], BF16, "dva")
        r_acc = self.ring(2, [2, S], F32, "acc")
        r_scz = self.ring(2, [S], BF16, "scz")
        r_sq = self.ring(2, [TT], BF16, "sq")
        r_sd = self.ring(2, [TT], F32, "sd")
        r_rs = self.ring(2, [TT], F32, "rs")
        r_ta = self.ring(2, [TT], F32, "ta")
        r_pt = self.ring(4, [256], BF16, "dPT")
        pP = self.psring([2, 3, 4, 5])
        pS_ = self.psring([6, 7])
        pO_ = self.psring([0, 1])
        for _ in range(2):
            dva, kdva = r_dva()
            self.memset(dva[:, :, :, 64:65], 1.0, [kdva])

        def rstd(ssq, kss, rows, inv_n):
            sd, ksd = r_sd()
            rs, krs = r_rs()
            self.act(sd[0:rows], ssq[0:rows], AF.Sqrt, ["const"], [ksd, kss], scale=inv_n, bias=self.epsc[0:rows, 0:1])
            self.recip(rs[0:rows], sd[0:rows], [ksd], [krs])
            return rs, krs

        def perm_out(buf, g, t):
            if g == 0:
                return buf[:, t * TT:(t + 1) * TT], None
            if g == 1:
                return buf.rearrange("p (r m i) -> p r m i", r=4, m=4)[:, :, t, :], ("p (i r) -> p r i", 4)
            return buf.rearrange("p (r i) -> p r i", r=16)[:, :, 32 * t:32 * t + 32], ("p (i r) -> p r i", 16)

        def segments(g):
            segs = []
            for s_ in range(4):
                u = []
                if g == 0:
                    if s_ > 0:
                        u.append(((4 * s_ - 1) * 128, 4 * s_ - 1, 4 * s_ * 128, 128, 128))
                    for kb in range(4 * s_, 4 * s_ + 4):
                        u.append((kb * 128, kb, kb * 128, 256 if kb < 4 * s_ + 3 else 128, 0))
                elif g == 1:
                    for m in range(4):
                        blk = 4 * s_ + m
                        u.append((blk * 128, blk, blk * 128, 256 if m < 3 else 128, 0))
                else:
                    for c in range(4):
                        blk = 4 * s_ + c
                        u.append((blk * 128, blk, blk * 128, 128, 0))
                segs.append(u)
            return segs

        for j in range(4):
            scz, kscz = r_scz()
            acc, kacc = r_acc()
            wcz, kcz = self.ldw(self.win_cols(l, C_CZ + j * 128, 128))
            for t in range(NT):
                pc, kpc = pP()
                for k in range(8):
                    self.mm(pc, wcz[:, k, :], hT[:, k, t * TT:(t + 1) * TT], k == 0, k == 7, [kcz, "hT"], [kpc])
                self.act(scz[:, t * TT:(t + 1) * TT], pc, AF.Silu, [], [kscz, kpc])
            for g in range(3):
                c_off = g * 512 + j * 128
                wq, kwq = self.ldw(self.win_cols(l, C_DQ + c_off, 128))
                wk_, kwk = self.ldw(self.win_cols(l, C_DK + c_off, 128))
                wv, kwv = self.ldw(self.win_cols(l, C_DV + c_off, 128))
                dqT, kdq = r_dq()
                dkT, kdk = r_dk()
                dvT, kdv = r_dv()
                dva, kdva = r_dva()
                for t in range(NT):
                    hs = lambda k: hT[:, k, t * TT:(t + 1) * TT]
                    for (wt, wkey, dst, dkey, gcol) in ((wq, kwq, dqT, kdq, CV_GDQ + g), (wk_, kwk, dkT, kdk, CV_GDK + g)):
                        pp, pk = pP()
                        for k in range(8):
                            self.mm(pp, wt[:, k, :], hs(k), k == 0, k == 7, [wkey, "hT"], [pk])
                        sq, ksq = r_sq()
                        self.act(sq, pp, AF.Square, [], [ksq, pk])
                        ss, kss = pP()
                        self.mm(ss, self.ind2, sq, True, True, ["const", ksq], [kss])
                        rs, krs = rstd(ss, kss, 128, 1.0 / 64)
                        ov, pr = perm_out(dst, g, t)
                        if pr is None:
                            self.stt(ov, pp, cv[:, gcol:gcol + 1], rs, ALU.mult, ALU.mult, ["colv", krs], [dkey, pk])
                        else:
                            self.stt(ov, pp.rearrange(pr[0], r=pr[1]), cv[:, gcol:gcol + 1], rs.rearrange(pr[0], r=pr[1]),
                                     ALU.mult, ALU.mult, ["colv", krs], [dkey, pk])
                    pp, pk = pP()
                    for k in range(8):
                        self.mm(pp, wv[:, k, :], hs(k), k == 0, k == 7, [kwv, "hT"], [pk])
                    ov, pr = perm_out(dvT, g, t)
                    if pr is None:
                        self.act(ov, pp, AF.Copy, [], [kdv, pk])
                    else:
                        self.act(ov, pp.rearrange(pr[0], r=pr[1]), AF.Copy, [], [kdv, pk])
                for half in range(2):
                    pp, pk = pP()
                    pst = pp.bitcast(BF16)
                    for bb in range(8):
                        blk = half * 8 + bb
                        self.tr(pst[:, bb * 128:(bb + 1) * 128], dvT[:, blk * 128:(blk + 1) * 128], [kdv], [pk])
                    self.act(dva[:, half * 8:(half + 1) * 8, :, 0:64],
                             pst.rearrange("p (b h c) -> p b h c", b=8, h=2), AF.Copy, [], [kdva, pk])
                segs = segments(g)
                for hh in range(2):
                    Rr = slice(hh * 64, hh * 64 + 64)
                    em = self.emask[:, g * 8 + 2 * j + hh, :]
                    for b, units in enumerate(segs):
                        pO, ko = pO_()
                        first = True
                        for (K0, blk, q0, nq, m0) in units:
                            pS, ks = pS_()
                            pt, kpt = r_pt()
                            self.mm(pS[:, 0:nq], dkT[Rr, K0:K0 + 128], dqT[Rr, q0:q0 + nq], True, True, [kdk, kdq], [ks])
                            self.act(pt[:, 0:nq], pS[:, 0:nq], AF.Exp, [], [kpt, ks], scale=0.125)
                            self.tt(pt[:, 0:nq], pt[:, 0:nq], em[:, m0:m0 + nq], ALU.mult, ["const", kpt], [kpt])
                            self.mm(pO[0:65, q0 - b * 512:q0 - b * 512 + nq], dva[:, blk, hh, :], pt[:, 0:nq], first, False,
                                    [kdva, kpt], [ko])
                            first = False
                        if g == 0:
                            self.act(acc[0:65, hh, b * 512:(b + 1) * 512], pO[0:65], AF.Copy, [], [kacc, ko])
                        elif g == 1:
                            av = acc[0:65, hh, :].rearrange("p (m i r) -> p r m i", m=4, r=4)[:, b]
                            self.tt(av, av, pO[0:65].rearrange("p (m i) -> p m i", m=4), ALU.add, [kacc], [kacc, ko])
                        else:
                            av = acc[0:65, hh, :].rearrange("p (i r) -> p r i", r=16)[:, 4 * b:4 * b + 4, :]
                            self.tt(av, av, pO[0:65].rearrange("p (r i) -> p r i", r=4), ALU.add, [kacc], [kacc, ko])
            for hh in range(2):
                Rh = slice(hh * 64, hh * 64 + 64)
                self.recip(acc[64:65, hh, :], acc[64:65, hh, :], [kacc], [kacc])
                for t in range(NT):
                    cs = slice(t * TT, (t + 1) * TT)
                    pb, kb_ = pP()
                    self.mm(pb[0:64], self.onesf[64:65, 0:64], acc[64:65, hh, cs], True, True, ["const", kacc], [kb_])
                    ta, kta = r_ta()
                    self.tt(ta[Rh], acc[0:64, hh, cs], pb[0:64], ALU.mult, [kacc], [kta, kb_])
                    self.tt(self.yT[Rh, 4 + j, cs], ta[Rh], scz[Rh, cs], ALU.mult, [kta, kscz], ["yT_c"])

    def stage_conv(self, l, seq):
        A = self.A
        ps = self.ps
        cv = self.colv
        hT = self.hT
        r_up = self.ring(2, [S + 2], F32, "upad")
        r_acs = self.ring(2, [TT], F32, "acs")
        r_sz = self.ring(2, [TT], F32, "sz")
        r_c1 = self.ring(2, [TT], F32, "c1")
        r_c2 = self.ring(2, [TT], F32, "c2")
        n = 0
        for c in range(4):
            ws = [self.ldw(self.win_cols(l, base + c * 128, 128)) for base in (C_AB, C_AC, C_AX, C_AZ)]
            upad, kup = r_up()
            self.memset(upad[:, 0:2], 0.0, [kup])
            for t in range(NT):
                T0 = t * TT
                o = 4 * (n % 2)
                n += 1
                for i in range(4):
                    wt, wk = ws[i]
                    for k in range(8):
                        self.mm(ps[o + i], wt[:, k, :], hT[:, k, T0:T0 + TT], k == 0, k == 7, [wk, "hT"], ["ps%d" % (o + i)])
                kb_, kc_, kx_, kz_ = ["ps%d" % (o + i) for i in range(4)]
                acs, kacs = r_acs()
                sz, ksz = r_sz()
                c1, kc1 = r_c1()
                c2, kc2 = r_c2()
                self.act(acs, ps[o + 1], AF.Copy, [], [kacs, kc_])
                self.tt(upad[:, 2 + T0:2 + T0 + TT], acs, ps[o + 2], ALU.mult, [kacs], [kup, kx_])
                self.act(sz, ps[o + 3], AF.Silu, [], [ksz, kz_])
                self.ts(c1, upad[:, T0:T0 + TT], cv[:, CV_CW + c:CV_CW + c + 1], cv[:, CV_CB + c:CV_CB + c + 1], ALU.mult, ALU.add,
                        [kup, "colv"], [kc1])
                self.stt(c2, upad[:, T0 + 1:T0 + 1 + TT], cv[:, CV_CW + 4 + c:CV_CW + 5 + c], c1, ALU.mult, ALU.add,
                         [kup, "colv", kc1], [kc2])
                self.stt(c1, upad[:, T0 + 2:T0 + 2 + TT], cv[:, CV_CW + 8 + c:CV_CW + 9 + c], c2, ALU.mult, ALU.add,
                         [kup, "colv", kc2], [kc1])
                self.tt(c2, c1, ps[o + 0], ALU.mult, [kc1], [kc2, kb_])
                self.tt(self.yT[:, 8 + c, T0:T0 + TT], c2, sz, ALU.mult, [kc2, ksz], ["yT_a"])

    def stage_out(self, l, seq, src, dst, final):
        A = self.A
        ps = self.ps
        cv = self.colv
        hT = self.hT
        yT = self.yT
        Wo3 = [A.alloc([4, D], BF16) for _ in range(3)]
        Wo = A.alloc([8, D], BF16)
        G = A.alloc([24, TT], BF16)
        mT = A.alloc([8, TT], BF16)
        r_m1 = self.ring(2, [TT], F32, "m1")
        r_m2 = self.ring(2, [TT], F32, "m2")
        xb = [A.alloc([D], F32) for _ in range(2)]
        ob = [A.alloc([D], F32) for _ in range(2)]
        for i in range(3):
            self.dma("pool", Wo3[i], self.w_out[i][l].rearrange("(k p) n -> p k n", p=128), key_w="Wo3_%d" % i)
        self.dma("pool", Wo, self.w_o[l].rearrange("(k p) n -> p k n", p=128), key_w="Wo")
        och = self.cout if final else self.misc_chan("scrw")
        pG = self.psring([0, 1])
        for t in range(NT):
            T0 = t * TT
            DEPTHW = 4
            wq = [self.ldw(self.win_cols(l, C_GATE + jj * 128, 128)) for jj in range(DEPTHW)]
            for jj in range(24):
                wt, wk = wq[jj]
                pg, kg = pG()
                for k in range(8):
                    self.mm(pg, wt[:, k, :], hT[:, k, T0:T0 + TT], k == 0, k == 7, [wk, "hT"], [kg])
                self.act(G[:, jj, :], pg, AF.Sigmoid, ["colv"], ["G%d" % jj, kg], bias=cv[:, CV_BG + jj:CV_BG + jj + 1])
                if jj + DEPTHW < 24:
                    wq.append(self.ldw(self.win_cols(l, C_GATE + (jj + DEPTHW) * 128, 128)))
            for j in range(8):
                o = 2 + 3 * (j % 2)
                for i in range(3):
                    for k in range(4):
                        self.mm(ps[o + i], Wo3[i][:, k, j * 128:(j + 1) * 128], yT[:, (8, 0, 4)[i] + k, T0:T0 + TT], k == 0, k == 3,
                                ["Wo3_%d" % i, ("yT_a", "yT_b", "yT_c")[i]], ["ps%d" % (o + i)])
                m1, k1 = r_m1()
                m2, k2 = r_m2()
                self.tt(m1, ps[o], G[:, j, :], ALU.mult, ["G%d" % j], [k1, "ps%d" % o])
                self.tt(m2, ps[o + 1], G[:, 8 + j, :], ALU.mult, ["G%d" % (8 + j)], [k2, "ps%d" % (o + 1)])
                self.tt(m1, m1, m2, ALU.add, [k1, k2], [k1])
                self.tt(m2, ps[o + 2], G[:, 16 + j, :], ALU.mult, ["G%d" % (16 + j)], [k2, "ps%d" % (o + 2)])
                self.tt(mT[:, j, :], m1, m2, ALU.add, [k1, k2], ["mT%d" % j])
            for bb in range(4):
                b = t * 4 + bb
                i = b % 2
                r0 = seq * S + b * 128
                self.dma("sp", xb[i], src[r0:r0 + 128, :], key_w="oxb%d" % i, key_r=("scr" if l > 0 else None))
                for half in range(2):
                    pg, kg = pG()
                    for k in range(8):
                        self.mm(pg, mT[:, k, bb * 128:(bb + 1) * 128], Wo[:, k, half * 512:(half + 1) * 512], k == 0, k == 7,
                                ["mT%d" % k, "Wo"], [kg])
                    self.tt(ob[i][:, half * 512:(half + 1) * 512], pg, xb[i][:, half * 512:(half + 1) * 512], ALU.add,
                            ["oxb%d" % i], ["ob%d" % i, kg])
                self.dma("sp", dst[r0:r0 + 128, :], ob[i], key_r="ob%d" % i, key_w=(None if final else "scr"), chan=och)


def _rope_tables():
    inv = (10000.0 ** (-np.arange(0, 32, 2, dtype=np.float32) / 32)).astype(np.float32)
    ang = np.arange(S, dtype=np.float32)[:, None] * inv[None, :]
    cos, sin = np.cos(ang).astype(np.float32), np.sin(ang).astype(np.float32)
    ct = np.ones((96, S), np.float32)
    st = np.zeros((96, S), np.float32)
    ct[64:80] = cos.T
    ct[80:96] = cos.T
    st[64:80] = -sin.T
    st[80:96] = sin.T
    return ct, st


def _emask():
    n = 24
    slopes = (2.0 ** (-8.0 * np.arange(1, n + 1, dtype=np.float32) / n)).reshape(3, 8)
    k = np.arange(128)[:, None].astype(np.float32)
    q = np.arange(128)[None, :].astype(np.float32)
    em = np.zeros((128, 25, 256), np.float32)
    for g in range(3):
        for h in range(8):
            sl = slopes[g, h] * DIL[g]
            em[:, g * 8 + h, 0:128] = np.where(k <= q, np.exp(-sl * np.maximum(q - k, 0.0)), 0.0)
            if g < 2:
                em[:, g * 8 + h, 128:256] = np.where(k >= q, np.exp(-sl * np.maximum(128.0 + q - k, 0.0)), 0.0)
    em[:, 24, 0:128] = (k <= q).astype(np.float32)
    return em.reshape(128, 25 * 256)


def _host_layout(inp, layers):
    f = lambda a: np.ascontiguousarray(a, dtype=np.float32)
    L = len(layers)
    w_uq = f(inp["w_uq"][layers])
    w_uq_sw = np.zeros_like(w_uq)
    for h in range(8):
        b = h * 96
        w_uq_sw[:, :, b + 64:b + 80] = w_uq[:, :, b + 80:b + 96]
        w_uq_sw[:, :, b + 80:b + 96] = w_uq[:, :, b + 64:b + 80]
    w_ukv = f(inp["w_ukv"][layers]).reshape(L, 128, 8, 128)
    w_ukv_k = f(w_ukv[:, :, :, 0:64].reshape(L, 128, 512))
    w_ukv_v = f(w_ukv[:, :, :, 64:128].reshape(L, 128, 512))
    w_in = inp["w_in"]
    w_kpe = np.zeros((L, D, 96), np.float32)
    w_kpe_sw = np.zeros((L, D, 96), np.float32)
    for i, l in enumerate(layers):
        kp = w_in[l][:, C_KPE:C_KPE + 32]
        w_kpe[i, :, 64:96] = kp
        w_kpe_sw[i, :, 64:80] = kp[:, 16:32]
        w_kpe_sw[i, :, 80:96] = kp[:, 0:16]
    colv = np.zeros((L, 128, NCV), np.float32)
    gbc = np.zeros((L, 128, D), np.float32)
    for i, l in enumerate(layers):
        colv[i, :, CV_BG:CV_BG + 24] = inp["b_gate"][l].reshape(24, 128).T
        for tap in range(3):
            colv[i, :, CV_CW + tap * 4:CV_CW + tap * 4 + 4] = inp["conv_w"][l][tap].reshape(4, 128).T
        colv[i, :, CV_CB:CV_CB + 4] = inp["conv_b"][l].reshape(4, 128).T
        colv[i, :, CV_GQ:CV_GQ + 2] = inp["q_a_norm_g"][l].reshape(2, 128).T
        colv[i, :, CV_GKV] = inp["kv_a_norm_g"][l]
        for (col, src) in ((CV_GQM, inp["mla_q_norm_g"][l]), (CV_GKM, inp["mla_k_norm_g"][l])):
            colv[i, 0:96, col] = src
            colv[i, 64:80, col + 1] = src[80:96]
            colv[i, 80:96, col + 1] = src[64:80]
        for g in range(3):
            colv[i, :, CV_GDQ + g] = np.tile(inp["dil_q_norm_g"][l][g], 2)
            colv[i, :, CV_GDK + g] = np.tile(inp["dil_k_norm_g"][l][g], 2)
        gbc[i] = np.broadcast_to(inp["norm_g"][l][None, :], (128, D))
    ind2 = np.zeros((128, 128), np.float32)
    ind2[0:64, 0:64] = 1.0
    ind2[64:128, 64:128] = 1.0
    ct, st = _rope_tables()
    shared = {
        "w_in": f(w_in[layers]), "w_uq": w_uq, "w_uq_sw": w_uq_sw, "w_ukv_k": w_ukv_k, "w_ukv_v": w_ukv_v,
        "w_kpe": w_kpe, "w_kpe_sw": w_kpe_sw,
        "w_out_a": f(inp["w_out_a"][layers]), "w_out_b": f(inp["w_out_b"][layers]), "w_out_c": f(inp["w_out_c"][layers]),
        "w_o": f(inp["w_o"][layers]), "colv": colv, "gbc": gbc,
        "ident": np.eye(128, dtype=np.float32), "ind2": ind2, "emask": _emask(), "ctab": ct, "stab": st,
    }
    return shared


_CACHE = {}


def _get_builder(nseq, nlayers, dbg=False):
    key = (nseq, nlayers, dbg)
    if key not in _CACHE:
        _CACHE[key] = Builder(nseq, nlayers, dbg=dbg)
    return _CACHE[key]


def kernel(**inputs):
    inp = {k: np.asarray(v) for k, v in inputs.items()}
    x = np.ascontiguousarray(inp["x"], dtype=np.float32)
    B = x.shape[0]
    ncores = 8
    nseq = B // ncores
    shared = _host_layout(inp, list(range(DEPTH)))
    bld = Builder(nseq, DEPTH)
    in_maps = []
    for c in range(ncores):
        m = dict(shared)
        m["x"] = x[c * nseq:(c + 1) * nseq].reshape(nseq * S, D)
        in_maps.append(m)
    res = run_bass_kernel_spmd(bld.nc, in_maps, core_ids=list(range(ncores)))
    outs = [np.asarray(r["out"]).reshape(nseq, S, D) for r in res.results]
    return np.concatenate(outs, axis=0).astype(np.float32)
```

```python
import numpy as np
import concourse.bass as bass
import concourse.mybir as mybir
from concourse.bass_utils import run_bass_kernel_spmd

F32 = mybir.dt.float32
BF16 = mybir.dt.bfloat16
AF = mybir.ActivationFunctionType
ALU = mybir.AluOpType

ENGS = ("pe", "act", "dve", "pool", "sp")

D = 1024
S = 2048
DEPTH = 2
NB = 16
NT = 4
TT = 512
EPS = 1e-6
N_IN = 11168
C_AB, C_AC, C_AX, C_AZ = 0, 512, 1024, 1536
C_CQ, C_CKV, C_KPE, C_BZ = 2048, 2304, 2432, 2464
C_DQ, C_DK, C_DV, C_CZ, C_GATE = 2976, 4512, 6048, 7584, 8096
DIL = (1, 4, 16)
CV_BG, CV_CW, CV_CB, CV_GQ, CV_GKV, CV_GQM, CV_GQMS, CV_GKM, CV_GKMS, CV_GDQ, CV_GDK, NCV = 0, 24, 36, 40, 42, 43, 44, 45, 46, 47, 50, 53


import heapq

RAW, WAR, WAW = 0, 1, 2
ACT_SET = {"Exp": "exp", "Ln": "exp", "Sqrt": "sqrt", "Silu": "silu", "Sigmoid": "sigmoid"}


class Chan:
    def __init__(self, sem, name):
        self.sem = sem
        self.name = name
        self.count = 0


class Op:
    __slots__ = ("eng", "fn", "idx", "pidx", "deps", "succ", "ndeps", "waits", "signal", "sigcount", "chan",
                 "dma_deps", "dur", "tset", "ready", "finish", "seg", "isbar")

    def __init__(self, eng, fn):
        self.eng = eng
        self.fn = fn
        self.idx = -1
        self.pidx = -1
        self.deps = []
        self.succ = []
        self.ndeps = 0
        self.waits = {}
        self.dma_deps = []
        self.signal = False
        self.sigcount = 0
        self.chan = None
        self.dur = 0.5
        self.tset = None
        self.ready = 0.0
        self.finish = 0.0
        self.seg = 0
        self.isbar = False


class Prog:
    def __init__(self, nc):
        self.nc = nc
        self.ops = []
        self.kw = {}
        self.kr = {}
        self.sems = {e: nc.alloc_semaphore("sem_" + e) for e in ENGS}
        self.nops = 0
        self.chans = []
        self.seg = 0
        self.sched = True

    def chan(self, name):
        c = Chan(self.nc.alloc_semaphore("dsem_" + name), name)
        self.chans.append(c)
        return c

    def barrier(self):
        self.seg += 1

    def add(self, eng, fn, reads=(), writes=(), chan=None, dur=0.5, tset=None):
        op = Op(eng, fn)
        op.pidx = len(self.ops)
        op.seg = self.seg
        op.chan = chan
        op.dur = dur
        op.tset = tset
        self.ops.append(op)
        self.nops += 1
        deps = {}
        for k in reads:
            w = self.kw.get(k)
            if w is not None:
                deps[w] = RAW
        for k in writes:
            w = self.kw.get(k)
            if w is not None and w not in deps:
                deps[w] = WAW
            for r in self.kr.get(k, ()):
                if r is not op and r not in deps:
                    deps[r] = WAR
        for k in reads:
            self.kr.setdefault(k, []).append(op)
        for k in writes:
            self.kw[k] = op
            self.kr[k] = []
        op.deps = [(d, kind) for d, kind in deps.items() if d.seg == op.seg]
        return op

    def _schedule(self, ops):
        if not self.sched:
            return list(ops)
        for op in ops:
            op.ndeps = len(op.deps)
            op.succ = []
            op.ready = 0.0
        for op in ops:
            for d, _ in op.deps:
                d.succ.append(op)
        free = {e: 0.0 for e in ENGS}
        pending = {e: [] for e in ENGS}
        avail = {e: {} for e in ENGS}
        curset = [None]
        for op in ops:
            if op.ndeps == 0:
                heapq.heappush(pending[op.eng], (0.0, op.pidx, op))
        order = []
        LAT = 0.3
        n = len(ops)

        def cand(e):
            t = free[e]
            pend = pending[e]
            av = avail[e]
            while pend and pend[0][0] <= t:
                _, pi, o = heapq.heappop(pend)
                heapq.heappush(av.setdefault(o.tset if e == "act" else None, []), (pi, o))
            best = None
            if e == "act":
                for ts_ in (None, curset[0]):
                    h = av.get(ts_)
                    if h and (best is None or h[0][0] < best[0]):
                        best = (h[0][0], ts_)
                if best is None:
                    for ts_, h in av.items():
                        if h and (best is None or h[0][0] < best[0]):
                            best = (h[0][0], ts_)
            else:
                h = av.get(None)
                if h:
                    best = (h[0][0], None)
            if best is not None:
                return (t, best[0], best[1], False)
            if pend:
                return (pend[0][0], pend[0][1], None, True)
            return None

        while len(order) < n:
            bc = None
            be = None
            for e in ENGS:
                c = cand(e)
                if c is not None and (bc is None or (c[0], c[1]) < (bc[0], bc[1])):
                    bc, be = c, e
            assert bc is not None, "scheduler stuck"
            if bc[3]:
                _, pi, o = heapq.heappop(pending[be])
            else:
                pi, o = heapq.heappop(avail[be][bc[2]])
            start = max(bc[0], free[be])
            dur = o.dur
            if be == "act" and o.tset is not None and o.tset != curset[0]:
                dur += 2.7
                curset[0] = o.tset
            if o.chan is not None:
                free[be] = start + 0.15
                o.finish = start + 2.0 + dur
            else:
                free[be] = start + dur
                o.finish = start + dur
            order.append(o)
            for s_ in o.succ:
                s_.ndeps -= 1
                rt = o.finish + (LAT if s_.eng != o.eng else 0.05)
                if rt > s_.ready:
                    s_.ready = rt
                if s_.ndeps == 0:
                    heapq.heappush(pending[s_.eng], (s_.ready, s_.pidx, s_))
        return order

    def emit(self, final_chans=()):
        nc = self.nc
        nseg = self.seg + 1
        segs = [[] for _ in range(nseg)]
        for op in self.ops:
            segs[op.seg].append(op)
        glob = []
        for si in range(nseg):
            if si > 0:
                drains = []
                for e in ENGS:
                    if e == "sp":
                        continue
                    d = Op(e, lambda h: h.drain())
                    d.isbar = True
                    drains.append(d)
                    glob.append(d)
                for e in ENGS:
                    w = Op(e, lambda h: h.nop())
                    w.isbar = True
                    w.deps = [(d, RAW) for d in drains if d.eng != e]
                    w.dma_deps = "all"
                    glob.append(w)
            glob.extend(self._schedule(segs[si]))
        self.eng_ops = {e: [] for e in ENGS}
        for op in glob:
            op.idx = len(self.eng_ops[op.eng])
            self.eng_ops[op.eng].append(op)
        chan_cnt = {c: 0 for c in self.chans}
        dma_wait_vals = {}
        for op in glob:
            isdma = op.chan is not None
            dw = {}
            if op.dma_deps == "all":
                for c in self.chans:
                    if chan_cnt[c] > 0:
                        dw[c] = chan_cnt[c]
            for d, kind in op.deps:
                if d.chan is not None:
                    dw[d.chan] = chan_cnt[d.chan]
                    continue
                if d.eng == op.eng and not isdma and kind != RAW:
                    continue
                if d.eng == op.eng and d.eng == "pe":
                    continue
                cur = op.waits.get(d.eng)
                if cur is None or cur.idx < d.idx:
                    op.waits[d.eng] = d
            if isdma:
                chan_cnt[op.chan] += 1
            dma_wait_vals[op] = dw
        for op in glob:
            for p in op.waits.values():
                p.signal = True
        for e in ENGS:
            c = 0
            for op in self.eng_ops[e]:
                if op.signal:
                    c += 1
                op.sigcount = c
        final_cnt = {c: chan_cnt[c] for c in final_chans}

        def run_engine(e, h):
            seen = {}
            for op in self.eng_ops[e]:
                for pe_, p in op.waits.items():
                    need = p.sigcount
                    if seen.get(pe_, 0) < need:
                        h.wait_ge(self.sems[pe_], need)
                        seen[pe_] = need
                for c, cnt in dma_wait_vals[op].items():
                    if seen.get(c, 0) < cnt:
                        h.wait_ge(c.sem, 16 * cnt)
                        seen[c] = cnt
                ins = op.fn(h)
                if op.chan is not None:
                    ins.then_inc(op.chan.sem, 16)
                elif op.signal:
                    ins.then_inc(self.sems[e], 1)
            if e == "sp":
                for c, cnt in final_cnt.items():
                    h.wait_ge(c.sem, 16 * cnt)

        with nc.Block() as block:
            @block.tensor
            def _(h):
                run_engine("pe", h)

            @block.scalar
            def _(h):
                run_engine("act", h)

            @block.vector
            def _(h):
                run_engine("dve", h)

            @block.gpsimd
            def _(h):
                run_engine("pool", h)

            @block.sync
            def _(h):
                run_engine("sp", h)


class Arena:
    def __init__(self, nc, nbytes):
        self.cap = nbytes // 2
        self.t = nc.alloc_sbuf_tensor("arena", [128, self.cap], BF16).ap()
        self.top = 0

    def alloc(self, free_shape, dtype):
        n = 1
        for v in free_shape:
            n *= v
        ne = n * (2 if dtype == F32 else 1)
        ne_al = (ne + 31) // 32 * 32
        off = self.top
        self.top += ne_al
        assert self.top <= self.cap, ("arena overflow", self.top * 2, self.cap * 2)
        v = self.t[:, off:off + ne]
        if dtype == F32:
            v = v.bitcast(F32)
        if len(free_shape) > 1:
            names = " ".join("a%d" % i for i in range(len(free_shape)))
            kw = {"a%d" % i: free_shape[i] for i in range(len(free_shape))}
            v = v.rearrange("p (%s) -> p %s" % (names, names), **kw)
        return v


class Builder:
    def __init__(self, nseq, nlayers, layer0=0, dbg=False):
        self.nseq = nseq
        self.nlayers = nlayers
        self.dbg = dbg
        nc = bass.Bass("TRN2", target_bir_lowering=False)
        self.nc = nc
        P = Prog(nc)
        self.P = P
        L = nlayers
        dt = nc.dram_tensor
        self.x = dt("x", [nseq * S, D], F32, kind="ExternalInput").ap()
        self.out = dt("out", [nseq * S, D], F32, kind="ExternalOutput").ap()
        self.w_in = dt("w_in", [L, D, N_IN], F32, kind="ExternalInput").ap()
        self.w_uq = dt("w_uq", [L, 256, 768], F32, kind="ExternalInput").ap()
        self.w_uq_sw = dt("w_uq_sw", [L, 256, 768], F32, kind="ExternalInput").ap()
        self.w_ukv_k = dt("w_ukv_k", [L, 128, 512], F32, kind="ExternalInput").ap()
        self.w_ukv_v = dt("w_ukv_v", [L, 128, 512], F32, kind="ExternalInput").ap()
        self.w_kpe = dt("w_kpe", [L, D, 96], F32, kind="ExternalInput").ap()
        self.w_kpe_sw = dt("w_kpe_sw", [L, D, 96], F32, kind="ExternalInput").ap()
        self.w_out = [dt("w_out_%s" % n, [L, 512, D], F32, kind="ExternalInput").ap() for n in "abc"]
        self.w_o = dt("w_o", [L, D, D], F32, kind="ExternalInput").ap()
        self.colv_d = dt("colv", [L, 128, NCV], F32, kind="ExternalInput").ap()
        self.gbc_d = dt("gbc", [L, 128, D], F32, kind="ExternalInput").ap()
        self.ident_d = dt("ident", [128, 128], F32, kind="ExternalInput").ap()
        self.ind2_d = dt("ind2", [128, 128], F32, kind="ExternalInput").ap()
        self.emask_d = dt("emask", [128, 25 * 256], F32, kind="ExternalInput").ap()
        self.ct_d = dt("ctab", [96, S], F32, kind="ExternalInput").ap()
        self.st_d = dt("stab", [96, S], F32, kind="ExternalInput").ap()
        if nlayers > 1:
            self.scr = dt("scr", [nseq * S, D], F32).ap()
        if dbg:
            self.dbg_h = dt("dbg_h", [128, 8 * S], BF16, kind="ExternalOutput").ap()
            self.dbg_y = dt("dbg_y", [128, 12 * S], BF16, kind="ExternalOutput").ap()
        self.ps = [nc.alloc_psum_tensor("ps%d" % i, [128, 512], F32).ap() for i in range(8)]
        A = Arena(nc, 206 * 1024)
        self.A = A
        self.hT = A.alloc([8, S], BF16)
        self.yT = A.alloc([12, S], BF16)
        self.ident = A.alloc([128], BF16)
        self.ones = A.alloc([128], BF16)
        self.ind2 = A.alloc([128], BF16)
        self.onesf = A.alloc([64], F32)
        self.emask = A.alloc([25, 256], BF16)
        self.colv = A.alloc([NCV], F32)
        self.epsc = A.alloc([1], F32)
        self.NWR = 8
        self.wr = [A.alloc([8, 128], BF16) for _ in range(self.NWR)]
        self.wrc = [P.chan("wr%d" % i) for i in range(self.NWR)]
        self.wri = 0
        self.stage_base = A.top
        self.cst = P.chan("const")
        self.cout = P.chan("out")
        self.cmisc = {}
        self.build()

    def misc_chan(self, key):
        if key not in self.cmisc:
            self.cmisc[key] = self.P.chan("m%d" % len(self.cmisc))
        return self.cmisc[key]

    def dma(self, q, out, in_, key_w=None, key_r=None, chan=None):
        if chan is None:
            chan = self.misc_chan(key_w if key_w is not None else key_r)
        nbytes = out.free_size() * out.partition_size() * 4
        self.P.add(q, lambda h: h.dma_start(out=out, in_=in_),
                   reads=[key_r] if key_r else [], writes=[key_w] if key_w else [], chan=chan, dur=nbytes / 150e3)

    def mm(self, out, lhsT, rhs, start, stop, r, w):
        self.P.add("pe", lambda h: h.matmul(out, lhsT=lhsT, rhs=rhs, start=start, stop=stop, skip_group_check=True), r, w,
                   dur=0.064 + out.free_size() * 0.00045)

    def tr(self, out, in_, r, w):
        ident = self.ident
        self.P.add("pe", lambda h: h.transpose(out, in_, ident), list(r) + ["const"], w, dur=0.1)

    def act(self, out, in_, func, r, w, scale=None, bias=None, accum=None):
        kw = {}
        if scale is not None:
            kw["scale"] = scale
        if bias is not None:
            kw["bias"] = bias
        if accum is not None:
            kw["accum_out"] = accum
        self.P.add("act", lambda h: h.activation(out=out, in_=in_, func=func, **kw), r, w,
                   dur=0.22 + out.free_size() * 0.00075, tset=ACT_SET.get(func.name))

    def _dd(self, out, eng="dve"):
        if eng == "pool":
            return 0.25 + out.free_size() * 0.002
        return 0.15 + out.free_size() * 0.0011

    def tt(self, out, in0, in1, op, r, w, eng="dve"):
        self.P.add(eng, lambda h: h.tensor_tensor(out=out, in0=in0, in1=in1, op=op), r, w, dur=self._dd(out, eng))

    def stt(self, out, in0, scalar, in1, op0, op1, r, w, eng="dve"):
        self.P.add(eng, lambda h: h.scalar_tensor_tensor(out=out, in0=in0, scalar=scalar, in1=in1, op0=op0, op1=op1), r, w,
                   dur=self._dd(out, eng))

    def ts(self, out, in0, s1, s2, op0, op1, r, w, eng="dve"):
        if s2 is None:
            self.P.add(eng, lambda h: h.tensor_scalar(out=out, in0=in0, scalar1=s1, scalar2=None, op0=op0), r, w,
                       dur=self._dd(out, eng))
        else:
            self.P.add(eng, lambda h: h.tensor_scalar(out=out, in0=in0, scalar1=s1, scalar2=s2, op0=op0, op1=op1), r, w,
                       dur=self._dd(out, eng))

    def recip(self, out, in_, r, w):
        self.P.add("dve", lambda h: h.reciprocal(out=out, in_=in_), r, w, dur=self._dd(out))

    def cpy(self, out, in_, r, w, eng="dve"):
        self.P.add(eng, lambda h: h.tensor_copy(out=out, in_=in_), r, w, dur=self._dd(out))

    def memset(self, ap, val, w, eng="dve"):
        self.P.add(eng, lambda h: h.memset(ap, val), [], w, dur=self._dd(ap))

    def ldw(self, src):
        i = self.wri % self.NWR
        self.wri += 1
        self.dma("pool", self.wr[i], src, key_w="wr%d" % i, chan=self.wrc[i])
        return self.wr[i], "wr%d" % i

    def win_cols(self, l, c0, n):
        return self.w_in[l, :, c0:c0 + n].rearrange("(k p) n -> p k n", p=128)

    def rstd_from(self, ssq_ps, pkey, rows, inv_n, sd, rs):
        self.act(sd[0:rows], ssq_ps[0:rows], AF.Sqrt, ["const"], ["sd", pkey], scale=inv_n, bias=self.epsc[0:rows, 0:1])
        self.recip(rs[0:rows], sd[0:rows], ["sd"], ["rs"])

    def build(self):
        P = self.P
        self.dma("pool", self.ident, self.ident_d, key_w="const", chan=self.cst)
        self.dma("pool", self.ind2, self.ind2_d, key_w="const", chan=self.cst)
        self.dma("pool", self.emask, self.emask_d.rearrange("p (a b) -> p a b", a=25), key_w="const", chan=self.cst)
        self.memset(self.ones, 1.0, ["const"])
        self.memset(self.onesf, 1.0, ["const"])
        self.memset(self.epsc, EPS, ["const"])
        for l in range(self.nlayers):
            src = self.x if l == 0 else self.scr
            dst = self.out if l == self.nlayers - 1 else self.scr
            self.dma("sp", self.colv, self.colv_d[l], key_w="colv")
            for seq in range(self.nseq):
                self.A.top = self.stage_base
                self.P.barrier()
                self.stage0(l, seq, src)
                if self.dbg and l == 0 and seq == 0:
                    self.dma("sp", self.dbg_h.rearrange("p (a b) -> p a b", a=8), self.hT, key_r="hT", chan=self.cout)
                self.A.top = self.stage_base
                self.P.barrier()
                self.stage_mla(l, seq)
                self.A.top = self.stage_base
                self.P.barrier()
                self.stage_dil(l, seq)
                self.A.top = self.stage_base
                self.P.barrier()
                self.stage_conv(l, seq)
                if self.dbg and l == 0 and seq == 0:
                    self.P.add("sp", lambda h: h.dma_start(out=self.dbg_y.rearrange("p (a b) -> p a b", a=12), in_=self.yT), reads=["yT_a", "yT_b", "yT_c"], chan=self.cout)
                self.A.top = self.stage_base
                self.P.barrier()
                self.stage_out(l, seq, src, dst, final=(l == self.nlayers - 1))
        P.emit(final_chans=[self.cout] + ([self.cmisc["scrw"]] if "scrw" in self.cmisc else []))

    def stage0(self, l, seq, src):
        A = self.A
        xb = [A.alloc([D], F32) for _ in range(2)]
        hn = [A.alloc([D], BF16) for _ in range(2)]
        junk = A.alloc([D], BF16)
        gbc = A.alloc([D], F32)
        ss = A.alloc([1], F32)
        sd1 = A.alloc([1], F32)
        rs1 = A.alloc([1], F32)
        self.dma("sp", gbc, self.gbc_d[l], key_w="s0gbc")
        for b in range(NB):
            i = b % 2
            r0 = seq * S + b * 128
            self.dma("sp", xb[i], src[r0:r0 + 128, :], key_w="xb%d" % i, key_r=("scr" if l > 0 else None))
            self.memset(ss, 0.0, ["ss"])
            self.act(junk, xb[i], AF.Square, ["xb%d" % i], ["junk", "ss"], accum=ss)
            self.act(sd1, ss, AF.Ln, ["ss", "const"], ["sd1"], scale=1.0 / D, bias=self.epsc[:, 0:1])
            self.act(rs1, sd1, AF.Exp, ["sd1"], ["rs1"], scale=-0.5)
            self.stt(hn[i], xb[i], rs1[:, 0:1], gbc, ALU.mult, ALU.mult, ["xb%d" % i, "rs1", "s0gbc"], ["hn%d" % i])
            pst = self.ps[i].bitcast(BF16)
            for k in range(8):
                self.tr(pst[:, k * 128:(k + 1) * 128], hn[i][:, k * 128:(k + 1) * 128], ["hn%d" % i], ["ps%d" % i])
            self.act(self.hT[:, :, b * 128:(b + 1) * 128], pst.rearrange("p (k n) -> p k n", k=8), AF.Copy,
                     [], ["ps%d" % i, "hT"])

    def ring(self, n, shape, dtype, name):
        aps = [self.A.alloc(shape, dtype) for _ in range(n)]
        st = {"i": 0}

        def nxt():
            i = st["i"] % n
            st["i"] += 1
            return aps[i], "%s%d" % (name, i)
        return nxt

    def psring(self, banks):
        st = {"i": 0}

        def nxt():
            b = banks[st["i"] % len(banks)]
            st["i"] += 1
            return self.ps[b], "ps%d" % b
        return nxt

    def stage_mla(self, l, seq):
        A = self.A
        cv = self.colv
        Kc = self.yT[:, 4:12, :]
        KK = ["yT_c", "yT_a"]
        Va = A.alloc([NB, 8, 65], BF16)
        qT = A.alloc([8, TT], BF16)
        cqn = A.alloc([2, TT], BF16)
        ckvn = A.alloc([TT], BF16)
        sqpe = A.alloc([TT], BF16)
        kr = A.alloc([TT], F32)
        ctt = A.alloc([TT], F32)
        stt_ = A.alloc([TT], F32)
        cqf = A.alloc([TT], F32)
        sqf = A.alloc([TT], F32)
        ckf = A.alloc([TT], F32)
        skf = A.alloc([TT], F32)
        sbz = A.alloc([4, TT], BF16)
        r_sq = self.ring(2, [TT], BF16, "sq")
        r_sd = self.ring(2, [TT], F32, "sd")
        r_rs = self.ring(2, [TT], F32, "rs")
        r_ta = self.ring(2, [TT], F32, "ta")
        r_tb = self.ring(2, [TT], F32, "tb")
        r_pt = self.ring(3, [TT], BF16, "PT")
        r_osb = self.ring(2, [TT], F32, "osb")
        Wlat = A.alloc([8, 384], BF16)
        Wkpe = A.alloc([8, 96], BF16)
        Wkpes = A.alloc([8, 96], BF16)
        Wuq = A.alloc([2, 768], BF16)
        Wuqs = A.alloc([2, 768], BF16)
        Wk = A.alloc([512], BF16)
        Wv = A.alloc([512], BF16)
        pP = self.psring([0, 1, 2])
        pSS = self.psring([3, 4])
        pS_ = self.psring([6, 7])
        pO_ = self.psring([5])
        hT = self.hT
        self.dma("pool", Wlat, self.win_cols(l, C_CQ, 384), key_w="Wlat")
        self.dma("pool", Wkpe, self.w_kpe[l].rearrange("(k p) n -> p k n", p=128), key_w="Wkpe")
        self.dma("pool", Wkpes, self.w_kpe_sw[l].rearrange("(k p) n -> p k n", p=128), key_w="Wkpes")
        self.dma("pool", Wuq, self.w_uq[l].rearrange("(k p) n -> p k n", p=128), key_w="Wuq")
        self.dma("pool", Wuqs, self.w_uq_sw[l].rearrange("(k p) n -> p k n", p=128), key_w="Wuqs")
        self.dma("pool", Wk, self.w_ukv_k[l], key_w="Wk")
        self.dma("pool", Wv, self.w_ukv_v[l], key_w="Wv")
        self.memset(Va[:, :, :, 64:65], 1.0, ["Va"])
        sc_mla = 96.0 ** -0.5
        R = slice(64, 96)

        def rstd(ssq, kss, rows, inv_n):
            sd, ksd = r_sd()
            rs, krs = r_rs()
            self.act(sd[0:rows], ssq[0:rows], AF.Ln, ["const"], [ksd, kss], scale=inv_n, bias=self.epsc[0:rows, 0:1])
            self.act(rs[0:rows], sd[0:rows], AF.Exp, [ksd], [krs], scale=-0.5)
            return rs, krs

        for t in range(NT):
            T0 = t * TT
            hs = lambda k: hT[:, k, T0:T0 + TT]
            wbz = [self.ldw(self.win_cols(l, C_BZ + c * 128, 128)) for c in range(4)]
            self.dma("sp", ctt[0:96], self.ct_d[:, T0:T0 + TT], key_w="ctt")
            self.dma("sp", stt_[0:96], self.st_d[:, T0:T0 + TT], key_w="stt")
            self.ts(cqf[0:96], ctt[0:96], cv[0:96, CV_GQM:CV_GQM + 1], None, ALU.mult, None, ["ctt", "colv"], ["cqf"])
            self.ts(sqf[0:96], stt_[0:96], cv[0:96, CV_GQMS:CV_GQMS + 1], None, ALU.mult, None, ["stt", "colv"], ["sqf"])
            self.ts(ckf[0:96], ctt[0:96], cv[0:96, CV_GKM:CV_GKM + 1], None, ALU.mult, None, ["ctt", "colv"], ["ckf"])
            self.ts(skf[0:96], stt_[0:96], cv[0:96, CV_GKMS:CV_GKMS + 1], None, ALU.mult, None, ["stt", "colv"], ["skf"])
            pcq = [pP(), pP()]
            sqs = []
            for j in range(2):
                p_, k_ = pcq[j]
                for k in range(8):
                    self.mm(p_, Wlat[:, k, j * 128:(j + 1) * 128], hs(k), k == 0, k == 7, ["Wlat", "hT"], [k_])
                sq, ksq = r_sq()
                self.act(sq, p_, AF.Square, [], [ksq, k_])
                sqs.append((sq, ksq))
            ss, kss = pSS()
            self.mm(ss, self.ones, sqs[0][0], True, False, ["const", sqs[0][1]], [kss])
            self.mm(ss, self.ones, sqs[1][0], False, True, ["const", sqs[1][1]], [kss])
            rs, krs = rstd(ss, kss, 128, 1.0 / 256)
            for j in range(2):
                p_, k_ = pcq[j]
                self.stt(cqn[:, j, :], p_, cv[:, CV_GQ + j:CV_GQ + j + 1], rs, ALU.mult, ALU.mult, ["colv", krs], ["cqn", k_])
            p_, k_ = pP()
            for k in range(8):
                self.mm(p_, Wlat[:, k, 256:384], hs(k), k == 0, k == 7, ["Wlat", "hT"], [k_])
            sq, ksq = r_sq()
            self.act(sq, p_, AF.Square, [], [ksq, k_])
            ss, kss = pSS()
            self.mm(ss, self.ones, sq, True, True, ["const", ksq], [kss])
            rs, krs = rstd(ss, kss, 128, 1.0 / 128)
            self.stt(ckvn, p_, cv[:, CV_GKV:CV_GKV + 1], rs, ALU.mult, ALU.mult, ["colv", krs], ["ckvn", k_])
            p3, k3 = pP()
            for k in range(8):
                self.mm(p3[0:96], Wkpe[:, k, :], hs(k), k == 0, k == 7, ["Wkpe", "hT"], [k3])
            self.act(sqpe[R], p3[R], AF.Square, [], ["sqpe", k3])
            self.tt(kr[R], p3[R], ckf[R], ALU.mult, ["ckf"], ["kr", k3])
            p4, k4 = pP()
            for k in range(8):
                self.mm(p4[0:96], Wkpes[:, k, :], hs(k), k == 0, k == 7, ["Wkpes", "hT"], [k4])
            tb, ktb = r_tb()
            self.tt(tb[R], p4[R], skf[R], ALU.mult, ["skf"], [ktb, k4])
            self.tt(kr[R], kr[R], tb[R], ALU.add, [ktb, "kr"], ["kr"], eng="pool")
            for bb in range(4):
                blk = t * 4 + bb
                pv, pk = pP()
                self.mm(pv, ckvn[:, bb * 128:(bb + 1) * 128], Wv, True, True, ["ckvn", "Wv"], [pk])
                self.act(Va[:, blk, :, 0:64], pv.rearrange("p (h c) -> p h c", h=8), AF.Copy, [], ["Va", pk])
            for h in range(8):
                pK, pk = pP()
                self.mm(pK[0:64], Wk[:, h * 64:(h + 1) * 64], ckvn, True, True, ["Wk", "ckvn"], [pk])
                sq, ksq = r_sq()
                self.act(sq[0:64], pK[0:64], AF.Square, [], [ksq, pk])
                ss, kss = pSS()
                self.mm(ss[0:96], self.ones[0:64, 0:96], sq[0:64], True, False, ["const", ksq], [kss])
                self.mm(ss[0:96], self.ones[64:96, 0:96], sqpe[R], False, True, ["const", "sqpe"], [kss])
                rs, krs = rstd(ss, kss, 96, 1.0 / 96)
                self.stt(Kc[0:64, h, T0:T0 + TT], pK[0:64], cv[0:64, CV_GKM:CV_GKM + 1], rs[0:64], ALU.mult, ALU.mult,
                         ["colv", krs], KK + [pk])
                self.tt(Kc[R, h, T0:T0 + TT], kr[R], rs[R], ALU.mult, ["kr", krs], KK)
            for c in range(4):
                wt, wk = wbz[c]
                pb, pk = pP()
                for k in range(8):
                    self.mm(pb, wt[:, k, :], hs(k), k == 0, k == 7, [wk, "hT"], [pk])
                self.act(sbz[:, c, :], pb, AF.Silu, [], ["sbz", pk])
            for h in range(8):
                pQ, kq = pP()
                for j in range(2):
                    self.mm(pQ[0:96], Wuq[:, j, h * 96:(h + 1) * 96], cqn[:, j, :], j == 0, j == 1, ["Wuq", "cqn"], [kq])
                pQs, kqs = pP()
                for j in range(2):
                    self.mm(pQs[0:96], Wuqs[:, j, h * 96:(h + 1) * 96], cqn[:, j, :], j == 0, j == 1, ["Wuqs", "cqn"], [kqs])
                sq, ksq = r_sq()
                self.act(sq[0:96], pQ[0:96], AF.Square, [], [ksq, kq])
                ss, kss = pSS()
                self.mm(ss[0:96], self.ones[0:96, 0:96], sq[0:96], True, True, ["const", ksq], [kss])
                rs, krs = rstd(ss, kss, 96, 1.0 / 96)
                ta, kta = r_ta()
                tb, ktb = r_tb()
                self.tt(ta[0:96], pQ[0:96], cqf[0:96], ALU.mult, ["cqf"], [kta, kq])
                self.tt(tb[R], pQs[R], sqf[R], ALU.mult, ["sqf"], [ktb, kqs])
                self.tt(ta[R], ta[R], tb[R], ALU.add, [ktb, kta], [kta], eng="pool")
                self.tt(qT[0:96, h, :], ta[0:96], rs[0:96], ALU.mult, [kta, krs], ["qT%d" % h])
            for h in range(8):
                nkb = 4 * t + 4
                pO, ko = pO_()
                for kb in range(nkb):
                    c0 = 0 if kb < 4 * t else (kb - 4 * t) * 128
                    pS, ks = pS_()
                    pt, kpt = r_pt()
                    self.mm(pS[:, c0:TT], Kc[0:96, h, kb * 128:(kb + 1) * 128], qT[0:96, h, c0:TT], True, True,
                            KK + ["qT%d" % h], [ks])
                    self.act(pt[:, c0:TT], pS[:, c0:TT], AF.Exp, [], [kpt, ks], scale=sc_mla)
                    if kb >= 4 * t:
                        self.tt(pt[:, c0:c0 + 128], pt[:, c0:c0 + 128], self.emask[:, 24, 0:128], ALU.mult, ["const", kpt], [kpt], eng="pool")
                    self.mm(pO[0:65, c0:TT], Va[:, kb, h, :], pt[:, c0:TT], kb == 0, kb == nkb - 1, ["Va", kpt], [ko])
                osb, kos = r_osb()
                self.act(osb[0:65], pO[0:65], AF.Copy, [], [kos, ko])
                self.recip(osb[64:65], osb[64:65], [kos], [kos])
                pb, kb_ = pSS()
                self.mm(pb[0:64], self.onesf[64:65, 0:64], osb[64:65], True, True, ["const", kos], [kb_])
                ta, kta = r_ta()
                Rh = slice(0, 64) if h % 2 == 0 else slice(64, 128)
                self.tt(ta[Rh], osb[0:64], pb[0:64], ALU.mult, [kos], [kta, kb_])
                self.tt(self.yT[Rh, h // 2, T0:T0 + TT], ta[Rh], sbz[Rh, h // 2, :], ALU.mult, [kta, "sbz"], ["yT_b"])

    def stage_dil(self, l, seq):
        A = self.A
        cv = self.colv
        hT = self.hT
        r_dq = self.ring(2, [S], BF16, "dqT")
        r_dk = self.ring(2, [S], BF16, "dkT")
        r_dv = self.ring(2, [S], BF16, "dvT")
        r_dva = self.ring(2, [NB, 2, 65], BF16, "dva")
        r_acc = self.ring(2, [2, S], F32, "acc")
        r_scz = self.ring(2, [S], BF16, "scz")
        r_sq = self.ring(2, [TT], BF16, "sq")
        r_sd = self.ring(2, [TT], F32, "sd")
        r_rs = self.ring(2, [TT], F32, "rs")
        r_ta = self.ring(2, [TT], F32, "ta")
        r_pt = self.ring(4, [256], BF16, "dPT")
        pP = self.psring([2, 3, 4, 5])
        pS_ = self.psring([6, 7])
        pO_ = self.psring([0, 1])
        for _ in range(2):
            dva, kdva = r_dva()
            self.memset(dva[:, :, :, 64:65], 1.0, [kdva])

        def rstd(ssq, kss, rows, inv_n):
            sd, ksd = r_sd()
            rs, krs = r_rs()
            self.act(sd[0:rows], ssq[0:rows], AF.Ln, ["const"], [ksd, kss], scale=inv_n, bias=self.epsc[0:rows, 0:1])
            self.act(rs[0:rows], sd[0:rows], AF.Exp, [ksd], [krs], scale=-0.5)
            return rs, krs

        def perm_out(buf, g, t):
            if g == 0:
                return buf[:, t * TT:(t + 1) * TT], None
            if g == 1:
                return buf.rearrange("p (r m i) -> p r m i", r=4, m=4)[:, :, t, :], ("p (i r) -> p r i", 4)
            return buf.rearrange("p (r i) -> p r i", r=16)[:, :, 32 * t:32 * t + 32], ("p (i r) -> p r i", 16)

        def segments(g):
            segs = []
            for s_ in range(4):
                u = []
                if g == 0:
                    if s_ > 0:
                        u.append(((4 * s_ - 1) * 128, 4 * s_ - 1, 4 * s_ * 128, 128, 128))
                    for kb in range(4 * s_, 4 * s_ + 4):
                        u.append((kb * 128, kb, kb * 128, 256 if kb < 4 * s_ + 3 else 128, 0))
                elif g == 1:
                    for m in range(4):
                        blk = 4 * s_ + m
                        u.append((blk * 128, blk, blk * 128, 256 if m < 3 else 128, 0))
                else:
                    for c in range(4):
                        blk = 4 * s_ + c
                        u.append((blk * 128, blk, blk * 128, 128, 0))
                segs.append(u)
            return segs

        for j in range(4):
            scz, kscz = r_scz()
            acc, kacc = r_acc()
            wcz, kcz = self.ldw(self.win_cols(l, C_CZ + j * 128, 128))
            for t in range(NT):
                pc, kpc = pP()
                for k in range(8):
                    self.mm(pc, wcz[:, k, :], hT[:, k, t * TT:(t + 1) * TT], k == 0, k == 7, [kcz, "hT"], [kpc])
                self.act(scz[:, t * TT:(t + 1) * TT], pc, AF.Silu, [], [kscz, kpc])
            for g in range(3):
                c_off = g * 512 + j * 128
                wq, kwq = self.ldw(self.win_cols(l, C_DQ + c_off, 128))
                wk_, kwk = self.ldw(self.win_cols(l, C_DK + c_off, 128))
                wv, kwv = self.ldw(self.win_cols(l, C_DV + c_off, 128))
                dqT, kdq = r_dq()
                dkT, kdk = r_dk()
                dvT, kdv = r_dv()
                dva, kdva = r_dva()
                for t in range(NT):
                    hs = lambda k: hT[:, k, t * TT:(t + 1) * TT]
                    for (wt, wkey, dst, dkey, gcol) in ((wq, kwq, dqT, kdq, CV_GDQ + g), (wk_, kwk, dkT, kdk, CV_GDK + g)):
                        pp, pk = pP()
                        for k in range(8):
                            self.mm(pp, wt[:, k, :], hs(k), k == 0, k == 7, [wkey, "hT"], [pk])
                        sq, ksq = r_sq()
                        self.act(sq, pp, AF.Square, [], [ksq, pk])
                        ss, kss = pP()
                        self.mm(ss, self.ind2, sq, True, True, ["const", ksq], [kss])
                        rs, krs = rstd(ss, kss, 128, 1.0 / 64)
                        ov, pr = perm_out(dst, g, t)
                        if pr is None:
                            self.stt(ov, pp, cv[:, gcol:gcol + 1], rs, ALU.mult, ALU.mult, ["colv", krs], [dkey, pk])
                        else:
                            self.stt(ov, pp.rearrange(pr[0], r=pr[1]), cv[:, gcol:gcol + 1], rs.rearrange(pr[0], r=pr[1]),
                                     ALU.mult, ALU.mult, ["colv", krs], [dkey, pk])
                    pp, pk = pP()
                    for k in range(8):
                        self.mm(pp, wv[:, k, :], hs(k), k == 0, k == 7, [kwv, "hT"], [pk])
                    ov, pr = perm_out(dvT, g, t)
                    if pr is None:
                        self.act(ov, pp, AF.Copy, [], [kdv, pk])
                    else:
                        self.act(ov, pp.rearrange(pr[0], r=pr[1]), AF.Copy, [], [kdv, pk])
                for half in range(2):
                    pp, pk = pP()
                    pst = pp.bitcast(BF16)
                    for bb in range(8):
                        blk = half * 8 + bb
                        self.tr(pst[:, bb * 128:(bb + 1) * 128], dvT[:, blk * 128:(blk + 1) * 128], [kdv], [pk])
                    self.act(dva[:, half * 8:(half + 1) * 8, :, 0:64],
                             pst.rearrange("p (b h c) -> p b h c", b=8, h=2), AF.Copy, [], [kdva, pk])
                segs = segments(g)
                for hh in range(2):
                    Rr = slice(hh * 64, hh * 64 + 64)
                    em = self.emask[:, g * 8 + 2 * j + hh, :]
                    for b, units in enumerate(segs):
                        pO, ko = pO_()
                        first = True
                        for (K0, blk, q0, nq, m0) in units:
                            pS, ks = pS_()
                            pt, kpt = r_pt()
                            self.mm(pS[:, 0:nq], dkT[Rr, K0:K0 + 128], dqT[Rr, q0:q0 + nq], True, True, [kdk, kdq], [ks])
                            self.act(pt[:, 0:nq], pS[:, 0:nq], AF.Exp, [], [kpt, ks], scale=0.125)
                            self.tt(pt[:, 0:nq], pt[:, 0:nq], em[:, m0:m0 + nq], ALU.mult, ["const", kpt], [kpt], eng=("pool" if g == 2 else "dve"))
                            self.mm(pO[0:65, q0 - b * 512:q0 - b * 512 + nq], dva[:, blk, hh, :], pt[:, 0:nq], first, False,
                                    [kdva, kpt], [ko])
                            first = False
                        if g == 0:
                            self.act(acc[0:65, hh, b * 512:(b + 1) * 512], pO[0:65], AF.Copy, [], [kacc, ko])
                        elif g == 1:
                            av = acc[0:65, hh, :].rearrange("p (m i r) -> p r m i", m=4, r=4)[:, b]
                            self.tt(av, av, pO[0:65].rearrange("p (m i) -> p m i", m=4), ALU.add, [kacc], [kacc, ko])
                        else:
                            av = acc[0:65, hh, :].rearrange("p (i r) -> p r i", r=16)[:, 4 * b:4 * b + 4, :]
                            self.tt(av, av, pO[0:65].rearrange("p (r i) -> p r i", r=4), ALU.add, [kacc], [kacc, ko])
            for hh in range(2):
                Rh = slice(hh * 64, hh * 64 + 64)
                self.recip(acc[64:65, hh, :], acc[64:65, hh, :], [kacc], [kacc])
                for t in range(NT):
                    cs = slice(t * TT, (t + 1) * TT)
                    pb, kb_ = pP()
                    self.mm(pb[0:64], self.onesf[64:65, 0:64], acc[64:65, hh, cs], True, True, ["const", kacc], [kb_])
                    ta, kta = r_ta()
                    self.tt(ta[Rh], acc[0:64, hh, cs], pb[0:64], ALU.mult, [kacc], [kta, kb_])
                    self.tt(self.yT[Rh, 4 + j, cs], ta[Rh], scz[Rh, cs], ALU.mult, [kta, kscz], ["yT_c"])

    def stage_conv(self, l, seq):
        A = self.A
        ps = self.ps
        cv = self.colv
        hT = self.hT
        r_up = self.ring(2, [S + 2], F32, "upad")
        r_acs = self.ring(2, [TT], F32, "acs")
        r_sz = self.ring(2, [TT], F32, "sz")
        r_c1 = self.ring(2, [TT], F32, "c1")
        r_c2 = self.ring(2, [TT], F32, "c2")
        r_t1 = self.ring(3, [TT], F32, "t1")
        n = 0
        for c in range(4):
            ws = [self.ldw(self.win_cols(l, base + c * 128, 128)) for base in (C_AB, C_AC, C_AX, C_AZ)]
            upad, kup = r_up()
            self.memset(upad[:, 0:2], 0.0, [kup])
            for t in range(NT):
                T0 = t * TT
                o = 4 * (n % 2)
                n += 1
                for i in range(4):
                    wt, wk = ws[i]
                    for k in range(8):
                        self.mm(ps[o + i], wt[:, k, :], hT[:, k, T0:T0 + TT], k == 0, k == 7, [wk, "hT"], ["ps%d" % (o + i)])
                kb_, kc_, kx_, kz_ = ["ps%d" % (o + i) for i in range(4)]
                acs, kacs = r_acs()
                sz, ksz = r_sz()
                c1, kc1 = r_c1()
                c2, kc2 = r_c2()
                self.act(acs, ps[o + 1], AF.Copy, [], [kacs, kc_])
                self.tt(upad[:, 2 + T0:2 + T0 + TT], acs, ps[o + 2], ALU.mult, [kacs], [kup, kx_])
                self.act(sz, ps[o + 3], AF.Silu, [], [ksz, kz_])
                t1, kt1 = r_t1()
                self.act(c1, upad[:, T0:T0 + TT], AF.Identity, [kup, "colv"], [kc1], scale=cv[:, CV_CW + c:CV_CW + c + 1],
                         bias=cv[:, CV_CB + c:CV_CB + c + 1])
                self.act(t1, upad[:, T0 + 1:T0 + 1 + TT], AF.Identity, [kup, "colv"], [kt1], scale=cv[:, CV_CW + 4 + c:CV_CW + 5 + c])
                self.tt(c1, c1, t1, ALU.add, [kc1, kt1], [kc1], eng="pool")
                t1, kt1 = r_t1()
                self.act(t1, upad[:, T0 + 2:T0 + 2 + TT], AF.Identity, [kup, "colv"], [kt1], scale=cv[:, CV_CW + 8 + c:CV_CW + 9 + c])
                self.tt(c1, c1, t1, ALU.add, [kc1, kt1], [kc1], eng="pool")
                self.tt(c2, c1, ps[o + 0], ALU.mult, [kc1], [kc2, kb_])
                self.tt(self.yT[:, 8 + c, T0:T0 + TT], c2, sz, ALU.mult, [kc2, ksz], ["yT_a"])

    def stage_out(self, l, seq, src, dst, final):
        A = self.A
        ps = self.ps
        cv = self.colv
        hT = self.hT
        yT = self.yT
        Wo3 = [A.alloc([4, D], BF16) for _ in range(3)]
        Wo = A.alloc([8, D], BF16)
        G = A.alloc([24, TT], BF16)
        mT = A.alloc([8, TT], BF16)
        r_m1 = self.ring(2, [TT], F32, "m1")
        r_m2 = self.ring(2, [TT], F32, "m2")
        xb = [A.alloc([D], F32) for _ in range(2)]
        ob = [A.alloc([D], F32) for _ in range(2)]
        for i in range(3):
            self.dma("pool", Wo3[i], self.w_out[i][l].rearrange("(k p) n -> p k n", p=128), key_w="Wo3_%d" % i)
        self.dma("pool", Wo, self.w_o[l].rearrange("(k p) n -> p k n", p=128), key_w="Wo")
        och = self.cout if final else self.misc_chan("scrw")
        pG = self.psring([0, 1])
        for t in range(NT):
            T0 = t * TT
            DEPTHW = 4
            wq = [self.ldw(self.win_cols(l, C_GATE + jj * 128, 128)) for jj in range(DEPTHW)]
            for jj in range(24):
                wt, wk = wq[jj]
                pg, kg = pG()
                for k in range(8):
                    self.mm(pg, wt[:, k, :], hT[:, k, T0:T0 + TT], k == 0, k == 7, [wk, "hT"], [kg])
                self.act(G[:, jj, :], pg, AF.Sigmoid, ["colv"], ["G%d" % jj, kg], bias=cv[:, CV_BG + jj:CV_BG + jj + 1])
                if jj + DEPTHW < 24:
                    wq.append(self.ldw(self.win_cols(l, C_GATE + (jj + DEPTHW) * 128, 128)))
            for j in range(8):
                o = 2 + 3 * (j % 2)
                for i in range(3):
                    for k in range(4):
                        self.mm(ps[o + i], Wo3[i][:, k, j * 128:(j + 1) * 128], yT[:, (8, 0, 4)[i] + k, T0:T0 + TT], k == 0, k == 3,
                                ["Wo3_%d" % i, ("yT_a", "yT_b", "yT_c")[i]], ["ps%d" % (o + i)])
                m1, k1 = r_m1()
                m2, k2 = r_m2()
                self.tt(m1, ps[o], G[:, j, :], ALU.mult, ["G%d" % j], [k1, "ps%d" % o])
                self.tt(m2, ps[o + 1], G[:, 8 + j, :], ALU.mult, ["G%d" % (8 + j)], [k2, "ps%d" % (o + 1)])
                self.tt(m1, m1, m2, ALU.add, [k1, k2], [k1], eng="pool")
                self.tt(m2, ps[o + 2], G[:, 16 + j, :], ALU.mult, ["G%d" % (16 + j)], [k2, "ps%d" % (o + 2)])
                self.tt(mT[:, j, :], m1, m2, ALU.add, [k1, k2], ["mT%d" % j], eng="pool")
            for bb in range(4):
                b = t * 4 + bb
                i = b % 2
                r0 = seq * S + b * 128
                self.dma("sp", xb[i], src[r0:r0 + 128, :], key_w="oxb%d" % i, key_r=("scr" if l > 0 else None))
                for half in range(2):
                    pg, kg = pG()
                    for k in range(8):
                        self.mm(pg, mT[:, k, bb * 128:(bb + 1) * 128], Wo[:, k, half * 512:(half + 1) * 512], k == 0, k == 7,
                                ["mT%d" % k, "Wo"], [kg])
                    self.tt(ob[i][:, half * 512:(half + 1) * 512], pg, xb[i][:, half * 512:(half + 1) * 512], ALU.add,
                            ["oxb%d" % i], ["ob%d" % i, kg])
                self.dma("sp", dst[r0:r0 + 128, :], ob[i], key_r="ob%d" % i, key_w=(None if final else "scr"), chan=och)


def _rope_tables():
    inv = (10000.0 ** (-np.arange(0, 32, 2, dtype=np.float32) / 32)).astype(np.float32)
    ang = np.arange(S, dtype=np.float32)[:, None] * inv[None, :]
    cos, sin = np.cos(ang).astype(np.float32), np.sin(ang).astype(np.float32)
    ct = np.ones((96, S), np.float32)
    st = np.zeros((96, S), np.float32)
    ct[64:80] = cos.T
    ct[80:96] = cos.T
    st[64:80] = -sin.T
    st[80:96] = sin.T
    return ct, st


def _emask():
    n = 24
    slopes = (2.0 ** (-8.0 * np.arange(1, n + 1, dtype=np.float32) / n)).reshape(3, 8)
    k = np.arange(128)[:, None].astype(np.float32)
    q = np.arange(128)[None, :].astype(np.float32)
    em = np.zeros((128, 25, 256), np.float32)
    for g in range(3):
        for h in range(8):
            sl = slopes[g, h] * DIL[g]
            em[:, g * 8 + h, 0:128] = np.where(k <= q, np.exp(-sl * np.maximum(q - k, 0.0)), 0.0)
            if g < 2:
                em[:, g * 8 + h, 128:256] = np.where(k >= q, np.exp(-sl * np.maximum(128.0 + q - k, 0.0)), 0.0)
    em[:, 24, 0:128] = (k <= q).astype(np.float32)
    return em.reshape(128, 25 * 256)


def _host_layout(inp, layers):
    f = lambda a: np.ascontiguousarray(a, dtype=np.float32)
    L = len(layers)
    w_uq = f(inp["w_uq"][layers])
    w_uq_sw = np.zeros_like(w_uq)
    for h in range(8):
        b = h * 96
        w_uq_sw[:, :, b + 64:b + 80] = w_uq[:, :, b + 80:b + 96]
        w_uq_sw[:, :, b + 80:b + 96] = w_uq[:, :, b + 64:b + 80]
    w_ukv = f(inp["w_ukv"][layers]).reshape(L, 128, 8, 128)
    w_ukv_k = f(w_ukv[:, :, :, 0:64].reshape(L, 128, 512))
    w_ukv_v = f(w_ukv[:, :, :, 64:128].reshape(L, 128, 512))
    w_in = inp["w_in"]
    w_kpe = np.zeros((L, D, 96), np.float32)
    w_kpe_sw = np.zeros((L, D, 96), np.float32)
    for i, l in enumerate(layers):
        kp = w_in[l][:, C_KPE:C_KPE + 32]
        w_kpe[i, :, 64:96] = kp
        w_kpe_sw[i, :, 64:80] = kp[:, 16:32]
        w_kpe_sw[i, :, 80:96] = kp[:, 0:16]
    colv = np.zeros((L, 128, NCV), np.float32)
    gbc = np.zeros((L, 128, D), np.float32)
    for i, l in enumerate(layers):
        colv[i, :, CV_BG:CV_BG + 24] = inp["b_gate"][l].reshape(24, 128).T
        for tap in range(3):
            colv[i, :, CV_CW + tap * 4:CV_CW + tap * 4 + 4] = inp["conv_w"][l][tap].reshape(4, 128).T
        colv[i, :, CV_CB:CV_CB + 4] = inp["conv_b"][l].reshape(4, 128).T
        colv[i, :, CV_GQ:CV_GQ + 2] = inp["q_a_norm_g"][l].reshape(2, 128).T
        colv[i, :, CV_GKV] = inp["kv_a_norm_g"][l]
        for (col, src) in ((CV_GQM, inp["mla_q_norm_g"][l]), (CV_GKM, inp["mla_k_norm_g"][l])):
            colv[i, 0:96, col] = src
            colv[i, 64:80, col + 1] = src[80:96]
            colv[i, 80:96, col + 1] = src[64:80]
        for g in range(3):
            colv[i, :, CV_GDQ + g] = np.tile(inp["dil_q_norm_g"][l][g], 2)
            colv[i, :, CV_GDK + g] = np.tile(inp["dil_k_norm_g"][l][g], 2)
        gbc[i] = np.broadcast_to(inp["norm_g"][l][None, :], (128, D))
    ind2 = np.zeros((128, 128), np.float32)
    ind2[0:64, 0:64] = 1.0
    ind2[64:128, 64:128] = 1.0
    ct, st = _rope_tables()
    shared = {
        "w_in": f(w_in[layers]), "w_uq": w_uq, "w_uq_sw": w_uq_sw, "w_ukv_k": w_ukv_k, "w_ukv_v": w_ukv_v,
        "w_kpe": w_kpe, "w_kpe_sw": w_kpe_sw,
        "w_out_a": f(inp["w_out_a"][layers]), "w_out_b": f(inp["w_out_b"][layers]), "w_out_c": f(inp["w_out_c"][layers]),
        "w_o": f(inp["w_o"][layers]), "colv": colv, "gbc": gbc,
        "ident": np.eye(128, dtype=np.float32), "ind2": ind2, "emask": _emask(), "ctab": ct, "stab": st,
    }
    return shared


_CACHE = {}


def _get_builder(nseq, nlayers, dbg=False):
    key = (nseq, nlayers, dbg)
    if key not in _CACHE:
        _CACHE[key] = Builder(nseq, nlayers, dbg=dbg)
    return _CACHE[key]


def kernel(**inputs):
    inp = {k: np.asarray(v) for k, v in inputs.items()}
    x = np.ascontiguousarray(inp["x"], dtype=np.float32)
    B = x.shape[0]
    ncores = 8
    nseq = B // ncores
    shared = _host_layout(inp, list(range(DEPTH)))
    bld = Builder(nseq, DEPTH)
    in_maps = []
    for c in range(ncores):
        m = dict(shared)
        m["x"] = x[c * nseq:(c + 1) * nseq].reshape(nseq * S, D)
        in_maps.append(m)
    res = run_bass_kernel_spmd(bld.nc, in_maps, core_ids=list(range(ncores)))
    outs = [np.asarray(r["out"]).reshape(nseq, S, D) for r in res.results]
    return np.concatenate(outs, axis=0).astype(np.float32)
```

```python
import numpy as np
import concourse.bass as bass
import concourse.mybir as mybir
from concourse.bass_utils import run_bass_kernel_spmd

F32 = mybir.dt.float32
BF16 = mybir.dt.bfloat16
AF = mybir.ActivationFunctionType
ALU = mybir.AluOpType

ENGS = ("pe", "act", "dve", "pool", "sp")

D = 1024
S = 2048
DEPTH = 2
NB = 16
NT = 4
TT = 512
EPS = 1e-6
N_IN = 11168
C_AB, C_AC, C_AX, C_AZ = 0, 512, 1024, 1536
C_CQ, C_CKV, C_KPE, C_BZ = 2048, 2304, 2432, 2464
C_DQ, C_DK, C_DV, C_CZ, C_GATE = 2976, 4512, 6048, 7584, 8096
DIL = (1, 4, 16)
CV_BG, CV_CW, CV_CB, CV_GQ, CV_GKV, CV_GQM, CV_GQMS, CV_GKM, CV_GKMS, CV_GDQ, CV_GDK, NCV = 0, 24, 36, 40, 42, 43, 44, 45, 46, 47, 50, 53


import heapq

RAW, WAR, WAW = 0, 1, 2
ACT_SET = {"Exp": "exp", "Ln": "exp", "Sqrt": "sqrt", "Silu": "silu", "Sigmoid": "sigmoid"}


class Chan:
    def __init__(self, sem, name):
        self.sem = sem
        self.name = name
        self.count = 0


class Op:
    __slots__ = ("eng", "fn", "idx", "pidx", "deps", "succ", "ndeps", "waits", "signal", "sigcount", "chan",
                 "dma_deps", "dur", "tset", "ready", "finish", "seg", "isbar")

    def __init__(self, eng, fn):
        self.eng = eng
        self.fn = fn
        self.idx = -1
        self.pidx = -1
        self.deps = []
        self.succ = []
        self.ndeps = 0
        self.waits = {}
        self.dma_deps = []
        self.signal = False
        self.sigcount = 0
        self.chan = None
        self.dur = 0.5
        self.tset = None
        self.ready = 0.0
        self.finish = 0.0
        self.seg = 0
        self.isbar = False


class Prog:
    def __init__(self, nc):
        self.nc = nc
        self.ops = []
        self.kw = {}
        self.kr = {}
        self.sems = {e: nc.alloc_semaphore("sem_" + e) for e in ENGS}
        self.nops = 0
        self.chans = []
        self.seg = 0
        self.sched = True

    def chan(self, name):
        c = Chan(self.nc.alloc_semaphore("dsem_" + name), name)
        self.chans.append(c)
        return c

    def barrier(self):
        self.seg += 1

    def add(self, eng, fn, reads=(), writes=(), chan=None, dur=0.5, tset=None):
        op = Op(eng, fn)
        op.pidx = len(self.ops)
        op.seg = self.seg
        op.chan = chan
        op.dur = dur
        op.tset = tset
        self.ops.append(op)
        self.nops += 1
        deps = {}
        for k in reads:
            w = self.kw.get(k)
            if w is not None:
                deps[w] = RAW
        for k in writes:
            w = self.kw.get(k)
            if w is not None and w not in deps:
                deps[w] = WAW
            for r in self.kr.get(k, ()):
                if r is not op and r not in deps:
                    deps[r] = WAR
        for k in reads:
            self.kr.setdefault(k, []).append(op)
        for k in writes:
            self.kw[k] = op
            self.kr[k] = []
        op.deps = [(d, kind) for d, kind in deps.items() if d.seg == op.seg]
        return op

    def _schedule(self, ops):
        if not self.sched:
            return list(ops)
        for op in ops:
            op.ndeps = len(op.deps)
            op.succ = []
            op.ready = 0.0
        for op in ops:
            for d, _ in op.deps:
                d.succ.append(op)
        free = {e: 0.0 for e in ENGS}
        pending = {e: [] for e in ENGS}
        avail = {e: {} for e in ENGS}
        curset = [None]
        for op in ops:
            if op.ndeps == 0:
                heapq.heappush(pending[op.eng], (0.0, op.pidx, op))
        order = []
        LAT = 0.3
        n = len(ops)

        def cand(e):
            t = free[e]
            pend = pending[e]
            av = avail[e]
            while pend and pend[0][0] <= t:
                _, pi, o = heapq.heappop(pend)
                heapq.heappush(av.setdefault(o.tset if e == "act" else None, []), (pi, o))
            best = None
            if e == "act":
                for ts_ in (None, curset[0]):
                    h = av.get(ts_)
                    if h and (best is None or h[0][0] < best[0]):
                        best = (h[0][0], ts_)
                if best is None:
                    for ts_, h in av.items():
                        if h and (best is None or h[0][0] < best[0]):
                            best = (h[0][0], ts_)
            else:
                h = av.get(None)
                if h:
                    best = (h[0][0], None)
            if best is not None:
                return (t, best[0], best[1], False)
            if pend:
                return (pend[0][0], pend[0][1], None, True)
            return None

        while len(order) < n:
            bc = None
            be = None
            for e in ENGS:
                c = cand(e)
                if c is not None and (bc is None or (c[0], c[1]) < (bc[0], bc[1])):
                    bc, be = c, e
            assert bc is not None, "scheduler stuck"
            if bc[3]:
                _, pi, o = heapq.heappop(pending[be])
            else:
                pi, o = heapq.heappop(avail[be][bc[2]])
            start = max(bc[0], free[be])
            dur = o.dur
            if be == "act" and o.tset is not None and o.tset != curset[0]:
                dur += 2.7
                curset[0] = o.tset
            if o.chan is not None:
                free[be] = start + 0.15
                o.finish = start + 2.0 + dur
            else:
                free[be] = start + dur
                o.finish = start + dur
            order.append(o)
            for s_ in o.succ:
                s_.ndeps -= 1
                rt = o.finish + (LAT if s_.eng != o.eng else 0.05)
                if rt > s_.ready:
                    s_.ready = rt
                if s_.ndeps == 0:
                    heapq.heappush(pending[s_.eng], (s_.ready, s_.pidx, s_))
        return order

    def emit(self, final_chans=()):
        nc = self.nc
        nseg = self.seg + 1
        segs = [[] for _ in range(nseg)]
        for op in self.ops:
            segs[op.seg].append(op)
        glob = []
        for si in range(nseg):
            if si > 0:
                drains = []
                for e in ENGS:
                    if e == "sp":
                        continue
                    d = Op(e, lambda h: h.drain())
                    d.isbar = True
                    drains.append(d)
                    glob.append(d)
                for e in ENGS:
                    w = Op(e, lambda h: h.nop())
                    w.isbar = True
                    w.deps = [(d, RAW) for d in drains if d.eng != e]
                    w.dma_deps = "all"
                    glob.append(w)
            glob.extend(self._schedule(segs[si]))
        self.eng_ops = {e: [] for e in ENGS}
        for op in glob:
            op.idx = len(self.eng_ops[op.eng])
            self.eng_ops[op.eng].append(op)
        chan_cnt = {c: 0 for c in self.chans}
        dma_wait_vals = {}
        for op in glob:
            isdma = op.chan is not None
            dw = {}
            if op.dma_deps == "all":
                for c in self.chans:
                    if chan_cnt[c] > 0:
                        dw[c] = chan_cnt[c]
            for d, kind in op.deps:
                if d.chan is not None:
                    dw[d.chan] = chan_cnt[d.chan]
                    continue
                if d.eng == op.eng and not isdma and kind != RAW:
                    continue
                if d.eng == op.eng and d.eng == "pe":
                    continue
                cur = op.waits.get(d.eng)
                if cur is None or cur.idx < d.idx:
                    op.waits[d.eng] = d
            if isdma:
                chan_cnt[op.chan] += 1
            dma_wait_vals[op] = dw
        for op in glob:
            for p in op.waits.values():
                p.signal = True
        for e in ENGS:
            c = 0
            for op in self.eng_ops[e]:
                if op.signal:
                    c += 1
                op.sigcount = c
        final_cnt = {c: chan_cnt[c] for c in final_chans}

        def run_engine(e, h):
            seen = {}
            for op in self.eng_ops[e]:
                for pe_, p in op.waits.items():
                    need = p.sigcount
                    if seen.get(pe_, 0) < need:
                        h.wait_ge(self.sems[pe_], need)
                        seen[pe_] = need
                for c, cnt in dma_wait_vals[op].items():
                    if seen.get(c, 0) < cnt:
                        h.wait_ge(c.sem, 16 * cnt)
                        seen[c] = cnt
                ins = op.fn(h)
                if op.chan is not None:
                    ins.then_inc(op.chan.sem, 16)
                elif op.signal:
                    ins.then_inc(self.sems[e], 1)
            if e == "sp":
                for c, cnt in final_cnt.items():
                    h.wait_ge(c.sem, 16 * cnt)

        with nc.Block() as block:
            @block.tensor
            def _(h):
                run_engine("pe", h)

            @block.scalar
            def _(h):
                run_engine("act", h)

            @block.vector
            def _(h):
                run_engine("dve", h)

            @block.gpsimd
            def _(h):
                run_engine("pool", h)

            @block.sync
            def _(h):
                run_engine("sp", h)


class Arena:
    def __init__(self, nc, nbytes):
        self.cap = nbytes // 2
        self.t = nc.alloc_sbuf_tensor("arena", [128, self.cap], BF16).ap()
        self.top = 0

    def alloc(self, free_shape, dtype):
        n = 1
        for v in free_shape:
            n *= v
        ne = n * (2 if dtype == F32 else 1)
        ne_al = (ne + 31) // 32 * 32
        off = self.top
        self.top += ne_al
        assert self.top <= self.cap, ("arena overflow", self.top * 2, self.cap * 2)
        v = self.t[:, off:off + ne]
        if dtype == F32:
            v = v.bitcast(F32)
        if len(free_shape) > 1:
            names = " ".join("a%d" % i for i in range(len(free_shape)))
            kw = {"a%d" % i: free_shape[i] for i in range(len(free_shape))}
            v = v.rearrange("p (%s) -> p %s" % (names, names), **kw)
        return v


class Builder:
    def __init__(self, nseq, nlayers, layer0=0, dbg=False):
        self.nseq = nseq
        self.nlayers = nlayers
        self.dbg = dbg
        nc = bass.Bass("TRN2", target_bir_lowering=False)
        self.nc = nc
        P = Prog(nc)
        self.P = P
        L = nlayers
        dt = nc.dram_tensor
        self.x = dt("x", [nseq * S, D], F32, kind="ExternalInput").ap()
        self.out = dt("out", [nseq * S, D], F32, kind="ExternalOutput").ap()
        self.w_in = dt("w_in", [L, D, N_IN], F32, kind="ExternalInput").ap()
        self.w_uq = dt("w_uq", [L, 256, 768], F32, kind="ExternalInput").ap()
        self.w_uq_sw = dt("w_uq_sw", [L, 256, 768], F32, kind="ExternalInput").ap()
        self.w_ukv_k = dt("w_ukv_k", [L, 128, 512], F32, kind="ExternalInput").ap()
        self.w_ukv_v = dt("w_ukv_v", [L, 128, 512], F32, kind="ExternalInput").ap()
        self.w_kpe = dt("w_kpe", [L, D, 96], F32, kind="ExternalInput").ap()
        self.w_kpe_sw = dt("w_kpe_sw", [L, D, 96], F32, kind="ExternalInput").ap()
        self.w_out = [dt("w_out_%s" % n, [L, 512, D], F32, kind="ExternalInput").ap() for n in "abc"]
        self.w_o = dt("w_o", [L, D, D], F32, kind="ExternalInput").ap()
        self.colv_d = dt("colv", [L, 128, NCV], F32, kind="ExternalInput").ap()
        self.gbc_d = dt("gbc", [L, 128, D], F32, kind="ExternalInput").ap()
        self.ident_d = dt("ident", [128, 128], F32, kind="ExternalInput").ap()
        self.ind2_d = dt("ind2", [128, 128], F32, kind="ExternalInput").ap()
        self.emask_d = dt("emask", [128, 25 * 256], F32, kind="ExternalInput").ap()
        self.ct_d = dt("ctab", [96, S], F32, kind="ExternalInput").ap()
        self.st_d = dt("stab", [96, S], F32, kind="ExternalInput").ap()
        if nlayers > 1:
            self.scr = dt("scr", [nseq * S, D], F32).ap()
        self.bsc = dt("bsc", [4, S], F32).ap()
        self.bsi = 0
        if dbg:
            self.dbg_h = dt("dbg_h", [128, 8 * S], BF16, kind="ExternalOutput").ap()
            self.dbg_y = dt("dbg_y", [128, 12 * S], BF16, kind="ExternalOutput").ap()
        self.ps = [nc.alloc_psum_tensor("ps%d" % i, [128, 512], F32).ap() for i in range(8)]
        A = Arena(nc, 206 * 1024)
        self.A = A
        self.hT = A.alloc([8, S], BF16)
        self.yT = A.alloc([12, S], BF16)
        self.ident = A.alloc([128], BF16)
        self.ones = A.alloc([128], BF16)
        self.ind2 = A.alloc([128], BF16)
        self.onesf = A.alloc([64], F32)
        self.emask = A.alloc([25, 256], BF16)
        self.colv = A.alloc([NCV], F32)
        self.epsc = A.alloc([1], F32)
        self.NWR = 8
        self.wr = [A.alloc([8, 128], BF16) for _ in range(self.NWR)]
        self.wrc = [P.chan("wr%d" % i) for i in range(self.NWR)]
        self.wri = 0
        self.stage_base = A.top
        self.cst = P.chan("const")
        self.cout = P.chan("out")
        self.cmisc = {}
        self.build()

    def misc_chan(self, key):
        if key not in self.cmisc:
            self.cmisc[key] = self.P.chan("m%d" % len(self.cmisc))
        return self.cmisc[key]

    def dma(self, q, out, in_, key_w=None, key_r=None, chan=None):
        if chan is None:
            chan = self.misc_chan(key_w if key_w is not None else key_r)
        nbytes = out.free_size() * out.partition_size() * 4
        self.P.add(q, lambda h: h.dma_start(out=out, in_=in_),
                   reads=[key_r] if key_r else [], writes=[key_w] if key_w else [], chan=chan, dur=nbytes / 150e3)

    def mm(self, out, lhsT, rhs, start, stop, r, w):
        self.P.add("pe", lambda h: h.matmul(out, lhsT=lhsT, rhs=rhs, start=start, stop=stop, skip_group_check=True), r, w,
                   dur=(0.03 + out.free_size() * 0.00043) * (4.0 if lhsT.dtype == F32 else 1.0))

    def tr(self, out, in_, r, w):
        ident = self.ident
        self.P.add("pe", lambda h: h.transpose(out, in_, ident), list(r) + ["const"], w, dur=0.1)

    def act(self, out, in_, func, r, w, scale=None, bias=None, accum=None):
        kw = {}
        if scale is not None:
            kw["scale"] = scale
        if bias is not None:
            kw["bias"] = bias
        if accum is not None:
            kw["accum_out"] = accum
        self.P.add("act", lambda h: h.activation(out=out, in_=in_, func=func, **kw), r, w,
                   dur=0.2 + out.free_size() * 0.0009, tset=ACT_SET.get(func.name))

    def _dd(self, out, eng="dve"):
        if eng == "pool":
            return 0.25 + out.free_size() * (0.0021 if out.dtype == F32 else 0.0009)
        return 0.12 + out.free_size() * (0.0015 if out.dtype == F32 else 0.0009)

    def tt(self, out, in0, in1, op, r, w, eng="dve"):
        self.P.add(eng, lambda h: h.tensor_tensor(out=out, in0=in0, in1=in1, op=op), r, w, dur=self._dd(out, eng))

    def stt(self, out, in0, scalar, in1, op0, op1, r, w, eng="dve"):
        self.P.add(eng, lambda h: h.scalar_tensor_tensor(out=out, in0=in0, scalar=scalar, in1=in1, op0=op0, op1=op1), r, w,
                   dur=self._dd(out, eng))

    def ts(self, out, in0, s1, s2, op0, op1, r, w, eng="dve"):
        if s2 is None:
            self.P.add(eng, lambda h: h.tensor_scalar(out=out, in0=in0, scalar1=s1, scalar2=None, op0=op0), r, w,
                       dur=self._dd(out, eng))
        else:
            self.P.add(eng, lambda h: h.tensor_scalar(out=out, in0=in0, scalar1=s1, scalar2=s2, op0=op0, op1=op1), r, w,
                       dur=self._dd(out, eng))

    def recip(self, out, in_, r, w):
        self.P.add("dve", lambda h: h.reciprocal(out=out, in_=in_), r, w, dur=self._dd(out))

    def cpy(self, out, in_, r, w, eng="dve"):
        self.P.add(eng, lambda h: h.tensor_copy(out=out, in_=in_), r, w, dur=self._dd(out))

    def memset(self, ap, val, w, eng="dve"):
        self.P.add(eng, lambda h: h.memset(ap, val), [], w, dur=self._dd(ap))

    def bcast_rows(self, src_row, key_src, dsts):
        from concourse.ap import AP
        i = self.bsi % 4
        self.bsi += 1
        n = src_row.free_size()
        self.dma("sp", self.bsc[i:i + 1, 0:n], src_row, key_w="bsc%d" % i, key_r=key_src)
        for (dst, kdst, c0, cn) in dsts:
            row = self.bsc[i:i + 1, c0:c0 + cn]
            src = AP(row.tensor, row.offset, [[0, 64], [1, cn]])
            self.dma("sp", dst, src, key_w=kdst, key_r="bsc%d" % i)

    def ldw(self, src):
        i = self.wri % self.NWR
        self.wri += 1
        self.dma("pool", self.wr[i], src, key_w="wr%d" % i, chan=self.wrc[i])
        return self.wr[i], "wr%d" % i

    def win_cols(self, l, c0, n):
        return self.w_in[l, :, c0:c0 + n].rearrange("(k p) n -> p k n", p=128)

    def rstd_from(self, ssq_ps, pkey, rows, inv_n, sd, rs):
        self.act(sd[0:rows], ssq_ps[0:rows], AF.Sqrt, ["const"], ["sd", pkey], scale=inv_n, bias=self.epsc[0:rows, 0:1])
        self.recip(rs[0:rows], sd[0:rows], ["sd"], ["rs"])

    def build(self):
        P = self.P
        self.dma("pool", self.ident, self.ident_d, key_w="const", chan=self.cst)
        self.dma("pool", self.ind2, self.ind2_d, key_w="const", chan=self.cst)
        self.dma("pool", self.emask, self.emask_d.rearrange("p (a b) -> p a b", a=25), key_w="const", chan=self.cst)
        self.memset(self.ones, 1.0, ["const"])
        self.memset(self.onesf, 1.0, ["const"])
        self.memset(self.epsc, EPS, ["const"])
        for l in range(self.nlayers):
            src = self.x if l == 0 else self.scr
            dst = self.out if l == self.nlayers - 1 else self.scr
            self.dma("sp", self.colv, self.colv_d[l], key_w="colv")
            for seq in range(self.nseq):
                self.A.top = self.stage_base
                self.P.barrier()
                self.stage0(l, seq, src)
                if self.dbg and l == 0 and seq == 0:
                    self.dma("sp", self.dbg_h.rearrange("p (a b) -> p a b", a=8), self.hT, key_r="hT", chan=self.cout)
                self.A.top = self.stage_base
                self.P.barrier()
                self.stage_mla(l, seq)
                self.A.top = self.stage_base
                self.P.barrier()
                self.stage_dil(l, seq)
                self.A.top = self.stage_base
                self.P.barrier()
                self.stage_conv(l, seq)
                if self.dbg and l == 0 and seq == 0:
                    self.P.add("sp", lambda h: h.dma_start(out=self.dbg_y.rearrange("p (a b) -> p a b", a=12), in_=self.yT), reads=["yT_a", "yT_b", "yT_c"], chan=self.cout)
                self.A.top = self.stage_base
                self.P.barrier()
                self.stage_out(l, seq, src, dst, final=(l == self.nlayers - 1))
        P.emit(final_chans=[self.cout] + ([self.cmisc["scrw"]] if "scrw" in self.cmisc else []))

    def stage0(self, l, seq, src):
        A = self.A
        NBUF = 4
        xb = [A.alloc([D], F32) for _ in range(NBUF)]
        hn = [A.alloc([D], BF16) for _ in range(NBUF)]
        junk = A.alloc([D], BF16)
        gbc = A.alloc([D], F32)
        r_ss = self.ring(NBUF, [1], F32, "ss")
        r_sd = self.ring(NBUF, [1], F32, "sd1")
        r_rs = self.ring(NBUF, [1], F32, "rs1")
        self.dma("sp", gbc, self.gbc_d[l], key_w="s0gbc")
        for b in range(NB):
            i = b % NBUF
            r0 = seq * S + b * 128
            ss, kss = r_ss()
            sd1, ksd = r_sd()
            rs1, krs = r_rs()
            self.dma("sp", xb[i], src[r0:r0 + 128, :], key_w="xb%d" % i, key_r=("scr" if l > 0 else None))
            self.memset(ss, 0.0, [kss])
            self.act(junk, xb[i], AF.Square, ["xb%d" % i], ["junk", kss], accum=ss)
            self.act(sd1, ss, AF.Ln, [kss, "const"], [ksd], scale=1.0 / D, bias=self.epsc[:, 0:1])
            self.act(rs1, sd1, AF.Exp, [ksd], [krs], scale=-0.5)
            self.stt(hn[i], xb[i], rs1[:, 0:1], gbc, ALU.mult, ALU.mult, ["xb%d" % i, krs, "s0gbc"], ["hn%d" % i])
            pb = b % 8
            pst = self.ps[pb].bitcast(BF16)
            for k in range(8):
                self.tr(pst[:, k * 128:(k + 1) * 128], hn[i][:, k * 128:(k + 1) * 128], ["hn%d" % i], ["ps%d" % pb])
            if b % 2 == 0:
                self.act(self.hT[:, :, b * 128:(b + 1) * 128], pst.rearrange("p (k n) -> p k n", k=8), AF.Copy,
                         [], ["ps%d" % pb, "hT"])
            else:
                self.cpy(self.hT[:, :, b * 128:(b + 1) * 128], pst.rearrange("p (k n) -> p k n", k=8), [], ["ps%d" % pb, "hT"])

    def ring(self, n, shape, dtype, name):
        aps = [self.A.alloc(shape, dtype) for _ in range(n)]
        st = {"i": 0}

        def nxt():
            i = st["i"] % n
            st["i"] += 1
            return aps[i], "%s%d" % (name, i)
        return nxt

    def psring(self, banks):
        st = {"i": 0}

        def nxt():
            b = banks[st["i"] % len(banks)]
            st["i"] += 1
            return self.ps[b], "ps%d" % b
        return nxt

    def stage_mla(self, l, seq):
        A = self.A
        cv = self.colv
        Kc = self.yT[:, 4:12, :]
        KK = ["yT_c", "yT_a"]
        Va = A.alloc([NB, 8, 65], BF16)
        qT = A.alloc([8, TT], BF16)
        cqn = A.alloc([2, TT], BF16)
        ckvn = A.alloc([TT], BF16)
        sqpe = A.alloc([TT], BF16)
        kr = A.alloc([TT], F32)
        ctt = A.alloc([TT], F32)
        stt_ = A.alloc([TT], F32)
        cqf = A.alloc([TT], F32)
        sqf = A.alloc([TT], F32)
        ckf = A.alloc([TT], F32)
        skf = A.alloc([TT], F32)
        sbz = A.alloc([4, TT], BF16)
        r_sq = self.ring(2, [TT], BF16, "sq")
        r_sd = self.ring(2, [TT], F32, "sd")
        r_rs = self.ring(2, [TT], F32, "rs")
        r_ta = self.ring(2, [TT], F32, "ta")
        r_tb = self.ring(2, [TT], F32, "tb")
        r_pt = self.ring(3, [TT], BF16, "PT")
        r_osb = self.ring(2, [TT], F32, "osb")
        r_rb = self.ring(2, [TT], F32, "rb")
        Wlat = A.alloc([8, 384], BF16)
        Wkpe = A.alloc([8, 96], BF16)
        Wkpes = A.alloc([8, 96], BF16)
        Wuq = A.alloc([2, 768], BF16)
        Wuqs = A.alloc([2, 768], BF16)
        Wk = A.alloc([512], BF16)
        Wv = A.alloc([512], BF16)
        pP = self.psring([0, 1, 2])
        pSS = self.psring([3, 4])
        pS_ = self.psring([6, 7])
        pO_ = self.psring([5])
        hT = self.hT
        self.dma("pool", Wlat, self.win_cols(l, C_CQ, 384), key_w="Wlat")
        self.dma("pool", Wkpe, self.w_kpe[l].rearrange("(k p) n -> p k n", p=128), key_w="Wkpe")
        self.dma("pool", Wkpes, self.w_kpe_sw[l].rearrange("(k p) n -> p k n", p=128), key_w="Wkpes")
        self.dma("pool", Wuq, self.w_uq[l].rearrange("(k p) n -> p k n", p=128), key_w="Wuq")
        self.dma("pool", Wuqs, self.w_uq_sw[l].rearrange("(k p) n -> p k n", p=128), key_w="Wuqs")
        self.dma("pool", Wk, self.w_ukv_k[l], key_w="Wk")
        self.dma("pool", Wv, self.w_ukv_v[l], key_w="Wv")
        self.memset(Va[:, :, :, 64:65], 1.0, ["Va"])
        sc_mla = 96.0 ** -0.5
        R = slice(64, 96)

        def rstd(ssq, kss, rows, inv_n):
            sd, ksd = r_sd()
            rs, krs = r_rs()
            self.act(sd[0:rows], ssq[0:rows], AF.Ln, ["const"], [ksd, kss], scale=inv_n, bias=self.epsc[0:rows, 0:1])
            self.act(rs[0:rows], sd[0:rows], AF.Exp, [ksd], [krs], scale=-0.5)
            return rs, krs

        for t in range(NT):
            T0 = t * TT
            hs = lambda k: hT[:, k, T0:T0 + TT]
            wbz = [self.ldw(self.win_cols(l, C_BZ + c * 128, 128)) for c in range(4)]
            self.dma("sp", ctt[0:96], self.ct_d[:, T0:T0 + TT], key_w="ctt")
            self.dma("sp", stt_[0:96], self.st_d[:, T0:T0 + TT], key_w="stt")
            self.ts(cqf[0:96], ctt[0:96], cv[0:96, CV_GQM:CV_GQM + 1], None, ALU.mult, None, ["ctt", "colv"], ["cqf"])
            self.ts(sqf[0:96], stt_[0:96], cv[0:96, CV_GQMS:CV_GQMS + 1], None, ALU.mult, None, ["stt", "colv"], ["sqf"])
            self.ts(ckf[0:96], ctt[0:96], cv[0:96, CV_GKM:CV_GKM + 1], None, ALU.mult, None, ["ctt", "colv"], ["ckf"])
            self.ts(skf[0:96], stt_[0:96], cv[0:96, CV_GKMS:CV_GKMS + 1], None, ALU.mult, None, ["stt", "colv"], ["skf"])
            pcq = [pP(), pP()]
            sqs = []
            for j in range(2):
                p_, k_ = pcq[j]
                for k in range(8):
                    self.mm(p_, Wlat[:, k, j * 128:(j + 1) * 128], hs(k), k == 0, k == 7, ["Wlat", "hT"], [k_])
                sq, ksq = r_sq()
                self.act(sq, p_, AF.Square, [], [ksq, k_])
                sqs.append((sq, ksq))
            ss, kss = pSS()
            self.mm(ss, self.ones, sqs[0][0], True, False, ["const", sqs[0][1]], [kss])
            self.mm(ss, self.ones, sqs[1][0], False, True, ["const", sqs[1][1]], [kss])
            rs, krs = rstd(ss, kss, 128, 1.0 / 256)
            for j in range(2):
                p_, k_ = pcq[j]
                self.stt(cqn[:, j, :], p_, cv[:, CV_GQ + j:CV_GQ + j + 1], rs, ALU.mult, ALU.mult, ["colv", krs], ["cqn", k_])
            p_, k_ = pP()
            for k in range(8):
                self.mm(p_, Wlat[:, k, 256:384], hs(k), k == 0, k == 7, ["Wlat", "hT"], [k_])
            sq, ksq = r_sq()
            self.act(sq, p_, AF.Square, [], [ksq, k_])
            ss, kss = pSS()
            self.mm(ss, self.ones, sq, True, True, ["const", ksq], [kss])
            rs, krs = rstd(ss, kss, 128, 1.0 / 128)
            self.stt(ckvn, p_, cv[:, CV_GKV:CV_GKV + 1], rs, ALU.mult, ALU.mult, ["colv", krs], ["ckvn", k_])
            p3, k3 = pP()
            for k in range(8):
                self.mm(p3[0:96], Wkpe[:, k, :], hs(k), k == 0, k == 7, ["Wkpe", "hT"], [k3])
            self.act(sqpe[R], p3[R], AF.Square, [], ["sqpe", k3])
            self.tt(kr[R], p3[R], ckf[R], ALU.mult, ["ckf"], ["kr", k3])
            p4, k4 = pP()
            for k in range(8):
                self.mm(p4[0:96], Wkpes[:, k, :], hs(k), k == 0, k == 7, ["Wkpes", "hT"], [k4])
            tb, ktb = r_tb()
            self.tt(tb[R], p4[R], skf[R], ALU.mult, ["skf"], [ktb, k4])
            self.tt(kr[R], kr[R], tb[R], ALU.add, [ktb, "kr"], ["kr"], eng="pool")
            for bb in range(4):
                blk = t * 4 + bb
                pv, pk = pP()
                self.mm(pv, ckvn[:, bb * 128:(bb + 1) * 128], Wv, True, True, ["ckvn", "Wv"], [pk])
                self.act(Va[:, blk, :, 0:64], pv.rearrange("p (h c) -> p h c", h=8), AF.Copy, ["Va"], ["Va%d" % t, pk])
            for h in range(8):
                pK, pk = pP()
                self.mm(pK[0:64], Wk[:, h * 64:(h + 1) * 64], ckvn, True, True, ["Wk", "ckvn"], [pk])
                sq, ksq = r_sq()
                self.act(sq[0:64], pK[0:64], AF.Square, [], [ksq, pk])
                ss, kss = pSS()
                self.mm(ss[0:96], self.ones[0:64, 0:96], sq[0:64], True, False, ["const", ksq], [kss])
                self.mm(ss[0:96], self.ones[64:96, 0:96], sqpe[R], False, True, ["const", "sqpe"], [kss])
                rs, krs = rstd(ss, kss, 96, 1.0 / 96)
                self.stt(Kc[0:64, h, T0:T0 + TT], pK[0:64], cv[0:64, CV_GKM:CV_GKM + 1], rs[0:64], ALU.mult, ALU.mult,
                         ["colv", krs, "yT_c", "yT_a"], ["Kc%d_%d" % (t, h), pk])
                self.tt(Kc[R, h, T0:T0 + TT], kr[R], rs[R], ALU.mult, ["kr", krs, "yT_c", "yT_a"], ["Kc%d_%d" % (t, h)])
            for c in range(4):
                wt, wk = wbz[c]
                pb, pk = pP()
                for k in range(8):
                    self.mm(pb, wt[:, k, :], hs(k), k == 0, k == 7, [wk, "hT"], [pk])
                self.act(sbz[:, c, :], pb, AF.Silu, [], ["sbz", pk])
            for h in range(8):
                pQ, kq = pP()
                for j in range(2):
                    self.mm(pQ[0:96], Wuq[:, j, h * 96:(h + 1) * 96], cqn[:, j, :], j == 0, j == 1, ["Wuq", "cqn"], [kq])
                pQs, kqs = pP()
                for j in range(2):
                    self.mm(pQs[0:96], Wuqs[:, j, h * 96:(h + 1) * 96], cqn[:, j, :], j == 0, j == 1, ["Wuqs", "cqn"], [kqs])
                sq, ksq = r_sq()
                self.act(sq[0:96], pQ[0:96], AF.Square, [], [ksq, kq])
                ss, kss = pSS()
                self.mm(ss[0:96], self.ones[0:96, 0:96], sq[0:96], True, True, ["const", ksq], [kss])
                rs, krs = rstd(ss, kss, 96, 1.0 / 96)
                ta, kta = r_ta()
                tb, ktb = r_tb()
                self.tt(ta[0:96], pQ[0:96], cqf[0:96], ALU.mult, ["cqf"], [kta, kq])
                self.tt(tb[R], pQs[R], sqf[R], ALU.mult, ["sqf"], [ktb, kqs])
                self.tt(ta[R], ta[R], tb[R], ALU.add, [ktb, kta], [kta], eng="pool")
                self.tt(qT[0:96, h, :], ta[0:96], rs[0:96], ALU.mult, [kta, krs], ["qT%d" % h])
            for h in range(8):
                nkb = 4 * t + 4
                pO, ko = pO_()
                for kb in range(nkb):
                    c0 = 0 if kb < 4 * t else (kb - 4 * t) * 128
                    pS, ks = pS_()
                    pt, kpt = r_pt()
                    self.mm(pS[:, c0:TT], Kc[0:96, h, kb * 128:(kb + 1) * 128], qT[0:96, h, c0:TT], True, True,
                            ["Kc%d_%d" % (kb // 4, h), "qT%d" % h], [ks])
                    self.act(pt[:, c0:TT], pS[:, c0:TT], AF.Exp, [], [kpt, ks], scale=sc_mla)
                    if kb >= 4 * t:
                        self.tt(pt[:, c0:c0 + 128], pt[:, c0:c0 + 128], self.emask[:, 24, 0:128], ALU.mult, ["const", kpt], [kpt], eng="pool")
                    self.mm(pO[0:65, c0:TT], Va[:, kb, h, :], pt[:, c0:TT], kb == 0, kb == nkb - 1, ["Va", "Va%d" % (kb // 4), kpt], [ko])
                osb, kos = r_osb()
                self.act(osb[0:65], pO[0:65], AF.Copy, [], [kos, ko])
                self.act(osb[64:65], osb[64:65], AF.Ln, [kos], [kos])
                self.act(osb[64:65], osb[64:65], AF.Exp, [kos], [kos], scale=-1.0)
                rb, krb = r_rb()
                self.bcast_rows(osb[64:65, :], kos, [(rb[0:64], krb, 0, TT)])
                ta, kta = r_ta()
                Rh = slice(0, 64) if h % 2 == 0 else slice(64, 128)
                self.tt(ta[Rh], osb[0:64], rb[0:64], ALU.mult, [kos, krb], [kta])
                self.tt(self.yT[Rh, h // 2, T0:T0 + TT], ta[Rh], sbz[Rh, h // 2, :], ALU.mult, [kta, "sbz"], ["yT_b"])

    def stage_dil(self, l, seq):
        A = self.A
        cv = self.colv
        hT = self.hT
        r_dq = self.ring(2, [S], BF16, "dqT")
        r_dk = self.ring(2, [S], BF16, "dkT")
        r_dv = self.ring(2, [S], BF16, "dvT")
        r_dva = self.ring(2, [NB, 2, 65], BF16, "dva")
        r_acc = self.ring(2, [2, S], F32, "acc")
        r_scz = self.ring(2, [S], BF16, "scz")
        r_sq = self.ring(2, [TT], BF16, "sq")
        r_sd = self.ring(2, [TT], F32, "sd")
        r_rs = self.ring(2, [TT], F32, "rs")
        r_ta = self.ring(2, [TT], F32, "ta")
        r_pt = self.ring(4, [256], BF16, "dPT")
        r_rb = self.ring(4, [TT], F32, "rb")
        pP = self.psring([2, 3, 4, 5])
        pS_ = self.psring([6, 7])
        pO_ = self.psring([0, 1])
        for _ in range(2):
            dva, kdva = r_dva()
            self.memset(dva[:, :, :, 64:65], 1.0, [kdva])

        def rstd(ssq, kss, rows, inv_n):
            sd, ksd = r_sd()
            rs, krs = r_rs()
            self.act(sd[0:rows], ssq[0:rows], AF.Ln, ["const"], [ksd, kss], scale=inv_n, bias=self.epsc[0:rows, 0:1])
            self.act(rs[0:rows], sd[0:rows], AF.Exp, [ksd], [krs], scale=-0.5)
            return rs, krs

        def perm_out(buf, g, t):
            if g == 0:
                return buf[:, t * TT:(t + 1) * TT], None
            if g == 1:
                return buf.rearrange("p (r m i) -> p r m i", r=4, m=4)[:, :, t, :], ("p (i r) -> p r i", 4)
            return buf.rearrange("p (r i) -> p r i", r=16)[:, :, 32 * t:32 * t + 32], ("p (i r) -> p r i", 16)

        def segments(g):
            segs = []
            for s_ in range(4):
                u = []
                if g == 0:
                    if s_ > 0:
                        u.append(((4 * s_ - 1) * 128, 4 * s_ - 1, 4 * s_ * 128, 128, 128))
                    for kb in range(4 * s_, 4 * s_ + 4):
                        u.append((kb * 128, kb, kb * 128, 256 if kb < 4 * s_ + 3 else 128, 0))
                elif g == 1:
                    for m in range(4):
                        blk = 4 * s_ + m
                        u.append((blk * 128, blk, blk * 128, 256 if m < 3 else 128, 0))
                else:
                    for c in range(4):
                        blk = 4 * s_ + c
                        u.append((blk * 128, blk, blk * 128, 128, 0))
                segs.append(u)
            return segs

        for j in range(4):
            scz, kscz = r_scz()
            acc, kacc = r_acc()
            wcz, kcz = self.ldw(self.win_cols(l, C_CZ + j * 128, 128))
            for t in range(NT):
                pc, kpc = pP()
                for k in range(8):
                    self.mm(pc, wcz[:, k, :], hT[:, k, t * TT:(t + 1) * TT], k == 0, k == 7, [kcz, "hT"], [kpc])
                self.act(scz[:, t * TT:(t + 1) * TT], pc, AF.Silu, [], [kscz, kpc])
            for g in range(3):
                c_off = g * 512 + j * 128
                wq, kwq = self.ldw(self.win_cols(l, C_DQ + c_off, 128))
                wk_, kwk = self.ldw(self.win_cols(l, C_DK + c_off, 128))
                wv, kwv = self.ldw(self.win_cols(l, C_DV + c_off, 128))
                dqT, kdq = r_dq()
                dkT, kdk = r_dk()
                dvT, kdv = r_dv()
                dva, kdva = r_dva()
                for t in range(NT):
                    hs = lambda k: hT[:, k, t * TT:(t + 1) * TT]
                    for (wt, wkey, dst, dkey, gcol) in ((wq, kwq, dqT, kdq, CV_GDQ + g), (wk_, kwk, dkT, kdk, CV_GDK + g)):
                        pp, pk = pP()
                        for k in range(8):
                            self.mm(pp, wt[:, k, :], hs(k), k == 0, k == 7, [wkey, "hT"], [pk])
                        sq, ksq = r_sq()
                        self.act(sq, pp, AF.Square, [], [ksq, pk])
                        ss, kss = pP()
                        self.mm(ss, self.ind2, sq, True, True, ["const", ksq], [kss])
                        rs, krs = rstd(ss, kss, 128, 1.0 / 64)
                        ov, pr = perm_out(dst, g, t)
                        if pr is None:
                            self.stt(ov, pp, cv[:, gcol:gcol + 1], rs, ALU.mult, ALU.mult, ["colv", krs], [dkey, pk])
                        else:
                            self.stt(ov, pp.rearrange(pr[0], r=pr[1]), cv[:, gcol:gcol + 1], rs.rearrange(pr[0], r=pr[1]),
                                     ALU.mult, ALU.mult, ["colv", krs], [dkey, pk])
                    pp, pk = pP()
                    for k in range(8):
                        self.mm(pp, wv[:, k, :], hs(k), k == 0, k == 7, [kwv, "hT"], [pk])
                    ov, pr = perm_out(dvT, g, t)
                    if pr is None:
                        self.act(ov, pp, AF.Copy, [], [kdv, pk])
                    else:
                        self.act(ov, pp.rearrange(pr[0], r=pr[1]), AF.Copy, [], [kdv, pk])
                for half in range(2):
                    pp, pk = pP()
                    pst = pp.bitcast(BF16)
                    for bb in range(8):
                        blk = half * 8 + bb
                        self.tr(pst[:, bb * 128:(bb + 1) * 128], dvT[:, blk * 128:(blk + 1) * 128], [kdv], [pk])
                    self.act(dva[:, half * 8:(half + 1) * 8, :, 0:64],
                             pst.rearrange("p (b h c) -> p b h c", b=8, h=2), AF.Copy, [], [kdva, pk])
                segs = segments(g)
                for hh in range(2):
                    Rr = slice(hh * 64, hh * 64 + 64)
                    em = self.emask[:, g * 8 + 2 * j + hh, :]
                    for b, units in enumerate(segs):
                        pO, ko = pO_()
                        first = True
                        for (K0, blk, q0, nq, m0) in units:
                            pS, ks = pS_()
                            pt, kpt = r_pt()
                            self.mm(pS[:, 0:nq], dkT[Rr, K0:K0 + 128], dqT[Rr, q0:q0 + nq], True, True, [kdk, kdq], [ks])
                            self.act(pt[:, 0:nq], pS[:, 0:nq], AF.Exp, [], [kpt, ks], scale=0.125)
                            self.tt(pt[:, 0:nq], pt[:, 0:nq], em[:, m0:m0 + nq], ALU.mult, ["const", kpt], [kpt], eng=("pool" if g == 2 else "dve"))
                            self.mm(pO[0:65, q0 - b * 512:q0 - b * 512 + nq], dva[:, blk, hh, :], pt[:, 0:nq], first, False,
                                    [kdva, kpt], [ko])
                            first = False
                        if g == 0:
                            self.act(acc[0:65, hh, b * 512:(b + 1) * 512], pO[0:65], AF.Copy, [], [kacc, ko])
                        elif g == 1:
                            av = acc[0:65, hh, :].rearrange("p (m i r) -> p r m i", m=4, r=4)[:, b]
                            self.tt(av, av, pO[0:65].rearrange("p (m i) -> p m i", m=4), ALU.add, [kacc], [kacc, ko])
                        else:
                            av = acc[0:65, hh, :].rearrange("p (i r) -> p r i", r=16)[:, 4 * b:4 * b + 4, :]
                            self.tt(av, av, pO[0:65].rearrange("p (r i) -> p r i", r=4), ALU.add, [kacc], [kacc, ko])
            for hh in range(2):
                Rh = slice(hh * 64, hh * 64 + 64)
                self.act(acc[64:65, hh, :], acc[64:65, hh, :], AF.Ln, [kacc], [kacc])
                self.act(acc[64:65, hh, :], acc[64:65, hh, :], AF.Exp, [kacc], [kacc], scale=-1.0)
                rbs = [r_rb() for _ in range(NT)]
                self.bcast_rows(acc[64:65, hh, :], kacc, [(rbs[t][0][0:64], rbs[t][1], t * TT, TT) for t in range(NT)])
                for t in range(NT):
                    cs = slice(t * TT, (t + 1) * TT)
                    rb, krb = rbs[t]
                    ta, kta = r_ta()
                    self.tt(ta[Rh], acc[0:64, hh, cs], rb[0:64], ALU.mult, [kacc, krb], [kta])
                    self.tt(self.yT[Rh, 4 + j, cs], ta[Rh], scz[Rh, cs], ALU.mult, [kta, kscz], ["yT_c"])

    def stage_conv(self, l, seq):
        A = self.A
        ps = self.ps
        cv = self.colv
        hT = self.hT
        r_up = self.ring(2, [S + 2], F32, "upad")
        r_acs = self.ring(2, [TT], F32, "acs")
        r_sz = self.ring(2, [TT], F32, "sz")
        r_c1 = self.ring(2, [TT], F32, "c1")
        r_c2 = self.ring(2, [TT], F32, "c2")
        r_t1 = self.ring(3, [TT], F32, "t1")
        n = 0
        for c in range(4):
            ws = [self.ldw(self.win_cols(l, base + c * 128, 128)) for base in (C_AB, C_AC, C_AX, C_AZ)]
            upad, kup = r_up()
            self.memset(upad[:, 0:2], 0.0, [kup])
            for t in range(NT):
                T0 = t * TT
                o = 4 * (n % 2)
                n += 1
                for i in range(4):
                    wt, wk = ws[i]
                    for k in range(8):
                        self.mm(ps[o + i], wt[:, k, :], hT[:, k, T0:T0 + TT], k == 0, k == 7, [wk, "hT"], ["ps%d" % (o + i)])
                kb_, kc_, kx_, kz_ = ["ps%d" % (o + i) for i in range(4)]
                acs, kacs = r_acs()
                sz, ksz = r_sz()
                c1, kc1 = r_c1()
                c2, kc2 = r_c2()
                self.act(acs, ps[o + 1], AF.Copy, [], [kacs, kc_])
                self.tt(upad[:, 2 + T0:2 + T0 + TT], acs, ps[o + 2], ALU.mult, [kacs], [kup, kx_])
                self.act(sz, ps[o + 3], AF.Silu, [], [ksz, kz_])
                t1, kt1 = r_t1()
                self.act(c1, upad[:, T0:T0 + TT], AF.Identity, [kup, "colv"], [kc1], scale=cv[:, CV_CW + c:CV_CW + c + 1],
                         bias=cv[:, CV_CB + c:CV_CB + c + 1])
                self.act(t1, upad[:, T0 + 1:T0 + 1 + TT], AF.Identity, [kup, "colv"], [kt1], scale=cv[:, CV_CW + 4 + c:CV_CW + 5 + c])
                self.tt(c1, c1, t1, ALU.add, [kc1, kt1], [kc1], eng="pool")
                t1, kt1 = r_t1()
                self.act(t1, upad[:, T0 + 2:T0 + 2 + TT], AF.Identity, [kup, "colv"], [kt1], scale=cv[:, CV_CW + 8 + c:CV_CW + 9 + c])
                self.tt(c1, c1, t1, ALU.add, [kc1, kt1], [kc1], eng="pool")
                self.tt(c2, c1, ps[o + 0], ALU.mult, [kc1], [kc2, kb_])
                self.tt(self.yT[:, 8 + c, T0:T0 + TT], c2, sz, ALU.mult, [kc2, ksz], ["yT_a"])

    def stage_out(self, l, seq, src, dst, final):
        A = self.A
        ps = self.ps
        cv = self.colv
        hT = self.hT
        yT = self.yT
        Wo3 = [A.alloc([4, D], BF16) for _ in range(3)]
        Wo = A.alloc([8, D], BF16)
        G = A.alloc([24, TT], BF16)
        mT = A.alloc([8, TT], BF16)
        r_m1 = self.ring(2, [TT], F32, "m1")
        r_m2 = self.ring(2, [TT], F32, "m2")
        xb = [A.alloc([D], F32) for _ in range(2)]
        ob = [A.alloc([D], F32) for _ in range(2)]
        for i in range(3):
            self.dma("pool", Wo3[i], self.w_out[i][l].rearrange("(k p) n -> p k n", p=128), key_w="Wo3_%d" % i)
        self.dma("pool", Wo, self.w_o[l].rearrange("(k p) n -> p k n", p=128), key_w="Wo")
        och = self.cout if final else self.misc_chan("scrw")
        pG = self.psring([0, 1])
        for t in range(NT):
            T0 = t * TT
            DEPTHW = 4
            wq = [self.ldw(self.win_cols(l, C_GATE + jj * 128, 128)) for jj in range(DEPTHW)]
            for jj in range(24):
                wt, wk = wq[jj]
                pg, kg = pG()
                for k in range(8):
                    self.mm(pg, wt[:, k, :], hT[:, k, T0:T0 + TT], k == 0, k == 7, [wk, "hT"], [kg])
                self.act(G[:, jj, :], pg, AF.Sigmoid, ["colv"], ["G%d" % jj, kg], bias=cv[:, CV_BG + jj:CV_BG + jj + 1])
                if jj + DEPTHW < 24:
                    wq.append(self.ldw(self.win_cols(l, C_GATE + (jj + DEPTHW) * 128, 128)))
            for j in range(8):
                o = 2 + 3 * (j % 2)
                for i in range(3):
                    for k in range(4):
                        self.mm(ps[o + i], Wo3[i][:, k, j * 128:(j + 1) * 128], yT[:, (8, 0, 4)[i] + k, T0:T0 + TT], k == 0, k == 3,
                                ["Wo3_%d" % i, ("yT_a", "yT_b", "yT_c")[i]], ["ps%d" % (o + i)])
                m1, k1 = r_m1()
                m2, k2 = r_m2()
                self.tt(m1, ps[o], G[:, j, :], ALU.mult, ["G%d" % j], [k1, "ps%d" % o])
                self.tt(m2, ps[o + 1], G[:, 8 + j, :], ALU.mult, ["G%d" % (8 + j)], [k2, "ps%d" % (o + 1)])
                self.tt(m1, m1, m2, ALU.add, [k1, k2], [k1], eng="pool")
                self.tt(m2, ps[o + 2], G[:, 16 + j, :], ALU.mult, ["G%d" % (16 + j)], [k2, "ps%d" % (o + 2)])
                self.tt(mT[:, j, :], m1, m2, ALU.add, [k1, k2], ["mT%d" % j], eng="pool")
            for bb in range(4):
                b = t * 4 + bb
                i = b % 2
                r0 = seq * S + b * 128
                self.dma("sp", xb[i], src[r0:r0 + 128, :], key_w="oxb%d" % i, key_r=("scr" if l > 0 else None))
                for half in range(2):
                    pg, kg = pG()
                    for k in range(8):
                        self.mm(pg, mT[:, k, bb * 128:(bb + 1) * 128], Wo[:, k, half * 512:(half + 1) * 512], k == 0, k == 7,
                                ["mT%d" % k, "Wo"], [kg])
                    self.tt(ob[i][:, half * 512:(half + 1) * 512], pg, xb[i][:, half * 512:(half + 1) * 512], ALU.add,
                            ["oxb%d" % i], ["ob%d" % i, kg])
                self.dma("sp", dst[r0:r0 + 128, :], ob[i], key_r="ob%d" % i, key_w=(None if final else "scr"), chan=och)


def _rope_tables():
    inv = (10000.0 ** (-np.arange(0, 32, 2, dtype=np.float32) / 32)).astype(np.float32)
    ang = np.arange(S, dtype=np.float32)[:, None] * inv[None, :]
    cos, sin = np.cos(ang).astype(np.float32), np.sin(ang).astype(np.float32)
    ct = np.ones((96, S), np.float32)
    st = np.zeros((96, S), np.float32)
    ct[64:80] = cos.T
    ct[80:96] = cos.T
    st[64:80] = -sin.T
    st[80:96] = sin.T
    return ct, st


def _emask():
    n = 24
    slopes = (2.0 ** (-8.0 * np.arange(1, n + 1, dtype=np.float32) / n)).reshape(3, 8)
    k = np.arange(128)[:, None].astype(np.float32)
    q = np.arange(128)[None, :].astype(np.float32)
    em = np.zeros((128, 25, 256), np.float32)
    for g in range(3):
        for h in range(8):
            sl = slopes[g, h] * DIL[g]
            em[:, g * 8 + h, 0:128] = np.where(k <= q, np.exp(-sl * np.maximum(q - k, 0.0)), 0.0)
            if g < 2:
                em[:, g * 8 + h, 128:256] = np.where(k >= q, np.exp(-sl * np.maximum(128.0 + q - k, 0.0)), 0.0)
    em[:, 24, 0:128] = (k <= q).astype(np.float32)
    return em.reshape(128, 25 * 256)


def _host_layout(inp, layers):
    f = lambda a: np.ascontiguousarray(a, dtype=np.float32)
    L = len(layers)
    w_uq = f(inp["w_uq"][layers])
    w_uq_sw = np.zeros_like(w_uq)
    for h in range(8):
        b = h * 96
        w_uq_sw[:, :, b + 64:b + 80] = w_uq[:, :, b + 80:b + 96]
        w_uq_sw[:, :, b + 80:b + 96] = w_uq[:, :, b + 64:b + 80]
    w_ukv = f(inp["w_ukv"][layers]).reshape(L, 128, 8, 128)
    w_ukv_k = f(w_ukv[:, :, :, 0:64].reshape(L, 128, 512))
    w_ukv_v = f(w_ukv[:, :, :, 64:128].reshape(L, 128, 512))
    w_in = inp["w_in"]
    w_kpe = np.zeros((L, D, 96), np.float32)
    w_kpe_sw = np.zeros((L, D, 96), np.float32)
    for i, l in enumerate(layers):
        kp = w_in[l][:, C_KPE:C_KPE + 32]
        w_kpe[i, :, 64:96] = kp
        w_kpe_sw[i, :, 64:80] = kp[:, 16:32]
        w_kpe_sw[i, :, 80:96] = kp[:, 0:16]
    colv = np.zeros((L, 128, NCV), np.float32)
    gbc = np.zeros((L, 128, D), np.float32)
    for i, l in enumerate(layers):
        colv[i, :, CV_BG:CV_BG + 24] = inp["b_gate"][l].reshape(24, 128).T
        for tap in range(3):
            colv[i, :, CV_CW + tap * 4:CV_CW + tap * 4 + 4] = inp["conv_w"][l][tap].reshape(4, 128).T
        colv[i, :, CV_CB:CV_CB + 4] = inp["conv_b"][l].reshape(4, 128).T
        colv[i, :, CV_GQ:CV_GQ + 2] = inp["q_a_norm_g"][l].reshape(2, 128).T
        colv[i, :, CV_GKV] = inp["kv_a_norm_g"][l]
        for (col, src) in ((CV_GQM, inp["mla_q_norm_g"][l]), (CV_GKM, inp["mla_k_norm_g"][l])):
            colv[i, 0:96, col] = src
            colv[i, 64:80, col + 1] = src[80:96]
            colv[i, 80:96, col + 1] = src[64:80]
        for g in range(3):
            colv[i, :, CV_GDQ + g] = np.tile(inp["dil_q_norm_g"][l][g], 2)
            colv[i, :, CV_GDK + g] = np.tile(inp["dil_k_norm_g"][l][g], 2)
        gbc[i] = np.broadcast_to(inp["norm_g"][l][None, :], (128, D))
    ind2 = np.zeros((128, 128), np.float32)
    ind2[0:64, 0:64] = 1.0
    ind2[64:128, 64:128] = 1.0
    ct, st = _rope_tables()
    shared = {
        "w_in": f(w_in[layers]), "w_uq": w_uq, "w_uq_sw": w_uq_sw, "w_ukv_k": w_ukv_k, "w_ukv_v": w_ukv_v,
        "w_kpe": w_kpe, "w_kpe_sw": w_kpe_sw,
        "w_out_a": f(inp["w_out_a"][layers]), "w_out_b": f(inp["w_out_b"][layers]), "w_out_c": f(inp["w_out_c"][layers]),
        "w_o": f(inp["w_o"][layers]), "colv": colv, "gbc": gbc,
        "ident": np.eye(128, dtype=np.float32), "ind2": ind2, "emask": _emask(), "ctab": ct, "stab": st,
    }
    return shared


_CACHE = {}


def _get_builder(nseq, nlayers, dbg=False):
    key = (nseq, nlayers, dbg)
    if key not in _CACHE:
        _CACHE[key] = Builder(nseq, nlayers, dbg=dbg)
    return _CACHE[key]


def kernel(**inputs):
    inp = {k: np.asarray(v) for k, v in inputs.items()}
    x = np.ascontiguousarray(inp["x"], dtype=np.float32)
    B = x.shape[0]
    ncores = 8
    nseq = B // ncores
    shared = _host_layout(inp, list(range(DEPTH)))
    bld = Builder(nseq, DEPTH)
    in_maps = []
    for c in range(ncores):
        m = dict(shared)
        m["x"] = x[c * nseq:(c + 1) * nseq].reshape(nseq * S, D)
        in_maps.append(m)
    res = run_bass_kernel_spmd(bld.nc, in_maps, core_ids=list(range(ncores)))
    outs = [np.asarray(r["out"]).reshape(nseq, S, D) for r in res.results]
    return np.concatenate(outs, axis=0).astype(np.float32)
```

```python
import numpy as np
import concourse.bass as bass
import concourse.mybir as mybir
from concourse.bass_utils import run_bass_kernel_spmd

F32 = mybir.dt.float32
BF16 = mybir.dt.bfloat16
AF = mybir.ActivationFunctionType
ALU = mybir.AluOpType

ENGS = ("pe", "act", "dve", "pool", "sp")

D = 1024
S = 2048
DEPTH = 2
NB = 16
NT = 4
TT = 512
EPS = 1e-6
N_IN = 11168
C_AB, C_AC, C_AX, C_AZ = 0, 512, 1024, 1536
C_CQ, C_CKV, C_KPE, C_BZ = 2048, 2304, 2432, 2464
C_DQ, C_DK, C_DV, C_CZ, C_GATE = 2976, 4512, 6048, 7584, 8096
DIL = (1, 4, 16)
CV_BG, CV_CW, CV_CB, CV_GQ, CV_GKV, CV_GQM, CV_GQMS, CV_GKM, CV_GKMS, CV_GDQ, CV_GDK, NCV = 0, 24, 36, 40, 42, 43, 44, 45, 46, 47, 50, 53


import heapq

RAW, WAR, WAW = 0, 1, 2
ACT_SET = {"Exp": "exp", "Ln": "exp", "Sqrt": "sqrt", "Silu": "silu", "Sigmoid": "sigmoid"}


class Chan:
    def __init__(self, sem, name):
        self.sem = sem
        self.name = name
        self.count = 0


class Op:
    __slots__ = ("eng", "fn", "idx", "pidx", "deps", "succ", "ndeps", "waits", "signal", "sigcount", "chan",
                 "dma_deps", "dur", "tset", "ready", "finish", "seg", "isbar")

    def __init__(self, eng, fn):
        self.eng = eng
        self.fn = fn
        self.idx = -1
        self.pidx = -1
        self.deps = []
        self.succ = []
        self.ndeps = 0
        self.waits = {}
        self.dma_deps = []
        self.signal = False
        self.sigcount = 0
        self.chan = None
        self.dur = 0.5
        self.tset = None
        self.ready = 0.0
        self.finish = 0.0
        self.seg = 0
        self.isbar = False


class Prog:
    def __init__(self, nc):
        self.nc = nc
        self.ops = []
        self.kw = {}
        self.kr = {}
        self.sems = {e: nc.alloc_semaphore("sem_" + e) for e in ENGS}
        self.nops = 0
        self.chans = []
        self.seg = 0
        self.sched = True
        self.keymap = {}

    def chan(self, name):
        c = Chan(self.nc.alloc_semaphore("dsem_" + name), name)
        self.chans.append(c)
        return c

    def barrier(self):
        self.seg += 1

    def add(self, eng, fn, reads=(), writes=(), chan=None, dur=0.5, tset=None):
        op = Op(eng, fn)
        op.pidx = len(self.ops)
        op.seg = self.seg
        op.chan = chan
        op.dur = dur
        op.tset = tset
        self.ops.append(op)
        self.nops += 1
        km = self.keymap
        if km:
            reads = list(reads) + [p for k in reads for p in km.get(k, ())]
            writes = list(writes) + [p for k in writes for p in km.get(k, ())]
        deps = {}
        for k in reads:
            w = self.kw.get(k)
            if w is not None:
                deps[w] = RAW
        for k in writes:
            w = self.kw.get(k)
            if w is not None and w not in deps:
                deps[w] = WAW
            for r in self.kr.get(k, ()):
                if r is not op and r not in deps:
                    deps[r] = WAR
        for k in reads:
            self.kr.setdefault(k, []).append(op)
        for k in writes:
            self.kw[k] = op
            self.kr[k] = []
        op.deps = [(d, kind) for d, kind in deps.items() if d.seg == op.seg]
        return op

    def _schedule(self, ops):
        if not self.sched:
            return list(ops)
        for op in ops:
            op.ndeps = len(op.deps)
            op.succ = []
            op.ready = 0.0
        for op in ops:
            for d, _ in op.deps:
                d.succ.append(op)
        free = {e: 0.0 for e in ENGS}
        pending = {e: [] for e in ENGS}
        avail = {e: {} for e in ENGS}
        curset = [None]
        for op in ops:
            if op.ndeps == 0:
                heapq.heappush(pending[op.eng], (0.0, op.pidx, op))
        order = []
        LAT = 0.3
        n = len(ops)

        def cand(e):
            t = free[e]
            pend = pending[e]
            av = avail[e]
            while pend and pend[0][0] <= t:
                _, pi, o = heapq.heappop(pend)
                heapq.heappush(av.setdefault(o.tset if e == "act" else None, []), (pi, o))
            best = None
            if e == "act":
                for ts_ in (None, curset[0]):
                    h = av.get(ts_)
                    if h and (best is None or h[0][0] < best[0]):
                        best = (h[0][0], ts_)
                if best is None:
                    for ts_, h in av.items():
                        if h and (best is None or h[0][0] < best[0]):
                            best = (h[0][0], ts_)
            else:
                h = av.get(None)
                if h:
                    best = (h[0][0], None)
            if best is not None:
                return (t, best[0], best[1], False)
            if pend:
                return (pend[0][0], pend[0][1], None, True)
            return None

        while len(order) < n:
            bc = None
            be = None
            for e in ENGS:
                c = cand(e)
                if c is not None and (bc is None or (c[0], c[1]) < (bc[0], bc[1])):
                    bc, be = c, e
            assert bc is not None, "scheduler stuck"
            if bc[3]:
                _, pi, o = heapq.heappop(pending[be])
            else:
                pi, o = heapq.heappop(avail[be][bc[2]])
            start = max(bc[0], free[be])
            dur = o.dur
            if be == "act" and o.tset is not None and o.tset != curset[0]:
                dur += 2.7
                curset[0] = o.tset
            if o.chan is not None:
                free[be] = start + 0.15
                o.finish = start + 2.0 + dur
            else:
                free[be] = start + dur
                o.finish = start + dur
            order.append(o)
            for s_ in o.succ:
                s_.ndeps -= 1
                rt = o.finish + (LAT if s_.eng != o.eng else 0.05)
                if rt > s_.ready:
                    s_.ready = rt
                if s_.ndeps == 0:
                    heapq.heappush(pending[s_.eng], (s_.ready, s_.pidx, s_))
        return order

    def emit(self, final_chans=()):
        nc = self.nc
        nseg = self.seg + 1
        segs = [[] for _ in range(nseg)]
        for op in self.ops:
            segs[op.seg].append(op)
        glob = []
        for si in range(nseg):
            if si > 0:
                drains = []
                for e in ENGS:
                    if e == "sp":
                        continue
                    d = Op(e, lambda h: h.drain())
                    d.isbar = True
                    drains.append(d)
                    glob.append(d)
                for e in ENGS:
                    w = Op(e, lambda h: h.nop())
                    w.isbar = True
                    w.deps = [(d, RAW) for d in drains if d.eng != e]
                    w.dma_deps = "all"
                    glob.append(w)
            glob.extend(self._schedule(segs[si]))
        self.eng_ops = {e: [] for e in ENGS}
        for op in glob:
            op.idx = len(self.eng_ops[op.eng])
            self.eng_ops[op.eng].append(op)
        chan_cnt = {c: 0 for c in self.chans}
        dma_wait_vals = {}
        for op in glob:
            isdma = op.chan is not None
            dw = {}
            if op.dma_deps == "all":
                for c in self.chans:
                    if chan_cnt[c] > 0:
                        dw[c] = chan_cnt[c]
            for d, kind in op.deps:
                if d.chan is not None:
                    dw[d.chan] = chan_cnt[d.chan]
                    continue
                if d.eng == op.eng and not isdma and kind != RAW:
                    continue
                if d.eng == op.eng and d.eng == "pe":
                    continue
                cur = op.waits.get(d.eng)
                if cur is None or cur.idx < d.idx:
                    op.waits[d.eng] = d
            if isdma:
                chan_cnt[op.chan] += 1
            dma_wait_vals[op] = dw
        for op in glob:
            for p in op.waits.values():
                p.signal = True
        for e in ENGS:
            c = 0
            for op in self.eng_ops[e]:
                if op.signal:
                    c += 1
                op.sigcount = c
        final_cnt = {c: chan_cnt[c] for c in final_chans}

        def run_engine(e, h):
            seen = {}
            for op in self.eng_ops[e]:
                for pe_, p in op.waits.items():
                    need = p.sigcount
                    if seen.get(pe_, 0) < need:
                        h.wait_ge(self.sems[pe_], need)
                        seen[pe_] = need
                for c, cnt in dma_wait_vals[op].items():
                    if seen.get(c, 0) < cnt:
                        h.wait_ge(c.sem, 16 * cnt)
                        seen[c] = cnt
                ins = op.fn(h)
                if op.chan is not None:
                    ins.then_inc(op.chan.sem, 16)
                elif op.signal:
                    ins.then_inc(self.sems[e], 1)
            if e == "sp":
                for c, cnt in final_cnt.items():
                    h.wait_ge(c.sem, 16 * cnt)

        with nc.Block() as block:
            @block.tensor
            def _(h):
                run_engine("pe", h)

            @block.scalar
            def _(h):
                run_engine("act", h)

            @block.vector
            def _(h):
                run_engine("dve", h)

            @block.gpsimd
            def _(h):
                run_engine("pool", h)

            @block.sync
            def _(h):
                run_engine("sp", h)


class Arena:
    def __init__(self, nc, nbytes):
        self.cap = nbytes // 2
        self.t = nc.alloc_sbuf_tensor("arena", [128, self.cap], BF16).ap()
        self.top = 0
        self.paged = False
        self.keymap = {}

    def pages(self, off, ne):
        return ["pg%d" % i for i in range(off // 256, (off + ne - 1) // 256 + 1)]

    def alloc(self, free_shape, dtype, key=None):
        n = 1
        for v in free_shape:
            n *= v
        ne = n * (2 if dtype == F32 else 1)
        ne_al = (ne + 255) // 256 * 256 if self.paged else (ne + 31) // 32 * 32
        off = self.top
        self.top += ne_al
        assert self.top <= self.cap, ("arena overflow", self.top * 2, self.cap * 2)
        self.last = (off, ne)
        if key is not None and self.paged:
            for k in ([key] if isinstance(key, str) else key):
                self.keymap[k] = self.pages(off, ne)
        v = self.t[:, off:off + ne]
        if dtype == F32:
            v = v.bitcast(F32)
        if len(free_shape) > 1:
            names = " ".join("a%d" % i for i in range(len(free_shape)))
            kw = {"a%d" % i: free_shape[i] for i in range(len(free_shape))}
            v = v.rearrange("p (%s) -> p %s" % (names, names), **kw)
        return v


class Builder:
    def __init__(self, nseq, nlayers, layer0=0, dbg=False):
        self.nseq = nseq
        self.nlayers = nlayers
        self.dbg = dbg
        nc = bass.Bass("TRN2", target_bir_lowering=False)
        self.nc = nc
        P = Prog(nc)
        self.P = P
        L = nlayers
        dt = nc.dram_tensor
        self.x = dt("x", [nseq * S, D], F32, kind="ExternalInput").ap()
        self.out = dt("out", [nseq * S, D], F32, kind="ExternalOutput").ap()
        self.w_in = dt("w_in", [L, D, N_IN], F32, kind="ExternalInput").ap()
        self.w_uq = dt("w_uq", [L, 256, 768], F32, kind="ExternalInput").ap()
        self.w_uq_sw = dt("w_uq_sw", [L, 256, 768], F32, kind="ExternalInput").ap()
        self.w_ukv_k = dt("w_ukv_k", [L, 128, 512], F32, kind="ExternalInput").ap()
        self.w_ukv_v = dt("w_ukv_v", [L, 128, 512], F32, kind="ExternalInput").ap()
        self.w_kpe = dt("w_kpe", [L, D, 96], F32, kind="ExternalInput").ap()
        self.w_kpe_sw = dt("w_kpe_sw", [L, D, 96], F32, kind="ExternalInput").ap()
        self.w_out = [dt("w_out_%s" % n, [L, 512, D], F32, kind="ExternalInput").ap() for n in "abc"]
        self.w_o = dt("w_o", [L, D, D], F32, kind="ExternalInput").ap()
        self.colv_d = dt("colv", [L, 128, NCV], F32, kind="ExternalInput").ap()
        self.gbc_d = dt("gbc", [L, 128, D], F32, kind="ExternalInput").ap()
        self.ident_d = dt("ident", [128, 128], F32, kind="ExternalInput").ap()
        self.ind2_d = dt("ind2", [128, 128], F32, kind="ExternalInput").ap()
        self.emask_d = dt("emask", [128, 25 * 256], F32, kind="ExternalInput").ap()
        self.ct_d = dt("ctab", [96, S], F32, kind="ExternalInput").ap()
        self.st_d = dt("stab", [96, S], F32, kind="ExternalInput").ap()
        if nlayers > 1:
            self.scr = dt("scr", [nseq * S, D], F32).ap()
        self.bsc = dt("bsc", [4, S], F32).ap()
        self.bsi = 0
        if dbg:
            self.dbg_h = dt("dbg_h", [128, 8 * S], BF16, kind="ExternalOutput").ap()
            self.dbg_y = dt("dbg_y", [128, 12 * S], BF16, kind="ExternalOutput").ap()
        self.ps = [nc.alloc_psum_tensor("ps%d" % i, [128, 512], F32).ap() for i in range(8)]
        A = Arena(nc, 212480)
        self.A = A
        self.hT = A.alloc([8, S], BF16)
        self.yT = A.alloc([12, S], BF16)
        self.ident = A.alloc([128], BF16)
        self.ones = A.alloc([128], BF16)
        self.ind2 = A.alloc([128], BF16)
        self.onesf = A.alloc([64], F32)
        self.emask = A.alloc([25, 256], BF16)
        self.colv = A.alloc([NCV], F32)
        self.epsc = A.alloc([1], F32)
        self.NWR = 8
        self.wr = [A.alloc([8, 128], BF16) for _ in range(self.NWR)]
        self.wrc = [P.chan("wr%d" % i) for i in range(self.NWR)]
        self.wri = 0
        A.top = (A.top + 255) // 256 * 256
        self.stage_base = A.top
        A.paged = True
        A.keymap = P.keymap
        self.cst = P.chan("const")
        self.cout = P.chan("out")
        self.cmisc = {}
        self.build()

    def misc_chan(self, key):
        if key not in self.cmisc:
            self.cmisc[key] = self.P.chan("m%d" % len(self.cmisc))
        return self.cmisc[key]

    def dma(self, q, out, in_, key_w=None, key_r=None, chan=None):
        if chan is None:
            chan = self.misc_chan(key_w if key_w is not None else key_r)
        nbytes = out.free_size() * out.partition_size() * 4
        self.P.add(q, lambda h: h.dma_start(out=out, in_=in_),
                   reads=[key_r] if key_r else [], writes=[key_w] if key_w else [], chan=chan, dur=nbytes / 150e3)

    def mm(self, out, lhsT, rhs, start, stop, r, w):
        self.P.add("pe", lambda h: h.matmul(out, lhsT=lhsT, rhs=rhs, start=start, stop=stop, skip_group_check=True), r, w,
                   dur=(0.03 + out.free_size() * 0.00043) * (4.0 if lhsT.dtype == F32 else 1.0))

    def tr(self, out, in_, r, w):
        ident = self.ident
        self.P.add("pe", lambda h: h.transpose(out, in_, ident), list(r) + ["const"], w, dur=0.1)

    def act(self, out, in_, func, r, w, scale=None, bias=None, accum=None):
        kw = {}
        if scale is not None:
            kw["scale"] = scale
        if bias is not None:
            kw["bias"] = bias
        if accum is not None:
            kw["accum_out"] = accum
        self.P.add("act", lambda h: h.activation(out=out, in_=in_, func=func, **kw), r, w,
                   dur=0.2 + out.free_size() * 0.0009, tset=ACT_SET.get(func.name))

    def _dd(self, out, eng="dve"):
        if eng == "pool":
            return 0.25 + out.free_size() * (0.0021 if out.dtype == F32 else 0.0009)
        return 0.12 + out.free_size() * (0.0015 if out.dtype == F32 else 0.0009)

    def tt(self, out, in0, in1, op, r, w, eng="dve"):
        self.P.add(eng, lambda h: h.tensor_tensor(out=out, in0=in0, in1=in1, op=op), r, w, dur=self._dd(out, eng))

    def stt(self, out, in0, scalar, in1, op0, op1, r, w, eng="dve"):
        self.P.add(eng, lambda h: h.scalar_tensor_tensor(out=out, in0=in0, scalar=scalar, in1=in1, op0=op0, op1=op1), r, w,
                   dur=self._dd(out, eng))

    def ts(self, out, in0, s1, s2, op0, op1, r, w, eng="dve"):
        if s2 is None:
            self.P.add(eng, lambda h: h.tensor_scalar(out=out, in0=in0, scalar1=s1, scalar2=None, op0=op0), r, w,
                       dur=self._dd(out, eng))
        else:
            self.P.add(eng, lambda h: h.tensor_scalar(out=out, in0=in0, scalar1=s1, scalar2=s2, op0=op0, op1=op1), r, w,
                       dur=self._dd(out, eng))

    def recip(self, out, in_, r, w):
        self.P.add("dve", lambda h: h.reciprocal(out=out, in_=in_), r, w, dur=self._dd(out))

    def cpy(self, out, in_, r, w, eng="dve"):
        self.P.add(eng, lambda h: h.tensor_copy(out=out, in_=in_), r, w, dur=self._dd(out))

    def memset(self, ap, val, w, eng="dve"):
        self.P.add(eng, lambda h: h.memset(ap, val), [], w, dur=self._dd(ap))

    def bcast_rows(self, src_row, key_src, dsts):
        from concourse.ap import AP
        i = self.bsi % 4
        self.bsi += 1
        n = src_row.free_size()
        self.dma("sp", self.bsc[i:i + 1, 0:n], src_row, key_w="bsc%d" % i, key_r=key_src)
        for (dst, kdst, c0, cn) in dsts:
            row = self.bsc[i:i + 1, c0:c0 + cn]
            src = AP(row.tensor, row.offset, [[0, 64], [1, cn]])
            self.dma("sp", dst, src, key_w=kdst, key_r="bsc%d" % i)

    def ldw(self, src):
        i = self.wri % self.NWR
        self.wri += 1
        self.dma("pool", self.wr[i], src, key_w="wr%d" % i, chan=self.wrc[i])
        return self.wr[i], "wr%d" % i

    def win_cols(self, l, c0, n):
        return self.w_in[l, :, c0:c0 + n].rearrange("(k p) n -> p k n", p=128)

    def rstd_from(self, ssq_ps, pkey, rows, inv_n, sd, rs):
        self.act(sd[0:rows], ssq_ps[0:rows], AF.Sqrt, ["const"], ["sd", pkey], scale=inv_n, bias=self.epsc[0:rows, 0:1])
        self.recip(rs[0:rows], sd[0:rows], ["sd"], ["rs"])

    def build(self):
        P = self.P
        self.dma("pool", self.ident, self.ident_d, key_w="const", chan=self.cst)
        self.dma("pool", self.ind2, self.ind2_d, key_w="const", chan=self.cst)
        self.dma("pool", self.emask, self.emask_d.rearrange("p (a b) -> p a b", a=25), key_w="const", chan=self.cst)
        self.memset(self.ones, 1.0, ["const"])
        self.memset(self.onesf, 1.0, ["const"])
        self.memset(self.epsc, EPS, ["const"])
        for l in range(self.nlayers):
            src = self.x if l == 0 else self.scr
            dst = self.out if l == self.nlayers - 1 else self.scr
            self.dma("sp", self.colv, self.colv_d[l], key_w="colv")
            for seq in range(self.nseq):
                self.A.top = self.stage_base
                self.stage0(l, seq, src)
                if self.dbg and l == 0 and seq == 0:
                    self.dma("sp", self.dbg_h.rearrange("p (a b) -> p a b", a=8), self.hT, key_r="hT", chan=self.cout)
                self.A.top = self.stage_base
                self.stage_mla(l, seq)
                self.A.top = self.stage_base
                self.stage_dil(l, seq)
                self.A.top = self.stage_base
                self.stage_conv(l, seq)
                if self.dbg and l == 0 and seq == 0:
                    self.P.add("sp", lambda h: h.dma_start(out=self.dbg_y.rearrange("p (a b) -> p a b", a=12), in_=self.yT), reads=["Y%d_%d" % (c_, t_) for c_ in range(12) for t_ in range(4)], chan=self.cout)
                self.A.top = self.stage_base
                self.stage_out(l, seq, src, dst, final=(l == self.nlayers - 1))
        P.emit(final_chans=[self.cout] + ([self.cmisc["scrw"]] if "scrw" in self.cmisc else []))

    def stage0(self, l, seq, src):
        A = self.A
        NBUF = 4
        xb = [A.alloc([D], F32, key="xb%d" % i) for i in range(NBUF)]
        hn = [A.alloc([D], BF16, key="hn%d" % i) for i in range(NBUF)]
        junk = A.alloc([D], BF16, key="junk")
        gbc = A.alloc([D], F32, key="s0gbc")
        r_ss = self.ring(NBUF, [1], F32, "ss")
        r_sd = self.ring(NBUF, [1], F32, "sd1")
        r_rs = self.ring(NBUF, [1], F32, "rs1")
        self.dma("sp", gbc, self.gbc_d[l], key_w="s0gbc")
        for b in range(NB):
            i = b % NBUF
            r0 = seq * S + b * 128
            ss, kss = r_ss()
            sd1, ksd = r_sd()
            rs1, krs = r_rs()
            self.dma("sp", xb[i], src[r0:r0 + 128, :], key_w="xb%d" % i, key_r=("scr" if l > 0 else None))
            self.memset(ss, 0.0, [kss])
            self.act(junk, xb[i], AF.Square, ["xb%d" % i], ["junk", kss], accum=ss)
            self.act(sd1, ss, AF.Ln, [kss, "const"], [ksd], scale=1.0 / D, bias=self.epsc[:, 0:1])
            self.act(rs1, sd1, AF.Exp, [ksd], [krs], scale=-0.5)
            self.stt(hn[i], xb[i], rs1[:, 0:1], gbc, ALU.mult, ALU.mult, ["xb%d" % i, krs, "s0gbc"], ["hn%d" % i])
            pb = b % 8
            pst = self.ps[pb].bitcast(BF16)
            for k in range(8):
                self.tr(pst[:, k * 128:(k + 1) * 128], hn[i][:, k * 128:(k + 1) * 128], ["hn%d" % i], ["ps%d" % pb])
            if b % 2 == 0:
                self.act(self.hT[:, :, b * 128:(b + 1) * 128], pst.rearrange("p (k n) -> p k n", k=8), AF.Copy,
                         [], ["ps%d" % pb, "hT"])
            else:
                self.cpy(self.hT[:, :, b * 128:(b + 1) * 128], pst.rearrange("p (k n) -> p k n", k=8), [], ["ps%d" % pb, "hT"])

    def ring(self, n, shape, dtype, name):
        aps = [self.A.alloc(shape, dtype, key="%s%d" % (name, i)) for i in range(n)]
        st = {"i": 0}

        def nxt():
            i = st["i"] % n
            st["i"] += 1
            return aps[i], "%s%d" % (name, i)
        return nxt

    def psring(self, banks):
        st = {"i": 0}

        def nxt():
            b = banks[st["i"] % len(banks)]
            st["i"] += 1
            return self.ps[b], "ps%d" % b
        return nxt

    def stage_mla(self, l, seq):
        A = self.A
        cv = self.colv
        Kc = self.yT[:, 4:12, :]
        KK = ["yT_c", "yT_a"]
        Va = A.alloc([NB, 8, 65], BF16, key="Va")
        for t_ in range(NT):
            self.P.keymap["Va%d" % t_] = A.pages(A.last[0] + t_ * 4 * 8 * 65, 4 * 8 * 65)
        qT = A.alloc([8, TT], BF16, key=["qT%d" % h_ for h_ in range(8)])
        cqn = A.alloc([2, TT], BF16, key="cqn")
        ckvn = A.alloc([TT], BF16, key="ckvn")
        sqpe = A.alloc([TT], BF16, key="sqpe")
        kr = A.alloc([TT], F32, key="kr")
        ctt = A.alloc([TT], F32, key="ctt")
        stt_ = A.alloc([TT], F32, key="stt")
        cqf = A.alloc([TT], F32, key="cqf")
        sqf = A.alloc([TT], F32, key="sqf")
        ckf = A.alloc([TT], F32, key="ckf")
        skf = A.alloc([TT], F32, key="skf")
        sbz = A.alloc([4, TT], BF16, key="sbz")
        r_sq = self.ring(2, [TT], BF16, "sq")
        r_sd = self.ring(2, [TT], F32, "sd")
        r_rs = self.ring(2, [TT], F32, "rs")
        r_ta = self.ring(2, [TT], F32, "ta")
        r_tb = self.ring(2, [TT], F32, "tb")
        r_pt = self.ring(3, [TT], BF16, "PT")
        r_osb = self.ring(2, [TT], F32, "osb")
        r_rb = self.ring(2, [TT], F32, "rb")
        Wlat = A.alloc([8, 384], BF16, key="Wlat")
        Wkpe = A.alloc([8, 96], BF16, key="Wkpe")
        Wkpes = A.alloc([8, 96], BF16, key="Wkpes")
        Wuq = A.alloc([2, 768], BF16, key="Wuq")
        Wuqs = A.alloc([2, 768], BF16, key="Wuqs")
        Wk = A.alloc([512], BF16, key="Wk")
        Wv = A.alloc([512], BF16, key="Wv")
        pP = self.psring([0, 1, 2])
        pSS = self.psring([3, 4])
        pS_ = self.psring([6, 7])
        pO_ = self.psring([5])
        hT = self.hT
        self.dma("pool", Wlat, self.win_cols(l, C_CQ, 384), key_w="Wlat")
        self.dma("pool", Wkpe, self.w_kpe[l].rearrange("(k p) n -> p k n", p=128), key_w="Wkpe")
        self.dma("pool", Wkpes, self.w_kpe_sw[l].rearrange("(k p) n -> p k n", p=128), key_w="Wkpes")
        self.dma("pool", Wuq, self.w_uq[l].rearrange("(k p) n -> p k n", p=128), key_w="Wuq")
        self.dma("pool", Wuqs, self.w_uq_sw[l].rearrange("(k p) n -> p k n", p=128), key_w="Wuqs")
        self.dma("pool", Wk, self.w_ukv_k[l], key_w="Wk")
        self.dma("pool", Wv, self.w_ukv_v[l], key_w="Wv")
        self.memset(Va[:, :, :, 64:65], 1.0, ["Va"])
        sc_mla = 96.0 ** -0.5
        R = slice(64, 96)

        def rstd(ssq, kss, rows, inv_n):
            sd, ksd = r_sd()
            rs, krs = r_rs()
            self.act(sd[0:rows], ssq[0:rows], AF.Ln, ["const"], [ksd, kss], scale=inv_n, bias=self.epsc[0:rows, 0:1])
            self.act(rs[0:rows], sd[0:rows], AF.Exp, [ksd], [krs], scale=-0.5)
            return rs, krs

        for t in range(NT):
            T0 = t * TT
            hs = lambda k: hT[:, k, T0:T0 + TT]
            wbz = [self.ldw(self.win_cols(l, C_BZ + c * 128, 128)) for c in range(4)]
            self.dma("sp", ctt[0:96], self.ct_d[:, T0:T0 + TT], key_w="ctt")
            self.dma("sp", stt_[0:96], self.st_d[:, T0:T0 + TT], key_w="stt")
            self.ts(cqf[0:96], ctt[0:96], cv[0:96, CV_GQM:CV_GQM + 1], None, ALU.mult, None, ["ctt", "colv"], ["cqf"])
            self.ts(sqf[0:96], stt_[0:96], cv[0:96, CV_GQMS:CV_GQMS + 1], None, ALU.mult, None, ["stt", "colv"], ["sqf"])
            self.ts(ckf[0:96], ctt[0:96], cv[0:96, CV_GKM:CV_GKM + 1], None, ALU.mult, None, ["ctt", "colv"], ["ckf"])
            self.ts(skf[0:96], stt_[0:96], cv[0:96, CV_GKMS:CV_GKMS + 1], None, ALU.mult, None, ["stt", "colv"], ["skf"])
            pcq = [pP(), pP()]
            sqs = []
            for j in range(2):
                p_, k_ = pcq[j]
                for k in range(8):
                    self.mm(p_, Wlat[:, k, j * 128:(j + 1) * 128], hs(k), k == 0, k == 7, ["Wlat", "hT"], [k_])
                sq, ksq = r_sq()
                self.act(sq, p_, AF.Square, [], [ksq, k_])
                sqs.append((sq, ksq))
            ss, kss = pSS()
            self.mm(ss, self.ones, sqs[0][0], True, False, ["const", sqs[0][1]], [kss])
            self.mm(ss, self.ones, sqs[1][0], False, True, ["const", sqs[1][1]], [kss])
            rs, krs = rstd(ss, kss, 128, 1.0 / 256)
            for j in range(2):
                p_, k_ = pcq[j]
                self.stt(cqn[:, j, :], p_, cv[:, CV_GQ + j:CV_GQ + j + 1], rs, ALU.mult, ALU.mult, ["colv", krs], ["cqn", k_])
            p_, k_ = pP()
            for k in range(8):
                self.mm(p_, Wlat[:, k, 256:384], hs(k), k == 0, k == 7, ["Wlat", "hT"], [k_])
            sq, ksq = r_sq()
            self.act(sq, p_, AF.Square, [], [ksq, k_])
            ss, kss = pSS()
            self.mm(ss, self.ones, sq, True, True, ["const", ksq], [kss])
            rs, krs = rstd(ss, kss, 128, 1.0 / 128)
            self.stt(ckvn, p_, cv[:, CV_GKV:CV_GKV + 1], rs, ALU.mult, ALU.mult, ["colv", krs], ["ckvn", k_])
            p3, k3 = pP()
            for k in range(8):
                self.mm(p3[0:96], Wkpe[:, k, :], hs(k), k == 0, k == 7, ["Wkpe", "hT"], [k3])
            self.act(sqpe[R], p3[R], AF.Square, [], ["sqpe", k3])
            self.tt(kr[R], p3[R], ckf[R], ALU.mult, ["ckf"], ["kr", k3])
            p4, k4 = pP()
            for k in range(8):
                self.mm(p4[0:96], Wkpes[:, k, :], hs(k), k == 0, k == 7, ["Wkpes", "hT"], [k4])
            tb, ktb = r_tb()
            self.tt(tb[R], p4[R], skf[R], ALU.mult, ["skf"], [ktb, k4])
            self.tt(kr[R], kr[R], tb[R], ALU.add, [ktb, "kr"], ["kr"], eng="pool")
            for bb in range(4):
                blk = t * 4 + bb
                pv, pk = pP()
                self.mm(pv, ckvn[:, bb * 128:(bb + 1) * 128], Wv, True, True, ["ckvn", "Wv"], [pk])
                self.act(Va[:, blk, :, 0:64], pv.rearrange("p (h c) -> p h c", h=8), AF.Copy, ["Va"], ["Va%d" % t, pk])
            for h in range(8):
                pK, pk = pP()
                self.mm(pK[0:64], Wk[:, h * 64:(h + 1) * 64], ckvn, True, True, ["Wk", "ckvn"], [pk])
                sq, ksq = r_sq()
                self.act(sq[0:64], pK[0:64], AF.Square, [], [ksq, pk])
                ss, kss = pSS()
                self.mm(ss[0:96], self.ones[0:64, 0:96], sq[0:64], True, False, ["const", ksq], [kss])
                self.mm(ss[0:96], self.ones[64:96, 0:96], sqpe[R], False, True, ["const", "sqpe"], [kss])
                rs, krs = rstd(ss, kss, 96, 1.0 / 96)
                self.stt(Kc[0:64, h, T0:T0 + TT], pK[0:64], cv[0:64, CV_GKM:CV_GKM + 1], rs[0:64], ALU.mult, ALU.mult,
                         ["colv", krs], ["Y%d_%d" % (4 + h, t), pk])
                self.tt(Kc[R, h, T0:T0 + TT], kr[R], rs[R], ALU.mult, ["kr", krs], ["Y%d_%d" % (4 + h, t)])
            for c in range(4):
                wt, wk = wbz[c]
                pb, pk = pP()
                for k in range(8):
                    self.mm(pb, wt[:, k, :], hs(k), k == 0, k == 7, [wk, "hT"], [pk])
                self.act(sbz[:, c, :], pb, AF.Silu, [], ["sbz", pk])
            for h in range(8):
                pQ, kq = pP()
                for j in range(2):
                    self.mm(pQ[0:96], Wuq[:, j, h * 96:(h + 1) * 96], cqn[:, j, :], j == 0, j == 1, ["Wuq", "cqn"], [kq])
                pQs, kqs = pP()
                for j in range(2):
                    self.mm(pQs[0:96], Wuqs[:, j, h * 96:(h + 1) * 96], cqn[:, j, :], j == 0, j == 1, ["Wuqs", "cqn"], [kqs])
                sq, ksq = r_sq()
                self.act(sq[0:96], pQ[0:96], AF.Square, [], [ksq, kq])
                ss, kss = pSS()
                self.mm(ss[0:96], self.ones[0:96, 0:96], sq[0:96], True, True, ["const", ksq], [kss])
                rs, krs = rstd(ss, kss, 96, 1.0 / 96)
                ta, kta = r_ta()
                tb, ktb = r_tb()
                self.tt(ta[0:96], pQ[0:96], cqf[0:96], ALU.mult, ["cqf"], [kta, kq])
                self.tt(tb[R], pQs[R], sqf[R], ALU.mult, ["sqf"], [ktb, kqs])
                self.tt(ta[R], ta[R], tb[R], ALU.add, [ktb, kta], [kta], eng="pool")
                self.tt(qT[0:96, h, :], ta[0:96], rs[0:96], ALU.mult, [kta, krs], ["qT%d" % h])
            for h in range(8):
                nkb = 4 * t + 4
                pO, ko = pO_()
                for kb in range(nkb):
                    c0 = 0 if kb < 4 * t else (kb - 4 * t) * 128
                    pS, ks = pS_()
                    pt, kpt = r_pt()
                    self.mm(pS[:, c0:TT], Kc[0:96, h, kb * 128:(kb + 1) * 128], qT[0:96, h, c0:TT], True, True,
                            ["Y%d_%d" % (4 + h, kb // 4), "qT%d" % h], [ks])
                    self.act(pt[:, c0:TT], pS[:, c0:TT], AF.Exp, [], [kpt, ks], scale=sc_mla)
                    if kb >= 4 * t:
                        self.tt(pt[:, c0:c0 + 128], pt[:, c0:c0 + 128], self.emask[:, 24, 0:128], ALU.mult, ["const", kpt], [kpt], eng="pool")
                    self.mm(pO[0:65, c0:TT], Va[:, kb, h, :], pt[:, c0:TT], kb == 0, kb == nkb - 1, ["Va", "Va%d" % (kb // 4), kpt], [ko])
                osb, kos = r_osb()
                self.act(osb[0:65], pO[0:65], AF.Copy, [], [kos, ko])
                self.act(osb[64:65], osb[64:65], AF.Ln, [kos], [kos])
                self.act(osb[64:65], osb[64:65], AF.Exp, [kos], [kos], scale=-1.0)
                rb, krb = r_rb()
                self.bcast_rows(osb[64:65, :], kos, [(rb[0:64], krb, 0, TT)])
                ta, kta = r_ta()
                Rh = slice(0, 64) if h % 2 == 0 else slice(64, 128)
                self.tt(ta[Rh], osb[0:64], rb[0:64], ALU.mult, [kos, krb], [kta])
                self.tt(self.yT[Rh, h // 2, T0:T0 + TT], ta[Rh], sbz[Rh, h // 2, :], ALU.mult, [kta, "sbz"], ["Y%d_%d" % (h // 2, t)])

    def stage_dil(self, l, seq):
        A = self.A
        cv = self.colv
        hT = self.hT
        r_dq = self.ring(2, [S], BF16, "dqT")
        r_dk = self.ring(2, [S], BF16, "dkT")
        r_dv = self.ring(2, [S], BF16, "dvT")
        r_dva = self.ring(2, [NB, 2, 65], BF16, "dva")
        r_acc = self.ring(2, [2, S], F32, "acc")
        r_scz = self.ring(2, [S], BF16, "scz")
        r_sq = self.ring(2, [TT], BF16, "sq")
        r_sd = self.ring(2, [TT], F32, "sd")
        r_rs = self.ring(2, [TT], F32, "rs")
        r_ta = self.ring(2, [TT], F32, "ta")
        r_pt = self.ring(4, [256], BF16, "dPT")
        r_rb = self.ring(4, [TT], F32, "rb")
        pP = self.psring([2, 3, 4, 5])
        pS_ = self.psring([6, 7])
        pO_ = self.psring([0, 1])
        for _ in range(2):
            dva, kdva = r_dva()
            self.memset(dva[:, :, :, 64:65], 1.0, [kdva])

        def rstd(ssq, kss, rows, inv_n):
            sd, ksd = r_sd()
            rs, krs = r_rs()
            self.act(sd[0:rows], ssq[0:rows], AF.Ln, ["const"], [ksd, kss], scale=inv_n, bias=self.epsc[0:rows, 0:1])
            self.act(rs[0:rows], sd[0:rows], AF.Exp, [ksd], [krs], scale=-0.5)
            return rs, krs

        def perm_out(buf, g, t):
            if g == 0:
                return buf[:, t * TT:(t + 1) * TT], None
            if g == 1:
                return buf.rearrange("p (r m i) -> p r m i", r=4, m=4)[:, :, t, :], ("p (i r) -> p r i", 4)
            return buf.rearrange("p (r i) -> p r i", r=16)[:, :, 32 * t:32 * t + 32], ("p (i r) -> p r i", 16)

        def segments(g):
            segs = []
            for s_ in range(4):
                u = []
                if g == 0:
                    if s_ > 0:
                        u.append(((4 * s_ - 1) * 128, 4 * s_ - 1, 4 * s_ * 128, 128, 128))
                    for kb in range(4 * s_, 4 * s_ + 4):
                        u.append((kb * 128, kb, kb * 128, 256 if kb < 4 * s_ + 3 else 128, 0))
                elif g == 1:
                    for m in range(4):
                        blk = 4 * s_ + m
                        u.append((blk * 128, blk, blk * 128, 256 if m < 3 else 128, 0))
                else:
                    for c in range(4):
                        blk = 4 * s_ + c
                        u.append((blk * 128, blk, blk * 128, 128, 0))
                segs.append(u)
            return segs

        for j in range(4):
            scz, kscz = r_scz()
            acc, kacc = r_acc()
            wcz, kcz = self.ldw(self.win_cols(l, C_CZ + j * 128, 128))
            for t in range(NT):
                pc, kpc = pP()
                for k in range(8):
                    self.mm(pc, wcz[:, k, :], hT[:, k, t * TT:(t + 1) * TT], k == 0, k == 7, [kcz, "hT"], [kpc])
                self.act(scz[:, t * TT:(t + 1) * TT], pc, AF.Silu, [], [kscz, kpc])
            for g in range(3):
                c_off = g * 512 + j * 128
                wq, kwq = self.ldw(self.win_cols(l, C_DQ + c_off, 128))
                wk_, kwk = self.ldw(self.win_cols(l, C_DK + c_off, 128))
                wv, kwv = self.ldw(self.win_cols(l, C_DV + c_off, 128))
                dqT, kdq = r_dq()
                dkT, kdk = r_dk()
                dvT, kdv = r_dv()
                dva, kdva = r_dva()
                for t in range(NT):
                    hs = lambda k: hT[:, k, t * TT:(t + 1) * TT]
                    for (wt, wkey, dst, dkey, gcol) in ((wq, kwq, dqT, kdq, CV_GDQ + g), (wk_, kwk, dkT, kdk, CV_GDK + g)):
                        pp, pk = pP()
                        for k in range(8):
                            self.mm(pp, wt[:, k, :], hs(k), k == 0, k == 7, [wkey, "hT"], [pk])
                        sq, ksq = r_sq()
                        self.act(sq, pp, AF.Square, [], [ksq, pk])
                        ss, kss = pP()
                        self.mm(ss, self.ind2, sq, True, True, ["const", ksq], [kss])
                        rs, krs = rstd(ss, kss, 128, 1.0 / 64)
                        ov, pr = perm_out(dst, g, t)
                        if pr is None:
                            self.stt(ov, pp, cv[:, gcol:gcol + 1], rs, ALU.mult, ALU.mult, ["colv", krs], [dkey, pk])
                        else:
                            self.stt(ov, pp.rearrange(pr[0], r=pr[1]), cv[:, gcol:gcol + 1], rs.rearrange(pr[0], r=pr[1]),
                                     ALU.mult, ALU.mult, ["colv", krs], [dkey, pk])
                    pp, pk = pP()
                    for k in range(8):
                        self.mm(pp, wv[:, k, :], hs(k), k == 0, k == 7, [kwv, "hT"], [pk])
                    ov, pr = perm_out(dvT, g, t)
                    if pr is None:
                        self.act(ov, pp, AF.Copy, [], [kdv, pk])
                    else:
                        self.act(ov, pp.rearrange(pr[0], r=pr[1]), AF.Copy, [], [kdv, pk])
                for half in range(2):
                    pp, pk = pP()
                    pst = pp.bitcast(BF16)
                    for bb in range(8):
                        blk = half * 8 + bb
                        self.tr(pst[:, bb * 128:(bb + 1) * 128], dvT[:, blk * 128:(blk + 1) * 128], [kdv], [pk])
                    self.act(dva[:, half * 8:(half + 1) * 8, :, 0:64],
                             pst.rearrange("p (b h c) -> p b h c", b=8, h=2), AF.Copy, [], [kdva, pk])
                segs = segments(g)
                for hh in range(2):
                    Rr = slice(hh * 64, hh * 64 + 64)
                    em = self.emask[:, g * 8 + 2 * j + hh, :]
                    for b, units in enumerate(segs):
                        pO, ko = pO_()
                        first = True
                        for (K0, blk, q0, nq, m0) in units:
                            pS, ks = pS_()
                            pt, kpt = r_pt()
                            self.mm(pS[:, 0:nq], dkT[Rr, K0:K0 + 128], dqT[Rr, q0:q0 + nq], True, True, [kdk, kdq], [ks])
                            self.act(pt[:, 0:nq], pS[:, 0:nq], AF.Exp, [], [kpt, ks], scale=0.125)
                            self.tt(pt[:, 0:nq], pt[:, 0:nq], em[:, m0:m0 + nq], ALU.mult, ["const", kpt], [kpt], eng=("pool" if g == 2 else "dve"))
                            self.mm(pO[0:65, q0 - b * 512:q0 - b * 512 + nq], dva[:, blk, hh, :], pt[:, 0:nq], first, False,
                                    [kdva, kpt], [ko])
                            first = False
                        if g == 0:
                            self.act(acc[0:65, hh, b * 512:(b + 1) * 512], pO[0:65], AF.Copy, [], [kacc, ko])
                        elif g == 1:
                            av = acc[0:65, hh, :].rearrange("p (m i r) -> p r m i", m=4, r=4)[:, b]
                            self.tt(av, av, pO[0:65].rearrange("p (m i) -> p m i", m=4), ALU.add, [kacc], [kacc, ko])
                        else:
                            av = acc[0:65, hh, :].rearrange("p (i r) -> p r i", r=16)[:, 4 * b:4 * b + 4, :]
                            self.tt(av, av, pO[0:65].rearrange("p (r i) -> p r i", r=4), ALU.add, [kacc], [kacc, ko])
            for hh in range(2):
                Rh = slice(hh * 64, hh * 64 + 64)
                self.act(acc[64:65, hh, :], acc[64:65, hh, :], AF.Ln, [kacc], [kacc])
                self.act(acc[64:65, hh, :], acc[64:65, hh, :], AF.Exp, [kacc], [kacc], scale=-1.0)
                rbs = [r_rb() for _ in range(NT)]
                self.bcast_rows(acc[64:65, hh, :], kacc, [(rbs[t][0][0:64], rbs[t][1], t * TT, TT) for t in range(NT)])
                for t in range(NT):
                    cs = slice(t * TT, (t + 1) * TT)
                    rb, krb = rbs[t]
                    ta, kta = r_ta()
                    self.tt(ta[Rh], acc[0:64, hh, cs], rb[0:64], ALU.mult, [kacc, krb], [kta])
                    self.tt(self.yT[Rh, 4 + j, cs], ta[Rh], scz[Rh, cs], ALU.mult, [kta, kscz], ["Y%d_%d" % (4 + j, t)])

    def stage_conv(self, l, seq):
        A = self.A
        ps = self.ps
        cv = self.colv
        hT = self.hT
        r_up = self.ring(2, [S + 2], F32, "upad")
        r_acs = self.ring(2, [TT], F32, "acs")
        r_sz = self.ring(2, [TT], F32, "sz")
        r_c1 = self.ring(2, [TT], F32, "c1")
        r_c2 = self.ring(2, [TT], F32, "c2")
        r_t1 = self.ring(3, [TT], F32, "t1")
        n = 0
        for c in range(4):
            ws = [self.ldw(self.win_cols(l, base + c * 128, 128)) for base in (C_AB, C_AC, C_AX, C_AZ)]
            upad, kup = r_up()
            self.memset(upad[:, 0:2], 0.0, [kup])
            for t in range(NT):
                T0 = t * TT
                o = 4 * (n % 2)
                n += 1
                for i in range(4):
                    wt, wk = ws[i]
                    for k in range(8):
                        self.mm(ps[o + i], wt[:, k, :], hT[:, k, T0:T0 + TT], k == 0, k == 7, [wk, "hT"], ["ps%d" % (o + i)])
                kb_, kc_, kx_, kz_ = ["ps%d" % (o + i) for i in range(4)]
                acs, kacs = r_acs()
                sz, ksz = r_sz()
                c1, kc1 = r_c1()
                c2, kc2 = r_c2()
                self.act(acs, ps[o + 1], AF.Copy, [], [kacs, kc_])
                self.tt(upad[:, 2 + T0:2 + T0 + TT], acs, ps[o + 2], ALU.mult, [kacs], [kup, kx_])
                self.act(sz, ps[o + 3], AF.Silu, [], [ksz, kz_])
                t1, kt1 = r_t1()
                self.act(c1, upad[:, T0:T0 + TT], AF.Identity, [kup, "colv"], [kc1], scale=cv[:, CV_CW + c:CV_CW + c + 1],
                         bias=cv[:, CV_CB + c:CV_CB + c + 1])
                self.act(t1, upad[:, T0 + 1:T0 + 1 + TT], AF.Identity, [kup, "colv"], [kt1], scale=cv[:, CV_CW + 4 + c:CV_CW + 5 + c])
                self.tt(c1, c1, t1, ALU.add, [kc1, kt1], [kc1], eng="pool")
                t1, kt1 = r_t1()
                self.act(t1, upad[:, T0 + 2:T0 + 2 + TT], AF.Identity, [kup, "colv"], [kt1], scale=cv[:, CV_CW + 8 + c:CV_CW + 9 + c])
                self.tt(c1, c1, t1, ALU.add, [kc1, kt1], [kc1], eng="pool")
                self.tt(c2, c1, ps[o + 0], ALU.mult, [kc1], [kc2, kb_])
                self.tt(self.yT[:, 8 + c, T0:T0 + TT], c2, sz, ALU.mult, [kc2, ksz], ["Y%d_%d" % (8 + c, t)])

    def stage_out(self, l, seq, src, dst, final):
        A = self.A
        ps = self.ps
        cv = self.colv
        hT = self.hT
        yT = self.yT
        Wo3 = [A.alloc([4, D], BF16, key="Wo3_%d" % i) for i in range(3)]
        Wo = A.alloc([8, D], BF16, key="Wo")
        G = A.alloc([24, TT], BF16)
        for jj in range(24):
            self.P.keymap["G%d" % jj] = A.pages(A.last[0] + jj * TT, TT)
        mT = A.alloc([8, TT], BF16)
        for jj in range(8):
            self.P.keymap["mT%d" % jj] = A.pages(A.last[0] + jj * TT, TT)
        r_m1 = self.ring(2, [TT], F32, "m1")
        r_m2 = self.ring(2, [TT], F32, "m2")
        xb = [A.alloc([D], F32, key="oxb%d" % i) for i in range(2)]
        ob = [A.alloc([D], F32, key="ob%d" % i) for i in range(2)]
        for i in range(3):
            self.dma("pool", Wo3[i], self.w_out[i][l].rearrange("(k p) n -> p k n", p=128), key_w="Wo3_%d" % i)
        self.dma("pool", Wo, self.w_o[l].rearrange("(k p) n -> p k n", p=128), key_w="Wo")
        och = self.cout if final else self.misc_chan("scrw")
        pG = self.psring([0, 1])
        for t in range(NT):
            T0 = t * TT
            DEPTHW = 4
            wq = [self.ldw(self.win_cols(l, C_GATE + jj * 128, 128)) for jj in range(DEPTHW)]
            for jj in range(24):
                wt, wk = wq[jj]
                pg, kg = pG()
                for k in range(8):
                    self.mm(pg, wt[:, k, :], hT[:, k, T0:T0 + TT], k == 0, k == 7, [wk, "hT"], [kg])
                self.act(G[:, jj, :], pg, AF.Sigmoid, ["colv"], ["G%d" % jj, kg], bias=cv[:, CV_BG + jj:CV_BG + jj + 1])
                if jj + DEPTHW < 24:
                    wq.append(self.ldw(self.win_cols(l, C_GATE + (jj + DEPTHW) * 128, 128)))
            for j in range(8):
                o = 2 + 3 * (j % 2)
                for i in range(3):
                    for k in range(4):
                        self.mm(ps[o + i], Wo3[i][:, k, j * 128:(j + 1) * 128], yT[:, (8, 0, 4)[i] + k, T0:T0 + TT], k == 0, k == 3,
                                ["Wo3_%d" % i, "Y%d_%d" % ((8, 0, 4)[i] + k, t)], ["ps%d" % (o + i)])
                m1, k1 = r_m1()
                m2, k2 = r_m2()
                self.tt(m1, ps[o], G[:, j, :], ALU.mult, ["G%d" % j], [k1, "ps%d" % o])
                self.tt(m2, ps[o + 1], G[:, 8 + j, :], ALU.mult, ["G%d" % (8 + j)], [k2, "ps%d" % (o + 1)])
                self.tt(m1, m1, m2, ALU.add, [k1, k2], [k1], eng="pool")
                self.tt(m2, ps[o + 2], G[:, 16 + j, :], ALU.mult, ["G%d" % (16 + j)], [k2, "ps%d" % (o + 2)])
                self.tt(mT[:, j, :], m1, m2, ALU.add, [k1, k2], ["mT%d" % j], eng="pool")
            for bb in range(4):
                b = t * 4 + bb
                i = b % 2
                r0 = seq * S + b * 128
                self.dma("sp", xb[i], src[r0:r0 + 128, :], key_w="oxb%d" % i, key_r=("scr" if l > 0 else None))
                for half in range(2):
                    pg, kg = pG()
                    for k in range(8):
                        self.mm(pg, mT[:, k, bb * 128:(bb + 1) * 128], Wo[:, k, half * 512:(half + 1) * 512], k == 0, k == 7,
                                ["mT%d" % k, "Wo"], [kg])
                    self.tt(ob[i][:, half * 512:(half + 1) * 512], pg, xb[i][:, half * 512:(half + 1) * 512], ALU.add,
                            ["oxb%d" % i], ["ob%d" % i, kg])
                self.dma("sp", dst[r0:r0 + 128, :], ob[i], key_r="ob%d" % i, key_w=(None if final else "scr"), chan=och)


def _rope_tables():
    inv = (10000.0 ** (-np.arange(0, 32, 2, dtype=np.float32) / 32)).astype(np.float32)
    ang = np.arange(S, dtype=np.float32)[:, None] * inv[None, :]
    cos, sin = np.cos(ang).astype(np.float32), np.sin(ang).astype(np.float32)
    ct = np.ones((96, S), np.float32)
    st = np.zeros((96, S), np.float32)
    ct[64:80] = cos.T
    ct[80:96] = cos.T
    st[64:80] = -sin.T
    st[80:96] = sin.T
    return ct, st


def _emask():
    n = 24
    slopes = (2.0 ** (-8.0 * np.arange(1, n + 1, dtype=np.float32) / n)).reshape(3, 8)
    k = np.arange(128)[:, None].astype(np.float32)
    q = np.arange(128)[None, :].astype(np.float32)
    em = np.zeros((128, 25, 256), np.float32)
    for g in range(3):
        for h in range(8):
            sl = slopes[g, h] * DIL[g]
            em[:, g * 8 + h, 0:128] = np.where(k <= q, np.exp(-sl * np.maximum(q - k, 0.0)), 0.0)
            if g < 2:
                em[:, g * 8 + h, 128:256] = np.where(k >= q, np.exp(-sl * np.maximum(128.0 + q - k, 0.0)), 0.0)
    em[:, 24, 0:128] = (k <= q).astype(np.float32)
    return em.reshape(128, 25 * 256)


def _host_layout(inp, layers):
    f = lambda a: np.ascontiguousarray(a, dtype=np.float32)
    L = len(layers)
    w_uq = f(inp["w_uq"][layers])
    w_uq_sw = np.zeros_like(w_uq)
    for h in range(8):
        b = h * 96
        w_uq_sw[:, :, b + 64:b + 80] = w_uq[:, :, b + 80:b + 96]
        w_uq_sw[:, :, b + 80:b + 96] = w_uq[:, :, b + 64:b + 80]
    w_ukv = f(inp["w_ukv"][layers]).reshape(L, 128, 8, 128)
    w_ukv_k = f(w_ukv[:, :, :, 0:64].reshape(L, 128, 512))
    w_ukv_v = f(w_ukv[:, :, :, 64:128].reshape(L, 128, 512))
    w_in = inp["w_in"]
    w_kpe = np.zeros((L, D, 96), np.float32)
    w_kpe_sw = np.zeros((L, D, 96), np.float32)
    for i, l in enumerate(layers):
        kp = w_in[l][:, C_KPE:C_KPE + 32]
        w_kpe[i, :, 64:96] = kp
        w_kpe_sw[i, :, 64:80] = kp[:, 16:32]
        w_kpe_sw[i, :, 80:96] = kp[:, 0:16]
    colv = np.zeros((L, 128, NCV), np.float32)
    gbc = np.zeros((L, 128, D), np.float32)
    for i, l in enumerate(layers):
        colv[i, :, CV_BG:CV_BG + 24] = inp["b_gate"][l].reshape(24, 128).T
        for tap in range(3):
            colv[i, :, CV_CW + tap * 4:CV_CW + tap * 4 + 4] = inp["conv_w"][l][tap].reshape(4, 128).T
        colv[i, :, CV_CB:CV_CB + 4] = inp["conv_b"][l].reshape(4, 128).T
        colv[i, :, CV_GQ:CV_GQ + 2] = inp["q_a_norm_g"][l].reshape(2, 128).T
        colv[i, :, CV_GKV] = inp["kv_a_norm_g"][l]
        for (col, src) in ((CV_GQM, inp["mla_q_norm_g"][l]), (CV_GKM, inp["mla_k_norm_g"][l])):
            colv[i, 0:96, col] = src
            colv[i, 64:80, col + 1] = src[80:96]
            colv[i, 80:96, col + 1] = src[64:80]
        for g in range(3):
            colv[i, :, CV_GDQ + g] = np.tile(inp["dil_q_norm_g"][l][g], 2)
            colv[i, :, CV_GDK + g] = np.tile(inp["dil_k_norm_g"][l][g], 2)
        gbc[i] = np.broadcast_to(inp["norm_g"][l][None, :], (128, D))
    ind2 = np.zeros((128, 128), np.float32)
    ind2[0:64, 0:64] = 1.0
    ind2[64:128, 64:128] = 1.0
    ct, st = _rope_tables()
    shared = {
        "w_in": f(w_in[layers]), "w_uq": w_uq, "w_uq_sw": w_uq_sw, "w_ukv_k": w_ukv_k, "w_ukv_v": w_ukv_v,
        "w_kpe": w_kpe, "w_kpe_sw": w_kpe_sw,
        "w_out_a": f(inp["w_out_a"][layers]), "w_out_b": f(inp["w_out_b"][layers]), "w_out_c": f(inp["w_out_c"][layers]),
        "w_o": f(inp["w_o"][layers]), "colv": colv, "gbc": gbc,
        "ident": np.eye(128, dtype=np.float32), "ind2": ind2, "emask": _emask(), "ctab": ct, "stab": st,
    }
    return shared


_CACHE = {}


def _get_builder(nseq, nlayers, dbg=False):
    key = (nseq, nlayers, dbg)
    if key not in _CACHE:
        _CACHE[key] = Builder(nseq, nlayers, dbg=dbg)
    return _CACHE[key]


def kernel(**inputs):
    inp = {k: np.asarray(v) for k, v in inputs.items()}
    x = np.ascontiguousarray(inp["x"], dtype=np.float32)
    B = x.shape[0]
    ncores = 8
    nseq = B // ncores
    shared = _host_layout(inp, list(range(DEPTH)))
    bld = Builder(nseq, DEPTH)
    in_maps = []
    for c in range(ncores):
        m = dict(shared)
        m["x"] = x[c * nseq:(c + 1) * nseq].reshape(nseq * S, D)
        in_maps.append(m)
    res = run_bass_kernel_spmd(bld.nc, in_maps, core_ids=list(range(ncores)))
    outs = [np.asarray(r["out"]).reshape(nseq, S, D) for r in res.results]
    return np.concatenate(outs, axis=0).astype(np.float32)
```

```python
import numpy as np
import concourse.bass as bass
import concourse.mybir as mybir
from concourse.bass_utils import run_bass_kernel_spmd

F32 = mybir.dt.float32
BF16 = mybir.dt.bfloat16
AF = mybir.ActivationFunctionType
ALU = mybir.AluOpType

ENGS = ("pe", "act", "dve", "pool", "sp")

D = 1024
S = 2048
DEPTH = 2
NB = 16
NT = 4
TT = 512
EPS = 1e-6
N_IN = 11168
C_AB, C_AC, C_AX, C_AZ = 0, 512, 1024, 1536
C_CQ, C_CKV, C_KPE, C_BZ = 2048, 2304, 2432, 2464
C_DQ, C_DK, C_DV, C_CZ, C_GATE = 2976, 4512, 6048, 7584, 8096
DIL = (1, 4, 16)
CV_BG, CV_CW, CV_CB, CV_GQ, CV_GKV, CV_GQM, CV_GQMS, CV_GKM, CV_GKMS, CV_GDQ, CV_GDK, NCV = 0, 24, 36, 40, 42, 43, 44, 45, 46, 47, 50, 53


import heapq
import os

RAW, WAR, WAW = 0, 1, 2
ACT_SET = {"Exp": "exp", "Ln": "exp", "Sqrt": "sqrt", "Silu": "silu", "Sigmoid": "sigmoid"}


class Chan:
    def __init__(self, sem, name):
        self.sem = sem
        self.name = name
        self.count = 0


class Op:
    __slots__ = ("eng", "fn", "idx", "pidx", "deps", "succ", "ndeps", "waits", "signal", "sigcount", "chan",
                 "dma_deps", "dur", "tset", "ready", "finish", "seg", "isbar")

    def __init__(self, eng, fn):
        self.eng = eng
        self.fn = fn
        self.idx = -1
        self.pidx = -1
        self.deps = []
        self.succ = []
        self.ndeps = 0
        self.waits = {}
        self.dma_deps = []
        self.signal = False
        self.sigcount = 0
        self.chan = None
        self.dur = 0.5
        self.tset = None
        self.ready = 0.0
        self.finish = 0.0
        self.seg = 0
        self.isbar = False


class Prog:
    def __init__(self, nc):
        self.nc = nc
        self.ops = []
        self.kw = {}
        self.kr = {}
        self.sems = {e: nc.alloc_semaphore("sem_" + e) for e in ENGS}
        self.nops = 0
        self.chans = []
        self.seg = 0
        self.sched = True
        self.prio = os.environ.get("KPRIO", "bl")
        self.keymap = {}

    def chan(self, name):
        c = Chan(self.nc.alloc_semaphore("dsem_" + name), name)
        self.chans.append(c)
        return c

    def barrier(self):
        self.seg += 1

    def add(self, eng, fn, reads=(), writes=(), chan=None, dur=0.5, tset=None):
        op = Op(eng, fn)
        op.pidx = len(self.ops)
        op.seg = self.seg
        op.chan = chan
        op.dur = dur
        op.tset = tset
        self.ops.append(op)
        self.nops += 1
        km = self.keymap
        if km:
            reads = list(reads) + [p for k in reads for p in km.get(k, ())]
            writes = list(writes) + [p for k in writes for p in km.get(k, ())]
        deps = {}
        for k in reads:
            w = self.kw.get(k)
            if w is not None:
                deps[w] = RAW
        for k in writes:
            w = self.kw.get(k)
            if w is not None and w not in deps:
                deps[w] = WAW
            for r in self.kr.get(k, ()):
                if r is not op and r not in deps:
                    deps[r] = WAR
        for k in reads:
            self.kr.setdefault(k, []).append(op)
        for k in writes:
            self.kw[k] = op
            self.kr[k] = []
        op.deps = [(d, kind) for d, kind in deps.items() if d.seg == op.seg]
        return op

    def _schedule(self, ops):
        if not self.sched:
            return list(ops)
        for op in ops:
            op.ndeps = len(op.deps)
            op.succ = []
            op.ready = 0.0
        for op in ops:
            for d, _ in op.deps:
                d.succ.append(op)
        LAT = 0.3
        if self.prio == "bl":
            for op in reversed(ops):
                b = 0.0
                for s_ in op.succ:
                    v = s_.finish + (LAT if s_.eng != op.eng else 0.0)
                    if v > b:
                        b = v
                op.finish = b + op.dur + (2.0 if op.chan is not None else 0.0)
            mx = max(op.finish for op in ops) if ops else 0.0
            for i_, op in enumerate(ops):
                op.pidx = int((mx - op.finish) * 1000) * 100000 + i_
        free = {e: 0.0 for e in ENGS}
        pending = {e: [] for e in ENGS}
        avail = {e: {} for e in ENGS}
        curset = [None]
        for op in ops:
            if op.ndeps == 0:
                heapq.heappush(pending[op.eng], (0.0, op.pidx, op))
        order = []
        n = len(ops)

        def cand(e):
            t = free[e]
            pend = pending[e]
            av = avail[e]
            while pend and pend[0][0] <= t:
                _, pi, o = heapq.heappop(pend)
                heapq.heappush(av.setdefault(o.tset if e == "act" else None, []), (pi, o))
            best = None
            if e == "act":
                for ts_ in (None, curset[0]):
                    h = av.get(ts_)
                    if h and (best is None or h[0][0] < best[0]):
                        best = (h[0][0], ts_)
                if best is None:
                    for ts_, h in av.items():
                        if h and (best is None or h[0][0] < best[0]):
                            best = (h[0][0], ts_)
            else:
                h = av.get(None)
                if h:
                    best = (h[0][0], None)
            if best is not None:
                return (t, best[0], best[1], False)
            if pend:
                return (pend[0][0], pend[0][1], None, True)
            return None

        while len(order) < n:
            bc = None
            be = None
            for e in ENGS:
                c = cand(e)
                if c is not None and (bc is None or (c[0], c[1]) < (bc[0], bc[1])):
                    bc, be = c, e
            assert bc is not None, "scheduler stuck"
            if bc[3]:
                _, pi, o = heapq.heappop(pending[be])
            else:
                pi, o = heapq.heappop(avail[be][bc[2]])
            start = max(bc[0], free[be])
            dur = o.dur
            if be == "act" and o.tset is not None and o.tset != curset[0]:
                dur += 2.7
                curset[0] = o.tset
            if o.chan is not None:
                free[be] = start + 0.15
                o.finish = start + 2.0 + dur
            else:
                free[be] = start + dur
                o.finish = start + dur
            order.append(o)
            for s_ in o.succ:
                s_.ndeps -= 1
                rt = o.finish + (LAT if s_.eng != o.eng else 0.05)
                if rt > s_.ready:
                    s_.ready = rt
                if s_.ndeps == 0:
                    heapq.heappush(pending[s_.eng], (s_.ready, s_.pidx, s_))
        return order

    def emit(self, final_chans=()):
        nc = self.nc
        nseg = self.seg + 1
        segs = [[] for _ in range(nseg)]
        for op in self.ops:
            segs[op.seg].append(op)
        glob = []
        for si in range(nseg):
            if si > 0:
                drains = []
                for e in ENGS:
                    if e == "sp":
                        continue
                    d = Op(e, lambda h: h.drain())
                    d.isbar = True
                    drains.append(d)
                    glob.append(d)
                for e in ENGS:
                    w = Op(e, lambda h: h.nop())
                    w.isbar = True
                    w.deps = [(d, RAW) for d in drains if d.eng != e]
                    w.dma_deps = "all"
                    glob.append(w)
            glob.extend(self._schedule(segs[si]))
        self.eng_ops = {e: [] for e in ENGS}
        for op in glob:
            op.idx = len(self.eng_ops[op.eng])
            self.eng_ops[op.eng].append(op)
        chan_cnt = {c: 0 for c in self.chans}
        dma_wait_vals = {}
        for op in glob:
            isdma = op.chan is not None
            dw = {}
            if op.dma_deps == "all":
                for c in self.chans:
                    if chan_cnt[c] > 0:
                        dw[c] = chan_cnt[c]
            for d, kind in op.deps:
                if d.chan is not None:
                    dw[d.chan] = chan_cnt[d.chan]
                    continue
                if d.eng == op.eng and not isdma and kind != RAW:
                    continue
                if d.eng == op.eng and d.eng == "pe":
                    continue
                cur = op.waits.get(d.eng)
                if cur is None or cur.idx < d.idx:
                    op.waits[d.eng] = d
            if isdma:
                chan_cnt[op.chan] += 1
            dma_wait_vals[op] = dw
        for op in glob:
            for p in op.waits.values():
                p.signal = True
        for e in ENGS:
            c = 0
            for op in self.eng_ops[e]:
                if op.signal:
                    c += 1
                op.sigcount = c
        final_cnt = {c: chan_cnt[c] for c in final_chans}

        def run_engine(e, h):
            seen = {}
            for op in self.eng_ops[e]:
                for pe_, p in op.waits.items():
                    need = p.sigcount
                    if seen.get(pe_, 0) < need:
                        h.wait_ge(self.sems[pe_], need)
                        seen[pe_] = need
                for c, cnt in dma_wait_vals[op].items():
                    if seen.get(c, 0) < cnt:
                        h.wait_ge(c.sem, 16 * cnt)
                        seen[c] = cnt
                ins = op.fn(h)
                if op.chan is not None:
                    ins.then_inc(op.chan.sem, 16)
                elif op.signal:
                    ins.then_inc(self.sems[e], 1)
            if e == "sp":
                for c, cnt in final_cnt.items():
                    h.wait_ge(c.sem, 16 * cnt)

        with nc.Block() as block:
            @block.tensor
            def _(h):
                run_engine("pe", h)

            @block.scalar
            def _(h):
                run_engine("act", h)

            @block.vector
            def _(h):
                run_engine("dve", h)

            @block.gpsimd
            def _(h):
                run_engine("pool", h)

            @block.sync
            def _(h):
                run_engine("sp", h)


class Arena:
    def __init__(self, nc, nbytes):
        self.cap = nbytes // 2
        self.t = nc.alloc_sbuf_tensor("arena", [128, self.cap], BF16).ap()
        self.top = 0
        self.paged = False
        self.keymap = {}

    def pages(self, off, ne):
        return ["pg%d" % i for i in range(off // 256, (off + ne - 1) // 256 + 1)]

    def alloc(self, free_shape, dtype, key=None):
        n = 1
        for v in free_shape:
            n *= v
        ne = n * (2 if dtype == F32 else 1)
        ne_al = (ne + 255) // 256 * 256 if self.paged else (ne + 31) // 32 * 32
        off = self.top
        self.top += ne_al
        assert self.top <= self.cap, ("arena overflow", self.top * 2, self.cap * 2)
        self.last = (off, ne)
        if key is not None and self.paged:
            for k in ([key] if isinstance(key, str) else key):
                self.keymap[k] = self.pages(off, ne)
        v = self.t[:, off:off + ne]
        if dtype == F32:
            v = v.bitcast(F32)
        if len(free_shape) > 1:
            names = " ".join("a%d" % i for i in range(len(free_shape)))
            kw = {"a%d" % i: free_shape[i] for i in range(len(free_shape))}
            v = v.rearrange("p (%s) -> p %s" % (names, names), **kw)
        return v


class Builder:
    def __init__(self, nseq, nlayers, layer0=0, dbg=False):
        self.nseq = nseq
        self.nlayers = nlayers
        self.dbg = dbg
        nc = bass.Bass("TRN2", target_bir_lowering=False)
        self.nc = nc
        P = Prog(nc)
        self.P = P
        L = nlayers
        dt = nc.dram_tensor
        self.x = dt("x", [nseq * S, D], F32, kind="ExternalInput").ap()
        self.out = dt("out", [nseq * S, D], F32, kind="ExternalOutput").ap()
        self.w_in = dt("w_in", [L, D, N_IN], F32, kind="ExternalInput").ap()
        self.w_uq = dt("w_uq", [L, 256, 768], F32, kind="ExternalInput").ap()
        self.w_uq_sw = dt("w_uq_sw", [L, 256, 768], F32, kind="ExternalInput").ap()
        self.w_ukv_k = dt("w_ukv_k", [L, 128, 512], F32, kind="ExternalInput").ap()
        self.w_ukv_v = dt("w_ukv_v", [L, 128, 512], F32, kind="ExternalInput").ap()
        self.w_kpe = dt("w_kpe", [L, D, 96], F32, kind="ExternalInput").ap()
        self.w_kpe_sw = dt("w_kpe_sw", [L, D, 96], F32, kind="ExternalInput").ap()
        self.w_out = [dt("w_out_%s" % n, [L, 512, D], F32, kind="ExternalInput").ap() for n in "abc"]
        self.w_o = dt("w_o", [L, D, D], F32, kind="ExternalInput").ap()
        self.colv_d = dt("colv", [L, 128, NCV], F32, kind="ExternalInput").ap()
        self.gbc_d = dt("gbc", [L, 128, D], F32, kind="ExternalInput").ap()
        self.ident_d = dt("ident", [128, 128], F32, kind="ExternalInput").ap()
        self.ind2_d = dt("ind2", [128, 128], F32, kind="ExternalInput").ap()
        self.emask_d = dt("emask", [128, 25 * 256], F32, kind="ExternalInput").ap()
        self.ct_d = dt("ctab", [96, S], F32, kind="ExternalInput").ap()
        self.st_d = dt("stab", [96, S], F32, kind="ExternalInput").ap()
        if nlayers > 1:
            self.scr = dt("scr", [nseq * S, D], F32).ap()
        self.bsc = dt("bsc", [4, S], F32).ap()
        self.bsi = 0
        if dbg:
            self.dbg_h = dt("dbg_h", [128, 8 * S], BF16, kind="ExternalOutput").ap()
            self.dbg_y = dt("dbg_y", [128, 12 * S], BF16, kind="ExternalOutput").ap()
        self.ps = [nc.alloc_psum_tensor("ps%d" % i, [128, 512], F32).ap() for i in range(8)]
        A = Arena(nc, 212480)
        self.A = A
        self.hT = A.alloc([8, S], BF16)
        self.yT = A.alloc([12, S], BF16)
        self.ident = A.alloc([128], BF16)
        self.ones = A.alloc([128], BF16)
        self.ind2 = A.alloc([128], BF16)
        self.onesf = A.alloc([64], F32)
        self.emask = A.alloc([25, 256], BF16)
        self.colv = A.alloc([NCV], F32)
        self.epsc = A.alloc([1], F32)
        self.NWR = 8
        self.wr = [A.alloc([8, 128], BF16) for _ in range(self.NWR)]
        self.wrc = [P.chan("wr%d" % i) for i in range(self.NWR)]
        self.wri = 0
        A.top = (A.top + 255) // 256 * 256
        self.stage_base = A.top
        A.paged = True
        A.keymap = P.keymap
        self.cst = P.chan("const")
        self.cout = P.chan("out")
        self.cmisc = {}
        self.build()

    def misc_chan(self, key):
        if key not in self.cmisc:
            self.cmisc[key] = self.P.chan("m%d" % len(self.cmisc))
        return self.cmisc[key]

    def dma(self, q, out, in_, key_w=None, key_r=None, chan=None):
        if chan is None:
            chan = self.misc_chan(key_w if key_w is not None else key_r)
        nbytes = out.free_size() * out.partition_size() * 4
        self.P.add(q, lambda h: h.dma_start(out=out, in_=in_),
                   reads=[key_r] if key_r else [], writes=[key_w] if key_w else [], chan=chan, dur=nbytes / 150e3)

    def mm(self, out, lhsT, rhs, start, stop, r, w):
        self.P.add("pe", lambda h: h.matmul(out, lhsT=lhsT, rhs=rhs, start=start, stop=stop, skip_group_check=True), r, w,
                   dur=(0.03 + out.free_size() * 0.00043) * (4.0 if lhsT.dtype == F32 else 1.0))

    def tr(self, out, in_, r, w):
        ident = self.ident
        self.P.add("pe", lambda h: h.transpose(out, in_, ident), list(r) + ["const"], w, dur=0.1)

    def act(self, out, in_, func, r, w, scale=None, bias=None, accum=None):
        kw = {}
        if scale is not None:
            kw["scale"] = scale
        if bias is not None:
            kw["bias"] = bias
        if accum is not None:
            kw["accum_out"] = accum
        self.P.add("act", lambda h: h.activation(out=out, in_=in_, func=func, **kw), r, w,
                   dur=0.2 + out.free_size() * 0.0009, tset=ACT_SET.get(func.name))

    def _dd(self, out, eng="dve"):
        if eng == "pool":
            return 0.25 + out.free_size() * (0.0021 if out.dtype == F32 else 0.0009)
        return 0.12 + out.free_size() * (0.0015 if out.dtype == F32 else 0.0009)

    def tt(self, out, in0, in1, op, r, w, eng="dve"):
        self.P.add(eng, lambda h: h.tensor_tensor(out=out, in0=in0, in1=in1, op=op), r, w, dur=self._dd(out, eng))

    def stt(self, out, in0, scalar, in1, op0, op1, r, w, eng="dve"):
        self.P.add(eng, lambda h: h.scalar_tensor_tensor(out=out, in0=in0, scalar=scalar, in1=in1, op0=op0, op1=op1), r, w,
                   dur=self._dd(out, eng))

    def ts(self, out, in0, s1, s2, op0, op1, r, w, eng="dve"):
        if s2 is None:
            self.P.add(eng, lambda h: h.tensor_scalar(out=out, in0=in0, scalar1=s1, scalar2=None, op0=op0), r, w,
                       dur=self._dd(out, eng))
        else:
            self.P.add(eng, lambda h: h.tensor_scalar(out=out, in0=in0, scalar1=s1, scalar2=s2, op0=op0, op1=op1), r, w,
                       dur=self._dd(out, eng))

    def recip(self, out, in_, r, w):
        self.P.add("dve", lambda h: h.reciprocal(out=out, in_=in_), r, w, dur=self._dd(out))

    def cpy(self, out, in_, r, w, eng="dve"):
        self.P.add(eng, lambda h: h.tensor_copy(out=out, in_=in_), r, w, dur=self._dd(out))

    def memset(self, ap, val, w, eng="dve"):
        self.P.add(eng, lambda h: h.memset(ap, val), [], w, dur=self._dd(ap))

    def bcast_rows(self, src_row, key_src, dsts):
        from concourse.ap import AP
        i = self.bsi % 4
        self.bsi += 1
        n = src_row.free_size()
        self.dma("sp", self.bsc[i:i + 1, 0:n], src_row, key_w="bsc%d" % i, key_r=key_src)
        for (dst, kdst, c0, cn) in dsts:
            row = self.bsc[i:i + 1, c0:c0 + cn]
            src = AP(row.tensor, row.offset, [[0, 64], [1, cn]])
            self.dma("sp", dst, src, key_w=kdst, key_r="bsc%d" % i)

    def ldw(self, src):
        i = self.wri % self.NWR
        self.wri += 1
        self.dma("pool", self.wr[i], src, key_w="wr%d" % i, chan=self.wrc[i])
        return self.wr[i], "wr%d" % i

    def win_cols(self, l, c0, n):
        return self.w_in[l, :, c0:c0 + n].rearrange("(k p) n -> p k n", p=128)

    def rstd_from(self, ssq_ps, pkey, rows, inv_n, sd, rs):
        self.act(sd[0:rows], ssq_ps[0:rows], AF.Sqrt, ["const"], ["sd", pkey], scale=inv_n, bias=self.epsc[0:rows, 0:1])
        self.recip(rs[0:rows], sd[0:rows], ["sd"], ["rs"])

    def build(self):
        P = self.P
        self.dma("pool", self.ident, self.ident_d, key_w="const", chan=self.cst)
        self.dma("pool", self.ind2, self.ind2_d, key_w="const", chan=self.cst)
        self.dma("pool", self.emask, self.emask_d.rearrange("p (a b) -> p a b", a=25), key_w="const", chan=self.cst)
        self.memset(self.ones, 1.0, ["const"])
        self.memset(self.onesf, 1.0, ["const"])
        self.memset(self.epsc, EPS, ["const"])
        for l in range(self.nlayers):
            src = self.x if l == 0 else self.scr
            dst = self.out if l == self.nlayers - 1 else self.scr
            self.dma("sp", self.colv, self.colv_d[l], key_w="colv")
            for seq in range(self.nseq):
                self.A.top = self.stage_base
                self.stage0(l, seq, src)
                if self.dbg and l == 0 and seq == 0:
                    self.dma("sp", self.dbg_h.rearrange("p (a b) -> p a b", a=8), self.hT, key_r="hT", chan=self.cout)
                self.A.top = self.stage_base
                self.stage_mla(l, seq)
                self.A.top = self.stage_base
                self.stage_dil(l, seq)
                self.A.top = self.stage_base
                self.stage_conv(l, seq)
                if self.dbg and l == 0 and seq == 0:
                    self.P.add("sp", lambda h: h.dma_start(out=self.dbg_y.rearrange("p (a b) -> p a b", a=12), in_=self.yT), reads=["Y%d_%d" % (c_, t_) for c_ in range(12) for t_ in range(4)], chan=self.cout)
                self.A.top = self.stage_base
                self.stage_out(l, seq, src, dst, final=(l == self.nlayers - 1))
        P.emit(final_chans=[self.cout] + ([self.cmisc["scrw"]] if "scrw" in self.cmisc else []))

    def stage0(self, l, seq, src):
        A = self.A
        NBUF = 4
        xb = [A.alloc([D], F32, key="xb%d" % i) for i in range(NBUF)]
        hn = [A.alloc([D], BF16, key="hn%d" % i) for i in range(NBUF)]
        junk = A.alloc([D], BF16, key="junk")
        gbc = A.alloc([D], F32, key="s0gbc")
        r_ss = self.ring(NBUF, [1], F32, "ss")
        r_sd = self.ring(NBUF, [1], F32, "sd1")
        r_rs = self.ring(NBUF, [1], F32, "rs1")
        self.dma("sp", gbc, self.gbc_d[l], key_w="s0gbc")
        for b in range(NB):
            i = b % NBUF
            r0 = seq * S + b * 128
            ss, kss = r_ss()
            sd1, ksd = r_sd()
            rs1, krs = r_rs()
            self.dma("sp", xb[i], src[r0:r0 + 128, :], key_w="xb%d" % i, key_r=("scr" if l > 0 else None))
            self.memset(ss, 0.0, [kss])
            self.act(junk, xb[i], AF.Square, ["xb%d" % i], ["junk", kss], accum=ss)
            self.act(sd1, ss, AF.Ln, [kss, "const"], [ksd], scale=1.0 / D, bias=self.epsc[:, 0:1])
            self.act(rs1, sd1, AF.Exp, [ksd], [krs], scale=-0.5)
            self.stt(hn[i], xb[i], rs1[:, 0:1], gbc, ALU.mult, ALU.mult, ["xb%d" % i, krs, "s0gbc"], ["hn%d" % i])
            pb = b % 8
            pst = self.ps[pb].bitcast(BF16)
            for k in range(8):
                self.tr(pst[:, k * 128:(k + 1) * 128], hn[i][:, k * 128:(k + 1) * 128], ["hn%d" % i], ["ps%d" % pb])
            if b % 2 == 0:
                self.act(self.hT[:, :, b * 128:(b + 1) * 128], pst.rearrange("p (k n) -> p k n", k=8), AF.Copy,
                         [], ["ps%d" % pb, "hT"])
            else:
                self.cpy(self.hT[:, :, b * 128:(b + 1) * 128], pst.rearrange("p (k n) -> p k n", k=8), [], ["ps%d" % pb, "hT"])

    def ring(self, n, shape, dtype, name):
        aps = [self.A.alloc(shape, dtype, key="%s%d" % (name, i)) for i in range(n)]
        st = {"i": 0}

        def nxt():
            i = st["i"] % n
            st["i"] += 1
            return aps[i], "%s%d" % (name, i)
        return nxt

    def psring(self, banks):
        st = {"i": 0}

        def nxt():
            b = banks[st["i"] % len(banks)]
            st["i"] += 1
            return self.ps[b], "ps%d" % b
        return nxt

    def stage_mla(self, l, seq):
        A = self.A
        cv = self.colv
        Kc = self.yT[:, 4:12, :]
        KK = ["yT_c", "yT_a"]
        Va = A.alloc([NB, 8, 65], BF16, key="Va")
        for t_ in range(NT):
            self.P.keymap["Va%d" % t_] = A.pages(A.last[0] + t_ * 4 * 8 * 65, 4 * 8 * 65)
        qT = A.alloc([8, TT], BF16, key=["qT%d" % h_ for h_ in range(8)])
        cqn = A.alloc([2, TT], BF16, key="cqn")
        ckvn = A.alloc([TT], BF16, key="ckvn")
        sqpe = A.alloc([TT], BF16, key="sqpe")
        kr = A.alloc([TT], F32, key="kr")
        ctt = A.alloc([TT], F32, key="ctt")
        stt_ = A.alloc([TT], F32, key="stt")
        cqf = A.alloc([TT], F32, key="cqf")
        sqf = A.alloc([TT], F32, key="sqf")
        ckf = A.alloc([TT], F32, key="ckf")
        skf = A.alloc([TT], F32, key="skf")
        sbz = A.alloc([4, TT], BF16, key="sbz")
        r_sq = self.ring(2, [TT], BF16, "sq")
        r_sd = self.ring(2, [TT], F32, "sd")
        r_rs = self.ring(2, [TT], F32, "rs")
        r_ta = self.ring(2, [TT], F32, "ta")
        r_tb = self.ring(2, [TT], F32, "tb")
        r_pt = self.ring(3, [TT], BF16, "PT")
        r_osb = self.ring(2, [TT], F32, "osb")
        r_rb = self.ring(2, [TT], F32, "rb")
        Wlat = A.alloc([8, 384], BF16, key="Wlat")
        Wkpe = A.alloc([8, 96], BF16, key="Wkpe")
        Wkpes = A.alloc([8, 96], BF16, key="Wkpes")
        Wuq = A.alloc([2, 768], BF16, key="Wuq")
        Wuqs = A.alloc([2, 768], BF16, key="Wuqs")
        Wk = A.alloc([512], BF16, key="Wk")
        Wv = A.alloc([512], BF16, key="Wv")
        pP = self.psring([0, 1, 2])
        pSS = self.psring([3, 4])
        pS_ = self.psring([6, 7])
        pO_ = self.psring([5])
        hT = self.hT
        self.dma("pool", Wlat, self.win_cols(l, C_CQ, 384), key_w="Wlat")
        self.dma("pool", Wkpe, self.w_kpe[l].rearrange("(k p) n -> p k n", p=128), key_w="Wkpe")
        self.dma("pool", Wkpes, self.w_kpe_sw[l].rearrange("(k p) n -> p k n", p=128), key_w="Wkpes")
        self.dma("pool", Wuq, self.w_uq[l].rearrange("(k p) n -> p k n", p=128), key_w="Wuq")
        self.dma("pool", Wuqs, self.w_uq_sw[l].rearrange("(k p) n -> p k n", p=128), key_w="Wuqs")
        self.dma("pool", Wk, self.w_ukv_k[l], key_w="Wk")
        self.dma("pool", Wv, self.w_ukv_v[l], key_w="Wv")
        self.memset(Va[:, :, :, 64:65], 1.0, ["Va"])
        sc_mla = 96.0 ** -0.5
        R = slice(64, 96)

        def rstd(ssq, kss, rows, inv_n):
            sd, ksd = r_sd()
            rs, krs = r_rs()
            self.act(sd[0:rows], ssq[0:rows], AF.Ln, ["const"], [ksd, kss], scale=inv_n, bias=self.epsc[0:rows, 0:1])
            self.act(rs[0:rows], sd[0:rows], AF.Exp, [ksd], [krs], scale=-0.5)
            return rs, krs

        for t in range(NT):
            T0 = t * TT
            hs = lambda k: hT[:, k, T0:T0 + TT]
            wbz = [self.ldw(self.win_cols(l, C_BZ + c * 128, 128)) for c in range(4)]
            self.dma("sp", ctt[0:96], self.ct_d[:, T0:T0 + TT], key_w="ctt")
            self.dma("sp", stt_[0:96], self.st_d[:, T0:T0 + TT], key_w="stt")
            self.ts(cqf[0:96], ctt[0:96], cv[0:96, CV_GQM:CV_GQM + 1], None, ALU.mult, None, ["ctt", "colv"], ["cqf"])
            self.ts(sqf[0:96], stt_[0:96], cv[0:96, CV_GQMS:CV_GQMS + 1], None, ALU.mult, None, ["stt", "colv"], ["sqf"])
            self.ts(ckf[0:96], ctt[0:96], cv[0:96, CV_GKM:CV_GKM + 1], None, ALU.mult, None, ["ctt", "colv"], ["ckf"])
            self.ts(skf[0:96], stt_[0:96], cv[0:96, CV_GKMS:CV_GKMS + 1], None, ALU.mult, None, ["stt", "colv"], ["skf"])
            pcq = [pP(), pP()]
            sqs = []
            for j in range(2):
                p_, k_ = pcq[j]
                for k in range(8):
                    self.mm(p_, Wlat[:, k, j * 128:(j + 1) * 128], hs(k), k == 0, k == 7, ["Wlat", "hT"], [k_])
                sq, ksq = r_sq()
                self.act(sq, p_, AF.Square, [], [ksq, k_])
                sqs.append((sq, ksq))
            ss, kss = pSS()
            self.mm(ss, self.ones, sqs[0][0], True, False, ["const", sqs[0][1]], [kss])
            self.mm(ss, self.ones, sqs[1][0], False, True, ["const", sqs[1][1]], [kss])
            rs, krs = rstd(ss, kss, 128, 1.0 / 256)
            for j in range(2):
                p_, k_ = pcq[j]
                self.stt(cqn[:, j, :], p_, cv[:, CV_GQ + j:CV_GQ + j + 1], rs, ALU.mult, ALU.mult, ["colv", krs], ["cqn", k_])
            p_, k_ = pP()
            for k in range(8):
                self.mm(p_, Wlat[:, k, 256:384], hs(k), k == 0, k == 7, ["Wlat", "hT"], [k_])
            sq, ksq = r_sq()
            self.act(sq, p_, AF.Square, [], [ksq, k_])
            ss, kss = pSS()
            self.mm(ss, self.ones, sq, True, True, ["const", ksq], [kss])
            rs, krs = rstd(ss, kss, 128, 1.0 / 128)
            self.stt(ckvn, p_, cv[:, CV_GKV:CV_GKV + 1], rs, ALU.mult, ALU.mult, ["colv", krs], ["ckvn", k_])
            p3, k3 = pP()
            for k in range(8):
                self.mm(p3[0:96], Wkpe[:, k, :], hs(k), k == 0, k == 7, ["Wkpe", "hT"], [k3])
            self.act(sqpe[R], p3[R], AF.Square, [], ["sqpe", k3])
            self.tt(kr[R], p3[R], ckf[R], ALU.mult, ["ckf"], ["kr", k3])
            p4, k4 = pP()
            for k in range(8):
                self.mm(p4[0:96], Wkpes[:, k, :], hs(k), k == 0, k == 7, ["Wkpes", "hT"], [k4])
            tb, ktb = r_tb()
            self.tt(tb[R], p4[R], skf[R], ALU.mult, ["skf"], [ktb, k4])
            self.tt(kr[R], kr[R], tb[R], ALU.add, [ktb, "kr"], ["kr"], eng="pool")
            for bb in range(4):
                blk = t * 4 + bb
                pv, pk = pP()
                self.mm(pv, ckvn[:, bb * 128:(bb + 1) * 128], Wv, True, True, ["ckvn", "Wv"], [pk])
                self.act(Va[:, blk, :, 0:64], pv.rearrange("p (h c) -> p h c", h=8), AF.Copy, ["Va"], ["Va%d" % t, pk])
            for h in range(8):
                pK, pk = pP()
                self.mm(pK[0:64], Wk[:, h * 64:(h + 1) * 64], ckvn, True, True, ["Wk", "ckvn"], [pk])
                sq, ksq = r_sq()
                self.act(sq[0:64], pK[0:64], AF.Square, [], [ksq, pk])
                ss, kss = pSS()
                self.mm(ss[0:96], self.ones[0:64, 0:96], sq[0:64], True, False, ["const", ksq], [kss])
                self.mm(ss[0:96], self.ones[64:96, 0:96], sqpe[R], False, True, ["const", "sqpe"], [kss])
                rs, krs = rstd(ss, kss, 96, 1.0 / 96)
                self.stt(Kc[0:64, h, T0:T0 + TT], pK[0:64], cv[0:64, CV_GKM:CV_GKM + 1], rs[0:64], ALU.mult, ALU.mult,
                         ["colv", krs], ["Y%d_%d" % (4 + h, t), pk])
                self.tt(Kc[R, h, T0:T0 + TT], kr[R], rs[R], ALU.mult, ["kr", krs], ["Y%d_%d" % (4 + h, t)])
            for c in range(4):
                wt, wk = wbz[c]
                pb, pk = pP()
                for k in range(8):
                    self.mm(pb, wt[:, k, :], hs(k), k == 0, k == 7, [wk, "hT"], [pk])
                self.act(sbz[:, c, :], pb, AF.Silu, [], ["sbz", pk])
            for h in range(8):
                pQ, kq = pP()
                for j in range(2):
                    self.mm(pQ[0:96], Wuq[:, j, h * 96:(h + 1) * 96], cqn[:, j, :], j == 0, j == 1, ["Wuq", "cqn"], [kq])
                pQs, kqs = pP()
                for j in range(2):
                    self.mm(pQs[0:96], Wuqs[:, j, h * 96:(h + 1) * 96], cqn[:, j, :], j == 0, j == 1, ["Wuqs", "cqn"], [kqs])
                sq, ksq = r_sq()
                self.act(sq[0:96], pQ[0:96], AF.Square, [], [ksq, kq])
                ss, kss = pSS()
                self.mm(ss[0:96], self.ones[0:96, 0:96], sq[0:96], True, True, ["const", ksq], [kss])
                rs, krs = rstd(ss, kss, 96, 1.0 / 96)
                ta, kta = r_ta()
                tb, ktb = r_tb()
                self.tt(ta[0:96], pQ[0:96], cqf[0:96], ALU.mult, ["cqf"], [kta, kq])
                self.tt(tb[R], pQs[R], sqf[R], ALU.mult, ["sqf"], [ktb, kqs])
                self.tt(ta[R], ta[R], tb[R], ALU.add, [ktb, kta], [kta], eng="pool")
                self.tt(qT[0:96, h, :], ta[0:96], rs[0:96], ALU.mult, [kta, krs], ["qT%d" % h])
            for h in range(8):
                nkb = 4 * t + 4
                pO, ko = pO_()
                for kb in range(nkb):
                    c0 = 0 if kb < 4 * t else (kb - 4 * t) * 128
                    pS, ks = pS_()
                    pt, kpt = r_pt()
                    self.mm(pS[:, c0:TT], Kc[0:96, h, kb * 128:(kb + 1) * 128], qT[0:96, h, c0:TT], True, True,
                            ["Y%d_%d" % (4 + h, kb // 4), "qT%d" % h], [ks])
                    self.act(pt[:, c0:TT], pS[:, c0:TT], AF.Exp, [], [kpt, ks], scale=sc_mla)
                    if kb >= 4 * t:
                        self.tt(pt[:, c0:c0 + 128], pt[:, c0:c0 + 128], self.emask[:, 24, 0:128], ALU.mult, ["const", kpt], [kpt], eng="pool")
                    self.mm(pO[0:65, c0:TT], Va[:, kb, h, :], pt[:, c0:TT], kb == 0, kb == nkb - 1, ["Va", "Va%d" % (kb // 4), kpt], [ko])
                osb, kos = r_osb()
                self.act(osb[0:65], pO[0:65], AF.Copy, [], [kos, ko])
                self.act(osb[64:65], osb[64:65], AF.Ln, [kos], [kos])
                self.act(osb[64:65], osb[64:65], AF.Exp, [kos], [kos], scale=-1.0)
                rb, krb = r_rb()
                self.bcast_rows(osb[64:65, :], kos, [(rb[0:64], krb, 0, TT)])
                ta, kta = r_ta()
                Rh = slice(0, 64) if h % 2 == 0 else slice(64, 128)
                self.tt(ta[Rh], osb[0:64], rb[0:64], ALU.mult, [kos, krb], [kta])
                self.tt(self.yT[Rh, h // 2, T0:T0 + TT], ta[Rh], sbz[Rh, h // 2, :], ALU.mult, [kta, "sbz"], ["Y%d_%d" % (h // 2, t)])

    def stage_dil(self, l, seq):
        A = self.A
        cv = self.colv
        hT = self.hT
        r_dq = self.ring(2, [S], BF16, "dqT")
        r_dk = self.ring(2, [S], BF16, "dkT")
        r_dv = self.ring(2, [S], BF16, "dvT")
        r_dva = self.ring(2, [NB, 2, 65], BF16, "dva")
        r_acc = self.ring(2, [2, S], F32, "acc")
        r_scz = self.ring(2, [S], BF16, "scz")
        r_sq = self.ring(2, [TT], BF16, "sq")
        r_sd = self.ring(2, [TT], F32, "sd")
        r_rs = self.ring(2, [TT], F32, "rs")
        r_ta = self.ring(2, [TT], F32, "ta")
        r_pt = self.ring(4, [256], BF16, "dPT")
        r_rb = self.ring(4, [TT], F32, "rb")
        pP = self.psring([2, 3, 4, 5])
        pS_ = self.psring([6, 7])
        pO_ = self.psring([0, 1])
        for _ in range(2):
            dva, kdva = r_dva()
            self.memset(dva[:, :, :, 64:65], 1.0, [kdva])

        def rstd(ssq, kss, rows, inv_n):
            sd, ksd = r_sd()
            rs, krs = r_rs()
            self.act(sd[0:rows], ssq[0:rows], AF.Ln, ["const"], [ksd, kss], scale=inv_n, bias=self.epsc[0:rows, 0:1])
            self.act(rs[0:rows], sd[0:rows], AF.Exp, [ksd], [krs], scale=-0.5)
            return rs, krs

        def perm_out(buf, g, t):
            if g == 0:
                return buf[:, t * TT:(t + 1) * TT], None
            if g == 1:
                return buf.rearrange("p (r m i) -> p r m i", r=4, m=4)[:, :, t, :], ("p (i r) -> p r i", 4)
            return buf.rearrange("p (r i) -> p r i", r=16)[:, :, 32 * t:32 * t + 32], ("p (i r) -> p r i", 16)

        def segments(g):
            segs = []
            for s_ in range(4):
                u = []
                if g == 0:
                    if s_ > 0:
                        u.append(((4 * s_ - 1) * 128, 4 * s_ - 1, 4 * s_ * 128, 128, 128))
                    for kb in range(4 * s_, 4 * s_ + 4):
                        u.append((kb * 128, kb, kb * 128, 256 if kb < 4 * s_ + 3 else 128, 0))
                elif g == 1:
                    for m in range(4):
                        blk = 4 * s_ + m
                        u.append((blk * 128, blk, blk * 128, 256 if m < 3 else 128, 0))
                else:
                    for c in range(4):
                        blk = 4 * s_ + c
                        u.append((blk * 128, blk, blk * 128, 128, 0))
                segs.append(u)
            return segs

        for j in range(4):
            scz, kscz = r_scz()
            acc, kacc = r_acc()
            wcz, kcz = self.ldw(self.win_cols(l, C_CZ + j * 128, 128))
            for t in range(NT):
                pc, kpc = pP()
                for k in range(8):
                    self.mm(pc, wcz[:, k, :], hT[:, k, t * TT:(t + 1) * TT], k == 0, k == 7, [kcz, "hT"], [kpc])
                self.act(scz[:, t * TT:(t + 1) * TT], pc, AF.Silu, [], [kscz, kpc])
            for g in range(3):
                c_off = g * 512 + j * 128
                wq, kwq = self.ldw(self.win_cols(l, C_DQ + c_off, 128))
                wk_, kwk = self.ldw(self.win_cols(l, C_DK + c_off, 128))
                wv, kwv = self.ldw(self.win_cols(l, C_DV + c_off, 128))
                dqT, kdq = r_dq()
                dkT, kdk = r_dk()
                dvT, kdv = r_dv()
                dva, kdva = r_dva()
                for t in range(NT):
                    hs = lambda k: hT[:, k, t * TT:(t + 1) * TT]
                    for (wt, wkey, dst, dkey, gcol) in ((wq, kwq, dqT, kdq, CV_GDQ + g), (wk_, kwk, dkT, kdk, CV_GDK + g)):
                        pp, pk = pP()
                        for k in range(8):
                            self.mm(pp, wt[:, k, :], hs(k), k == 0, k == 7, [wkey, "hT"], [pk])
                        sq, ksq = r_sq()
                        self.act(sq, pp, AF.Square, [], [ksq, pk])
                        ss, kss = pP()
                        self.mm(ss, self.ind2, sq, True, True, ["const", ksq], [kss])
                        rs, krs = rstd(ss, kss, 128, 1.0 / 64)
                        ov, pr = perm_out(dst, g, t)
                        if pr is None:
                            self.stt(ov, pp, cv[:, gcol:gcol + 1], rs, ALU.mult, ALU.mult, ["colv", krs], [dkey, pk])
                        else:
                            self.stt(ov, pp.rearrange(pr[0], r=pr[1]), cv[:, gcol:gcol + 1], rs.rearrange(pr[0], r=pr[1]),
                                     ALU.mult, ALU.mult, ["colv", krs], [dkey, pk])
                    pp, pk = pP()
                    for k in range(8):
                        self.mm(pp, wv[:, k, :], hs(k), k == 0, k == 7, [kwv, "hT"], [pk])
                    ov, pr = perm_out(dvT, g, t)
                    if pr is None:
                        self.act(ov, pp, AF.Copy, [], [kdv, pk])
                    else:
                        self.act(ov, pp.rearrange(pr[0], r=pr[1]), AF.Copy, [], [kdv, pk])
                for half in range(2):
                    pp, pk = pP()
                    pst = pp.bitcast(BF16)
                    for bb in range(8):
                        blk = half * 8 + bb
                        self.tr(pst[:, bb * 128:(bb + 1) * 128], dvT[:, blk * 128:(blk + 1) * 128], [kdv], [pk])
                    self.act(dva[:, half * 8:(half + 1) * 8, :, 0:64],
                             pst.rearrange("p (b h c) -> p b h c", b=8, h=2), AF.Copy, [], [kdva, pk])
                segs = segments(g)
                for hh in range(2):
                    Rr = slice(hh * 64, hh * 64 + 64)
                    em = self.emask[:, g * 8 + 2 * j + hh, :]
                    for b, units in enumerate(segs):
                        pO, ko = pO_()
                        first = True
                        for (K0, blk, q0, nq, m0) in units:
                            pS, ks = pS_()
                            pt, kpt = r_pt()
                            self.mm(pS[:, 0:nq], dkT[Rr, K0:K0 + 128], dqT[Rr, q0:q0 + nq], True, True, [kdk, kdq], [ks])
                            self.act(pt[:, 0:nq], pS[:, 0:nq], AF.Exp, [], [kpt, ks], scale=0.125)
                            self.tt(pt[:, 0:nq], pt[:, 0:nq], em[:, m0:m0 + nq], ALU.mult, ["const", kpt], [kpt], eng=("pool" if g == 2 else "dve"))
                            self.mm(pO[0:65, q0 - b * 512:q0 - b * 512 + nq], dva[:, blk, hh, :], pt[:, 0:nq], first, False,
                                    [kdva, kpt], [ko])
                            first = False
                        if g == 0:
                            self.act(acc[0:65, hh, b * 512:(b + 1) * 512], pO[0:65], AF.Copy, [], [kacc, ko])
                        elif g == 1:
                            av = acc[0:65, hh, :].rearrange("p (m i r) -> p r m i", m=4, r=4)[:, b]
                            self.tt(av, av, pO[0:65].rearrange("p (m i) -> p m i", m=4), ALU.add, [kacc], [kacc, ko])
                        else:
                            av = acc[0:65, hh, :].rearrange("p (i r) -> p r i", r=16)[:, 4 * b:4 * b + 4, :]
                            self.tt(av, av, pO[0:65].rearrange("p (r i) -> p r i", r=4), ALU.add, [kacc], [kacc, ko])
            for hh in range(2):
                Rh = slice(hh * 64, hh * 64 + 64)
                self.act(acc[64:65, hh, :], acc[64:65, hh, :], AF.Ln, [kacc], [kacc])
                self.act(acc[64:65, hh, :], acc[64:65, hh, :], AF.Exp, [kacc], [kacc], scale=-1.0)
                rbs = [r_rb() for _ in range(NT)]
                self.bcast_rows(acc[64:65, hh, :], kacc, [(rbs[t][0][0:64], rbs[t][1], t * TT, TT) for t in range(NT)])
                for t in range(NT):
                    cs = slice(t * TT, (t + 1) * TT)
                    rb, krb = rbs[t]
                    ta, kta = r_ta()
                    self.tt(ta[Rh], acc[0:64, hh, cs], rb[0:64], ALU.mult, [kacc, krb], [kta])
                    self.tt(self.yT[Rh, 4 + j, cs], ta[Rh], scz[Rh, cs], ALU.mult, [kta, kscz], ["Y%d_%d" % (4 + j, t)])

    def stage_conv(self, l, seq):
        A = self.A
        ps = self.ps
        cv = self.colv
        hT = self.hT
        r_up = self.ring(2, [S + 2], F32, "upad")
        r_acs = self.ring(2, [TT], F32, "acs")
        r_sz = self.ring(2, [TT], F32, "sz")
        r_c1 = self.ring(2, [TT], F32, "c1")
        r_c2 = self.ring(2, [TT], F32, "c2")
        r_t1 = self.ring(3, [TT], F32, "t1")
        n = 0
        for c in range(4):
            ws = [self.ldw(self.win_cols(l, base + c * 128, 128)) for base in (C_AB, C_AC, C_AX, C_AZ)]
            upad, kup = r_up()
            self.memset(upad[:, 0:2], 0.0, [kup])
            for t in range(NT):
                T0 = t * TT
                o = 4 * (n % 2)
                n += 1
                for i in range(4):
                    wt, wk = ws[i]
                    for k in range(8):
                        self.mm(ps[o + i], wt[:, k, :], hT[:, k, T0:T0 + TT], k == 0, k == 7, [wk, "hT"], ["ps%d" % (o + i)])
                kb_, kc_, kx_, kz_ = ["ps%d" % (o + i) for i in range(4)]
                acs, kacs = r_acs()
                sz, ksz = r_sz()
                c1, kc1 = r_c1()
                c2, kc2 = r_c2()
                self.act(acs, ps[o + 1], AF.Copy, [], [kacs, kc_])
                self.tt(upad[:, 2 + T0:2 + T0 + TT], acs, ps[o + 2], ALU.mult, [kacs], [kup, kx_])
                self.act(sz, ps[o + 3], AF.Silu, [], [ksz, kz_])
                t1, kt1 = r_t1()
                self.act(c1, upad[:, T0:T0 + TT], AF.Identity, [kup, "colv"], [kc1], scale=cv[:, CV_CW + c:CV_CW + c + 1],
                         bias=cv[:, CV_CB + c:CV_CB + c + 1])
                self.act(t1, upad[:, T0 + 1:T0 + 1 + TT], AF.Identity, [kup, "colv"], [kt1], scale=cv[:, CV_CW + 4 + c:CV_CW + 5 + c])
                self.tt(c1, c1, t1, ALU.add, [kc1, kt1], [kc1], eng="pool")
                t1, kt1 = r_t1()
                self.act(t1, upad[:, T0 + 2:T0 + 2 + TT], AF.Identity, [kup, "colv"], [kt1], scale=cv[:, CV_CW + 8 + c:CV_CW + 9 + c])
                self.tt(c1, c1, t1, ALU.add, [kc1, kt1], [kc1], eng="pool")
                self.tt(c2, c1, ps[o + 0], ALU.mult, [kc1], [kc2, kb_])
                self.tt(self.yT[:, 8 + c, T0:T0 + TT], c2, sz, ALU.mult, [kc2, ksz], ["Y%d_%d" % (8 + c, t)])

    def stage_out(self, l, seq, src, dst, final):
        A = self.A
        ps = self.ps
        cv = self.colv
        hT = self.hT
        yT = self.yT
        Wo3 = [A.alloc([4, D], BF16, key="Wo3_%d" % i) for i in range(3)]
        Wo = A.alloc([8, D], BF16, key="Wo")
        G = A.alloc([24, TT], BF16)
        for jj in range(24):
            self.P.keymap["G%d" % jj] = A.pages(A.last[0] + jj * TT, TT)
        mT = A.alloc([8, TT], BF16)
        for jj in range(8):
            self.P.keymap["mT%d" % jj] = A.pages(A.last[0] + jj * TT, TT)
        r_m1 = self.ring(2, [TT], F32, "m1")
        r_m2 = self.ring(2, [TT], F32, "m2")
        xb = [A.alloc([D], F32, key="oxb%d" % i) for i in range(2)]
        ob = [A.alloc([D], F32, key="ob%d" % i) for i in range(2)]
        for i in range(3):
            self.dma("pool", Wo3[i], self.w_out[i][l].rearrange("(k p) n -> p k n", p=128), key_w="Wo3_%d" % i)
        self.dma("pool", Wo, self.w_o[l].rearrange("(k p) n -> p k n", p=128), key_w="Wo")
        och = self.cout if final else self.misc_chan("scrw")
        pG = self.psring([0, 1])
        for t in range(NT):
            T0 = t * TT
            DEPTHW = 4
            wq = [self.ldw(self.win_cols(l, C_GATE + jj * 128, 128)) for jj in range(DEPTHW)]
            for jj in range(24):
                wt, wk = wq[jj]
                pg, kg = pG()
                for k in range(8):
                    self.mm(pg, wt[:, k, :], hT[:, k, T0:T0 + TT], k == 0, k == 7, [wk, "hT"], [kg])
                self.act(G[:, jj, :], pg, AF.Sigmoid, ["colv"], ["G%d" % jj, kg], bias=cv[:, CV_BG + jj:CV_BG + jj + 1])
                if jj + DEPTHW < 24:
                    wq.append(self.ldw(self.win_cols(l, C_GATE + (jj + DEPTHW) * 128, 128)))
            for j in range(8):
                o = 2 + 3 * (j % 2)
                for i in range(3):
                    for k in range(4):
                        self.mm(ps[o + i], Wo3[i][:, k, j * 128:(j + 1) * 128], yT[:, (8, 0, 4)[i] + k, T0:T0 + TT], k == 0, k == 3,
                                ["Wo3_%d" % i, "Y%d_%d" % ((8, 0, 4)[i] + k, t)], ["ps%d" % (o + i)])
                m1, k1 = r_m1()
                m2, k2 = r_m2()
                self.tt(m1, ps[o], G[:, j, :], ALU.mult, ["G%d" % j], [k1, "ps%d" % o])
                self.tt(m2, ps[o + 1], G[:, 8 + j, :], ALU.mult, ["G%d" % (8 + j)], [k2, "ps%d" % (o + 1)])
                self.tt(m1, m1, m2, ALU.add, [k1, k2], [k1], eng="pool")
                self.tt(m2, ps[o + 2], G[:, 16 + j, :], ALU.mult, ["G%d" % (16 + j)], [k2, "ps%d" % (o + 2)])
                self.tt(mT[:, j, :], m1, m2, ALU.add, [k1, k2], ["mT%d" % j], eng="pool")
            for bb in range(4):
                b = t * 4 + bb
                i = b % 2
                r0 = seq * S + b * 128
                self.dma("sp", xb[i], src[r0:r0 + 128, :], key_w="oxb%d" % i, key_r=("scr" if l > 0 else None))
                for half in range(2):
                    pg, kg = pG()
                    for k in range(8):
                        self.mm(pg, mT[:, k, bb * 128:(bb + 1) * 128], Wo[:, k, half * 512:(half + 1) * 512], k == 0, k == 7,
                                ["mT%d" % k, "Wo"], [kg])
                    self.tt(ob[i][:, half * 512:(half + 1) * 512], pg, xb[i][:, half * 512:(half + 1) * 512], ALU.add,
                            ["oxb%d" % i], ["ob%d" % i, kg])
                self.dma("sp", dst[r0:r0 + 128, :], ob[i], key_r="ob%d" % i, key_w=(None if final else "scr"), chan=och)


def _rope_tables():
    inv = (10000.0 ** (-np.arange(0, 32, 2, dtype=np.float32) / 32)).astype(np.float32)
    ang = np.arange(S, dtype=np.float32)[:, None] * inv[None, :]
    cos, sin = np.cos(ang).astype(np.float32), np.sin(ang).astype(np.float32)
    ct = np.ones((96, S), np.float32)
    st = np.zeros((96, S), np.float32)
    ct[64:80] = cos.T
    ct[80:96] = cos.T
    st[64:80] = -sin.T
    st[80:96] = sin.T
    return ct, st


def _emask():
    n = 24
    slopes = (2.0 ** (-8.0 * np.arange(1, n + 1, dtype=np.float32) / n)).reshape(3, 8)
    k = np.arange(128)[:, None].astype(np.float32)
    q = np.arange(128)[None, :].astype(np.float32)
    em = np.zeros((128, 25, 256), np.float32)
    for g in range(3):
        for h in range(8):
            sl = slopes[g, h] * DIL[g]
            em[:, g * 8 + h, 0:128] = np.where(k <= q, np.exp(-sl * np.maximum(q - k, 0.0)), 0.0)
            if g < 2:
                em[:, g * 8 + h, 128:256] = np.where(k >= q, np.exp(-sl * np.maximum(128.0 + q - k, 0.0)), 0.0)
    em[:, 24, 0:128] = (k <= q).astype(np.float32)
    return em.reshape(128, 25 * 256)


def _host_layout(inp, layers):
    f = lambda a: np.ascontiguousarray(a, dtype=np.float32)
    L = len(layers)
    w_uq = f(inp["w_uq"][layers])
    w_uq_sw = np.zeros_like(w_uq)
    for h in range(8):
        b = h * 96
        w_uq_sw[:, :, b + 64:b + 80] = w_uq[:, :, b + 80:b + 96]
        w_uq_sw[:, :, b + 80:b + 96] = w_uq[:, :, b + 64:b + 80]
    w_ukv = f(inp["w_ukv"][layers]).reshape(L, 128, 8, 128)
    w_ukv_k = f(w_ukv[:, :, :, 0:64].reshape(L, 128, 512))
    w_ukv_v = f(w_ukv[:, :, :, 64:128].reshape(L, 128, 512))
    w_in = inp["w_in"]
    w_kpe = np.zeros((L, D, 96), np.float32)
    w_kpe_sw = np.zeros((L, D, 96), np.float32)
    for i, l in enumerate(layers):
        kp = w_in[l][:, C_KPE:C_KPE + 32]
        w_kpe[i, :, 64:96] = kp
        w_kpe_sw[i, :, 64:80] = kp[:, 16:32]
        w_kpe_sw[i, :, 80:96] = kp[:, 0:16]
    colv = np.zeros((L, 128, NCV), np.float32)
    gbc = np.zeros((L, 128, D), np.float32)
    for i, l in enumerate(layers):
        colv[i, :, CV_BG:CV_BG + 24] = inp["b_gate"][l].reshape(24, 128).T
        for tap in range(3):
            colv[i, :, CV_CW + tap * 4:CV_CW + tap * 4 + 4] = inp["conv_w"][l][tap].reshape(4, 128).T
        colv[i, :, CV_CB:CV_CB + 4] = inp["conv_b"][l].reshape(4, 128).T
        colv[i, :, CV_GQ:CV_GQ + 2] = inp["q_a_norm_g"][l].reshape(2, 128).T
        colv[i, :, CV_GKV] = inp["kv_a_norm_g"][l]
        for (col, src) in ((CV_GQM, inp["mla_q_norm_g"][l]), (CV_GKM, inp["mla_k_norm_g"][l])):
            colv[i, 0:96, col] = src
            colv[i, 64:80, col + 1] = src[80:96]
            colv[i, 80:96, col + 1] = src[64:80]
        for g in range(3):
            colv[i, :, CV_GDQ + g] = np.tile(inp["dil_q_norm_g"][l][g], 2)
            colv[i, :, CV_GDK + g] = np.tile(inp["dil_k_norm_g"][l][g], 2)
        gbc[i] = np.broadcast_to(inp["norm_g"][l][None, :], (128, D))
    ind2 = np.zeros((128, 128), np.float32)
    ind2[0:64, 0:64] = 1.0
    ind2[64:128, 64:128] = 1.0
    ct, st = _rope_tables()
    shared = {
        "w_in": f(w_in[layers]), "w_uq": w_uq, "w_uq_sw": w_uq_sw, "w_ukv_k": w_ukv_k, "w_ukv_v": w_ukv_v,
        "w_kpe": w_kpe, "w_kpe_sw": w_kpe_sw,
        "w_out_a": f(inp["w_out_a"][layers]), "w_out_b": f(inp["w_out_b"][layers]), "w_out_c": f(inp["w_out_c"][layers]),
        "w_o": f(inp["w_o"][layers]), "colv": colv, "gbc": gbc,
        "ident": np.eye(128, dtype=np.float32), "ind2": ind2, "emask": _emask(), "ctab": ct, "stab": st,
    }
    return shared


_CACHE = {}


def _get_builder(nseq, nlayers, dbg=False):
    key = (nseq, nlayers, dbg)
    if key not in _CACHE:
        _CACHE[key] = Builder(nseq, nlayers, dbg=dbg)
    return _CACHE[key]


def kernel(**inputs):
    inp = {k: np.asarray(v) for k, v in inputs.items()}
    x = np.ascontiguousarray(inp["x"], dtype=np.float32)
    B = x.shape[0]
    ncores = 8
    nseq = B // ncores
    shared = _host_layout(inp, list(range(DEPTH)))
    bld = Builder(nseq, DEPTH)
    in_maps = []
    for c in range(ncores):
        m = dict(shared)
        m["x"] = x[c * nseq:(c + 1) * nseq].reshape(nseq * S, D)
        in_maps.append(m)
    res = run_bass_kernel_spmd(bld.nc, in_maps, core_ids=list(range(ncores)))
    outs = [np.asarray(r["out"]).reshape(nseq, S, D) for r in res.results]
    return np.concatenate(outs, axis=0).astype(np.float32)
```

```python
import numpy as np
import concourse.bass as bass
import concourse.mybir as mybir
from concourse.bass_utils import run_bass_kernel_spmd

F32 = mybir.dt.float32
BF16 = mybir.dt.bfloat16
AF = mybir.ActivationFunctionType
ALU = mybir.AluOpType

ENGS = ("pe", "act", "dve", "pool", "sp")

D = 1024
S = 2048
DEPTH = 2
NB = 16
NT = 4
TT = 512
EPS = 1e-6
N_IN = 11168
C_AB, C_AC, C_AX, C_AZ = 0, 512, 1024, 1536
C_CQ, C_CKV, C_KPE, C_BZ = 2048, 2304, 2432, 2464
C_DQ, C_DK, C_DV, C_CZ, C_GATE = 2976, 4512, 6048, 7584, 8096
DIL = (1, 4, 16)
CV_BG, CV_CW, CV_CB, CV_GQ, CV_GKV, CV_GQM, CV_GQMS, CV_GKM, CV_GKMS, CV_GDQ, CV_GDK, NCV = 0, 24, 36, 40, 42, 43, 44, 45, 46, 47, 50, 53


import heapq
import os

RAW, WAR, WAW = 0, 1, 2
ACT_SET = {"Exp": "exp", "Ln": "exp", "Sqrt": "sqrt", "Silu": "silu", "Sigmoid": "sigmoid"}


class Chan:
    def __init__(self, sem, name):
        self.sem = sem
        self.name = name
        self.count = 0


class Op:
    __slots__ = ("eng", "fn", "idx", "pidx", "deps", "succ", "ndeps", "waits", "signal", "sigcount", "chan",
                 "dma_deps", "dur", "tset", "ready", "finish", "seg", "isbar")

    def __init__(self, eng, fn):
        self.eng = eng
        self.fn = fn
        self.idx = -1
        self.pidx = -1
        self.deps = []
        self.succ = []
        self.ndeps = 0
        self.waits = {}
        self.dma_deps = []
        self.signal = False
        self.sigcount = 0
        self.chan = None
        self.dur = 0.5
        self.tset = None
        self.ready = 0.0
        self.finish = 0.0
        self.seg = 0
        self.isbar = False


class Prog:
    def __init__(self, nc):
        self.nc = nc
        self.ops = []
        self.kw = {}
        self.kr = {}
        self.sems = {e: nc.alloc_semaphore("sem_" + e) for e in ENGS}
        self.nops = 0
        self.chans = []
        self.seg = 0
        self.sched = True
        self.prio = os.environ.get("KPRIO", "bl")
        self.keymap = {}

    def chan(self, name):
        c = Chan(self.nc.alloc_semaphore("dsem_" + name), name)
        self.chans.append(c)
        return c

    def barrier(self):
        self.seg += 1

    def add(self, eng, fn, reads=(), writes=(), chan=None, dur=0.5, tset=None):
        op = Op(eng, fn)
        op.pidx = len(self.ops)
        op.seg = self.seg
        op.chan = chan
        op.dur = dur
        op.tset = tset
        self.ops.append(op)
        self.nops += 1
        km = self.keymap
        if km:
            reads = list(reads) + [p for k in reads for p in km.get(k, ())]
            writes = list(writes) + [p for k in writes for p in km.get(k, ())]
        deps = {}
        for k in reads:
            w = self.kw.get(k)
            if w is not None:
                deps[w] = RAW
        for k in writes:
            w = self.kw.get(k)
            if w is not None and w not in deps:
                deps[w] = WAW
            for r in self.kr.get(k, ()):
                if r is not op and r not in deps:
                    deps[r] = WAR
        for k in reads:
            self.kr.setdefault(k, []).append(op)
        for k in writes:
            self.kw[k] = op
            self.kr[k] = []
        op.deps = [(d, kind) for d, kind in deps.items() if d.seg == op.seg]
        return op

    def _schedule(self, ops):
        if not self.sched:
            return list(ops)
        for op in ops:
            op.ndeps = len(op.deps)
            op.succ = []
            op.ready = 0.0
        for op in ops:
            for d, _ in op.deps:
                d.succ.append(op)
        LAT = float(os.environ.get("KLAT", "0.3"))
        if self.prio == "bl":
            for op in reversed(ops):
                b = 0.0
                for s_ in op.succ:
                    v = s_.finish + (LAT if s_.eng != op.eng else 0.0)
                    if v > b:
                        b = v
                op.finish = b + op.dur + (2.0 if op.chan is not None else 0.0)
            mx = max(op.finish for op in ops) if ops else 0.0
            for i_, op in enumerate(ops):
                op.pidx = int((mx - op.finish) * 1000) * 100000 + i_
        free = {e: 0.0 for e in ENGS}
        pending = {e: [] for e in ENGS}
        avail = {e: {} for e in ENGS}
        curset = [None]
        for op in ops:
            if op.ndeps == 0:
                heapq.heappush(pending[op.eng], (0.0, op.pidx, op))
        order = []
        n = len(ops)

        def cand(e):
            t = free[e]
            pend = pending[e]
            av = avail[e]
            while pend and pend[0][0] <= t:
                _, pi, o = heapq.heappop(pend)
                heapq.heappush(av.setdefault(o.tset if e == "act" else None, []), (pi, o))
            best = None
            if e == "act":
                for ts_ in (None, curset[0]):
                    h = av.get(ts_)
                    if h and (best is None or h[0][0] < best[0]):
                        best = (h[0][0], ts_)
                if best is None:
                    for ts_, h in av.items():
                        if h and (best is None or h[0][0] < best[0]):
                            best = (h[0][0], ts_)
            else:
                h = av.get(None)
                if h:
                    best = (h[0][0], None)
            if best is not None:
                return (t, best[0], best[1], False)
            if pend:
                return (pend[0][0], pend[0][1], None, True)
            return None

        while len(order) < n:
            bc = None
            be = None
            for e in ENGS:
                c = cand(e)
                if c is not None and (bc is None or (c[0], c[1]) < (bc[0], bc[1])):
                    bc, be = c, e
            assert bc is not None, "scheduler stuck"
            if bc[3]:
                _, pi, o = heapq.heappop(pending[be])
            else:
                pi, o = heapq.heappop(avail[be][bc[2]])
            start = max(bc[0], free[be])
            dur = o.dur
            if be == "act" and o.tset is not None and o.tset != curset[0]:
                dur += 2.7
                curset[0] = o.tset
            if o.chan is not None:
                free[be] = start + 0.15
                o.finish = start + 2.0 + dur
            else:
                free[be] = start + dur
                o.finish = start + dur
            order.append(o)
            for s_ in o.succ:
                s_.ndeps -= 1
                rt = o.finish + (LAT if s_.eng != o.eng else 0.05)
                if rt > s_.ready:
                    s_.ready = rt
                if s_.ndeps == 0:
                    heapq.heappush(pending[s_.eng], (s_.ready, s_.pidx, s_))
        return order

    def emit(self, final_chans=()):
        nc = self.nc
        nseg = self.seg + 1
        segs = [[] for _ in range(nseg)]
        for op in self.ops:
            segs[op.seg].append(op)
        glob = []
        for si in range(nseg):
            if si > 0:
                drains = []
                for e in ENGS:
                    if e == "sp":
                        continue
                    d = Op(e, lambda h: h.drain())
                    d.isbar = True
                    drains.append(d)
                    glob.append(d)
                for e in ENGS:
                    w = Op(e, lambda h: h.nop())
                    w.isbar = True
                    w.deps = [(d, RAW) for d in drains if d.eng != e]
                    w.dma_deps = "all"
                    glob.append(w)
            glob.extend(self._schedule(segs[si]))
        self.eng_ops = {e: [] for e in ENGS}
        for op in glob:
            op.idx = len(self.eng_ops[op.eng])
            self.eng_ops[op.eng].append(op)
        chan_cnt = {c: 0 for c in self.chans}
        dma_wait_vals = {}
        for op in glob:
            isdma = op.chan is not None
            dw = {}
            if op.dma_deps == "all":
                for c in self.chans:
                    if chan_cnt[c] > 0:
                        dw[c] = chan_cnt[c]
            for d, kind in op.deps:
                if d.chan is not None:
                    dw[d.chan] = chan_cnt[d.chan]
                    continue
                if d.eng == op.eng and not isdma and kind != RAW:
                    continue
                if d.eng == op.eng and d.eng == "pe":
                    continue
                cur = op.waits.get(d.eng)
                if cur is None or cur.idx < d.idx:
                    op.waits[d.eng] = d
            if isdma:
                chan_cnt[op.chan] += 1
            dma_wait_vals[op] = dw
        for op in glob:
            for p in op.waits.values():
                p.signal = True
        for e in ENGS:
            c = 0
            for op in self.eng_ops[e]:
                if op.signal:
                    c += 1
                op.sigcount = c
        final_cnt = {c: chan_cnt[c] for c in final_chans}

        def run_engine(e, h):
            seen = {}
            for op in self.eng_ops[e]:
                for pe_, p in op.waits.items():
                    need = p.sigcount
                    if seen.get(pe_, 0) < need:
                        h.wait_ge(self.sems[pe_], need)
                        seen[pe_] = need
                for c, cnt in dma_wait_vals[op].items():
                    if seen.get(c, 0) < cnt:
                        h.wait_ge(c.sem, 16 * cnt)
                        seen[c] = cnt
                ins = op.fn(h)
                if op.chan is not None:
                    ins.then_inc(op.chan.sem, 16)
                elif op.signal:
                    ins.then_inc(self.sems[e], 1)
            if e == "sp":
                for c, cnt in final_cnt.items():
                    h.wait_ge(c.sem, 16 * cnt)

        with nc.Block() as block:
            @block.tensor
            def _(h):
                run_engine("pe", h)

            @block.scalar
            def _(h):
                run_engine("act", h)

            @block.vector
            def _(h):
                run_engine("dve", h)

            @block.gpsimd
            def _(h):
                run_engine("pool", h)

            @block.sync
            def _(h):
                run_engine("sp", h)


class Arena:
    def __init__(self, nc, nbytes):
        self.cap = nbytes // 2
        self.t = nc.alloc_sbuf_tensor("arena", [128, self.cap], BF16).ap()
        self.top = 0
        self.paged = False
        self.keymap = {}

    def pages(self, off, ne):
        return ["pg%d" % i for i in range(off // 256, (off + ne - 1) // 256 + 1)]

    def alloc(self, free_shape, dtype, key=None):
        n = 1
        for v in free_shape:
            n *= v
        ne = n * (2 if dtype == F32 else 1)
        ne_al = (ne + 255) // 256 * 256 if self.paged else (ne + 31) // 32 * 32
        off = self.top
        self.top += ne_al
        assert self.top <= self.cap, ("arena overflow", self.top * 2, self.cap * 2)
        self.last = (off, ne)
        if key is not None and self.paged:
            for k in ([key] if isinstance(key, str) else key):
                self.keymap[k] = self.pages(off, ne)
        v = self.t[:, off:off + ne]
        if dtype == F32:
            v = v.bitcast(F32)
        if len(free_shape) > 1:
            names = " ".join("a%d" % i for i in range(len(free_shape)))
            kw = {"a%d" % i: free_shape[i] for i in range(len(free_shape))}
            v = v.rearrange("p (%s) -> p %s" % (names, names), **kw)
        return v


class Builder:
    def __init__(self, nseq, nlayers, layer0=0, dbg=False):
        self.nseq = nseq
        self.nlayers = nlayers
        self.dbg = dbg
        nc = bass.Bass("TRN2", target_bir_lowering=False)
        self.nc = nc
        P = Prog(nc)
        self.P = P
        L = nlayers
        dt = nc.dram_tensor
        self.x = dt("x", [nseq * S, D], F32, kind="ExternalInput").ap()
        self.out = dt("out", [nseq * S, D], F32, kind="ExternalOutput").ap()
        self.w_in = dt("w_in", [L, D, N_IN], F32, kind="ExternalInput").ap()
        self.w_uq = dt("w_uq", [L, 256, 768], F32, kind="ExternalInput").ap()
        self.w_uq_sw = dt("w_uq_sw", [L, 256, 768], F32, kind="ExternalInput").ap()
        self.w_ukv_k = dt("w_ukv_k", [L, 128, 512], F32, kind="ExternalInput").ap()
        self.w_ukv_v = dt("w_ukv_v", [L, 128, 512], F32, kind="ExternalInput").ap()
        self.w_kpe = dt("w_kpe", [L, D, 96], F32, kind="ExternalInput").ap()
        self.w_kpe_sw = dt("w_kpe_sw", [L, D, 96], F32, kind="ExternalInput").ap()
        self.w_out = [dt("w_out_%s" % n, [L, 512, D], F32, kind="ExternalInput").ap() for n in "abc"]
        self.w_o = dt("w_o", [L, D, D], F32, kind="ExternalInput").ap()
        self.colv_d = dt("colv", [L, 128, NCV], F32, kind="ExternalInput").ap()
        self.gbc_d = dt("gbc", [L, 128, D], F32, kind="ExternalInput").ap()
        self.ident_d = dt("ident", [128, 128], F32, kind="ExternalInput").ap()
        self.ind2_d = dt("ind2", [128, 128], F32, kind="ExternalInput").ap()
        self.emask_d = dt("emask", [128, 25 * 256], F32, kind="ExternalInput").ap()
        self.ct_d = dt("ctab", [96, S], F32, kind="ExternalInput").ap()
        self.st_d = dt("stab", [96, S], F32, kind="ExternalInput").ap()
        if nlayers > 1:
            self.scr = dt("scr", [nseq * S, D], F32).ap()
        self.bsc = dt("bsc", [4, S], F32).ap()
        self.bsi = 0
        if dbg:
            self.dbg_h = dt("dbg_h", [128, 8 * S], BF16, kind="ExternalOutput").ap()
            self.dbg_y = dt("dbg_y", [128, 12 * S], BF16, kind="ExternalOutput").ap()
        self.ps = [nc.alloc_psum_tensor("ps%d" % i, [128, 512], F32).ap() for i in range(8)]
        A = Arena(nc, 212480)
        self.A = A
        self.hT = A.alloc([8, S], BF16)
        self.yT = A.alloc([12, S], BF16)
        self.ident = A.alloc([128], BF16)
        self.ones = A.alloc([128], BF16)
        self.ind2 = A.alloc([128], BF16)
        self.onesf = A.alloc([64], F32)
        self.emask = A.alloc([25, 256], BF16)
        self.colv = A.alloc([NCV], F32)
        self.epsc = A.alloc([1], F32)
        self.NWR = 8
        self.wr = [A.alloc([8, 128], BF16) for _ in range(self.NWR)]
        self.wrc = [P.chan("wr%d" % i) for i in range(self.NWR)]
        self.wri = 0
        A.top = (A.top + 255) // 256 * 256
        self.stage_base = A.top
        A.paged = True
        A.keymap = P.keymap
        self.cst = P.chan("const")
        self.cout = P.chan("out")
        self.cmisc = {}
        self.build()

    def misc_chan(self, key):
        if key not in self.cmisc:
            self.cmisc[key] = self.P.chan("m%d" % len(self.cmisc))
        return self.cmisc[key]

    def dma(self, q, out, in_, key_w=None, key_r=None, chan=None):
        if chan is None:
            chan = self.misc_chan(key_w if key_w is not None else key_r)
        nbytes = out.free_size() * out.partition_size() * 4
        self.P.add(q, lambda h: h.dma_start(out=out, in_=in_),
                   reads=[key_r] if key_r else [], writes=[key_w] if key_w else [], chan=chan, dur=nbytes / 150e3)

    def mm(self, out, lhsT, rhs, start, stop, r, w):
        self.P.add("pe", lambda h: h.matmul(out, lhsT=lhsT, rhs=rhs, start=start, stop=stop, skip_group_check=True), r, w,
                   dur=(0.03 + out.free_size() * 0.00043) * (4.0 if lhsT.dtype == F32 else 1.0))

    def tr(self, out, in_, r, w):
        ident = self.ident
        self.P.add("pe", lambda h: h.transpose(out, in_, ident), list(r) + ["const"], w, dur=0.1)

    def act(self, out, in_, func, r, w, scale=None, bias=None, accum=None):
        kw = {}
        if scale is not None:
            kw["scale"] = scale
        if bias is not None:
            kw["bias"] = bias
        if accum is not None:
            kw["accum_out"] = accum
        self.P.add("act", lambda h: h.activation(out=out, in_=in_, func=func, **kw), r, w,
                   dur=0.2 + out.free_size() * 0.0009, tset=ACT_SET.get(func.name))

    def _dd(self, out, eng="dve"):
        if eng == "pool":
            return 0.25 + out.free_size() * (0.0021 if out.dtype == F32 else 0.0009)
        return 0.12 + out.free_size() * (0.0015 if out.dtype == F32 else 0.0009)

    def tt(self, out, in0, in1, op, r, w, eng="dve"):
        self.P.add(eng, lambda h: h.tensor_tensor(out=out, in0=in0, in1=in1, op=op), r, w, dur=self._dd(out, eng))

    def stt(self, out, in0, scalar, in1, op0, op1, r, w, eng="dve"):
        self.P.add(eng, lambda h: h.scalar_tensor_tensor(out=out, in0=in0, scalar=scalar, in1=in1, op0=op0, op1=op1), r, w,
                   dur=self._dd(out, eng))

    def ts(self, out, in0, s1, s2, op0, op1, r, w, eng="dve"):
        if s2 is None:
            self.P.add(eng, lambda h: h.tensor_scalar(out=out, in0=in0, scalar1=s1, scalar2=None, op0=op0), r, w,
                       dur=self._dd(out, eng))
        else:
            self.P.add(eng, lambda h: h.tensor_scalar(out=out, in0=in0, scalar1=s1, scalar2=s2, op0=op0, op1=op1), r, w,
                       dur=self._dd(out, eng))

    def recip(self, out, in_, r, w):
        self.P.add("dve", lambda h: h.reciprocal(out=out, in_=in_), r, w, dur=self._dd(out))

    def cpy(self, out, in_, r, w, eng="dve"):
        self.P.add(eng, lambda h: h.tensor_copy(out=out, in_=in_), r, w, dur=self._dd(out))

    def memset(self, ap, val, w, eng="dve"):
        self.P.add(eng, lambda h: h.memset(ap, val), [], w, dur=self._dd(ap))

    def bcast_rows(self, src_row, key_src, dsts):
        from concourse.ap import AP
        i = self.bsi % 4
        self.bsi += 1
        n = src_row.free_size()
        self.dma("sp", self.bsc[i:i + 1, 0:n], src_row, key_w="bsc%d" % i, key_r=key_src)
        for (dst, kdst, c0, cn) in dsts:
            row = self.bsc[i:i + 1, c0:c0 + cn]
            src = AP(row.tensor, row.offset, [[0, 64], [1, cn]])
            self.dma("sp", dst, src, key_w=kdst, key_r="bsc%d" % i)

    def ldw(self, src):
        i = self.wri % self.NWR
        self.wri += 1
        self.dma("pool", self.wr[i], src, key_w="wr%d" % i, chan=self.wrc[i])
        return self.wr[i], "wr%d" % i

    def win_cols(self, l, c0, n):
        return self.w_in[l, :, c0:c0 + n].rearrange("(k p) n -> p k n", p=128)

    def rstd_from(self, ssq_ps, pkey, rows, inv_n, sd, rs):
        self.act(sd[0:rows], ssq_ps[0:rows], AF.Sqrt, ["const"], ["sd", pkey], scale=inv_n, bias=self.epsc[0:rows, 0:1])
        self.recip(rs[0:rows], sd[0:rows], ["sd"], ["rs"])

    def build(self):
        P = self.P
        self.dma("pool", self.ident, self.ident_d, key_w="const", chan=self.cst)
        self.dma("pool", self.ind2, self.ind2_d, key_w="const", chan=self.cst)
        self.dma("pool", self.emask, self.emask_d.rearrange("p (a b) -> p a b", a=25), key_w="const", chan=self.cst)
        self.memset(self.ones, 1.0, ["const"])
        self.memset(self.onesf, 1.0, ["const"])
        self.memset(self.epsc, EPS, ["const"])
        for l in range(self.nlayers):
            src = self.x if l == 0 else self.scr
            dst = self.out if l == self.nlayers - 1 else self.scr
            self.dma("sp", self.colv, self.colv_d[l], key_w="colv")
            for seq in range(self.nseq):
                self.A.top = self.stage_base
                self.stage0(l, seq, src)
                if self.dbg and l == 0 and seq == 0:
                    self.dma("sp", self.dbg_h.rearrange("p (a b) -> p a b", a=8), self.hT, key_r="hT", chan=self.cout)
                self.A.top = self.stage_base
                self.stage_mla(l, seq)
                self.A.top = self.stage_base
                self.stage_dil(l, seq)
                self.A.top = self.stage_base
                self.stage_conv(l, seq)
                if self.dbg and l == 0 and seq == 0:
                    self.P.add("sp", lambda h: h.dma_start(out=self.dbg_y.rearrange("p (a b) -> p a b", a=12), in_=self.yT), reads=["Y%d_%d" % (c_, t_) for c_ in range(12) for t_ in range(4)], chan=self.cout)
                self.A.top = self.stage_base
                self.stage_out(l, seq, src, dst, final=(l == self.nlayers - 1))
        P.emit(final_chans=[self.cout] + ([self.cmisc["scrw"]] if "scrw" in self.cmisc else []))

    def stage0(self, l, seq, src):
        A = self.A
        NBUF = 4
        xb = [A.alloc([D], F32, key="xb%d" % i) for i in range(NBUF)]
        hn = [A.alloc([D], BF16, key="hn%d" % i) for i in range(NBUF)]
        junk = A.alloc([D], BF16, key="junk")
        gbc = A.alloc([D], F32, key="s0gbc")
        r_ss = self.ring(NBUF, [1], F32, "ss")
        r_sd = self.ring(NBUF, [1], F32, "sd1")
        r_rs = self.ring(NBUF, [1], F32, "rs1")
        self.dma("sp", gbc, self.gbc_d[l], key_w="s0gbc")
        for b in range(NB):
            i = b % NBUF
            r0 = seq * S + b * 128
            ss, kss = r_ss()
            sd1, ksd = r_sd()
            rs1, krs = r_rs()
            self.dma("sp", xb[i], src[r0:r0 + 128, :], key_w="xb%d" % i, key_r=("scr" if l > 0 else None))
            self.memset(ss, 0.0, [kss])
            self.act(junk, xb[i], AF.Square, ["xb%d" % i], ["junk", kss], accum=ss)
            self.act(sd1, ss, AF.Ln, [kss, "const"], [ksd], scale=1.0 / D, bias=self.epsc[:, 0:1])
            self.act(rs1, sd1, AF.Exp, [ksd], [krs], scale=-0.5)
            self.stt(hn[i], xb[i], rs1[:, 0:1], gbc, ALU.mult, ALU.mult, ["xb%d" % i, krs, "s0gbc"], ["hn%d" % i])
            pb = b % 8
            pst = self.ps[pb].bitcast(BF16)
            for k in range(8):
                self.tr(pst[:, k * 128:(k + 1) * 128], hn[i][:, k * 128:(k + 1) * 128], ["hn%d" % i], ["ps%d" % pb])
            if b % 2 == 0:
                self.act(self.hT[:, :, b * 128:(b + 1) * 128], pst.rearrange("p (k n) -> p k n", k=8), AF.Copy,
                         [], ["ps%d" % pb, "hT"])
            else:
                self.cpy(self.hT[:, :, b * 128:(b + 1) * 128], pst.rearrange("p (k n) -> p k n", k=8), [], ["ps%d" % pb, "hT"])

    def ring(self, n, shape, dtype, name):
        aps = [self.A.alloc(shape, dtype, key="%s%d" % (name, i)) for i in range(n)]
        st = {"i": 0}

        def nxt():
            i = st["i"] % n
            st["i"] += 1
            return aps[i], "%s%d" % (name, i)
        return nxt

    def psring(self, banks):
        st = {"i": 0}

        def nxt():
            b = banks[st["i"] % len(banks)]
            st["i"] += 1
            return self.ps[b], "ps%d" % b
        return nxt

    def stage_mla(self, l, seq):
        A = self.A
        cv = self.colv
        Kc = self.yT[:, 4:12, :]
        KK = ["yT_c", "yT_a"]
        Va = A.alloc([NB, 8, 65], BF16, key="Va")
        for t_ in range(NT):
            self.P.keymap["Va%d" % t_] = A.pages(A.last[0] + t_ * 4 * 8 * 65, 4 * 8 * 65)
        qT = A.alloc([8, TT], BF16, key=["qT%d" % h_ for h_ in range(8)])
        cqn = A.alloc([2, TT], BF16, key="cqn")
        ckvn = A.alloc([TT], BF16, key="ckvn")
        sqpe = A.alloc([TT], BF16, key="sqpe")
        kr = A.alloc([TT], F32, key="kr")
        ctt = A.alloc([TT], F32, key="ctt")
        stt_ = A.alloc([TT], F32, key="stt")
        cqf = A.alloc([TT], F32, key="cqf")
        sqf = A.alloc([TT], F32, key="sqf")
        ckf = A.alloc([TT], F32, key="ckf")
        skf = A.alloc([TT], F32, key="skf")
        sbz = A.alloc([4, TT], BF16, key="sbz")
        r_sq = self.ring(2, [TT], BF16, "sq")
        r_sd = self.ring(2, [TT], F32, "sd")
        r_rs = self.ring(2, [TT], F32, "rs")
        r_ta = self.ring(2, [TT], F32, "ta")
        r_tb = self.ring(2, [TT], F32, "tb")
        r_pt = self.ring(3, [TT], BF16, "PT")
        r_osb = self.ring(2, [TT], F32, "osb")
        r_rb = self.ring(2, [TT], F32, "rb")
        Wlat = A.alloc([8, 384], BF16, key="Wlat")
        Wkpe = A.alloc([8, 96], BF16, key="Wkpe")
        Wkpes = A.alloc([8, 96], BF16, key="Wkpes")
        Wuq = A.alloc([2, 768], BF16, key="Wuq")
        Wuqs = A.alloc([2, 768], BF16, key="Wuqs")
        Wk = A.alloc([512], BF16, key="Wk")
        Wv = A.alloc([512], BF16, key="Wv")
        cfg = os.environ.get("KMLA", "a")
        if cfg == "a":
            pP = self.psring([0, 1, 2])
            pSS = self.psring([3, 4])
            pS_ = self.psring([6, 7])
            pO_ = self.psring([5])
        elif cfg == "b":
            pP = self.psring([0, 1])
            pSS = self.psring([2, 3])
            pS_ = self.psring([4, 6, 7])
            pO_ = self.psring([5])
        else:
            pP = self.psring([0, 1])
            pSS = self.psring([2])
            pS_ = self.psring([3, 6, 7])
            pO_ = self.psring([4, 5])
        hT = self.hT
        self.dma("pool", Wlat, self.win_cols(l, C_CQ, 384), key_w="Wlat")
        self.dma("pool", Wkpe, self.w_kpe[l].rearrange("(k p) n -> p k n", p=128), key_w="Wkpe")
        self.dma("pool", Wkpes, self.w_kpe_sw[l].rearrange("(k p) n -> p k n", p=128), key_w="Wkpes")
        self.dma("pool", Wuq, self.w_uq[l].rearrange("(k p) n -> p k n", p=128), key_w="Wuq")
        self.dma("pool", Wuqs, self.w_uq_sw[l].rearrange("(k p) n -> p k n", p=128), key_w="Wuqs")
        self.dma("pool", Wk, self.w_ukv_k[l], key_w="Wk")
        self.dma("pool", Wv, self.w_ukv_v[l], key_w="Wv")
        self.memset(Va[:, :, :, 64:65], 1.0, ["Va"])
        sc_mla = 96.0 ** -0.5
        R = slice(64, 96)

        def rstd(ssq, kss, rows, inv_n):
            sd, ksd = r_sd()
            rs, krs = r_rs()
            self.act(sd[0:rows], ssq[0:rows], AF.Ln, ["const"], [ksd, kss], scale=inv_n, bias=self.epsc[0:rows, 0:1])
            self.act(rs[0:rows], sd[0:rows], AF.Exp, [ksd], [krs], scale=-0.5)
            return rs, krs

        for t in range(NT):
            T0 = t * TT
            hs = lambda k: hT[:, k, T0:T0 + TT]
            wbz = [self.ldw(self.win_cols(l, C_BZ + c * 128, 128)) for c in range(4)]
            self.dma("sp", ctt[0:96], self.ct_d[:, T0:T0 + TT], key_w="ctt")
            self.dma("sp", stt_[0:96], self.st_d[:, T0:T0 + TT], key_w="stt")
            self.ts(cqf[0:96], ctt[0:96], cv[0:96, CV_GQM:CV_GQM + 1], None, ALU.mult, None, ["ctt", "colv"], ["cqf"])
            self.ts(sqf[0:96], stt_[0:96], cv[0:96, CV_GQMS:CV_GQMS + 1], None, ALU.mult, None, ["stt", "colv"], ["sqf"])
            self.ts(ckf[0:96], ctt[0:96], cv[0:96, CV_GKM:CV_GKM + 1], None, ALU.mult, None, ["ctt", "colv"], ["ckf"])
            self.ts(skf[0:96], stt_[0:96], cv[0:96, CV_GKMS:CV_GKMS + 1], None, ALU.mult, None, ["stt", "colv"], ["skf"])
            pcq = [pP(), pP()]
            sqs = []
            for j in range(2):
                p_, k_ = pcq[j]
                for k in range(8):
                    self.mm(p_, Wlat[:, k, j * 128:(j + 1) * 128], hs(k), k == 0, k == 7, ["Wlat", "hT"], [k_])
                sq, ksq = r_sq()
                self.act(sq, p_, AF.Square, [], [ksq, k_])
                sqs.append((sq, ksq))
            ss, kss = pSS()
            self.mm(ss, self.ones, sqs[0][0], True, False, ["const", sqs[0][1]], [kss])
            self.mm(ss, self.ones, sqs[1][0], False, True, ["const", sqs[1][1]], [kss])
            rs, krs = rstd(ss, kss, 128, 1.0 / 256)
            for j in range(2):
                p_, k_ = pcq[j]
                self.stt(cqn[:, j, :], p_, cv[:, CV_GQ + j:CV_GQ + j + 1], rs, ALU.mult, ALU.mult, ["colv", krs], ["cqn", k_])
            p_, k_ = pP()
            for k in range(8):
                self.mm(p_, Wlat[:, k, 256:384], hs(k), k == 0, k == 7, ["Wlat", "hT"], [k_])
            sq, ksq = r_sq()
            self.act(sq, p_, AF.Square, [], [ksq, k_])
            ss, kss = pSS()
            self.mm(ss, self.ones, sq, True, True, ["const", ksq], [kss])
            rs, krs = rstd(ss, kss, 128, 1.0 / 128)
            self.stt(ckvn, p_, cv[:, CV_GKV:CV_GKV + 1], rs, ALU.mult, ALU.mult, ["colv", krs], ["ckvn", k_])
            p3, k3 = pP()
            for k in range(8):
                self.mm(p3[0:96], Wkpe[:, k, :], hs(k), k == 0, k == 7, ["Wkpe", "hT"], [k3])
            self.act(sqpe[R], p3[R], AF.Square, [], ["sqpe", k3])
            self.tt(kr[R], p3[R], ckf[R], ALU.mult, ["ckf"], ["kr", k3])
            p4, k4 = pP()
            for k in range(8):
                self.mm(p4[0:96], Wkpes[:, k, :], hs(k), k == 0, k == 7, ["Wkpes", "hT"], [k4])
            tb, ktb = r_tb()
            self.tt(tb[R], p4[R], skf[R], ALU.mult, ["skf"], [ktb, k4])
            self.tt(kr[R], kr[R], tb[R], ALU.add, [ktb, "kr"], ["kr"], eng="pool")
            for bb in range(4):
                blk = t * 4 + bb
                pv, pk = pP()
                self.mm(pv, ckvn[:, bb * 128:(bb + 1) * 128], Wv, True, True, ["ckvn", "Wv"], [pk])
                self.cpy(Va[:, blk, :, 0:64], pv.rearrange("p (h c) -> p h c", h=8), ["Va"], ["Va%d" % t, pk])
            for h in range(8):
                pK, pk = pP()
                self.mm(pK[0:64], Wk[:, h * 64:(h + 1) * 64], ckvn, True, True, ["Wk", "ckvn"], [pk])
                sq, ksq = r_sq()
                self.act(sq[0:64], pK[0:64], AF.Square, [], [ksq, pk])
                ss, kss = pSS()
                self.mm(ss[0:96], self.ones[0:64, 0:96], sq[0:64], True, False, ["const", ksq], [kss])
                self.mm(ss[0:96], self.ones[64:96, 0:96], sqpe[R], False, True, ["const", "sqpe"], [kss])
                rs, krs = rstd(ss, kss, 96, 1.0 / 96)
                self.stt(Kc[0:64, h, T0:T0 + TT], pK[0:64], cv[0:64, CV_GKM:CV_GKM + 1], rs[0:64], ALU.mult, ALU.mult,
                         ["colv", krs], ["Y%d_%d" % (4 + h, t), pk])
                self.tt(Kc[R, h, T0:T0 + TT], kr[R], rs[R], ALU.mult, ["kr", krs], ["Y%d_%d" % (4 + h, t)])
            for c in range(4):
                wt, wk = wbz[c]
                pb, pk = pP()
                for k in range(8):
                    self.mm(pb, wt[:, k, :], hs(k), k == 0, k == 7, [wk, "hT"], [pk])
                self.act(sbz[:, c, :], pb, AF.Silu, [], ["sbz", pk])
            for h in range(8):
                pQ, kq = pP()
                for j in range(2):
                    self.mm(pQ[0:96], Wuq[:, j, h * 96:(h + 1) * 96], cqn[:, j, :], j == 0, j == 1, ["Wuq", "cqn"], [kq])
                pQs, kqs = pP()
                for j in range(2):
                    self.mm(pQs[0:96], Wuqs[:, j, h * 96:(h + 1) * 96], cqn[:, j, :], j == 0, j == 1, ["Wuqs", "cqn"], [kqs])
                sq, ksq = r_sq()
                self.act(sq[0:96], pQ[0:96], AF.Square, [], [ksq, kq])
                ss, kss = pSS()
                self.mm(ss[0:96], self.ones[0:96, 0:96], sq[0:96], True, True, ["const", ksq], [kss])
                rs, krs = rstd(ss, kss, 96, 1.0 / 96)
                ta, kta = r_ta()
                tb, ktb = r_tb()
                self.tt(ta[0:96], pQ[0:96], cqf[0:96], ALU.mult, ["cqf"], [kta, kq])
                self.tt(tb[R], pQs[R], sqf[R], ALU.mult, ["sqf"], [ktb, kqs])
                self.tt(ta[R], ta[R], tb[R], ALU.add, [ktb, kta], [kta], eng="pool")
                self.tt(qT[0:96, h, :], ta[0:96], rs[0:96], ALU.mult, [kta, krs], ["qT%d" % h])
            for h in range(8):
                nkb = 4 * t + 4
                pO, ko = pO_()
                for kb in range(nkb):
                    c0 = 0 if kb < 4 * t else (kb - 4 * t) * 128
                    pS, ks = pS_()
                    pt, kpt = r_pt()
                    self.mm(pS[:, c0:TT], Kc[0:96, h, kb * 128:(kb + 1) * 128], qT[0:96, h, c0:TT], True, True,
                            ["Y%d_%d" % (4 + h, kb // 4), "qT%d" % h], [ks])
                    self.act(pt[:, c0:TT], pS[:, c0:TT], AF.Exp, [], [kpt, ks], scale=sc_mla)
                    if kb >= 4 * t:
                        self.tt(pt[:, c0:c0 + 128], pt[:, c0:c0 + 128], self.emask[:, 24, 0:128], ALU.mult, ["const", kpt], [kpt], eng="pool")
                    self.mm(pO[0:65, c0:TT], Va[:, kb, h, :], pt[:, c0:TT], kb == 0, kb == nkb - 1, ["Va", "Va%d" % (kb // 4), kpt], [ko])
                osb, kos = r_osb()
                self.cpy(osb[0:65], pO[0:65], [], [kos, ko])
                self.act(osb[64:65], osb[64:65], AF.Ln, [kos], [kos])
                self.act(osb[64:65], osb[64:65], AF.Exp, [kos], [kos], scale=-1.0)
                rb, krb = r_rb()
                self.bcast_rows(osb[64:65, :], kos, [(rb[0:64], krb, 0, TT)])
                ta, kta = r_ta()
                Rh = slice(0, 64) if h % 2 == 0 else slice(64, 128)
                self.tt(ta[Rh], osb[0:64], rb[0:64], ALU.mult, [kos, krb], [kta])
                self.tt(self.yT[Rh, h // 2, T0:T0 + TT], ta[Rh], sbz[Rh, h // 2, :], ALU.mult, [kta, "sbz"], ["Y%d_%d" % (h // 2, t)])

    def stage_dil(self, l, seq):
        A = self.A
        cv = self.colv
        hT = self.hT
        r_dq = self.ring(2, [S], BF16, "dqT")
        r_dk = self.ring(2, [S], BF16, "dkT")
        r_dv = self.ring(2, [S], BF16, "dvT")
        r_dva = self.ring(2, [NB, 2, 65], BF16, "dva")
        r_acc = self.ring(2, [2, S], F32, "acc")
        r_scz = self.ring(2, [S], BF16, "scz")
        r_sq = self.ring(2, [TT], BF16, "sq")
        r_sd = self.ring(2, [TT], F32, "sd")
        r_rs = self.ring(2, [TT], F32, "rs")
        r_ta = self.ring(2, [TT], F32, "ta")
        r_pt = self.ring(4, [256], BF16, "dPT")
        r_rb = self.ring(4, [TT], F32, "rb")
        cfg = os.environ.get("KDIL", "a")
        if cfg == "a":
            pP = self.psring([2, 3, 4, 5])
            pS_ = self.psring([6, 7])
            pO_ = self.psring([0, 1])
        else:
            pP = self.psring([3, 4, 5])
            pS_ = self.psring([2, 6, 7])
            pO_ = self.psring([0, 1])
        for _ in range(2):
            dva, kdva = r_dva()
            self.memset(dva[:, :, :, 64:65], 1.0, [kdva])

        def rstd(ssq, kss, rows, inv_n):
            sd, ksd = r_sd()
            rs, krs = r_rs()
            self.act(sd[0:rows], ssq[0:rows], AF.Ln, ["const"], [ksd, kss], scale=inv_n, bias=self.epsc[0:rows, 0:1])
            self.act(rs[0:rows], sd[0:rows], AF.Exp, [ksd], [krs], scale=-0.5)
            return rs, krs

        def perm_out(buf, g, t):
            if g == 0:
                return buf[:, t * TT:(t + 1) * TT], None
            if g == 1:
                return buf.rearrange("p (r m i) -> p r m i", r=4, m=4)[:, :, t, :], ("p (i r) -> p r i", 4)
            return buf.rearrange("p (r i) -> p r i", r=16)[:, :, 32 * t:32 * t + 32], ("p (i r) -> p r i", 16)

        def segments(g):
            segs = []
            for s_ in range(4):
                u = []
                if g == 0:
                    if s_ > 0:
                        u.append(((4 * s_ - 1) * 128, 4 * s_ - 1, 4 * s_ * 128, 128, 128))
                    for kb in range(4 * s_, 4 * s_ + 4):
                        u.append((kb * 128, kb, kb * 128, 256 if kb < 4 * s_ + 3 else 128, 0))
                elif g == 1:
                    for m in range(4):
                        blk = 4 * s_ + m
                        u.append((blk * 128, blk, blk * 128, 256 if m < 3 else 128, 0))
                else:
                    for c in range(4):
                        blk = 4 * s_ + c
                        u.append((blk * 128, blk, blk * 128, 128, 0))
                segs.append(u)
            return segs

        for j in range(4):
            scz, kscz = r_scz()
            acc, kacc = r_acc()
            wcz, kcz = self.ldw(self.win_cols(l, C_CZ + j * 128, 128))
            for t in range(NT):
                pc, kpc = pP()
                for k in range(8):
                    self.mm(pc, wcz[:, k, :], hT[:, k, t * TT:(t + 1) * TT], k == 0, k == 7, [kcz, "hT"], [kpc])
                self.act(scz[:, t * TT:(t + 1) * TT], pc, AF.Silu, [], [kscz, kpc])
            for g in range(3):
                c_off = g * 512 + j * 128
                wq, kwq = self.ldw(self.win_cols(l, C_DQ + c_off, 128))
                wk_, kwk = self.ldw(self.win_cols(l, C_DK + c_off, 128))
                wv, kwv = self.ldw(self.win_cols(l, C_DV + c_off, 128))
                dqT, kdq = r_dq()
                dkT, kdk = r_dk()
                dvT, kdv = r_dv()
                dva, kdva = r_dva()
                for t in range(NT):
                    hs = lambda k: hT[:, k, t * TT:(t + 1) * TT]
                    for (wt, wkey, dst, dkey, gcol) in ((wq, kwq, dqT, kdq, CV_GDQ + g), (wk_, kwk, dkT, kdk, CV_GDK + g)):
                        pp, pk = pP()
                        for k in range(8):
                            self.mm(pp, wt[:, k, :], hs(k), k == 0, k == 7, [wkey, "hT"], [pk])
                        sq, ksq = r_sq()
                        self.act(sq, pp, AF.Square, [], [ksq, pk])
                        ss, kss = pP()
                        self.mm(ss, self.ind2, sq, True, True, ["const", ksq], [kss])
                        rs, krs = rstd(ss, kss, 128, 1.0 / 64)
                        ov, pr = perm_out(dst, g, t)
                        if pr is None:
                            self.stt(ov, pp, cv[:, gcol:gcol + 1], rs, ALU.mult, ALU.mult, ["colv", krs], [dkey, pk])
                        else:
                            self.stt(ov, pp.rearrange(pr[0], r=pr[1]), cv[:, gcol:gcol + 1], rs.rearrange(pr[0], r=pr[1]),
                                     ALU.mult, ALU.mult, ["colv", krs], [dkey, pk])
                    pp, pk = pP()
                    for k in range(8):
                        self.mm(pp, wv[:, k, :], hs(k), k == 0, k == 7, [kwv, "hT"], [pk])
                    ov, pr = perm_out(dvT, g, t)
                    if pr is None:
                        self.cpy(ov, pp, [], [kdv, pk])
                    else:
                        self.cpy(ov, pp.rearrange(pr[0], r=pr[1]), [], [kdv, pk])
                for half in range(2):
                    pp, pk = pP()
                    pst = pp.bitcast(BF16)
                    for bb in range(8):
                        blk = half * 8 + bb
                        self.tr(pst[:, bb * 128:(bb + 1) * 128], dvT[:, blk * 128:(blk + 1) * 128], [kdv], [pk])
                    self.act(dva[:, half * 8:(half + 1) * 8, :, 0:64],
                             pst.rearrange("p (b h c) -> p b h c", b=8, h=2), AF.Copy, [], [kdva, pk])
                segs = segments(g)
                for hh in range(2):
                    Rr = slice(hh * 64, hh * 64 + 64)
                    em = self.emask[:, g * 8 + 2 * j + hh, :]
                    for b, units in enumerate(segs):
                        pO, ko = pO_()
                        first = True
                        for (K0, blk, q0, nq, m0) in units:
                            pS, ks = pS_()
                            pt, kpt = r_pt()
                            self.mm(pS[:, 0:nq], dkT[Rr, K0:K0 + 128], dqT[Rr, q0:q0 + nq], True, True, [kdk, kdq], [ks])
                            self.act(pt[:, 0:nq], pS[:, 0:nq], AF.Exp, [], [kpt, ks], scale=0.125)
                            self.tt(pt[:, 0:nq], pt[:, 0:nq], em[:, m0:m0 + nq], ALU.mult, ["const", kpt], [kpt], eng=("pool" if g == 2 else "dve"))
                            self.mm(pO[0:65, q0 - b * 512:q0 - b * 512 + nq], dva[:, blk, hh, :], pt[:, 0:nq], first, False,
                                    [kdva, kpt], [ko])
                            first = False
                        if g == 0:
                            self.act(acc[0:65, hh, b * 512:(b + 1) * 512], pO[0:65], AF.Copy, [], [kacc, ko])
                        elif g == 1:
                            av = acc[0:65, hh, :].rearrange("p (m i r) -> p r m i", m=4, r=4)[:, b]
                            self.tt(av, av, pO[0:65].rearrange("p (m i) -> p m i", m=4), ALU.add, [kacc], [kacc, ko])
                        else:
                            av = acc[0:65, hh, :].rearrange("p (i r) -> p r i", r=16)[:, 4 * b:4 * b + 4, :]
                            self.tt(av, av, pO[0:65].rearrange("p (r i) -> p r i", r=4), ALU.add, [kacc], [kacc, ko])
            for hh in range(2):
                Rh = slice(hh * 64, hh * 64 + 64)
                self.act(acc[64:65, hh, :], acc[64:65, hh, :], AF.Ln, [kacc], [kacc])
                self.act(acc[64:65, hh, :], acc[64:65, hh, :], AF.Exp, [kacc], [kacc], scale=-1.0)
                rbs = [r_rb() for _ in range(NT)]
                self.bcast_rows(acc[64:65, hh, :], kacc, [(rbs[t][0][0:64], rbs[t][1], t * TT, TT) for t in range(NT)])
                for t in range(NT):
                    cs = slice(t * TT, (t + 1) * TT)
                    rb, krb = rbs[t]
                    ta, kta = r_ta()
                    self.tt(ta[Rh], acc[0:64, hh, cs], rb[0:64], ALU.mult, [kacc, krb], [kta])
                    self.tt(self.yT[Rh, 4 + j, cs], ta[Rh], scz[Rh, cs], ALU.mult, [kta, kscz], ["Y%d_%d" % (4 + j, t)])

    def stage_conv(self, l, seq):
        A = self.A
        ps = self.ps
        cv = self.colv
        hT = self.hT
        r_up = self.ring(2, [S + 2], F32, "upad")
        r_acs = self.ring(2, [TT], F32, "acs")
        r_sz = self.ring(2, [TT], F32, "sz")
        r_c1 = self.ring(2, [TT], F32, "c1")
        r_c2 = self.ring(2, [TT], F32, "c2")
        r_t1 = self.ring(3, [TT], F32, "t1")
        n = 0
        for c in range(4):
            ws = [self.ldw(self.win_cols(l, base + c * 128, 128)) for base in (C_AB, C_AC, C_AX, C_AZ)]
            upad, kup = r_up()
            self.memset(upad[:, 0:2], 0.0, [kup])
            for t in range(NT):
                T0 = t * TT
                o = 4 * (n % 2)
                n += 1
                for i in range(4):
                    wt, wk = ws[i]
                    for k in range(8):
                        self.mm(ps[o + i], wt[:, k, :], hT[:, k, T0:T0 + TT], k == 0, k == 7, [wk, "hT"], ["ps%d" % (o + i)])
                kb_, kc_, kx_, kz_ = ["ps%d" % (o + i) for i in range(4)]
                acs, kacs = r_acs()
                sz, ksz = r_sz()
                c1, kc1 = r_c1()
                c2, kc2 = r_c2()
                self.act(acs, ps[o + 1], AF.Copy, [], [kacs, kc_])
                self.tt(upad[:, 2 + T0:2 + T0 + TT], acs, ps[o + 2], ALU.mult, [kacs], [kup, kx_])
                self.act(sz, ps[o + 3], AF.Silu, [], [ksz, kz_])
                t1, kt1 = r_t1()
                self.act(c1, upad[:, T0:T0 + TT], AF.Identity, [kup, "colv"], [kc1], scale=cv[:, CV_CW + c:CV_CW + c + 1],
                         bias=cv[:, CV_CB + c:CV_CB + c + 1])
                self.act(t1, upad[:, T0 + 1:T0 + 1 + TT], AF.Identity, [kup, "colv"], [kt1], scale=cv[:, CV_CW + 4 + c:CV_CW + 5 + c])
                self.tt(c1, c1, t1, ALU.add, [kc1, kt1], [kc1], eng="pool")
                t1, kt1 = r_t1()
                self.act(t1, upad[:, T0 + 2:T0 + 2 + TT], AF.Identity, [kup, "colv"], [kt1], scale=cv[:, CV_CW + 8 + c:CV_CW + 9 + c])
                self.tt(c1, c1, t1, ALU.add, [kc1, kt1], [kc1], eng="pool")
                self.tt(c2, c1, ps[o + 0], ALU.mult, [kc1], [kc2, kb_])
                self.tt(self.yT[:, 8 + c, T0:T0 + TT], c2, sz, ALU.mult, [kc2, ksz], ["Y%d_%d" % (8 + c, t)])

    def stage_out(self, l, seq, src, dst, final):
        A = self.A
        ps = self.ps
        cv = self.colv
        hT = self.hT
        yT = self.yT
        Wo3 = [A.alloc([4, D], BF16, key="Wo3_%d" % i) for i in range(3)]
        Wo = A.alloc([8, D], BF16, key="Wo")
        G = A.alloc([24, TT], BF16)
        for jj in range(24):
            self.P.keymap["G%d" % jj] = A.pages(A.last[0] + jj * TT, TT)
        mT = A.alloc([8, TT], BF16)
        for jj in range(8):
            self.P.keymap["mT%d" % jj] = A.pages(A.last[0] + jj * TT, TT)
        r_m1 = self.ring(2, [TT], F32, "m1")
        r_m2 = self.ring(2, [TT], F32, "m2")
        xb = [A.alloc([D], F32, key="oxb%d" % i) for i in range(2)]
        ob = [A.alloc([D], F32, key="ob%d" % i) for i in range(2)]
        for i in range(3):
            self.dma("pool", Wo3[i], self.w_out[i][l].rearrange("(k p) n -> p k n", p=128), key_w="Wo3_%d" % i)
        self.dma("pool", Wo, self.w_o[l].rearrange("(k p) n -> p k n", p=128), key_w="Wo")
        och = self.cout if final else self.misc_chan("scrw")
        pG = self.psring([0, 1])
        for t in range(NT):
            T0 = t * TT
            DEPTHW = 4
            wq = [self.ldw(self.win_cols(l, C_GATE + jj * 128, 128)) for jj in range(DEPTHW)]
            for jj in range(24):
                wt, wk = wq[jj]
                pg, kg = pG()
                for k in range(8):
                    self.mm(pg, wt[:, k, :], hT[:, k, T0:T0 + TT], k == 0, k == 7, [wk, "hT"], [kg])
                self.act(G[:, jj, :], pg, AF.Sigmoid, ["colv"], ["G%d" % jj, kg], bias=cv[:, CV_BG + jj:CV_BG + jj + 1])
                if jj + DEPTHW < 24:
                    wq.append(self.ldw(self.win_cols(l, C_GATE + (jj + DEPTHW) * 128, 128)))
            for j in range(8):
                o = 2 + 3 * (j % 2)
                for i in range(3):
                    for k in range(4):
                        self.mm(ps[o + i], Wo3[i][:, k, j * 128:(j + 1) * 128], yT[:, (8, 0, 4)[i] + k, T0:T0 + TT], k == 0, k == 3,
                                ["Wo3_%d" % i, "Y%d_%d" % ((8, 0, 4)[i] + k, t)], ["ps%d" % (o + i)])
                m1, k1 = r_m1()
                m2, k2 = r_m2()
                self.tt(m1, ps[o], G[:, j, :], ALU.mult, ["G%d" % j], [k1, "ps%d" % o])
                self.tt(m2, ps[o + 1], G[:, 8 + j, :], ALU.mult, ["G%d" % (8 + j)], [k2, "ps%d" % (o + 1)])
                self.tt(m1, m1, m2, ALU.add, [k1, k2], [k1], eng="pool")
                self.tt(m2, ps[o + 2], G[:, 16 + j, :], ALU.mult, ["G%d" % (16 + j)], [k2, "ps%d" % (o + 2)])
                self.tt(mT[:, j, :], m1, m2, ALU.add, [k1, k2], ["mT%d" % j], eng="pool")
            for bb in range(4):
                b = t * 4 + bb
                i = b % 2
                r0 = seq * S + b * 128
                self.dma("sp", xb[i], src[r0:r0 + 128, :], key_w="oxb%d" % i, key_r=("scr" if l > 0 else None))
                for half in range(2):
                    pg, kg = pG()
                    for k in range(8):
                        self.mm(pg, mT[:, k, bb * 128:(bb + 1) * 128], Wo[:, k, half * 512:(half + 1) * 512], k == 0, k == 7,
                                ["mT%d" % k, "Wo"], [kg])
                    self.tt(ob[i][:, half * 512:(half + 1) * 512], pg, xb[i][:, half * 512:(half + 1) * 512], ALU.add,
                            ["oxb%d" % i], ["ob%d" % i, kg])
                self.dma("sp", dst[r0:r0 + 128, :], ob[i], key_r="ob%d" % i, key_w=(None if final else "scr"), chan=och)


def _rope_tables():
    inv = (10000.0 ** (-np.arange(0, 32, 2, dtype=np.float32) / 32)).astype(np.float32)
    ang = np.arange(S, dtype=np.float32)[:, None] * inv[None, :]
    cos, sin = np.cos(ang).astype(np.float32), np.sin(ang).astype(np.float32)
    ct = np.ones((96, S), np.float32)
    st = np.zeros((96, S), np.float32)
    ct[64:80] = cos.T
    ct[80:96] = cos.T
    st[64:80] = -sin.T
    st[80:96] = sin.T
    return ct, st


def _emask():
    n = 24
    slopes = (2.0 ** (-8.0 * np.arange(1, n + 1, dtype=np.float32) / n)).reshape(3, 8)
    k = np.arange(128)[:, None].astype(np.float32)
    q = np.arange(128)[None, :].astype(np.float32)
    em = np.zeros((128, 25, 256), np.float32)
    for g in range(3):
        for h in range(8):
            sl = slopes[g, h] * DIL[g]
            em[:, g * 8 + h, 0:128] = np.where(k <= q, np.exp(-sl * np.maximum(q - k, 0.0)), 0.0)
            if g < 2:
                em[:, g * 8 + h, 128:256] = np.where(k >= q, np.exp(-sl * np.maximum(128.0 + q - k, 0.0)), 0.0)
    em[:, 24, 0:128] = (k <= q).astype(np.float32)
    return em.reshape(128, 25 * 256)


def _host_layout(inp, layers):
    f = lambda a: np.ascontiguousarray(a, dtype=np.float32)
    L = len(layers)
    w_uq = f(inp["w_uq"][layers])
    w_uq_sw = np.zeros_like(w_uq)
    for h in range(8):
        b = h * 96
        w_uq_sw[:, :, b + 64:b + 80] = w_uq[:, :, b + 80:b + 96]
        w_uq_sw[:, :, b + 80:b + 96] = w_uq[:, :, b + 64:b + 80]
    w_ukv = f(inp["w_ukv"][layers]).reshape(L, 128, 8, 128)
    w_ukv_k = f(w_ukv[:, :, :, 0:64].reshape(L, 128, 512))
    w_ukv_v = f(w_ukv[:, :, :, 64:128].reshape(L, 128, 512))
    w_in = inp["w_in"]
    w_kpe = np.zeros((L, D, 96), np.float32)
    w_kpe_sw = np.zeros((L, D, 96), np.float32)
    for i, l in enumerate(layers):
        kp = w_in[l][:, C_KPE:C_KPE + 32]
        w_kpe[i, :, 64:96] = kp
        w_kpe_sw[i, :, 64:80] = kp[:, 16:32]
        w_kpe_sw[i, :, 80:96] = kp[:, 0:16]
    colv = np.zeros((L, 128, NCV), np.float32)
    gbc = np.zeros((L, 128, D), np.float32)
    for i, l in enumerate(layers):
        colv[i, :, CV_BG:CV_BG + 24] = inp["b_gate"][l].reshape(24, 128).T
        for tap in range(3):
            colv[i, :, CV_CW + tap * 4:CV_CW + tap * 4 + 4] = inp["conv_w"][l][tap].reshape(4, 128).T
        colv[i, :, CV_CB:CV_CB + 4] = inp["conv_b"][l].reshape(4, 128).T
        colv[i, :, CV_GQ:CV_GQ + 2] = inp["q_a_norm_g"][l].reshape(2, 128).T
        colv[i, :, CV_GKV] = inp["kv_a_norm_g"][l]
        for (col, src) in ((CV_GQM, inp["mla_q_norm_g"][l]), (CV_GKM, inp["mla_k_norm_g"][l])):
            colv[i, 0:96, col] = src
            colv[i, 64:80, col + 1] = src[80:96]
            colv[i, 80:96, col + 1] = src[64:80]
        for g in range(3):
            colv[i, :, CV_GDQ + g] = np.tile(inp["dil_q_norm_g"][l][g], 2)
            colv[i, :, CV_GDK + g] = np.tile(inp["dil_k_norm_g"][l][g], 2)
        gbc[i] = np.broadcast_to(inp["norm_g"][l][None, :], (128, D))
    ind2 = np.zeros((128, 128), np.float32)
    ind2[0:64, 0:64] = 1.0
    ind2[64:128, 64:128] = 1.0
    ct, st = _rope_tables()
    shared = {
        "w_in": f(w_in[layers]), "w_uq": w_uq, "w_uq_sw": w_uq_sw, "w_ukv_k": w_ukv_k, "w_ukv_v": w_ukv_v,
        "w_kpe": w_kpe, "w_kpe_sw": w_kpe_sw,
        "w_out_a": f(inp["w_out_a"][layers]), "w_out_b": f(inp["w_out_b"][layers]), "w_out_c": f(inp["w_out_c"][layers]),
        "w_o": f(inp["w_o"][layers]), "colv": colv, "gbc": gbc,
        "ident": np.eye(128, dtype=np.float32), "ind2": ind2, "emask": _emask(), "ctab": ct, "stab": st,
    }
    return shared


_CACHE = {}


def _get_builder(nseq, nlayers, dbg=False):
    key = (nseq, nlayers, dbg)
    if key not in _CACHE:
        _CACHE[key] = Builder(nseq, nlayers, dbg=dbg)
    return _CACHE[key]


def kernel(**inputs):
    inp = {k: np.asarray(v) for k, v in inputs.items()}
    x = np.ascontiguousarray(inp["x"], dtype=np.float32)
    B = x.shape[0]
    ncores = 8
    nseq = B // ncores
    shared = _host_layout(inp, list(range(DEPTH)))
    bld = Builder(nseq, DEPTH)
    in_maps = []
    for c in range(ncores):
        m = dict(shared)
        m["x"] = x[c * nseq:(c + 1) * nseq].reshape(nseq * S, D)
        in_maps.append(m)
    res = run_bass_kernel_spmd(bld.nc, in_maps, core_ids=list(range(ncores)))
    outs = [np.asarray(r["out"]).reshape(nseq, S, D) for r in res.results]
    return np.concatenate(outs, axis=0).astype(np.float32)
```

```python
import numpy as np
import concourse.bass as bass
import concourse.mybir as mybir
from concourse.bass_utils import run_bass_kernel_spmd

F32 = mybir.dt.float32
BF16 = mybir.dt.bfloat16
AF = mybir.ActivationFunctionType
ALU = mybir.AluOpType

ENGS = ("pe", "act", "dve", "pool", "sp")

D = 1024
S = 2048
DEPTH = 2
NB = 16
NT = 4
TT = 512
EPS = 1e-6
N_IN = 11168
C_AB, C_AC, C_AX, C_AZ = 0, 512, 1024, 1536
C_CQ, C_CKV, C_KPE, C_BZ = 2048, 2304, 2432, 2464
C_DQ, C_DK, C_DV, C_CZ, C_GATE = 2976, 4512, 6048, 7584, 8096
DIL = (1, 4, 16)
CV_BG, CV_CW, CV_CB, CV_GQ, CV_GKV, CV_GQM, CV_GQMS, CV_GKM, CV_GKMS, CV_GDQ, CV_GDK, NCV = 0, 24, 36, 40, 42, 43, 44, 45, 46, 47, 50, 53


import heapq
import os

RAW, WAR, WAW = 0, 1, 2
ACT_SET = {"Exp": "exp", "Ln": "exp", "Sqrt": "sqrt", "Silu": "silu", "Sigmoid": "sigmoid"}


class Chan:
    def __init__(self, sem, name):
        self.sem = sem
        self.name = name
        self.count = 0


class Op:
    __slots__ = ("eng", "fn", "idx", "pidx", "deps", "succ", "ndeps", "waits", "signal", "sigcount", "chan",
                 "dma_deps", "dur", "tset", "ready", "finish", "seg", "isbar")

    def __init__(self, eng, fn):
        self.eng = eng
        self.fn = fn
        self.idx = -1
        self.pidx = -1
        self.deps = []
        self.succ = []
        self.ndeps = 0
        self.waits = {}
        self.dma_deps = []
        self.signal = False
        self.sigcount = 0
        self.chan = None
        self.dur = 0.5
        self.tset = None
        self.ready = 0.0
        self.finish = 0.0
        self.seg = 0
        self.isbar = False


class Prog:
    def __init__(self, nc):
        self.nc = nc
        self.ops = []
        self.kw = {}
        self.kr = {}
        self.sems = {e: nc.alloc_semaphore("sem_" + e) for e in ENGS}
        self.nops = 0
        self.chans = []
        self.seg = 0
        self.sched = True
        self.prio = os.environ.get("KPRIO", "bl")
        self.strict = os.environ.get("KSTRICT", "1") == "1"
        self.keymap = {}

    def chan(self, name):
        c = Chan(self.nc.alloc_semaphore("dsem_" + name), name)
        self.chans.append(c)
        return c

    def barrier(self):
        self.seg += 1

    def add(self, eng, fn, reads=(), writes=(), chan=None, dur=0.5, tset=None):
        op = Op(eng, fn)
        op.pidx = len(self.ops)
        op.seg = self.seg
        op.chan = chan
        op.dur = dur
        op.tset = tset
        self.ops.append(op)
        self.nops += 1
        km = self.keymap
        if km:
            reads = list(reads) + [p for k in reads for p in km.get(k, ())]
            writes = list(writes) + [p for k in writes for p in km.get(k, ())]
        deps = {}
        for k in reads:
            w = self.kw.get(k)
            if w is not None:
                deps[w] = RAW
        for k in writes:
            w = self.kw.get(k)
            if w is not None and w not in deps:
                deps[w] = WAW
            for r in self.kr.get(k, ()):
                if r is not op and r not in deps:
                    deps[r] = WAR
        for k in reads:
            self.kr.setdefault(k, []).append(op)
        for k in writes:
            self.kw[k] = op
            self.kr[k] = []
        op.deps = [(d, kind) for d, kind in deps.items() if d.seg == op.seg]
        return op

    def _schedule(self, ops):
        if not self.sched:
            return list(ops)
        for op in ops:
            op.ndeps = len(op.deps)
            op.succ = []
            op.ready = 0.0
        for op in ops:
            for d, _ in op.deps:
                d.succ.append(op)
        LAT = float(os.environ.get("KLAT", "0.3"))
        if self.prio == "bl":
            for op in reversed(ops):
                b = 0.0
                for s_ in op.succ:
                    v = s_.finish + (LAT if s_.eng != op.eng else 0.0)
                    if v > b:
                        b = v
                op.finish = b + op.dur + (2.0 if op.chan is not None else 0.0)
            mx = max(op.finish for op in ops) if ops else 0.0
            for i_, op in enumerate(ops):
                op.pidx = int((mx - op.finish) * 1000) * 100000 + i_
        free = {e: 0.0 for e in ENGS}
        pending = {e: [] for e in ENGS}
        avail = {e: {} for e in ENGS}
        curset = [None]
        for op in ops:
            if op.ndeps == 0:
                heapq.heappush(pending[op.eng], (0.0, op.pidx, op))
        order = []
        n = len(ops)

        def cand(e):
            t = free[e]
            pend = pending[e]
            av = avail[e]
            while pend and pend[0][0] <= t:
                _, pi, o = heapq.heappop(pend)
                heapq.heappush(av.setdefault(o.tset if e == "act" else None, []), (pi, o))
            best = None
            if e == "act":
                for ts_ in (None, curset[0]):
                    h = av.get(ts_)
                    if h and (best is None or h[0][0] < best[0]):
                        best = (h[0][0], ts_)
                if best is None:
                    for ts_, h in av.items():
                        if h and (best is None or h[0][0] < best[0]):
                            best = (h[0][0], ts_)
            else:
                h = av.get(None)
                if h:
                    best = (h[0][0], None)
            if best is not None:
                return (t, best[0], best[1], False)
            if pend:
                return (pend[0][0], pend[0][1], None, True)
            return None

        while len(order) < n:
            bc = None
            be = None
            for e in ENGS:
                c = cand(e)
                if c is not None and (bc is None or (c[0], c[1]) < (bc[0], bc[1])):
                    bc, be = c, e
            assert bc is not None, "scheduler stuck"
            if bc[3]:
                _, pi, o = heapq.heappop(pending[be])
            else:
                pi, o = heapq.heappop(avail[be][bc[2]])
            start = max(bc[0], free[be])
            dur = o.dur
            if be == "act" and o.tset is not None and o.tset != curset[0]:
                dur += 2.7
                curset[0] = o.tset
            if o.chan is not None:
                free[be] = start + 0.15
                o.finish = start + 2.0 + dur
            else:
                free[be] = start + dur
                o.finish = start + dur
            order.append(o)
            for s_ in o.succ:
                s_.ndeps -= 1
                rt = o.finish + (LAT if s_.eng != o.eng else 0.05)
                if rt > s_.ready:
                    s_.ready = rt
                if s_.ndeps == 0:
                    heapq.heappush(pending[s_.eng], (s_.ready, s_.pidx, s_))
        return order

    def emit(self, final_chans=()):
        nc = self.nc
        nseg = self.seg + 1
        segs = [[] for _ in range(nseg)]
        for op in self.ops:
            segs[op.seg].append(op)
        glob = []
        for si in range(nseg):
            if si > 0:
                drains = []
                for e in ENGS:
                    if e == "sp":
                        continue
                    d = Op(e, lambda h: h.drain())
                    d.isbar = True
                    drains.append(d)
                    glob.append(d)
                for e in ENGS:
                    w = Op(e, lambda h: h.nop())
                    w.isbar = True
                    w.deps = [(d, RAW) for d in drains if d.eng != e]
                    w.dma_deps = "all"
                    glob.append(w)
            glob.extend(self._schedule(segs[si]))
        self.eng_ops = {e: [] for e in ENGS}
        for op in glob:
            op.idx = len(self.eng_ops[op.eng])
            self.eng_ops[op.eng].append(op)
        chan_cnt = {c: 0 for c in self.chans}
        dma_wait_vals = {}
        for op in glob:
            isdma = op.chan is not None
            dw = {}
            if op.dma_deps == "all":
                for c in self.chans:
                    if chan_cnt[c] > 0:
                        dw[c] = chan_cnt[c]
            for d, kind in op.deps:
                if d.chan is not None:
                    dw[d.chan] = chan_cnt[d.chan]
                    continue
                if d.eng == op.eng and not isdma and kind != RAW and not self.strict:
                    continue
                if d.eng == op.eng and d.eng == "pe":
                    continue
                cur = op.waits.get(d.eng)
                if cur is None or cur.idx < d.idx:
                    op.waits[d.eng] = d
            if isdma:
                chan_cnt[op.chan] += 1
            dma_wait_vals[op] = dw
        for op in glob:
            for p in op.waits.values():
                p.signal = True
        for e in ENGS:
            c = 0
            for op in self.eng_ops[e]:
                if op.signal:
                    c += 1
                op.sigcount = c
        final_cnt = {c: chan_cnt[c] for c in final_chans}

        def run_engine(e, h):
            seen = {}
            for op in self.eng_ops[e]:
                for pe_, p in op.waits.items():
                    need = p.sigcount
                    if seen.get(pe_, 0) < need:
                        h.wait_ge(self.sems[pe_], need)
                        seen[pe_] = need
                for c, cnt in dma_wait_vals[op].items():
                    if seen.get(c, 0) < cnt:
                        h.wait_ge(c.sem, 16 * cnt)
                        seen[c] = cnt
                ins = op.fn(h)
                if op.chan is not None:
                    ins.then_inc(op.chan.sem, 16)
                elif op.signal:
                    ins.then_inc(self.sems[e], 1)
            if e == "sp":
                for c, cnt in final_cnt.items():
                    h.wait_ge(c.sem, 16 * cnt)

        with nc.Block() as block:
            @block.tensor
            def _(h):
                run_engine("pe", h)

            @block.scalar
            def _(h):
                run_engine("act", h)

            @block.vector
            def _(h):
                run_engine("dve", h)

            @block.gpsimd
            def _(h):
                run_engine("pool", h)

            @block.sync
            def _(h):
                run_engine("sp", h)


class Arena:
    def __init__(self, nc, nbytes):
        self.cap = nbytes // 2
        self.t = nc.alloc_sbuf_tensor("arena", [128, self.cap], BF16).ap()
        self.top = 0
        self.paged = False
        self.keymap = {}

    def pages(self, off, ne):
        return ["pg%d" % i for i in range(off // 256, (off + ne - 1) // 256 + 1)]

    def alloc(self, free_shape, dtype, key=None):
        n = 1
        for v in free_shape:
            n *= v
        ne = n * (2 if dtype == F32 else 1)
        ne_al = (ne + 255) // 256 * 256 if self.paged else (ne + 31) // 32 * 32
        off = self.top
        self.top += ne_al
        assert self.top <= self.cap, ("arena overflow", self.top * 2, self.cap * 2)
        self.last = (off, ne)
        if key is not None and self.paged:
            for k in ([key] if isinstance(key, str) else key):
                self.keymap[k] = self.pages(off, ne)
        v = self.t[:, off:off + ne]
        if dtype == F32:
            v = v.bitcast(F32)
        if len(free_shape) > 1:
            names = " ".join("a%d" % i for i in range(len(free_shape)))
            kw = {"a%d" % i: free_shape[i] for i in range(len(free_shape))}
            v = v.rearrange("p (%s) -> p %s" % (names, names), **kw)
        return v


class Builder:
    def __init__(self, nseq, nlayers, layer0=0, dbg=False):
        self.nseq = nseq
        self.nlayers = nlayers
        self.dbg = dbg
        nc = bass.Bass("TRN2", target_bir_lowering=False)
        self.nc = nc
        P = Prog(nc)
        self.P = P
        L = nlayers
        dt = nc.dram_tensor
        self.x = dt("x", [nseq * S, D], F32, kind="ExternalInput").ap()
        self.out = dt("out", [nseq * S, D], F32, kind="ExternalOutput").ap()
        self.w_in = dt("w_in", [L, D, N_IN], F32, kind="ExternalInput").ap()
        self.w_uq = dt("w_uq", [L, 256, 768], F32, kind="ExternalInput").ap()
        self.w_uq_sw = dt("w_uq_sw", [L, 256, 768], F32, kind="ExternalInput").ap()
        self.w_ukv_k = dt("w_ukv_k", [L, 128, 512], F32, kind="ExternalInput").ap()
        self.w_ukv_v = dt("w_ukv_v", [L, 128, 512], F32, kind="ExternalInput").ap()
        self.w_kpe = dt("w_kpe", [L, D, 96], F32, kind="ExternalInput").ap()
        self.w_kpe_sw = dt("w_kpe_sw", [L, D, 96], F32, kind="ExternalInput").ap()
        self.w_out = [dt("w_out_%s" % n, [L, 512, D], F32, kind="ExternalInput").ap() for n in "abc"]
        self.w_o = dt("w_o", [L, D, D], F32, kind="ExternalInput").ap()
        self.colv_d = dt("colv", [L, 128, NCV], F32, kind="ExternalInput").ap()
        self.gbc_d = dt("gbc", [L, 128, D], F32, kind="ExternalInput").ap()
        self.ident_d = dt("ident", [128, 128], F32, kind="ExternalInput").ap()
        self.ind2_d = dt("ind2", [128, 128], F32, kind="ExternalInput").ap()
        self.emask_d = dt("emask", [128, 25 * 256], F32, kind="ExternalInput").ap()
        self.ct_d = dt("ctab", [96, S], F32, kind="ExternalInput").ap()
        self.st_d = dt("stab", [96, S], F32, kind="ExternalInput").ap()
        if nlayers > 1:
            self.scr = dt("scr", [nseq * S, D], F32).ap()
        self.bsc = dt("bsc", [4, S], F32).ap()
        self.bsi = 0
        if dbg:
            self.dbg_h = dt("dbg_h", [128, 8 * S], BF16, kind="ExternalOutput").ap()
            self.dbg_y = dt("dbg_y", [128, 12 * S], BF16, kind="ExternalOutput").ap()
        self.ps = [nc.alloc_psum_tensor("ps%d" % i, [128, 512], F32).ap() for i in range(8)]
        A = Arena(nc, 212480)
        self.A = A
        self.hT = A.alloc([8, S], BF16)
        self.yT = A.alloc([12, S], BF16)
        self.ident = A.alloc([128], BF16)
        self.ones = A.alloc([128], BF16)
        self.ind2 = A.alloc([128], BF16)
        self.onesf = A.alloc([64], F32)
        self.emask = A.alloc([25, 256], BF16)
        self.colv = A.alloc([NCV], F32)
        self.epsc = A.alloc([1], F32)
        self.NWR = 8
        self.wr = [A.alloc([8, 128], BF16) for _ in range(self.NWR)]
        self.wrc = [P.chan("wr%d" % i) for i in range(self.NWR)]
        self.wri = 0
        A.top = (A.top + 255) // 256 * 256
        self.stage_base = A.top
        A.paged = True
        A.keymap = P.keymap
        self.cst = P.chan("const")
        self.cout = P.chan("out")
        self.cmisc = {}
        self.build()

    def misc_chan(self, key):
        if key not in self.cmisc:
            self.cmisc[key] = self.P.chan("m%d" % len(self.cmisc))
        return self.cmisc[key]

    def dma(self, q, out, in_, key_w=None, key_r=None, chan=None):
        if chan is None:
            chan = self.misc_chan(key_w if key_w is not None else key_r)
        nbytes = out.free_size() * out.partition_size() * 4
        self.P.add(q, lambda h: h.dma_start(out=out, in_=in_),
                   reads=[key_r] if key_r else [], writes=[key_w] if key_w else [], chan=chan, dur=nbytes / 150e3)

    def mm(self, out, lhsT, rhs, start, stop, r, w):
        self.P.add("pe", lambda h: h.matmul(out, lhsT=lhsT, rhs=rhs, start=start, stop=stop, skip_group_check=True), r, w,
                   dur=(0.03 + out.free_size() * 0.00043) * (4.0 if lhsT.dtype == F32 else 1.0))

    def tr(self, out, in_, r, w):
        ident = self.ident
        self.P.add("pe", lambda h: h.transpose(out, in_, ident), list(r) + ["const"], w, dur=0.1)

    def act(self, out, in_, func, r, w, scale=None, bias=None, accum=None):
        kw = {}
        if scale is not None:
            kw["scale"] = scale
        if bias is not None:
            kw["bias"] = bias
        if accum is not None:
            kw["accum_out"] = accum
        self.P.add("act", lambda h: h.activation(out=out, in_=in_, func=func, **kw), r, w,
                   dur=0.2 + out.free_size() * 0.0009, tset=ACT_SET.get(func.name))

    def _dd(self, out, eng="dve"):
        if eng == "pool":
            return 0.25 + out.free_size() * (0.0021 if out.dtype == F32 else 0.0009)
        return 0.12 + out.free_size() * (0.0015 if out.dtype == F32 else 0.0009)

    def tt(self, out, in0, in1, op, r, w, eng="dve"):
        self.P.add(eng, lambda h: h.tensor_tensor(out=out, in0=in0, in1=in1, op=op), r, w, dur=self._dd(out, eng))

    def stt(self, out, in0, scalar, in1, op0, op1, r, w, eng="dve"):
        self.P.add(eng, lambda h: h.scalar_tensor_tensor(out=out, in0=in0, scalar=scalar, in1=in1, op0=op0, op1=op1), r, w,
                   dur=self._dd(out, eng))

    def ts(self, out, in0, s1, s2, op0, op1, r, w, eng="dve"):
        if s2 is None:
            self.P.add(eng, lambda h: h.tensor_scalar(out=out, in0=in0, scalar1=s1, scalar2=None, op0=op0), r, w,
                       dur=self._dd(out, eng))
        else:
            self.P.add(eng, lambda h: h.tensor_scalar(out=out, in0=in0, scalar1=s1, scalar2=s2, op0=op0, op1=op1), r, w,
                       dur=self._dd(out, eng))

    def recip(self, out, in_, r, w):
        self.P.add("dve", lambda h: h.reciprocal(out=out, in_=in_), r, w, dur=self._dd(out))

    def cpy(self, out, in_, r, w, eng="dve"):
        self.P.add(eng, lambda h: h.tensor_copy(out=out, in_=in_), r, w, dur=self._dd(out))

    def memset(self, ap, val, w, eng="dve"):
        self.P.add(eng, lambda h: h.memset(ap, val), [], w, dur=self._dd(ap))

    def bcast_rows(self, src_row, key_src, dsts):
        from concourse.ap import AP
        i = self.bsi % 4
        self.bsi += 1
        n = src_row.free_size()
        self.dma("sp", self.bsc[i:i + 1, 0:n], src_row, key_w="bsc%d" % i, key_r=key_src)
        for (dst, kdst, c0, cn) in dsts:
            row = self.bsc[i:i + 1, c0:c0 + cn]
            src = AP(row.tensor, row.offset, [[0, 64], [1, cn]])
            self.dma("sp", dst, src, key_w=kdst, key_r="bsc%d" % i)

    def ldw(self, src):
        i = self.wri % self.NWR
        self.wri += 1
        self.dma("pool", self.wr[i], src, key_w="wr%d" % i, chan=self.wrc[i])
        return self.wr[i], "wr%d" % i

    def win_cols(self, l, c0, n):
        return self.w_in[l, :, c0:c0 + n].rearrange("(k p) n -> p k n", p=128)

    def rstd_from(self, ssq_ps, pkey, rows, inv_n, sd, rs):
        self.act(sd[0:rows], ssq_ps[0:rows], AF.Sqrt, ["const"], ["sd", pkey], scale=inv_n, bias=self.epsc[0:rows, 0:1])
        self.recip(rs[0:rows], sd[0:rows], ["sd"], ["rs"])

    def build(self):
        P = self.P
        self.dma("pool", self.ident, self.ident_d, key_w="const", chan=self.cst)
        self.dma("pool", self.ind2, self.ind2_d, key_w="const", chan=self.cst)
        self.dma("pool", self.emask, self.emask_d.rearrange("p (a b) -> p a b", a=25), key_w="const", chan=self.cst)
        self.memset(self.ones, 1.0, ["const"])
        self.memset(self.onesf, 1.0, ["const"])
        self.memset(self.epsc, EPS, ["const"])
        for l in range(self.nlayers):
            src = self.x if l == 0 else self.scr
            dst = self.out if l == self.nlayers - 1 else self.scr
            self.dma("sp", self.colv, self.colv_d[l], key_w="colv")
            for seq in range(self.nseq):
                self.A.top = self.stage_base
                self.stage0(l, seq, src)
                if self.dbg and l == 0 and seq == 0:
                    self.dma("sp", self.dbg_h.rearrange("p (a b) -> p a b", a=8), self.hT, key_r="hT", chan=self.cout)
                self.A.top = self.stage_base
                self.stage_mla(l, seq)
                self.A.top = self.stage_base
                self.stage_dil(l, seq)
                self.A.top = self.stage_base
                self.stage_conv(l, seq)
                if self.dbg and l == 0 and seq == 0:
                    self.P.add("sp", lambda h: h.dma_start(out=self.dbg_y.rearrange("p (a b) -> p a b", a=12), in_=self.yT), reads=["Y%d_%d" % (c_, t_) for c_ in range(12) for t_ in range(4)], chan=self.cout)
                self.A.top = self.stage_base
                self.stage_out(l, seq, src, dst, final=(l == self.nlayers - 1))
        P.emit(final_chans=[self.cout] + ([self.cmisc["scrw"]] if "scrw" in self.cmisc else []))

    def stage0(self, l, seq, src):
        A = self.A
        NBUF = 4
        xb = [A.alloc([D], F32, key="xb%d" % i) for i in range(NBUF)]
        hn = [A.alloc([D], BF16, key="hn%d" % i) for i in range(NBUF)]
        junk = A.alloc([D], BF16, key="junk")
        gbc = A.alloc([D], F32, key="s0gbc")
        r_ss = self.ring(NBUF, [1], F32, "ss")
        r_sd = self.ring(NBUF, [1], F32, "sd1")
        r_rs = self.ring(NBUF, [1], F32, "rs1")
        self.dma("sp", gbc, self.gbc_d[l], key_w="s0gbc")
        for b in range(NB):
            i = b % NBUF
            r0 = seq * S + b * 128
            ss, kss = r_ss()
            sd1, ksd = r_sd()
            rs1, krs = r_rs()
            self.dma("sp", xb[i], src[r0:r0 + 128, :], key_w="xb%d" % i, key_r=("scr" if l > 0 else None))
            self.memset(ss, 0.0, [kss])
            self.act(junk, xb[i], AF.Square, ["xb%d" % i], ["junk", kss], accum=ss)
            self.act(sd1, ss, AF.Ln, [kss, "const"], [ksd], scale=1.0 / D, bias=self.epsc[:, 0:1])
            self.act(rs1, sd1, AF.Exp, [ksd], [krs], scale=-0.5)
            self.stt(hn[i], xb[i], rs1[:, 0:1], gbc, ALU.mult, ALU.mult, ["xb%d" % i, krs, "s0gbc"], ["hn%d" % i])
            pb = b % 8
            pst = self.ps[pb].bitcast(BF16)
            for k in range(8):
                self.tr(pst[:, k * 128:(k + 1) * 128], hn[i][:, k * 128:(k + 1) * 128], ["hn%d" % i], ["ps%d" % pb])
            if b % 2 == 0:
                self.act(self.hT[:, :, b * 128:(b + 1) * 128], pst.rearrange("p (k n) -> p k n", k=8), AF.Copy,
                         [], ["ps%d" % pb, "hT"])
            else:
                self.cpy(self.hT[:, :, b * 128:(b + 1) * 128], pst.rearrange("p (k n) -> p k n", k=8), [], ["ps%d" % pb, "hT"])

    def ring(self, n, shape, dtype, name):
        aps = [self.A.alloc(shape, dtype, key="%s%d" % (name, i)) for i in range(n)]
        st = {"i": 0}

        def nxt():
            i = st["i"] % n
            st["i"] += 1
            return aps[i], "%s%d" % (name, i)
        return nxt

    def psring(self, banks):
        st = {"i": 0}

        def nxt():
            b = banks[st["i"] % len(banks)]
            st["i"] += 1
            return self.ps[b], "ps%d" % b
        return nxt

    def stage_mla(self, l, seq):
        A = self.A
        cv = self.colv
        Kc = self.yT[:, 4:12, :]
        KK = ["yT_c", "yT_a"]
        Va = A.alloc([NB, 8, 65], BF16, key="Va")
        for t_ in range(NT):
            self.P.keymap["Va%d" % t_] = A.pages(A.last[0] + t_ * 4 * 8 * 65, 4 * 8 * 65)
        qT = A.alloc([8, TT], BF16, key=["qT%d" % h_ for h_ in range(8)])
        cqn = A.alloc([2, TT], BF16, key="cqn")
        ckvn = A.alloc([TT], BF16, key="ckvn")
        sqpe = A.alloc([TT], BF16, key="sqpe")
        kr = A.alloc([TT], F32, key="kr")
        ctt = A.alloc([TT], F32, key="ctt")
        stt_ = A.alloc([TT], F32, key="stt")
        cqf = A.alloc([TT], F32, key="cqf")
        sqf = A.alloc([TT], F32, key="sqf")
        ckf = A.alloc([TT], F32, key="ckf")
        skf = A.alloc([TT], F32, key="skf")
        sbz = A.alloc([4, TT], BF16, key="sbz")
        r_sq = self.ring(2, [TT], BF16, "sq")
        r_sd = self.ring(2, [TT], F32, "sd")
        r_rs = self.ring(2, [TT], F32, "rs")
        r_ta = self.ring(2, [TT], F32, "ta")
        r_tb = self.ring(2, [TT], F32, "tb")
        r_pt = self.ring(3, [TT], BF16, "PT")
        r_osb = self.ring(2, [TT], F32, "osb")
        r_rb = self.ring(2, [TT], F32, "rb")
        Wlat = A.alloc([8, 384], BF16, key="Wlat")
        Wkpe = A.alloc([8, 96], BF16, key="Wkpe")
        Wkpes = A.alloc([8, 96], BF16, key="Wkpes")
        Wuq = A.alloc([2, 768], BF16, key="Wuq")
        Wuqs = A.alloc([2, 768], BF16, key="Wuqs")
        Wk = A.alloc([512], BF16, key="Wk")
        Wv = A.alloc([512], BF16, key="Wv")
        cfg = os.environ.get("KMLA", "a")
        if cfg == "a":
            pP = self.psring([0, 1, 2])
            pSS = self.psring([3, 4])
            pS_ = self.psring([6, 7])
            pO_ = self.psring([5])
        elif cfg == "b":
            pP = self.psring([0, 1])
            pSS = self.psring([2, 3])
            pS_ = self.psring([4, 6, 7])
            pO_ = self.psring([5])
        else:
            pP = self.psring([0, 1])
            pSS = self.psring([2])
            pS_ = self.psring([3, 6, 7])
            pO_ = self.psring([4, 5])
        hT = self.hT
        self.dma("pool", Wlat, self.win_cols(l, C_CQ, 384), key_w="Wlat")
        self.dma("pool", Wkpe, self.w_kpe[l].rearrange("(k p) n -> p k n", p=128), key_w="Wkpe")
        self.dma("pool", Wkpes, self.w_kpe_sw[l].rearrange("(k p) n -> p k n", p=128), key_w="Wkpes")
        self.dma("pool", Wuq, self.w_uq[l].rearrange("(k p) n -> p k n", p=128), key_w="Wuq")
        self.dma("pool", Wuqs, self.w_uq_sw[l].rearrange("(k p) n -> p k n", p=128), key_w="Wuqs")
        self.dma("pool", Wk, self.w_ukv_k[l], key_w="Wk")
        self.dma("pool", Wv, self.w_ukv_v[l], key_w="Wv")
        self.memset(Va[:, :, :, 64:65], 1.0, ["Va"])
        sc_mla = 96.0 ** -0.5
        R = slice(64, 96)

        def rstd(ssq, kss, rows, inv_n):
            sd, ksd = r_sd()
            rs, krs = r_rs()
            self.act(sd[0:rows], ssq[0:rows], AF.Ln, ["const"], [ksd, kss], scale=inv_n, bias=self.epsc[0:rows, 0:1])
            self.act(rs[0:rows], sd[0:rows], AF.Exp, [ksd], [krs], scale=-0.5)
            return rs, krs

        for t in range(NT):
            T0 = t * TT
            hs = lambda k: hT[:, k, T0:T0 + TT]
            wbz = [self.ldw(self.win_cols(l, C_BZ + c * 128, 128)) for c in range(4)]
            self.dma("sp", ctt[0:96], self.ct_d[:, T0:T0 + TT], key_w="ctt")
            self.dma("sp", stt_[0:96], self.st_d[:, T0:T0 + TT], key_w="stt")
            self.ts(cqf[0:96], ctt[0:96], cv[0:96, CV_GQM:CV_GQM + 1], None, ALU.mult, None, ["ctt", "colv"], ["cqf"])
            self.ts(sqf[0:96], stt_[0:96], cv[0:96, CV_GQMS:CV_GQMS + 1], None, ALU.mult, None, ["stt", "colv"], ["sqf"])
            self.ts(ckf[0:96], ctt[0:96], cv[0:96, CV_GKM:CV_GKM + 1], None, ALU.mult, None, ["ctt", "colv"], ["ckf"])
            self.ts(skf[0:96], stt_[0:96], cv[0:96, CV_GKMS:CV_GKMS + 1], None, ALU.mult, None, ["stt", "colv"], ["skf"])
            pcq = [pP(), pP()]
            sqs = []
            for j in range(2):
                p_, k_ = pcq[j]
                for k in range(8):
                    self.mm(p_, Wlat[:, k, j * 128:(j + 1) * 128], hs(k), k == 0, k == 7, ["Wlat", "hT"], [k_])
                sq, ksq = r_sq()
                self.act(sq, p_, AF.Square, [], [ksq, k_])
                sqs.append((sq, ksq))
            ss, kss = pSS()
            self.mm(ss, self.ones, sqs[0][0], True, False, ["const", sqs[0][1]], [kss])
            self.mm(ss, self.ones, sqs[1][0], False, True, ["const", sqs[1][1]], [kss])
            rs, krs = rstd(ss, kss, 128, 1.0 / 256)
            for j in range(2):
                p_, k_ = pcq[j]
                self.stt(cqn[:, j, :], p_, cv[:, CV_GQ + j:CV_GQ + j + 1], rs, ALU.mult, ALU.mult, ["colv", krs], ["cqn", k_])
            p_, k_ = pP()
            for k in range(8):
                self.mm(p_, Wlat[:, k, 256:384], hs(k), k == 0, k == 7, ["Wlat", "hT"], [k_])
            sq, ksq = r_sq()
            self.act(sq, p_, AF.Square, [], [ksq, k_])
            ss, kss = pSS()
            self.mm(ss, self.ones, sq, True, True, ["const", ksq], [kss])
            rs, krs = rstd(ss, kss, 128, 1.0 / 128)
            self.stt(ckvn, p_, cv[:, CV_GKV:CV_GKV + 1], rs, ALU.mult, ALU.mult, ["colv", krs], ["ckvn", k_])
            p3, k3 = pP()
            for k in range(8):
                self.mm(p3[0:96], Wkpe[:, k, :], hs(k), k == 0, k == 7, ["Wkpe", "hT"], [k3])
            self.act(sqpe[R], p3[R], AF.Square, [], ["sqpe", k3])
            self.tt(kr[R], p3[R], ckf[R], ALU.mult, ["ckf"], ["kr", k3])
            p4, k4 = pP()
            for k in range(8):
                self.mm(p4[0:96], Wkpes[:, k, :], hs(k), k == 0, k == 7, ["Wkpes", "hT"], [k4])
            tb, ktb = r_tb()
            self.tt(tb[R], p4[R], skf[R], ALU.mult, ["skf"], [ktb, k4])
            self.tt(kr[R], kr[R], tb[R], ALU.add, [ktb, "kr"], ["kr"], eng="pool")
            for bb in range(4):
                blk = t * 4 + bb
                pv, pk = pP()
                self.mm(pv, ckvn[:, bb * 128:(bb + 1) * 128], Wv, True, True, ["ckvn", "Wv"], [pk])
                self.cpy(Va[:, blk, :, 0:64], pv.rearrange("p (h c) -> p h c", h=8), ["Va"], ["Va%d" % t, pk])
            for h in range(8):
                pK, pk = pP()
                self.mm(pK[0:64], Wk[:, h * 64:(h + 1) * 64], ckvn, True, True, ["Wk", "ckvn"], [pk])
                sq, ksq = r_sq()
                self.act(sq[0:64], pK[0:64], AF.Square, [], [ksq, pk])
                ss, kss = pSS()
                self.mm(ss[0:96], self.ones[0:64, 0:96], sq[0:64], True, False, ["const", ksq], [kss])
                self.mm(ss[0:96], self.ones[64:96, 0:96], sqpe[R], False, True, ["const", "sqpe"], [kss])
                rs, krs = rstd(ss, kss, 96, 1.0 / 96)
                self.stt(Kc[0:64, h, T0:T0 + TT], pK[0:64], cv[0:64, CV_GKM:CV_GKM + 1], rs[0:64], ALU.mult, ALU.mult,
                         ["colv", krs], ["Y%d_%d" % (4 + h, t), pk])
                self.tt(Kc[R, h, T0:T0 + TT], kr[R], rs[R], ALU.mult, ["kr", krs], ["Y%d_%d" % (4 + h, t)])
            for c in range(4):
                wt, wk = wbz[c]
                pb, pk = pP()
                for k in range(8):
                    self.mm(pb, wt[:, k, :], hs(k), k == 0, k == 7, [wk, "hT"], [pk])
                self.act(sbz[:, c, :], pb, AF.Silu, [], ["sbz", pk])
            for h in range(8):
                pQ, kq = pP()
                for j in range(2):
                    self.mm(pQ[0:96], Wuq[:, j, h * 96:(h + 1) * 96], cqn[:, j, :], j == 0, j == 1, ["Wuq", "cqn"], [kq])
                pQs, kqs = pP()
                for j in range(2):
                    self.mm(pQs[0:96], Wuqs[:, j, h * 96:(h + 1) * 96], cqn[:, j, :], j == 0, j == 1, ["Wuqs", "cqn"], [kqs])
                sq, ksq = r_sq()
                self.act(sq[0:96], pQ[0:96], AF.Square, [], [ksq, kq])
                ss, kss = pSS()
                self.mm(ss[0:96], self.ones[0:96, 0:96], sq[0:96], True, True, ["const", ksq], [kss])
                rs, krs = rstd(ss, kss, 96, 1.0 / 96)
                ta, kta = r_ta()
                tb, ktb = r_tb()
                self.tt(ta[0:96], pQ[0:96], cqf[0:96], ALU.mult, ["cqf"], [kta, kq])
                self.tt(tb[R], pQs[R], sqf[R], ALU.mult, ["sqf"], [ktb, kqs])
                self.tt(ta[R], ta[R], tb[R], ALU.add, [ktb, kta], [kta], eng="pool")
                self.tt(qT[0:96, h, :], ta[0:96], rs[0:96], ALU.mult, [kta, krs], ["qT%d" % h])
            for h in range(8):
                nkb = 4 * t + 4
                pO, ko = pO_()
                for kb in range(nkb):
                    c0 = 0 if kb < 4 * t else (kb - 4 * t) * 128
                    pS, ks = pS_()
                    pt, kpt = r_pt()
                    self.mm(pS[:, c0:TT], Kc[0:96, h, kb * 128:(kb + 1) * 128], qT[0:96, h, c0:TT], True, True,
                            ["Y%d_%d" % (4 + h, kb // 4), "qT%d" % h], [ks])
                    self.act(pt[:, c0:TT], pS[:, c0:TT], AF.Exp, [], [kpt, ks], scale=sc_mla)
                    if kb >= 4 * t:
                        self.tt(pt[:, c0:c0 + 128], pt[:, c0:c0 + 128], self.emask[:, 24, 0:128], ALU.mult, ["const", kpt], [kpt], eng="pool")
                    self.mm(pO[0:65, c0:TT], Va[:, kb, h, :], pt[:, c0:TT], kb == 0, kb == nkb - 1, ["Va", "Va%d" % (kb // 4), kpt], [ko])
                osb, kos = r_osb()
                self.cpy(osb[0:65], pO[0:65], [], [kos, ko])
                self.act(osb[64:65], osb[64:65], AF.Ln, [kos], [kos])
                self.act(osb[64:65], osb[64:65], AF.Exp, [kos], [kos], scale=-1.0)
                rb, krb = r_rb()
                self.bcast_rows(osb[64:65, :], kos, [(rb[0:64], krb, 0, TT)])
                ta, kta = r_ta()
                Rh = slice(0, 64) if h % 2 == 0 else slice(64, 128)
                self.tt(ta[Rh], osb[0:64], rb[0:64], ALU.mult, [kos, krb], [kta])
                self.tt(self.yT[Rh, h // 2, T0:T0 + TT], ta[Rh], sbz[Rh, h // 2, :], ALU.mult, [kta, "sbz"], ["Y%d_%d" % (h // 2, t)])

    def stage_dil(self, l, seq):
        A = self.A
        cv = self.colv
        hT = self.hT
        r_dq = self.ring(2, [S], BF16, "dqT")
        r_dk = self.ring(2, [S], BF16, "dkT")
        r_dv = self.ring(2, [S], BF16, "dvT")
        r_dva = self.ring(2, [NB, 2, 65], BF16, "dva")
        r_acc = self.ring(2, [2, S], F32, "acc")
        r_scz = self.ring(2, [S], BF16, "scz")
        r_sq = self.ring(2, [TT], BF16, "sq")
        r_sd = self.ring(2, [TT], F32, "sd")
        r_rs = self.ring(2, [TT], F32, "rs")
        r_ta = self.ring(2, [TT], F32, "ta")
        r_pt = self.ring(4, [256], BF16, "dPT")
        r_rb = self.ring(4, [TT], F32, "rb")
        cfg = os.environ.get("KDIL", "a")
        if cfg == "a":
            pP = self.psring([2, 3, 4, 5])
            pS_ = self.psring([6, 7])
            pO_ = self.psring([0, 1])
        else:
            pP = self.psring([3, 4, 5])
            pS_ = self.psring([2, 6, 7])
            pO_ = self.psring([0, 1])
        for _ in range(2):
            dva, kdva = r_dva()
            self.memset(dva[:, :, :, 64:65], 1.0, [kdva])

        def rstd(ssq, kss, rows, inv_n):
            sd, ksd = r_sd()
            rs, krs = r_rs()
            self.act(sd[0:rows], ssq[0:rows], AF.Ln, ["const"], [ksd, kss], scale=inv_n, bias=self.epsc[0:rows, 0:1])
            self.act(rs[0:rows], sd[0:rows], AF.Exp, [ksd], [krs], scale=-0.5)
            return rs, krs

        def perm_out(buf, g, t):
            if g == 0:
                return buf[:, t * TT:(t + 1) * TT], None
            if g == 1:
                return buf.rearrange("p (r m i) -> p r m i", r=4, m=4)[:, :, t, :], ("p (i r) -> p r i", 4)
            return buf.rearrange("p (r i) -> p r i", r=16)[:, :, 32 * t:32 * t + 32], ("p (i r) -> p r i", 16)

        def segments(g):
            segs = []
            for s_ in range(4):
                u = []
                if g == 0:
                    if s_ > 0:
                        u.append(((4 * s_ - 1) * 128, 4 * s_ - 1, 4 * s_ * 128, 128, 128))
                    for kb in range(4 * s_, 4 * s_ + 4):
                        u.append((kb * 128, kb, kb * 128, 256 if kb < 4 * s_ + 3 else 128, 0))
                elif g == 1:
                    for m in range(4):
                        blk = 4 * s_ + m
                        u.append((blk * 128, blk, blk * 128, 256 if m < 3 else 128, 0))
                else:
                    for c in range(4):
                        blk = 4 * s_ + c
                        u.append((blk * 128, blk, blk * 128, 128, 0))
                segs.append(u)
            return segs

        for j in range(4):
            scz, kscz = r_scz()
            acc, kacc = r_acc()
            wcz, kcz = self.ldw(self.win_cols(l, C_CZ + j * 128, 128))
            for t in range(NT):
                pc, kpc = pP()
                for k in range(8):
                    self.mm(pc, wcz[:, k, :], hT[:, k, t * TT:(t + 1) * TT], k == 0, k == 7, [kcz, "hT"], [kpc])
                self.act(scz[:, t * TT:(t + 1) * TT], pc, AF.Silu, [], [kscz, kpc])
            for g in range(3):
                c_off = g * 512 + j * 128
                wq, kwq = self.ldw(self.win_cols(l, C_DQ + c_off, 128))
                wk_, kwk = self.ldw(self.win_cols(l, C_DK + c_off, 128))
                wv, kwv = self.ldw(self.win_cols(l, C_DV + c_off, 128))
                dqT, kdq = r_dq()
                dkT, kdk = r_dk()
                dvT, kdv = r_dv()
                dva, kdva = r_dva()
                for t in range(NT):
                    hs = lambda k: hT[:, k, t * TT:(t + 1) * TT]
                    for (wt, wkey, dst, dkey, gcol) in ((wq, kwq, dqT, kdq, CV_GDQ + g), (wk_, kwk, dkT, kdk, CV_GDK + g)):
                        pp, pk = pP()
                        for k in range(8):
                            self.mm(pp, wt[:, k, :], hs(k), k == 0, k == 7, [wkey, "hT"], [pk])
                        sq, ksq = r_sq()
                        self.act(sq, pp, AF.Square, [], [ksq, pk])
                        ss, kss = pP()
                        self.mm(ss, self.ind2, sq, True, True, ["const", ksq], [kss])
                        rs, krs = rstd(ss, kss, 128, 1.0 / 64)
                        ov, pr = perm_out(dst, g, t)
                        if pr is None:
                            self.stt(ov, pp, cv[:, gcol:gcol + 1], rs, ALU.mult, ALU.mult, ["colv", krs], [dkey, pk])
                        else:
                            self.stt(ov, pp.rearrange(pr[0], r=pr[1]), cv[:, gcol:gcol + 1], rs.rearrange(pr[0], r=pr[1]),
                                     ALU.mult, ALU.mult, ["colv", krs], [dkey, pk])
                    pp, pk = pP()
                    for k in range(8):
                        self.mm(pp, wv[:, k, :], hs(k), k == 0, k == 7, [kwv, "hT"], [pk])
                    ov, pr = perm_out(dvT, g, t)
                    if pr is None:
                        self.cpy(ov, pp, [], [kdv, pk])
                    else:
                        self.cpy(ov, pp.rearrange(pr[0], r=pr[1]), [], [kdv, pk])
                for half in range(2):
                    pp, pk = pP()
                    pst = pp.bitcast(BF16)
                    for bb in range(8):
                        blk = half * 8 + bb
                        self.tr(pst[:, bb * 128:(bb + 1) * 128], dvT[:, blk * 128:(blk + 1) * 128], [kdv], [pk])
                    self.act(dva[:, half * 8:(half + 1) * 8, :, 0:64],
                             pst.rearrange("p (b h c) -> p b h c", b=8, h=2), AF.Copy, [], [kdva, pk])
                segs = segments(g)
                for hh in range(2):
                    Rr = slice(hh * 64, hh * 64 + 64)
                    em = self.emask[:, g * 8 + 2 * j + hh, :]
                    for b, units in enumerate(segs):
                        pO, ko = pO_()
                        first = True
                        for (K0, blk, q0, nq, m0) in units:
                            pS, ks = pS_()
                            pt, kpt = r_pt()
                            self.mm(pS[:, 0:nq], dkT[Rr, K0:K0 + 128], dqT[Rr, q0:q0 + nq], True, True, [kdk, kdq], [ks])
                            self.act(pt[:, 0:nq], pS[:, 0:nq], AF.Exp, [], [kpt, ks], scale=0.125)
                            self.tt(pt[:, 0:nq], pt[:, 0:nq], em[:, m0:m0 + nq], ALU.mult, ["const", kpt], [kpt], eng=("pool" if g == 2 else "dve"))
                            self.mm(pO[0:65, q0 - b * 512:q0 - b * 512 + nq], dva[:, blk, hh, :], pt[:, 0:nq], first, False,
                                    [kdva, kpt], [ko])
                            first = False
                        if g == 0:
                            self.act(acc[0:65, hh, b * 512:(b + 1) * 512], pO[0:65], AF.Copy, [], [kacc, ko])
                        elif g == 1:
                            av = acc[0:65, hh, :].rearrange("p (m i r) -> p r m i", m=4, r=4)[:, b]
                            self.tt(av, av, pO[0:65].rearrange("p (m i) -> p m i", m=4), ALU.add, [kacc], [kacc, ko])
                        else:
                            av = acc[0:65, hh, :].rearrange("p (i r) -> p r i", r=16)[:, 4 * b:4 * b + 4, :]
                            self.tt(av, av, pO[0:65].rearrange("p (r i) -> p r i", r=4), ALU.add, [kacc], [kacc, ko])
            for hh in range(2):
                Rh = slice(hh * 64, hh * 64 + 64)
                self.act(acc[64:65, hh, :], acc[64:65, hh, :], AF.Ln, [kacc], [kacc])
                self.act(acc[64:65, hh, :], acc[64:65, hh, :], AF.Exp, [kacc], [kacc], scale=-1.0)
                rbs = [r_rb() for _ in range(NT)]
                self.bcast_rows(acc[64:65, hh, :], kacc, [(rbs[t][0][0:64], rbs[t][1], t * TT, TT) for t in range(NT)])
                for t in range(NT):
                    cs = slice(t * TT, (t + 1) * TT)
                    rb, krb = rbs[t]
                    ta, kta = r_ta()
                    self.tt(ta[Rh], acc[0:64, hh, cs], rb[0:64], ALU.mult, [kacc, krb], [kta])
                    self.tt(self.yT[Rh, 4 + j, cs], ta[Rh], scz[Rh, cs], ALU.mult, [kta, kscz], ["Y%d_%d" % (4 + j, t)])

    def stage_conv(self, l, seq):
        A = self.A
        ps = self.ps
        cv = self.colv
        hT = self.hT
        r_up = self.ring(2, [S + 2], F32, "upad")
        r_acs = self.ring(2, [TT], F32, "acs")
        r_sz = self.ring(2, [TT], F32, "sz")
        r_c1 = self.ring(2, [TT], F32, "c1")
        r_c2 = self.ring(2, [TT], F32, "c2")
        r_t1 = self.ring(3, [TT], F32, "t1")
        n = 0
        for c in range(4):
            ws = [self.ldw(self.win_cols(l, base + c * 128, 128)) for base in (C_AB, C_AC, C_AX, C_AZ)]
            upad, kup = r_up()
            self.memset(upad[:, 0:2], 0.0, [kup])
            for t in range(NT):
                T0 = t * TT
                o = 4 * (n % 2)
                n += 1
                for i in range(4):
                    wt, wk = ws[i]
                    for k in range(8):
                        self.mm(ps[o + i], wt[:, k, :], hT[:, k, T0:T0 + TT], k == 0, k == 7, [wk, "hT"], ["ps%d" % (o + i)])
                kb_, kc_, kx_, kz_ = ["ps%d" % (o + i) for i in range(4)]
                acs, kacs = r_acs()
                sz, ksz = r_sz()
                c1, kc1 = r_c1()
                c2, kc2 = r_c2()
                self.act(acs, ps[o + 1], AF.Copy, [], [kacs, kc_])
                self.tt(upad[:, 2 + T0:2 + T0 + TT], acs, ps[o + 2], ALU.mult, [kacs], [kup, kx_])
                self.act(sz, ps[o + 3], AF.Silu, [], [ksz, kz_])
                t1, kt1 = r_t1()
                self.act(c1, upad[:, T0:T0 + TT], AF.Identity, [kup, "colv"], [kc1], scale=cv[:, CV_CW + c:CV_CW + c + 1],
                         bias=cv[:, CV_CB + c:CV_CB + c + 1])
                self.act(t1, upad[:, T0 + 1:T0 + 1 + TT], AF.Identity, [kup, "colv"], [kt1], scale=cv[:, CV_CW + 4 + c:CV_CW + 5 + c])
                self.tt(c1, c1, t1, ALU.add, [kc1, kt1], [kc1], eng="pool")
                t1, kt1 = r_t1()
                self.act(t1, upad[:, T0 + 2:T0 + 2 + TT], AF.Identity, [kup, "colv"], [kt1], scale=cv[:, CV_CW + 8 + c:CV_CW + 9 + c])
                self.tt(c1, c1, t1, ALU.add, [kc1, kt1], [kc1], eng="pool")
                self.tt(c2, c1, ps[o + 0], ALU.mult, [kc1], [kc2, kb_])
                self.tt(self.yT[:, 8 + c, T0:T0 + TT], c2, sz, ALU.mult, [kc2, ksz], ["Y%d_%d" % (8 + c, t)])

    def stage_out(self, l, seq, src, dst, final):
        A = self.A
        ps = self.ps
        cv = self.colv
        hT = self.hT
        yT = self.yT
        Wo3 = [A.alloc([4, D], BF16, key="Wo3_%d" % i) for i in range(3)]
        Wo = A.alloc([8, D], BF16, key="Wo")
        G = A.alloc([24, TT], BF16)
        for jj in range(24):
            self.P.keymap["G%d" % jj] = A.pages(A.last[0] + jj * TT, TT)
        mT = A.alloc([8, TT], BF16)
        for jj in range(8):
            self.P.keymap["mT%d" % jj] = A.pages(A.last[0] + jj * TT, TT)
        r_m1 = self.ring(2, [TT], F32, "m1")
        r_m2 = self.ring(2, [TT], F32, "m2")
        xb = [A.alloc([D], F32, key="oxb%d" % i) for i in range(2)]
        ob = [A.alloc([D], F32, key="ob%d" % i) for i in range(2)]
        for i in range(3):
            self.dma("pool", Wo3[i], self.w_out[i][l].rearrange("(k p) n -> p k n", p=128), key_w="Wo3_%d" % i)
        self.dma("pool", Wo, self.w_o[l].rearrange("(k p) n -> p k n", p=128), key_w="Wo")
        och = self.cout if final else self.misc_chan("scrw")
        pG = self.psring([0, 1])
        for t in range(NT):
            T0 = t * TT
            DEPTHW = 4
            wq = [self.ldw(self.win_cols(l, C_GATE + jj * 128, 128)) for jj in range(DEPTHW)]
            for jj in range(24):
                wt, wk = wq[jj]
                pg, kg = pG()
                for k in range(8):
                    self.mm(pg, wt[:, k, :], hT[:, k, T0:T0 + TT], k == 0, k == 7, [wk, "hT"], [kg])
                self.act(G[:, jj, :], pg, AF.Sigmoid, ["colv"], ["G%d" % jj, kg], bias=cv[:, CV_BG + jj:CV_BG + jj + 1])
                if jj + DEPTHW < 24:
                    wq.append(self.ldw(self.win_cols(l, C_GATE + (jj + DEPTHW) * 128, 128)))
            for j in range(8):
                o = 2 + 3 * (j % 2)
                for i in range(3):
                    for k in range(4):
                        self.mm(ps[o + i], Wo3[i][:, k, j * 128:(j + 1) * 128], yT[:, (8, 0, 4)[i] + k, T0:T0 + TT], k == 0, k == 3,
                                ["Wo3_%d" % i, "Y%d_%d" % ((8, 0, 4)[i] + k, t)], ["ps%d" % (o + i)])
                m1, k1 = r_m1()
                m2, k2 = r_m2()
                self.tt(m1, ps[o], G[:, j, :], ALU.mult, ["G%d" % j], [k1, "ps%d" % o])
                self.tt(m2, ps[o + 1], G[:, 8 + j, :], ALU.mult, ["G%d" % (8 + j)], [k2, "ps%d" % (o + 1)])
                self.tt(m1, m1, m2, ALU.add, [k1, k2], [k1], eng="pool")
                self.tt(m2, ps[o + 2], G[:, 16 + j, :], ALU.mult, ["G%d" % (16 + j)], [k2, "ps%d" % (o + 2)])
                self.tt(mT[:, j, :], m1, m2, ALU.add, [k1, k2], ["mT%d" % j], eng="pool")
            for bb in range(4):
                b = t * 4 + bb
                i = b % 2
                r0 = seq * S + b * 128
                self.dma("sp", xb[i], src[r0:r0 + 128, :], key_w="oxb%d" % i, key_r=("scr" if l > 0 else None))
                for half in range(2):
                    pg, kg = pG()
                    for k in range(8):
                        self.mm(pg, mT[:, k, bb * 128:(bb + 1) * 128], Wo[:, k, half * 512:(half + 1) * 512], k == 0, k == 7,
                                ["mT%d" % k, "Wo"], [kg])
                    self.tt(ob[i][:, half * 512:(half + 1) * 512], pg, xb[i][:, half * 512:(half + 1) * 512], ALU.add,
                            ["oxb%d" % i], ["ob%d" % i, kg])
                self.dma("sp", dst[r0:r0 + 128, :], ob[i], key_r="ob%d" % i, key_w=(None if final else "scr"), chan=och)


def _rope_tables():
    inv = (10000.0 ** (-np.arange(0, 32, 2, dtype=np.float32) / 32)).astype(np.float32)
    ang = np.arange(S, dtype=np.float32)[:, None] * inv[None, :]
    cos, sin = np.cos(ang).astype(np.float32), np.sin(ang).astype(np.float32)
    ct = np.ones((96, S), np.float32)
    st = np.zeros((96, S), np.float32)
    ct[64:80] = cos.T
    ct[80:96] = cos.T
    st[64:80] = -sin.T
    st[80:96] = sin.T
    return ct, st


def _emask():
    n = 24
    slopes = (2.0 ** (-8.0 * np.arange(1, n + 1, dtype=np.float32) / n)).reshape(3, 8)
    k = np.arange(128)[:, None].astype(np.float32)
    q = np.arange(128)[None, :].astype(np.float32)
    em = np.zeros((128, 25, 256), np.float32)
    for g in range(3):
        for h in range(8):
            sl = slopes[g, h] * DIL[g]
            em[:, g * 8 + h, 0:128] = np.where(k <= q, np.exp(-sl * np.maximum(q - k, 0.0)), 0.0)
            if g < 2:
                em[:, g * 8 + h, 128:256] = np.where(k >= q, np.exp(-sl * np.maximum(128.0 + q - k, 0.0)), 0.0)
    em[:, 24, 0:128] = (k <= q).astype(np.float32)
    return em.reshape(128, 25 * 256)


def _host_layout(inp, layers):
    f = lambda a: np.ascontiguousarray(a, dtype=np.float32)
    L = len(layers)
    w_uq = f(inp["w_uq"][layers])
    w_uq_sw = np.zeros_like(w_uq)
    for h in range(8):
        b = h * 96
        w_uq_sw[:, :, b + 64:b + 80] = w_uq[:, :, b + 80:b + 96]
        w_uq_sw[:, :, b + 80:b + 96] = w_uq[:, :, b + 64:b + 80]
    w_ukv = f(inp["w_ukv"][layers]).reshape(L, 128, 8, 128)
    w_ukv_k = f(w_ukv[:, :, :, 0:64].reshape(L, 128, 512))
    w_ukv_v = f(w_ukv[:, :, :, 64:128].reshape(L, 128, 512))
    w_in = inp["w_in"]
    w_kpe = np.zeros((L, D, 96), np.float32)
    w_kpe_sw = np.zeros((L, D, 96), np.float32)
    for i, l in enumerate(layers):
        kp = w_in[l][:, C_KPE:C_KPE + 32]
        w_kpe[i, :, 64:96] = kp
        w_kpe_sw[i, :, 64:80] = kp[:, 16:32]
        w_kpe_sw[i, :, 80:96] = kp[:, 0:16]
    colv = np.zeros((L, 128, NCV), np.float32)
    gbc = np.zeros((L, 128, D), np.float32)
    for i, l in enumerate(layers):
        colv[i, :, CV_BG:CV_BG + 24] = inp["b_gate"][l].reshape(24, 128).T
        for tap in range(3):
            colv[i, :, CV_CW + tap * 4:CV_CW + tap * 4 + 4] = inp["conv_w"][l][tap].reshape(4, 128).T
        colv[i, :, CV_CB:CV_CB + 4] = inp["conv_b"][l].reshape(4, 128).T
        colv[i, :, CV_GQ:CV_GQ + 2] = inp["q_a_norm_g"][l].reshape(2, 128).T
        colv[i, :, CV_GKV] = inp["kv_a_norm_g"][l]
        for (col, src) in ((CV_GQM, inp["mla_q_norm_g"][l]), (CV_GKM, inp["mla_k_norm_g"][l])):
            colv[i, 0:96, col] = src
            colv[i, 64:80, col + 1] = src[80:96]
            colv[i, 80:96, col + 1] = src[64:80]
        for g in range(3):
            colv[i, :, CV_GDQ + g] = np.tile(inp["dil_q_norm_g"][l][g], 2)
            colv[i, :, CV_GDK + g] = np.tile(inp["dil_k_norm_g"][l][g], 2)
        gbc[i] = np.broadcast_to(inp["norm_g"][l][None, :], (128, D))
    ind2 = np.zeros((128, 128), np.float32)
    ind2[0:64, 0:64] = 1.0
    ind2[64:128, 64:128] = 1.0
    ct, st = _rope_tables()
    shared = {
        "w_in": f(w_in[layers]), "w_uq": w_uq, "w_uq_sw": w_uq_sw, "w_ukv_k": w_ukv_k, "w_ukv_v": w_ukv_v,
        "w_kpe": w_kpe, "w_kpe_sw": w_kpe_sw,
        "w_out_a": f(inp["w_out_a"][layers]), "w_out_b": f(inp["w_out_b"][layers]), "w_out_c": f(inp["w_out_c"][layers]),
        "w_o": f(inp["w_o"][layers]), "colv": colv, "gbc": gbc,
        "ident": np.eye(128, dtype=np.float32), "ind2": ind2, "emask": _emask(), "ctab": ct, "stab": st,
    }
    return shared


_CACHE = {}


def _get_builder(nseq, nlayers, dbg=False):
    key = (nseq, nlayers, dbg)
    if key not in _CACHE:
        _CACHE[key] = Builder(nseq, nlayers, dbg=dbg)
    return _CACHE[key]


def kernel(**inputs):
    inp = {k: np.asarray(v) for k, v in inputs.items()}
    x = np.ascontiguousarray(inp["x"], dtype=np.float32)
    B = x.shape[0]
    ncores = 8
    nseq = B // ncores
    shared = _host_layout(inp, list(range(DEPTH)))
    bld = Builder(nseq, DEPTH)
    in_maps = []
    for c in range(ncores):
        m = dict(shared)
        m["x"] = x[c * nseq:(c + 1) * nseq].reshape(nseq * S, D)
        in_maps.append(m)
    res = run_bass_kernel_spmd(bld.nc, in_maps, core_ids=list(range(ncores)))
    outs = [np.asarray(r["out"]).reshape(nseq, S, D) for r in res.results]
    return np.concatenate(outs, axis=0).astype(np.float32)
```

```python
import numpy as np
import concourse.bass as bass
import concourse.mybir as mybir
from concourse.bass_utils import run_bass_kernel_spmd

F32 = mybir.dt.float32
BF16 = mybir.dt.bfloat16
AF = mybir.ActivationFunctionType
ALU = mybir.AluOpType

ENGS = ("pe", "act", "dve", "pool", "sp")

D = 1024
S = 2048
DEPTH = 2
NB = 16
NT = 4
TT = 512
EPS = 1e-6
N_IN = 11168
C_AB, C_AC, C_AX, C_AZ = 0, 512, 1024, 1536
C_CQ, C_CKV, C_KPE, C_BZ = 2048, 2304, 2432, 2464
C_DQ, C_DK, C_DV, C_CZ, C_GATE = 2976, 4512, 6048, 7584, 8096
DIL = (1, 4, 16)
CV_BG, CV_CW, CV_CB, CV_GQ, CV_GKV, CV_GQM, CV_GQMS, CV_GKM, CV_GKMS, CV_GDQ, CV_GDK, NCV = 0, 24, 36, 40, 42, 43, 44, 45, 46, 47, 50, 53


import heapq
import os

RAW, WAR, WAW = 0, 1, 2
ACT_SET = {"Exp": "exp", "Ln": "exp", "Sqrt": "sqrt", "Silu": "silu", "Sigmoid": "sigmoid"}


class Chan:
    def __init__(self, sem, name):
        self.sem = sem
        self.name = name
        self.count = 0


class Op:
    __slots__ = ("eng", "fn", "idx", "pidx", "deps", "succ", "ndeps", "waits", "signal", "sigcount", "chan",
                 "dma_deps", "dur", "tset", "ready", "finish", "seg", "isbar")

    def __init__(self, eng, fn):
        self.eng = eng
        self.fn = fn
        self.idx = -1
        self.pidx = -1
        self.deps = []
        self.succ = []
        self.ndeps = 0
        self.waits = {}
        self.dma_deps = []
        self.signal = False
        self.sigcount = 0
        self.chan = None
        self.dur = 0.5
        self.tset = None
        self.ready = 0.0
        self.finish = 0.0
        self.seg = 0
        self.isbar = False


class Prog:
    def __init__(self, nc):
        self.nc = nc
        self.ops = []
        self.kw = {}
        self.kr = {}
        self.sems = {e: nc.alloc_semaphore("sem_" + e) for e in ENGS}
        self.nops = 0
        self.chans = []
        self.seg = 0
        self.sched = True
        self.prio = os.environ.get("KPRIO", "bl")
        self.strict = os.environ.get("KSTRICT", "1") == "1"
        self.keymap = {}

    def chan(self, name):
        c = Chan(self.nc.alloc_semaphore("dsem_" + name), name)
        self.chans.append(c)
        return c

    def barrier(self):
        self.seg += 1

    def add(self, eng, fn, reads=(), writes=(), chan=None, dur=0.5, tset=None):
        op = Op(eng, fn)
        op.pidx = len(self.ops)
        op.seg = self.seg
        op.chan = chan
        op.dur = dur
        op.tset = tset
        self.ops.append(op)
        self.nops += 1
        km = self.keymap
        if km:
            reads = list(reads) + [p for k in reads for p in km.get(k, ())]
            writes = list(writes) + [p for k in writes for p in km.get(k, ())]
        deps = {}
        for k in reads:
            w = self.kw.get(k)
            if w is not None:
                deps[w] = RAW
        for k in writes:
            w = self.kw.get(k)
            if w is not None and w not in deps:
                deps[w] = WAW
            for r in self.kr.get(k, ()):
                if r is not op and r not in deps:
                    deps[r] = WAR
        for k in reads:
            self.kr.setdefault(k, []).append(op)
        for k in writes:
            self.kw[k] = op
            self.kr[k] = []
        op.deps = [(d, kind) for d, kind in deps.items() if d.seg == op.seg]
        return op

    def _schedule(self, ops):
        if not self.sched:
            return list(ops)
        for op in ops:
            op.ndeps = len(op.deps)
            op.succ = []
            op.ready = 0.0
        for op in ops:
            for d, _ in op.deps:
                d.succ.append(op)
        LAT = float(os.environ.get("KLAT", "0.3"))
        if self.prio == "bl":
            for op in reversed(ops):
                b = 0.0
                for s_ in op.succ:
                    v = s_.finish + (LAT if s_.eng != op.eng else 0.0)
                    if v > b:
                        b = v
                op.finish = b + op.dur + (2.0 if op.chan is not None else 0.0)
            mx = max(op.finish for op in ops) if ops else 0.0
            for i_, op in enumerate(ops):
                op.pidx = int((mx - op.finish) * 1000) * 100000 + i_
        free = {e: 0.0 for e in ENGS}
        pending = {e: [] for e in ENGS}
        avail = {e: {} for e in ENGS}
        curset = [None]
        for op in ops:
            if op.ndeps == 0:
                heapq.heappush(pending[op.eng], (0.0, op.pidx, op))
        order = []
        n = len(ops)

        def cand(e):
            t = free[e]
            pend = pending[e]
            av = avail[e]
            while pend and pend[0][0] <= t:
                _, pi, o = heapq.heappop(pend)
                heapq.heappush(av.setdefault(o.tset if e == "act" else None, []), (pi, o))
            best = None
            if e == "act":
                for ts_ in (None, curset[0]):
                    h = av.get(ts_)
                    if h and (best is None or h[0][0] < best[0]):
                        best = (h[0][0], ts_)
                if best is None:
                    for ts_, h in av.items():
                        if h and (best is None or h[0][0] < best[0]):
                            best = (h[0][0], ts_)
            else:
                h = av.get(None)
                if h:
                    best = (h[0][0], None)
            if best is not None:
                return (t, best[0], best[1], False)
            if pend:
                return (pend[0][0], pend[0][1], None, True)
            return None

        while len(order) < n:
            bc = None
            be = None
            for e in ENGS:
                c = cand(e)
                if c is not None and (bc is None or (c[0], c[1]) < (bc[0], bc[1])):
                    bc, be = c, e
            assert bc is not None, "scheduler stuck"
            if bc[3]:
                _, pi, o = heapq.heappop(pending[be])
            else:
                pi, o = heapq.heappop(avail[be][bc[2]])
            start = max(bc[0], free[be])
            dur = o.dur
            if be == "act" and o.tset is not None and o.tset != curset[0]:
                dur += 2.7
                curset[0] = o.tset
            if o.chan is not None:
                free[be] = start + 0.15
                o.finish = start + 2.0 + dur
            else:
                free[be] = start + dur
                o.finish = start + dur
            order.append(o)
            for s_ in o.succ:
                s_.ndeps -= 1
                rt = o.finish + (LAT if s_.eng != o.eng else 0.05)
                if rt > s_.ready:
                    s_.ready = rt
                if s_.ndeps == 0:
                    heapq.heappush(pending[s_.eng], (s_.ready, s_.pidx, s_))
        return order

    def emit(self, final_chans=()):
        nc = self.nc
        nseg = self.seg + 1
        segs = [[] for _ in range(nseg)]
        for op in self.ops:
            segs[op.seg].append(op)
        glob = []
        for si in range(nseg):
            if si > 0:
                drains = []
                for e in ENGS:
                    if e == "sp":
                        continue
                    d = Op(e, lambda h: h.drain())
                    d.isbar = True
                    drains.append(d)
                    glob.append(d)
                for e in ENGS:
                    w = Op(e, lambda h: h.nop())
                    w.isbar = True
                    w.deps = [(d, RAW) for d in drains if d.eng != e]
                    w.dma_deps = "all"
                    glob.append(w)
            glob.extend(self._schedule(segs[si]))
        self.eng_ops = {e: [] for e in ENGS}
        for op in glob:
            op.idx = len(self.eng_ops[op.eng])
            self.eng_ops[op.eng].append(op)
        chan_cnt = {c: 0 for c in self.chans}
        dma_wait_vals = {}
        for op in glob:
            isdma = op.chan is not None
            dw = {}
            if op.dma_deps == "all":
                for c in self.chans:
                    if chan_cnt[c] > 0:
                        dw[c] = chan_cnt[c]
            for d, kind in op.deps:
                if d.chan is not None:
                    dw[d.chan] = chan_cnt[d.chan]
                    continue
                if d.eng == op.eng and not isdma and kind != RAW and not self.strict:
                    continue
                if d.eng == op.eng and d.eng == "pe":
                    continue
                cur = op.waits.get(d.eng)
                if cur is None or cur.idx < d.idx:
                    op.waits[d.eng] = d
            if isdma:
                chan_cnt[op.chan] += 1
            dma_wait_vals[op] = dw
        for op in glob:
            for p in op.waits.values():
                p.signal = True
        for e in ENGS:
            c = 0
            for op in self.eng_ops[e]:
                if op.signal:
                    c += 1
                op.sigcount = c
        final_cnt = {c: chan_cnt[c] for c in final_chans}

        def run_engine(e, h):
            seen = {}
            for op in self.eng_ops[e]:
                for pe_, p in op.waits.items():
                    need = p.sigcount
                    if seen.get(pe_, 0) < need:
                        h.wait_ge(self.sems[pe_], need)
                        seen[pe_] = need
                for c, cnt in dma_wait_vals[op].items():
                    if seen.get(c, 0) < cnt:
                        h.wait_ge(c.sem, 16 * cnt)
                        seen[c] = cnt
                ins = op.fn(h)
                if op.chan is not None:
                    ins.then_inc(op.chan.sem, 16)
                elif op.signal:
                    ins.then_inc(self.sems[e], 1)
            if e == "sp":
                for c, cnt in final_cnt.items():
                    h.wait_ge(c.sem, 16 * cnt)

        with nc.Block() as block:
            @block.tensor
            def _(h):
                run_engine("pe", h)

            @block.scalar
            def _(h):
                run_engine("act", h)

            @block.vector
            def _(h):
                run_engine("dve", h)

            @block.gpsimd
            def _(h):
                run_engine("pool", h)

            @block.sync
            def _(h):
                run_engine("sp", h)


class Arena:
    def __init__(self, nc, nbytes):
        self.cap = nbytes // 2
        self.t = nc.alloc_sbuf_tensor("arena", [128, self.cap], BF16).ap()
        self.top = 0
        self.paged = False
        self.keymap = {}

    def pages(self, off, ne):
        return ["pg%d" % i for i in range(off // 256, (off + ne - 1) // 256 + 1)]

    def alloc(self, free_shape, dtype, key=None):
        n = 1
        for v in free_shape:
            n *= v
        ne = n * (2 if dtype == F32 else 1)
        ne_al = (ne + 255) // 256 * 256 if self.paged else (ne + 31) // 32 * 32
        off = self.top
        self.top += ne_al
        assert self.top <= self.cap, ("arena overflow", self.top * 2, self.cap * 2)
        self.last = (off, ne)
        if key is not None and self.paged:
            for k in ([key] if isinstance(key, str) else key):
                self.keymap[k] = self.pages(off, ne)
        v = self.t[:, off:off + ne]
        if dtype == F32:
            v = v.bitcast(F32)
        if len(free_shape) > 1:
            names = " ".join("a%d" % i for i in range(len(free_shape)))
            kw = {"a%d" % i: free_shape[i] for i in range(len(free_shape))}
            v = v.rearrange("p (%s) -> p %s" % (names, names), **kw)
        return v


class Builder:
    def __init__(self, nseq, nlayers, layer0=0, dbg=False):
        self.nseq = nseq
        self.nlayers = nlayers
        self.dbg = dbg
        nc = bass.Bass("TRN2", target_bir_lowering=False)
        self.nc = nc
        P = Prog(nc)
        self.P = P
        L = nlayers
        dt = nc.dram_tensor
        self.x = dt("x", [nseq * S, D], F32, kind="ExternalInput").ap()
        self.out = dt("out", [nseq * S, D], F32, kind="ExternalOutput").ap()
        self.w_in = dt("w_in", [L, D, N_IN], F32, kind="ExternalInput").ap()
        self.w_uq = dt("w_uq", [L, 256, 768], F32, kind="ExternalInput").ap()
        self.w_uq_sw = dt("w_uq_sw", [L, 256, 768], F32, kind="ExternalInput").ap()
        self.w_ukv_k = dt("w_ukv_k", [L, 128, 512], F32, kind="ExternalInput").ap()
        self.w_ukv_v = dt("w_ukv_v", [L, 128, 512], F32, kind="ExternalInput").ap()
        self.w_kpe = dt("w_kpe", [L, D, 96], F32, kind="ExternalInput").ap()
        self.w_kpe_sw = dt("w_kpe_sw", [L, D, 96], F32, kind="ExternalInput").ap()
        self.w_out = [dt("w_out_%s" % n, [L, 512, D], F32, kind="ExternalInput").ap() for n in "abc"]
        self.w_o = dt("w_o", [L, D, D], F32, kind="ExternalInput").ap()
        self.colv_d = dt("colv", [L, 128, NCV], F32, kind="ExternalInput").ap()
        self.gbc_d = dt("gbc", [L, 128, D], F32, kind="ExternalInput").ap()
        self.ident_d = dt("ident", [128, 128], F32, kind="ExternalInput").ap()
        self.ind2_d = dt("ind2", [128, 128], F32, kind="ExternalInput").ap()
        self.emask_d = dt("emask", [128, 25 * 256], F32, kind="ExternalInput").ap()
        self.ct_d = dt("ctab", [96, S], F32, kind="ExternalInput").ap()
        self.st_d = dt("stab", [96, S], F32, kind="ExternalInput").ap()
        if nlayers > 1:
            self.scr = dt("scr", [nseq * S, D], F32).ap()
        self.bsc = dt("bsc", [4, S], F32).ap()
        self.bsi = 0
        if dbg:
            self.dbg_h = dt("dbg_h", [128, 8 * S], BF16, kind="ExternalOutput").ap()
            self.dbg_y = dt("dbg_y", [128, 12 * S], BF16, kind="ExternalOutput").ap()
        self.ps = [nc.alloc_psum_tensor("ps%d" % i, [128, 512], F32).ap() for i in range(8)]
        A = Arena(nc, 212480)
        self.A = A
        self.hT = A.alloc([8, S], BF16)
        self.yT = A.alloc([12, S], BF16)
        self.ident = A.alloc([128], BF16)
        self.ones = A.alloc([128], BF16)
        self.ind2 = A.alloc([128], BF16)
        self.onesf = A.alloc([64], F32)
        self.emask = A.alloc([25, 256], BF16)
        self.colv = A.alloc([NCV], F32)
        self.epsc = A.alloc([1], F32)
        self.NWR = 8
        self.wr = [A.alloc([8, 128], BF16) for _ in range(self.NWR)]
        self.wrc = [P.chan("wr%d" % i) for i in range(self.NWR)]
        self.wri = 0
        A.top = (A.top + 255) // 256 * 256
        self.stage_base = A.top
        A.paged = True
        A.keymap = P.keymap
        self.cst = P.chan("const")
        self.cout = P.chan("out")
        self.cmisc = {}
        self.build()

    def misc_chan(self, key):
        if key not in self.cmisc:
            self.cmisc[key] = self.P.chan("m%d" % len(self.cmisc))
        return self.cmisc[key]

    def dma(self, q, out, in_, key_w=None, key_r=None, chan=None):
        if chan is None:
            chan = self.misc_chan(key_w if key_w is not None else key_r)
        nbytes = out.free_size() * out.partition_size() * 4
        self.P.add(q, lambda h: h.dma_start(out=out, in_=in_),
                   reads=[key_r] if key_r else [], writes=[key_w] if key_w else [], chan=chan, dur=nbytes / 150e3)

    def mm(self, out, lhsT, rhs, start, stop, r, w):
        self.P.add("pe", lambda h: h.matmul(out, lhsT=lhsT, rhs=rhs, start=start, stop=stop, skip_group_check=True), r, w,
                   dur=(0.03 + out.free_size() * 0.00043) * (4.0 if lhsT.dtype == F32 else 1.0))

    def tr(self, out, in_, r, w):
        ident = self.ident
        self.P.add("pe", lambda h: h.transpose(out, in_, ident), list(r) + ["const"], w, dur=0.1)

    def act(self, out, in_, func, r, w, scale=None, bias=None, accum=None):
        kw = {}
        if scale is not None:
            kw["scale"] = scale
        if bias is not None:
            kw["bias"] = bias
        if accum is not None:
            kw["accum_out"] = accum
        self.P.add("act", lambda h: h.activation(out=out, in_=in_, func=func, **kw), r, w,
                   dur=0.2 + out.free_size() * 0.0009, tset=ACT_SET.get(func.name))

    def _dd(self, out, eng="dve"):
        if eng == "pool":
            return 0.25 + out.free_size() * (0.0021 if out.dtype == F32 else 0.0009)
        return 0.12 + out.free_size() * (0.0015 if out.dtype == F32 else 0.0009)

    def tt(self, out, in0, in1, op, r, w, eng="dve"):
        self.P.add(eng, lambda h: h.tensor_tensor(out=out, in0=in0, in1=in1, op=op), r, w, dur=self._dd(out, eng))

    def stt(self, out, in0, scalar, in1, op0, op1, r, w, eng="dve"):
        self.P.add(eng, lambda h: h.scalar_tensor_tensor(out=out, in0=in0, scalar=scalar, in1=in1, op0=op0, op1=op1), r, w,
                   dur=self._dd(out, eng))

    def ts(self, out, in0, s1, s2, op0, op1, r, w, eng="dve"):
        if s2 is None:
            self.P.add(eng, lambda h: h.tensor_scalar(out=out, in0=in0, scalar1=s1, scalar2=None, op0=op0), r, w,
                       dur=self._dd(out, eng))
        else:
            self.P.add(eng, lambda h: h.tensor_scalar(out=out, in0=in0, scalar1=s1, scalar2=s2, op0=op0, op1=op1), r, w,
                       dur=self._dd(out, eng))

    def recip(self, out, in_, r, w):
        self.P.add("dve", lambda h: h.reciprocal(out=out, in_=in_), r, w, dur=self._dd(out))

    def cpy(self, out, in_, r, w, eng="dve"):
        self.P.add(eng, lambda h: h.tensor_copy(out=out, in_=in_), r, w, dur=self._dd(out))

    def memset(self, ap, val, w, eng="dve"):
        self.P.add(eng, lambda h: h.memset(ap, val), [], w, dur=self._dd(ap))

    def bcast_rows(self, src_row, key_src, dsts):
        from concourse.ap import AP
        i = self.bsi % 4
        self.bsi += 1
        n = src_row.free_size()
        self.dma("sp", self.bsc[i:i + 1, 0:n], src_row, key_w="bsc%d" % i, key_r=key_src)
        for (dst, kdst, c0, cn) in dsts:
            row = self.bsc[i:i + 1, c0:c0 + cn]
            src = AP(row.tensor, row.offset, [[0, 64], [1, cn]])
            self.dma("sp", dst, src, key_w=kdst, key_r="bsc%d" % i)

    def ldw(self, src):
        i = self.wri % self.NWR
        self.wri += 1
        self.dma("pool", self.wr[i], src, key_w="wr%d" % i, chan=self.wrc[i])
        return self.wr[i], "wr%d" % i

    def win_cols(self, l, c0, n):
        return self.w_in[l, :, c0:c0 + n].rearrange("(k p) n -> p k n", p=128)

    def rstd_from(self, ssq_ps, pkey, rows, inv_n, sd, rs):
        self.act(sd[0:rows], ssq_ps[0:rows], AF.Sqrt, ["const"], ["sd", pkey], scale=inv_n, bias=self.epsc[0:rows, 0:1])
        self.recip(rs[0:rows], sd[0:rows], ["sd"], ["rs"])

    def build(self):
        P = self.P
        self.dma("pool", self.ident, self.ident_d, key_w="const", chan=self.cst)
        self.dma("pool", self.ind2, self.ind2_d, key_w="const", chan=self.cst)
        self.dma("pool", self.emask, self.emask_d.rearrange("p (a b) -> p a b", a=25), key_w="const", chan=self.cst)
        self.memset(self.ones, 1.0, ["const"])
        self.memset(self.onesf, 1.0, ["const"])
        self.memset(self.epsc, EPS, ["const"])
        for l in range(self.nlayers):
            src = self.x if l == 0 else self.scr
            dst = self.out if l == self.nlayers - 1 else self.scr
            self.dma("sp", self.colv, self.colv_d[l], key_w="colv")
            for seq in range(self.nseq):
                self.A.top = self.stage_base
                self.stage0(l, seq, src)
                if self.dbg and l == 0 and seq == 0:
                    self.dma("sp", self.dbg_h.rearrange("p (a b) -> p a b", a=8), self.hT, key_r="hT3", chan=self.cout)
                self.A.top = self.stage_base
                self.stage_mla(l, seq)
                self.A.top = self.stage_base
                self.stage_dil(l, seq)
                self.A.top = self.stage_base
                self.stage_conv(l, seq)
                if self.dbg and l == 0 and seq == 0:
                    self.P.add("sp", lambda h: h.dma_start(out=self.dbg_y.rearrange("p (a b) -> p a b", a=12), in_=self.yT), reads=["Y%d_%d" % (c_, t_) for c_ in range(12) for t_ in range(4)], chan=self.cout)
                self.A.top = self.stage_base
                self.stage_out(l, seq, src, dst, final=(l == self.nlayers - 1))
        P.emit(final_chans=[self.cout] + ([self.cmisc["scrw"]] if "scrw" in self.cmisc else []))

    def stage0(self, l, seq, src):
        A = self.A
        NBUF = 4
        xb = [A.alloc([D], F32, key="xb%d" % i) for i in range(NBUF)]
        hn = [A.alloc([D], BF16, key="hn%d" % i) for i in range(NBUF)]
        junk = A.alloc([D], BF16, key="junk")
        gbc = A.alloc([D], F32, key="s0gbc")
        r_ss = self.ring(NBUF, [1], F32, "ss")
        r_sd = self.ring(NBUF, [1], F32, "sd1")
        r_rs = self.ring(NBUF, [1], F32, "rs1")
        self.dma("sp", gbc, self.gbc_d[l], key_w="s0gbc")
        for b in range(NB):
            i = b % NBUF
            r0 = seq * S + b * 128
            ss, kss = r_ss()
            sd1, ksd = r_sd()
            rs1, krs = r_rs()
            self.dma("sp", xb[i], src[r0:r0 + 128, :], key_w="xb%d" % i, key_r=("scr" if l > 0 else None))
            self.memset(ss, 0.0, [kss])
            self.act(junk, xb[i], AF.Square, ["xb%d" % i], ["junk", kss], accum=ss)
            self.act(sd1, ss, AF.Ln, [kss, "const"], [ksd], scale=1.0 / D, bias=self.epsc[:, 0:1])
            self.act(rs1, sd1, AF.Exp, [ksd], [krs], scale=-0.5)
            self.stt(hn[i], xb[i], rs1[:, 0:1], gbc, ALU.mult, ALU.mult, ["xb%d" % i, krs, "s0gbc"], ["hn%d" % i])
            pb = b % 8
            pst = self.ps[pb].bitcast(BF16)
            for k in range(8):
                self.tr(pst[:, k * 128:(k + 1) * 128], hn[i][:, k * 128:(k + 1) * 128], ["hn%d" % i], ["ps%d" % pb])
            if b % 2 == 0:
                self.act(self.hT[:, :, b * 128:(b + 1) * 128], pst.rearrange("p (k n) -> p k n", k=8), AF.Copy,
                         [], ["ps%d" % pb, "hT%d" % (b // 4)])
            else:
                self.cpy(self.hT[:, :, b * 128:(b + 1) * 128], pst.rearrange("p (k n) -> p k n", k=8), [], ["ps%d" % pb, "hT%d" % (b // 4)])

    def ring(self, n, shape, dtype, name):
        aps = [self.A.alloc(shape, dtype, key="%s%d" % (name, i)) for i in range(n)]
        st = {"i": 0}

        def nxt():
            i = st["i"] % n
            st["i"] += 1
            return aps[i], "%s%d" % (name, i)
        return nxt

    def psring(self, banks):
        st = {"i": 0}

        def nxt():
            b = banks[st["i"] % len(banks)]
            st["i"] += 1
            return self.ps[b], "ps%d" % b
        return nxt

    def stage_mla(self, l, seq):
        A = self.A
        cv = self.colv
        Kc = self.yT[:, 4:12, :]
        KK = ["yT_c", "yT_a"]
        Va = A.alloc([NB, 8, 65], BF16, key="Va")
        for t_ in range(NT):
            self.P.keymap["Va%d" % t_] = A.pages(A.last[0] + t_ * 4 * 8 * 65, 4 * 8 * 65)
        qT = A.alloc([8, TT], BF16, key=["qT%d" % h_ for h_ in range(8)])
        cqn = A.alloc([2, TT], BF16, key="cqn")
        ckvn = A.alloc([TT], BF16, key="ckvn")
        sqpe = A.alloc([TT], BF16, key="sqpe")
        kr = A.alloc([TT], F32, key="kr")
        ctt = A.alloc([TT], F32, key="ctt")
        stt_ = A.alloc([TT], F32, key="stt")
        cqf = A.alloc([TT], F32, key="cqf")
        sqf = A.alloc([TT], F32, key="sqf")
        ckf = A.alloc([TT], F32, key="ckf")
        skf = A.alloc([TT], F32, key="skf")
        sbz = A.alloc([4, TT], BF16, key="sbz")
        r_sq = self.ring(2, [TT], BF16, "sq")
        r_sd = self.ring(2, [TT], F32, "sd")
        r_rs = self.ring(2, [TT], F32, "rs")
        r_ta = self.ring(2, [TT], F32, "ta")
        r_tb = self.ring(2, [TT], F32, "tb")
        r_pt = self.ring(3, [TT], BF16, "PT")
        r_osb = self.ring(2, [TT], F32, "osb")
        r_rb = self.ring(2, [TT], F32, "rb")
        Wlat = A.alloc([8, 384], BF16, key="Wlat")
        Wkpe = A.alloc([8, 96], BF16, key="Wkpe")
        Wkpes = A.alloc([8, 96], BF16, key="Wkpes")
        Wuq = A.alloc([2, 768], BF16, key="Wuq")
        Wuqs = A.alloc([2, 768], BF16, key="Wuqs")
        Wk = A.alloc([512], BF16, key="Wk")
        Wv = A.alloc([512], BF16, key="Wv")
        cfg = os.environ.get("KMLA", "a")
        if cfg == "a":
            pP = self.psring([0, 1, 2])
            pSS = self.psring([3, 4])
            pS_ = self.psring([6, 7])
            pO_ = self.psring([5])
        elif cfg == "b":
            pP = self.psring([0, 1])
            pSS = self.psring([2, 3])
            pS_ = self.psring([4, 6, 7])
            pO_ = self.psring([5])
        else:
            pP = self.psring([0, 1])
            pSS = self.psring([2])
            pS_ = self.psring([3, 6, 7])
            pO_ = self.psring([4, 5])
        hT = self.hT
        self.dma("pool", Wlat, self.win_cols(l, C_CQ, 384), key_w="Wlat")
        self.dma("pool", Wkpe, self.w_kpe[l].rearrange("(k p) n -> p k n", p=128), key_w="Wkpe")
        self.dma("pool", Wkpes, self.w_kpe_sw[l].rearrange("(k p) n -> p k n", p=128), key_w="Wkpes")
        self.dma("pool", Wuq, self.w_uq[l].rearrange("(k p) n -> p k n", p=128), key_w="Wuq")
        self.dma("pool", Wuqs, self.w_uq_sw[l].rearrange("(k p) n -> p k n", p=128), key_w="Wuqs")
        self.dma("pool", Wk, self.w_ukv_k[l], key_w="Wk")
        self.dma("pool", Wv, self.w_ukv_v[l], key_w="Wv")
        self.memset(Va[:, :, :, 64:65], 1.0, ["Va"])
        sc_mla = 96.0 ** -0.5
        R = slice(64, 96)

        def rstd(ssq, kss, rows, inv_n):
            sd, ksd = r_sd()
            rs, krs = r_rs()
            self.act(sd[0:rows], ssq[0:rows], AF.Ln, ["const"], [ksd, kss], scale=inv_n, bias=self.epsc[0:rows, 0:1])
            self.act(rs[0:rows], sd[0:rows], AF.Exp, [ksd], [krs], scale=-0.5)
            return rs, krs

        for t in range(NT):
            T0 = t * TT
            hs = lambda k: hT[:, k, T0:T0 + TT]
            wbz = [self.ldw(self.win_cols(l, C_BZ + c * 128, 128)) for c in range(4)]
            self.dma("sp", ctt[0:96], self.ct_d[:, T0:T0 + TT], key_w="ctt")
            self.dma("sp", stt_[0:96], self.st_d[:, T0:T0 + TT], key_w="stt")
            self.ts(cqf[0:96], ctt[0:96], cv[0:96, CV_GQM:CV_GQM + 1], None, ALU.mult, None, ["ctt", "colv"], ["cqf"])
            self.ts(sqf[0:96], stt_[0:96], cv[0:96, CV_GQMS:CV_GQMS + 1], None, ALU.mult, None, ["stt", "colv"], ["sqf"])
            self.ts(ckf[0:96], ctt[0:96], cv[0:96, CV_GKM:CV_GKM + 1], None, ALU.mult, None, ["ctt", "colv"], ["ckf"])
            self.ts(skf[0:96], stt_[0:96], cv[0:96, CV_GKMS:CV_GKMS + 1], None, ALU.mult, None, ["stt", "colv"], ["skf"])
            pcq = [pP(), pP()]
            sqs = []
            for j in range(2):
                p_, k_ = pcq[j]
                for k in range(8):
                    self.mm(p_, Wlat[:, k, j * 128:(j + 1) * 128], hs(k), k == 0, k == 7, ["Wlat", "hT%d" % t], [k_])
                sq, ksq = r_sq()
                self.act(sq, p_, AF.Square, [], [ksq, k_])
                sqs.append((sq, ksq))
            ss, kss = pSS()
            self.mm(ss, self.ones, sqs[0][0], True, False, ["const", sqs[0][1]], [kss])
            self.mm(ss, self.ones, sqs[1][0], False, True, ["const", sqs[1][1]], [kss])
            rs, krs = rstd(ss, kss, 128, 1.0 / 256)
            for j in range(2):
                p_, k_ = pcq[j]
                self.stt(cqn[:, j, :], p_, cv[:, CV_GQ + j:CV_GQ + j + 1], rs, ALU.mult, ALU.mult, ["colv", krs], ["cqn", k_])
            p_, k_ = pP()
            for k in range(8):
                self.mm(p_, Wlat[:, k, 256:384], hs(k), k == 0, k == 7, ["Wlat", "hT%d" % t], [k_])
            sq, ksq = r_sq()
            self.act(sq, p_, AF.Square, [], [ksq, k_])
            ss, kss = pSS()
            self.mm(ss, self.ones, sq, True, True, ["const", ksq], [kss])
            rs, krs = rstd(ss, kss, 128, 1.0 / 128)
            self.stt(ckvn, p_, cv[:, CV_GKV:CV_GKV + 1], rs, ALU.mult, ALU.mult, ["colv", krs], ["ckvn", k_])
            p3, k3 = pP()
            for k in range(8):
                self.mm(p3[0:96], Wkpe[:, k, :], hs(k), k == 0, k == 7, ["Wkpe", "hT%d" % t], [k3])
            self.act(sqpe[R], p3[R], AF.Square, [], ["sqpe", k3])
            self.tt(kr[R], p3[R], ckf[R], ALU.mult, ["ckf"], ["kr", k3])
            p4, k4 = pP()
            for k in range(8):
                self.mm(p4[0:96], Wkpes[:, k, :], hs(k), k == 0, k == 7, ["Wkpes", "hT%d" % t], [k4])
            tb, ktb = r_tb()
            self.tt(tb[R], p4[R], skf[R], ALU.mult, ["skf"], [ktb, k4])
            self.tt(kr[R], kr[R], tb[R], ALU.add, [ktb, "kr"], ["kr"], eng="pool")
            for bb in range(4):
                blk = t * 4 + bb
                pv, pk = pP()
                self.mm(pv, ckvn[:, bb * 128:(bb + 1) * 128], Wv, True, True, ["ckvn", "Wv"], [pk])
                self.cpy(Va[:, blk, :, 0:64], pv.rearrange("p (h c) -> p h c", h=8), ["Va"], ["Va%d" % t, pk])
            for h in range(8):
                pK, pk = pP()
                self.mm(pK[0:64], Wk[:, h * 64:(h + 1) * 64], ckvn, True, True, ["Wk", "ckvn"], [pk])
                sq, ksq = r_sq()
                self.act(sq[0:64], pK[0:64], AF.Square, [], [ksq, pk])
                ss, kss = pSS()
                self.mm(ss[0:96], self.ones[0:64, 0:96], sq[0:64], True, False, ["const", ksq], [kss])
                self.mm(ss[0:96], self.ones[64:96, 0:96], sqpe[R], False, True, ["const", "sqpe"], [kss])
                rs, krs = rstd(ss, kss, 96, 1.0 / 96)
                self.stt(Kc[0:64, h, T0:T0 + TT], pK[0:64], cv[0:64, CV_GKM:CV_GKM + 1], rs[0:64], ALU.mult, ALU.mult,
                         ["colv", krs], ["Y%d_%d" % (4 + h, t), pk])
                self.tt(Kc[R, h, T0:T0 + TT], kr[R], rs[R], ALU.mult, ["kr", krs], ["Y%d_%d" % (4 + h, t)])
            for c in range(4):
                wt, wk = wbz[c]
                pb, pk = pP()
                for k in range(8):
                    self.mm(pb, wt[:, k, :], hs(k), k == 0, k == 7, [wk, "hT%d" % t], [pk])
                self.act(sbz[:, c, :], pb, AF.Silu, [], ["sbz", pk])
            for h in range(8):
                pQ, kq = pP()
                for j in range(2):
                    self.mm(pQ[0:96], Wuq[:, j, h * 96:(h + 1) * 96], cqn[:, j, :], j == 0, j == 1, ["Wuq", "cqn"], [kq])
                pQs, kqs = pP()
                for j in range(2):
                    self.mm(pQs[0:96], Wuqs[:, j, h * 96:(h + 1) * 96], cqn[:, j, :], j == 0, j == 1, ["Wuqs", "cqn"], [kqs])
                sq, ksq = r_sq()
                self.act(sq[0:96], pQ[0:96], AF.Square, [], [ksq, kq])
                ss, kss = pSS()
                self.mm(ss[0:96], self.ones[0:96, 0:96], sq[0:96], True, True, ["const", ksq], [kss])
                rs, krs = rstd(ss, kss, 96, 1.0 / 96)
                ta, kta = r_ta()
                tb, ktb = r_tb()
                self.tt(ta[0:96], pQ[0:96], cqf[0:96], ALU.mult, ["cqf"], [kta, kq])
                self.tt(tb[R], pQs[R], sqf[R], ALU.mult, ["sqf"], [ktb, kqs])
                self.tt(ta[R], ta[R], tb[R], ALU.add, [ktb, kta], [kta], eng="pool")
                self.tt(qT[0:96, h, :], ta[0:96], rs[0:96], ALU.mult, [kta, krs], ["qT%d" % h])
            for h in range(8):
                nkb = 4 * t + 4
                pO, ko = pO_()
                for kb in range(nkb):
                    c0 = 0 if kb < 4 * t else (kb - 4 * t) * 128
                    pS, ks = pS_()
                    pt, kpt = r_pt()
                    self.mm(pS[:, c0:TT], Kc[0:96, h, kb * 128:(kb + 1) * 128], qT[0:96, h, c0:TT], True, True,
                            ["Y%d_%d" % (4 + h, kb // 4), "qT%d" % h], [ks])
                    self.act(pt[:, c0:TT], pS[:, c0:TT], AF.Exp, [], [kpt, ks], scale=sc_mla)
                    if kb >= 4 * t:
                        self.tt(pt[:, c0:c0 + 128], pt[:, c0:c0 + 128], self.emask[:, 24, 0:128], ALU.mult, ["const", kpt], [kpt], eng="pool")
                    self.mm(pO[0:65, c0:TT], Va[:, kb, h, :], pt[:, c0:TT], kb == 0, kb == nkb - 1, ["Va", "Va%d" % (kb // 4), kpt], [ko])
                osb, kos = r_osb()
                self.cpy(osb[0:65], pO[0:65], [], [kos, ko])
                self.act(osb[64:65], osb[64:65], AF.Ln, [kos], [kos])
                self.act(osb[64:65], osb[64:65], AF.Exp, [kos], [kos], scale=-1.0)
                rb, krb = r_rb()
                self.bcast_rows(osb[64:65, :], kos, [(rb[0:64], krb, 0, TT)])
                ta, kta = r_ta()
                Rh = slice(0, 64) if h % 2 == 0 else slice(64, 128)
                self.tt(ta[Rh], osb[0:64], rb[0:64], ALU.mult, [kos, krb], [kta])
                self.tt(self.yT[Rh, h // 2, T0:T0 + TT], ta[Rh], sbz[Rh, h // 2, :], ALU.mult, [kta, "sbz"], ["Y%d_%d" % (h // 2, t)])

    def stage_dil(self, l, seq):
        A = self.A
        cv = self.colv
        hT = self.hT
        r_dq = self.ring(2, [S], BF16, "dqT")
        r_dk = self.ring(2, [S], BF16, "dkT")
        r_dv = self.ring(2, [S], BF16, "dvT")
        r_dva = self.ring(2, [NB, 2, 65], BF16, "dva")
        r_acc = self.ring(2, [2, S], F32, "acc")
        r_scz = self.ring(2, [S], BF16, "scz")
        r_sq = self.ring(2, [TT], BF16, "sq")
        r_sd = self.ring(2, [TT], F32, "sd")
        r_rs = self.ring(2, [TT], F32, "rs")
        r_ta = self.ring(2, [TT], F32, "ta")
        r_pt = self.ring(4, [256], BF16, "dPT")
        r_rb = self.ring(4, [TT], F32, "rb")
        cfg = os.environ.get("KDIL", "a")
        if cfg == "a":
            pP = self.psring([2, 3, 4, 5])
            pS_ = self.psring([6, 7])
            pO_ = self.psring([0, 1])
        else:
            pP = self.psring([3, 4, 5])
            pS_ = self.psring([2, 6, 7])
            pO_ = self.psring([0, 1])
        for _ in range(2):
            dva, kdva = r_dva()
            self.memset(dva[:, :, :, 64:65], 1.0, [kdva])

        def rstd(ssq, kss, rows, inv_n):
            sd, ksd = r_sd()
            rs, krs = r_rs()
            self.act(sd[0:rows], ssq[0:rows], AF.Ln, ["const"], [ksd, kss], scale=inv_n, bias=self.epsc[0:rows, 0:1])
            self.act(rs[0:rows], sd[0:rows], AF.Exp, [ksd], [krs], scale=-0.5)
            return rs, krs

        def perm_out(buf, g, t):
            if g == 0:
                return buf[:, t * TT:(t + 1) * TT], None
            if g == 1:
                return buf.rearrange("p (r m i) -> p r m i", r=4, m=4)[:, :, t, :], ("p (i r) -> p r i", 4)
            return buf.rearrange("p (r i) -> p r i", r=16)[:, :, 32 * t:32 * t + 32], ("p (i r) -> p r i", 16)

        def segments(g):
            segs = []
            for s_ in range(4):
                u = []
                if g == 0:
                    if s_ > 0:
                        u.append(((4 * s_ - 1) * 128, 4 * s_ - 1, 4 * s_ * 128, 128, 128))
                    for kb in range(4 * s_, 4 * s_ + 4):
                        u.append((kb * 128, kb, kb * 128, 256 if kb < 4 * s_ + 3 else 128, 0))
                elif g == 1:
                    for m in range(4):
                        blk = 4 * s_ + m
                        u.append((blk * 128, blk, blk * 128, 256 if m < 3 else 128, 0))
                else:
                    for c in range(4):
                        blk = 4 * s_ + c
                        u.append((blk * 128, blk, blk * 128, 128, 0))
                segs.append(u)
            return segs

        for j in range(4):
            scz, kscz = r_scz()
            acc, kacc = r_acc()
            wcz, kcz = self.ldw(self.win_cols(l, C_CZ + j * 128, 128))
            for t in range(NT):
                pc, kpc = pP()
                for k in range(8):
                    self.mm(pc, wcz[:, k, :], hT[:, k, t * TT:(t + 1) * TT], k == 0, k == 7, [kcz, "hT%d" % t], [kpc])
                self.act(scz[:, t * TT:(t + 1) * TT], pc, AF.Silu, [], [kscz, kpc])
            for g in range(3):
                c_off = g * 512 + j * 128
                wq, kwq = self.ldw(self.win_cols(l, C_DQ + c_off, 128))
                wk_, kwk = self.ldw(self.win_cols(l, C_DK + c_off, 128))
                wv, kwv = self.ldw(self.win_cols(l, C_DV + c_off, 128))
                dqT, kdq = r_dq()
                dkT, kdk = r_dk()
                dvT, kdv = r_dv()
                dva, kdva = r_dva()
                for t in range(NT):
                    hs = lambda k: hT[:, k, t * TT:(t + 1) * TT]
                    for (wt, wkey, dst, dkey, gcol) in ((wq, kwq, dqT, kdq, CV_GDQ + g), (wk_, kwk, dkT, kdk, CV_GDK + g)):
                        pp, pk = pP()
                        for k in range(8):
                            self.mm(pp, wt[:, k, :], hs(k), k == 0, k == 7, [wkey, "hT%d" % t], [pk])
                        sq, ksq = r_sq()
                        self.act(sq, pp, AF.Square, [], [ksq, pk])
                        ss, kss = pP()
                        self.mm(ss, self.ind2, sq, True, True, ["const", ksq], [kss])
                        rs, krs = rstd(ss, kss, 128, 1.0 / 64)
                        ov, pr = perm_out(dst, g, t)
                        if pr is None:
                            self.stt(ov, pp, cv[:, gcol:gcol + 1], rs, ALU.mult, ALU.mult, ["colv", krs], [dkey, pk])
                        else:
                            self.stt(ov, pp.rearrange(pr[0], r=pr[1]), cv[:, gcol:gcol + 1], rs.rearrange(pr[0], r=pr[1]),
                                     ALU.mult, ALU.mult, ["colv", krs], [dkey, pk])
                    pp, pk = pP()
                    for k in range(8):
                        self.mm(pp, wv[:, k, :], hs(k), k == 0, k == 7, [kwv, "hT%d" % t], [pk])
                    ov, pr = perm_out(dvT, g, t)
                    if pr is None:
                        self.cpy(ov, pp, [], [kdv, pk])
                    else:
                        self.cpy(ov, pp.rearrange(pr[0], r=pr[1]), [], [kdv, pk])
                for half in range(2):
                    pp, pk = pP()
                    pst = pp.bitcast(BF16)
                    for bb in range(8):
                        blk = half * 8 + bb
                        self.tr(pst[:, bb * 128:(bb + 1) * 128], dvT[:, blk * 128:(blk + 1) * 128], [kdv], [pk])
                    self.act(dva[:, half * 8:(half + 1) * 8, :, 0:64],
                             pst.rearrange("p (b h c) -> p b h c", b=8, h=2), AF.Copy, [], [kdva, pk])
                segs = segments(g)
                for hh in range(2):
                    Rr = slice(hh * 64, hh * 64 + 64)
                    em = self.emask[:, g * 8 + 2 * j + hh, :]
                    for b, units in enumerate(segs):
                        pO, ko = pO_()
                        first = True
                        for (K0, blk, q0, nq, m0) in units:
                            pS, ks = pS_()
                            pt, kpt = r_pt()
                            self.mm(pS[:, 0:nq], dkT[Rr, K0:K0 + 128], dqT[Rr, q0:q0 + nq], True, True, [kdk, kdq], [ks])
                            self.act(pt[:, 0:nq], pS[:, 0:nq], AF.Exp, [], [kpt, ks], scale=0.125)
                            self.tt(pt[:, 0:nq], pt[:, 0:nq], em[:, m0:m0 + nq], ALU.mult, ["const", kpt], [kpt], eng=("pool" if g == 2 else "dve"))
                            self.mm(pO[0:65, q0 - b * 512:q0 - b * 512 + nq], dva[:, blk, hh, :], pt[:, 0:nq], first, False,
                                    [kdva, kpt], [ko])
                            first = False
                        if g == 0:
                            self.act(acc[0:65, hh, b * 512:(b + 1) * 512], pO[0:65], AF.Copy, [], [kacc, ko])
                        elif g == 1:
                            av = acc[0:65, hh, :].rearrange("p (m i r) -> p r m i", m=4, r=4)[:, b]
                            self.tt(av, av, pO[0:65].rearrange("p (m i) -> p m i", m=4), ALU.add, [kacc], [kacc, ko])
                        else:
                            av = acc[0:65, hh, :].rearrange("p (i r) -> p r i", r=16)[:, 4 * b:4 * b + 4, :]
                            self.tt(av, av, pO[0:65].rearrange("p (r i) -> p r i", r=4), ALU.add, [kacc], [kacc, ko])
            for hh in range(2):
                Rh = slice(hh * 64, hh * 64 + 64)
                self.act(acc[64:65, hh, :], acc[64:65, hh, :], AF.Ln, [kacc], [kacc])
                self.act(acc[64:65, hh, :], acc[64:65, hh, :], AF.Exp, [kacc], [kacc], scale=-1.0)
                rbs = [r_rb() for _ in range(NT)]
                self.bcast_rows(acc[64:65, hh, :], kacc, [(rbs[t][0][0:64], rbs[t][1], t * TT, TT) for t in range(NT)])
                for t in range(NT):
                    cs = slice(t * TT, (t + 1) * TT)
                    rb, krb = rbs[t]
                    ta, kta = r_ta()
                    self.tt(ta[Rh], acc[0:64, hh, cs], rb[0:64], ALU.mult, [kacc, krb], [kta])
                    self.tt(self.yT[Rh, 4 + j, cs], ta[Rh], scz[Rh, cs], ALU.mult, [kta, kscz], ["Y%d_%d" % (4 + j, t)])

    def stage_conv(self, l, seq):
        A = self.A
        ps = self.ps
        cv = self.colv
        hT = self.hT
        r_up = self.ring(2, [S + 2], F32, "upad")
        r_acs = self.ring(2, [TT], F32, "acs")
        r_sz = self.ring(2, [TT], F32, "sz")
        r_c1 = self.ring(2, [TT], F32, "c1")
        r_c2 = self.ring(2, [TT], F32, "c2")
        r_t1 = self.ring(3, [TT], F32, "t1")
        n = 0
        for c in range(4):
            ws = [self.ldw(self.win_cols(l, base + c * 128, 128)) for base in (C_AB, C_AC, C_AX, C_AZ)]
            upad, kup = r_up()
            self.memset(upad[:, 0:2], 0.0, [kup])
            for t in range(NT):
                T0 = t * TT
                o = 4 * (n % 2)
                n += 1
                for i in range(4):
                    wt, wk = ws[i]
                    for k in range(8):
                        self.mm(ps[o + i], wt[:, k, :], hT[:, k, T0:T0 + TT], k == 0, k == 7, [wk, "hT%d" % t], ["ps%d" % (o + i)])
                kb_, kc_, kx_, kz_ = ["ps%d" % (o + i) for i in range(4)]
                acs, kacs = r_acs()
                sz, ksz = r_sz()
                c1, kc1 = r_c1()
                c2, kc2 = r_c2()
                self.act(acs, ps[o + 1], AF.Copy, [], [kacs, kc_])
                self.tt(upad[:, 2 + T0:2 + T0 + TT], acs, ps[o + 2], ALU.mult, [kacs], [kup, kx_])
                self.act(sz, ps[o + 3], AF.Silu, [], [ksz, kz_])
                t1, kt1 = r_t1()
                self.act(c1, upad[:, T0:T0 + TT], AF.Identity, [kup, "colv"], [kc1], scale=cv[:, CV_CW + c:CV_CW + c + 1],
                         bias=cv[:, CV_CB + c:CV_CB + c + 1])
                self.act(t1, upad[:, T0 + 1:T0 + 1 + TT], AF.Identity, [kup, "colv"], [kt1], scale=cv[:, CV_CW + 4 + c:CV_CW + 5 + c])
                self.tt(c1, c1, t1, ALU.add, [kc1, kt1], [kc1], eng="pool")
                t1, kt1 = r_t1()
                self.act(t1, upad[:, T0 + 2:T0 + 2 + TT], AF.Identity, [kup, "colv"], [kt1], scale=cv[:, CV_CW + 8 + c:CV_CW + 9 + c])
                self.tt(c1, c1, t1, ALU.add, [kc1, kt1], [kc1], eng="pool")
                self.tt(c2, c1, ps[o + 0], ALU.mult, [kc1], [kc2, kb_])
                self.tt(self.yT[:, 8 + c, T0:T0 + TT], c2, sz, ALU.mult, [kc2, ksz], ["Y%d_%d" % (8 + c, t)])

    def stage_out(self, l, seq, src, dst, final):
        A = self.A
        ps = self.ps
        cv = self.colv
        hT = self.hT
        yT = self.yT
        Wo3 = [A.alloc([4, D], BF16, key="Wo3_%d" % i) for i in range(3)]
        Wo = A.alloc([8, D], BF16, key="Wo")
        G = A.alloc([24, TT], BF16)
        for jj in range(24):
            self.P.keymap["G%d" % jj] = A.pages(A.last[0] + jj * TT, TT)
        mT = A.alloc([8, TT], BF16)
        for jj in range(8):
            self.P.keymap["mT%d" % jj] = A.pages(A.last[0] + jj * TT, TT)
        r_m1 = self.ring(2, [TT], F32, "m1")
        r_m2 = self.ring(2, [TT], F32, "m2")
        xb = [A.alloc([D], F32, key="oxb%d" % i) for i in range(2)]
        ob = [A.alloc([D], F32, key="ob%d" % i) for i in range(2)]
        for i in range(3):
            self.dma("pool", Wo3[i], self.w_out[i][l].rearrange("(k p) n -> p k n", p=128), key_w="Wo3_%d" % i)
        self.dma("pool", Wo, self.w_o[l].rearrange("(k p) n -> p k n", p=128), key_w="Wo")
        och = self.cout if final else self.misc_chan("scrw")
        pG = self.psring([0, 1])
        for t in range(NT):
            T0 = t * TT
            DEPTHW = 4
            wq = [self.ldw(self.win_cols(l, C_GATE + jj * 128, 128)) for jj in range(DEPTHW)]
            for jj in range(24):
                wt, wk = wq[jj]
                pg, kg = pG()
                for k in range(8):
                    self.mm(pg, wt[:, k, :], hT[:, k, T0:T0 + TT], k == 0, k == 7, [wk, "hT%d" % t], [kg])
                self.act(G[:, jj, :], pg, AF.Sigmoid, ["colv"], ["G%d" % jj, kg], bias=cv[:, CV_BG + jj:CV_BG + jj + 1])
                if jj + DEPTHW < 24:
                    wq.append(self.ldw(self.win_cols(l, C_GATE + (jj + DEPTHW) * 128, 128)))
            for j in range(8):
                o = 2 + 3 * (j % 2)
                for i in range(3):
                    for k in range(4):
                        self.mm(ps[o + i], Wo3[i][:, k, j * 128:(j + 1) * 128], yT[:, (8, 0, 4)[i] + k, T0:T0 + TT], k == 0, k == 3,
                                ["Wo3_%d" % i, "Y%d_%d" % ((8, 0, 4)[i] + k, t)], ["ps%d" % (o + i)])
                m1, k1 = r_m1()
                m2, k2 = r_m2()
                self.tt(m1, ps[o], G[:, j, :], ALU.mult, ["G%d" % j], [k1, "ps%d" % o])
                self.tt(m2, ps[o + 1], G[:, 8 + j, :], ALU.mult, ["G%d" % (8 + j)], [k2, "ps%d" % (o + 1)])
                self.tt(m1, m1, m2, ALU.add, [k1, k2], [k1], eng="pool")
                self.tt(m2, ps[o + 2], G[:, 16 + j, :], ALU.mult, ["G%d" % (16 + j)], [k2, "ps%d" % (o + 2)])
                self.tt(mT[:, j, :], m1, m2, ALU.add, [k1, k2], ["mT%d" % j], eng="pool")
            for bb in range(4):
                b = t * 4 + bb
                i = b % 2
                r0 = seq * S + b * 128
                self.dma("sp", xb[i], src[r0:r0 + 128, :], key_w="oxb%d" % i, key_r=("scr" if l > 0 else None))
                for half in range(2):
                    pg, kg = pG()
                    for k in range(8):
                        self.mm(pg, mT[:, k, bb * 128:(bb + 1) * 128], Wo[:, k, half * 512:(half + 1) * 512], k == 0, k == 7,
                                ["mT%d" % k, "Wo"], [kg])
                    self.tt(ob[i][:, half * 512:(half + 1) * 512], pg, xb[i][:, half * 512:(half + 1) * 512], ALU.add,
                            ["oxb%d" % i], ["ob%d" % i, kg])
                self.dma("sp", dst[r0:r0 + 128, :], ob[i], key_r="ob%d" % i, key_w=(None if final else "scr"), chan=och)


def _rope_tables():
    inv = (10000.0 ** (-np.arange(0, 32, 2, dtype=np.float32) / 32)).astype(np.float32)
    ang = np.arange(S, dtype=np.float32)[:, None] * inv[None, :]
    cos, sin = np.cos(ang).astype(np.float32), np.sin(ang).astype(np.float32)
    ct = np.ones((96, S), np.float32)
    st = np.zeros((96, S), np.float32)
    ct[64:80] = cos.T
    ct[80:96] = cos.T
    st[64:80] = -sin.T
    st[80:96] = sin.T
    return ct, st


def _emask():
    n = 24
    slopes = (2.0 ** (-8.0 * np.arange(1, n + 1, dtype=np.float32) / n)).reshape(3, 8)
    k = np.arange(128)[:, None].astype(np.float32)
    q = np.arange(128)[None, :].astype(np.float32)
    em = np.zeros((128, 25, 256), np.float32)
    for g in range(3):
        for h in range(8):
            sl = slopes[g, h] * DIL[g]
            em[:, g * 8 + h, 0:128] = np.where(k <= q, np.exp(-sl * np.maximum(q - k, 0.0)), 0.0)
            if g < 2:
                em[:, g * 8 + h, 128:256] = np.where(k >= q, np.exp(-sl * np.maximum(128.0 + q - k, 0.0)), 0.0)
    em[:, 24, 0:128] = (k <= q).astype(np.float32)
    return em.reshape(128, 25 * 256)


def _host_layout(inp, layers):
    f = lambda a: np.ascontiguousarray(a, dtype=np.float32)
    L = len(layers)
    w_uq = f(inp["w_uq"][layers])
    w_uq_sw = np.zeros_like(w_uq)
    for h in range(8):
        b = h * 96
        w_uq_sw[:, :, b + 64:b + 80] = w_uq[:, :, b + 80:b + 96]
        w_uq_sw[:, :, b + 80:b + 96] = w_uq[:, :, b + 64:b + 80]
    w_ukv = f(inp["w_ukv"][layers]).reshape(L, 128, 8, 128)
    w_ukv_k = f(w_ukv[:, :, :, 0:64].reshape(L, 128, 512))
    w_ukv_v = f(w_ukv[:, :, :, 64:128].reshape(L, 128, 512))
    w_in = inp["w_in"]
    w_kpe = np.zeros((L, D, 96), np.float32)
    w_kpe_sw = np.zeros((L, D, 96), np.float32)
    for i, l in enumerate(layers):
        kp = w_in[l][:, C_KPE:C_KPE + 32]
        w_kpe[i, :, 64:96] = kp
        w_kpe_sw[i, :, 64:80] = kp[:, 16:32]
        w_kpe_sw[i, :, 80:96] = kp[:, 0:16]
    colv = np.zeros((L, 128, NCV), np.float32)
    gbc = np.zeros((L, 128, D), np.float32)
    for i, l in enumerate(layers):
        colv[i, :, CV_BG:CV_BG + 24] = inp["b_gate"][l].reshape(24, 128).T
        for tap in range(3):
            colv[i, :, CV_CW + tap * 4:CV_CW + tap * 4 + 4] = inp["conv_w"][l][tap].reshape(4, 128).T
        colv[i, :, CV_CB:CV_CB + 4] = inp["conv_b"][l].reshape(4, 128).T
        colv[i, :, CV_GQ:CV_GQ + 2] = inp["q_a_norm_g"][l].reshape(2, 128).T
        colv[i, :, CV_GKV] = inp["kv_a_norm_g"][l]
        for (col, src) in ((CV_GQM, inp["mla_q_norm_g"][l]), (CV_GKM, inp["mla_k_norm_g"][l])):
            colv[i, 0:96, col] = src
            colv[i, 64:80, col + 1] = src[80:96]
            colv[i, 80:96, col + 1] = src[64:80]
        for g in range(3):
            colv[i, :, CV_GDQ + g] = np.tile(inp["dil_q_norm_g"][l][g], 2)
            colv[i, :, CV_GDK + g] = np.tile(inp["dil_k_norm_g"][l][g], 2)
        gbc[i] = np.broadcast_to(inp["norm_g"][l][None, :], (128, D))
    ind2 = np.zeros((128, 128), np.float32)
    ind2[0:64, 0:64] = 1.0
    ind2[64:128, 64:128] = 1.0
    ct, st = _rope_tables()
    shared = {
        "w_in": f(w_in[layers]), "w_uq": w_uq, "w_uq_sw": w_uq_sw, "w_ukv_k": w_ukv_k, "w_ukv_v": w_ukv_v,
        "w_kpe": w_kpe, "w_kpe_sw": w_kpe_sw,
        "w_out_a": f(inp["w_out_a"][layers]), "w_out_b": f(inp["w_out_b"][layers]), "w_out_c": f(inp["w_out_c"][layers]),
        "w_o": f(inp["w_o"][layers]), "colv": colv, "gbc": gbc,
        "ident": np.eye(128, dtype=np.float32), "ind2": ind2, "emask": _emask(), "ctab": ct, "stab": st,
    }
    return shared


_CACHE = {}


def _get_builder(nseq, nlayers, dbg=False):
    key = (nseq, nlayers, dbg)
    if key not in _CACHE:
        _CACHE[key] = Builder(nseq, nlayers, dbg=dbg)
    return _CACHE[key]


def kernel(**inputs):
    inp = {k: np.asarray(v) for k, v in inputs.items()}
    x = np.ascontiguousarray(inp["x"], dtype=np.float32)
    B = x.shape[0]
    ncores = 8
    nseq = B // ncores
    shared = _host_layout(inp, list(range(DEPTH)))
    bld = Builder(nseq, DEPTH)
    in_maps = []
    for c in range(ncores):
        m = dict(shared)
        m["x"] = x[c * nseq:(c + 1) * nseq].reshape(nseq * S, D)
        in_maps.append(m)
    res = run_bass_kernel_spmd(bld.nc, in_maps, core_ids=list(range(ncores)))
    outs = [np.asarray(r["out"]).reshape(nseq, S, D) for r in res.results]
    return np.concatenate(outs, axis=0).astype(np.float32)
```

```python
import numpy as np
import concourse.bass as bass
import concourse.mybir as mybir
from concourse.bass_utils import run_bass_kernel_spmd

F32 = mybir.dt.float32
BF16 = mybir.dt.bfloat16
AF = mybir.ActivationFunctionType
ALU = mybir.AluOpType

ENGS = ("pe", "act", "dve", "pool", "sp")

D = 1024
S = 2048
DEPTH = 2
NB = 16
NT = 4
TT = 512
EPS = 1e-6
N_IN = 11168
C_AB, C_AC, C_AX, C_AZ = 0, 512, 1024, 1536
C_CQ, C_CKV, C_KPE, C_BZ = 2048, 2304, 2432, 2464
C_DQ, C_DK, C_DV, C_CZ, C_GATE = 2976, 4512, 6048, 7584, 8096
DIL = (1, 4, 16)
CV_BG, CV_CW, CV_CB, CV_GQ, CV_GKV, CV_GQM, CV_GQMS, CV_GKM, CV_GKMS, CV_GDQ, CV_GDK, NCV = 0, 24, 36, 40, 42, 43, 44, 45, 46, 47, 50, 53


import heapq
import os

RAW, WAR, WAW = 0, 1, 2
ACT_SET = {"Exp": "exp", "Ln": "exp", "Sqrt": "sqrt", "Silu": "silu", "Sigmoid": "sigmoid"}


class Chan:
    def __init__(self, sem, name):
        self.sem = sem
        self.name = name
        self.count = 0


class Op:
    __slots__ = ("eng", "fn", "idx", "pidx", "deps", "succ", "ndeps", "waits", "signal", "sigcount", "chan",
                 "dma_deps", "dur", "tset", "ready", "finish", "seg", "isbar")

    def __init__(self, eng, fn):
        self.eng = eng
        self.fn = fn
        self.idx = -1
        self.pidx = -1
        self.deps = []
        self.succ = []
        self.ndeps = 0
        self.waits = {}
        self.dma_deps = []
        self.signal = False
        self.sigcount = 0
        self.chan = None
        self.dur = 0.5
        self.tset = None
        self.ready = 0.0
        self.finish = 0.0
        self.seg = 0
        self.isbar = False


class Prog:
    def __init__(self, nc):
        self.nc = nc
        self.ops = []
        self.kw = {}
        self.kr = {}
        self.sems = {e: nc.alloc_semaphore("sem_" + e) for e in ENGS}
        self.nops = 0
        self.chans = []
        self.seg = 0
        self.sched = True
        self.prio = os.environ.get("KPRIO", "bl")
        self.strict = os.environ.get("KSTRICT", "1") == "1"
        self.keymap = {}

    def chan(self, name):
        c = Chan(self.nc.alloc_semaphore("dsem_" + name), name)
        self.chans.append(c)
        return c

    def barrier(self):
        self.seg += 1

    def add(self, eng, fn, reads=(), writes=(), chan=None, dur=0.5, tset=None):
        op = Op(eng, fn)
        op.pidx = len(self.ops)
        op.seg = self.seg
        op.chan = chan
        op.dur = dur
        op.tset = tset
        self.ops.append(op)
        self.nops += 1
        km = self.keymap
        if km:
            reads = list(reads) + [p for k in reads for p in km.get(k, ())]
            writes = list(writes) + [p for k in writes for p in km.get(k, ())]
        deps = {}
        for k in reads:
            w = self.kw.get(k)
            if w is not None:
                deps[w] = RAW
        for k in writes:
            w = self.kw.get(k)
            if w is not None and w not in deps:
                deps[w] = WAW
            for r in self.kr.get(k, ()):
                if r is not op and r not in deps:
                    deps[r] = WAR
        for k in reads:
            self.kr.setdefault(k, []).append(op)
        for k in writes:
            self.kw[k] = op
            self.kr[k] = []
        op.deps = [(d, kind) for d, kind in deps.items() if d.seg == op.seg]
        return op

    def _schedule(self, ops):
        if not self.sched:
            return list(ops)
        for op in ops:
            op.ndeps = len(op.deps)
            op.succ = []
            op.ready = 0.0
        for op in ops:
            for d, _ in op.deps:
                d.succ.append(op)
        LAT = float(os.environ.get("KLAT", "0.3"))
        if self.prio == "bl":
            for op in reversed(ops):
                b = 0.0
                for s_ in op.succ:
                    v = s_.finish + (LAT if s_.eng != op.eng else 0.0)
                    if v > b:
                        b = v
                op.finish = b + op.dur + (2.0 if op.chan is not None else 0.0)
            mx = max(op.finish for op in ops) if ops else 0.0
            for i_, op in enumerate(ops):
                op.pidx = int((mx - op.finish) * 1000) * 100000 + i_
        free = {e: 0.0 for e in ENGS}
        pending = {e: [] for e in ENGS}
        avail = {e: {} for e in ENGS}
        curset = [None]
        for op in ops:
            if op.ndeps == 0:
                heapq.heappush(pending[op.eng], (0.0, op.pidx, op))
        order = []
        n = len(ops)

        def cand(e):
            t = free[e]
            pend = pending[e]
            av = avail[e]
            while pend and pend[0][0] <= t:
                _, pi, o = heapq.heappop(pend)
                heapq.heappush(av.setdefault(o.tset if e == "act" else None, []), (pi, o))
            best = None
            if e == "act":
                for ts_ in (None, curset[0]):
                    h = av.get(ts_)
                    if h and (best is None or h[0][0] < best[0]):
                        best = (h[0][0], ts_)
                if best is None:
                    for ts_, h in av.items():
                        if h and (best is None or h[0][0] < best[0]):
                            best = (h[0][0], ts_)
            else:
                h = av.get(None)
                if h:
                    best = (h[0][0], None)
            if best is not None:
                return (t, best[0], best[1], False)
            if pend:
                return (pend[0][0], pend[0][1], None, True)
            return None

        while len(order) < n:
            bc = None
            be = None
            for e in ENGS:
                c = cand(e)
                if c is not None and (bc is None or (c[0], c[1]) < (bc[0], bc[1])):
                    bc, be = c, e
            assert bc is not None, "scheduler stuck"
            if bc[3]:
                _, pi, o = heapq.heappop(pending[be])
            else:
                pi, o = heapq.heappop(avail[be][bc[2]])
            start = max(bc[0], free[be])
            dur = o.dur
            if be == "act" and o.tset is not None and o.tset != curset[0]:
                dur += 2.7
                curset[0] = o.tset
            if o.chan is not None:
                free[be] = start + 0.15
                o.finish = start + 2.0 + dur
            else:
                free[be] = start + dur
                o.finish = start + dur
            order.append(o)
            for s_ in o.succ:
                s_.ndeps -= 1
                rt = o.finish + (LAT if s_.eng != o.eng else 0.05)
                if rt > s_.ready:
                    s_.ready = rt
                if s_.ndeps == 0:
                    heapq.heappush(pending[s_.eng], (s_.ready, s_.pidx, s_))
        return order

    def emit(self, final_chans=()):
        nc = self.nc
        nseg = self.seg + 1
        segs = [[] for _ in range(nseg)]
        for op in self.ops:
            segs[op.seg].append(op)
        glob = []
        for si in range(nseg):
            if si > 0:
                drains = []
                for e in ENGS:
                    if e == "sp":
                        continue
                    d = Op(e, lambda h: h.drain())
                    d.isbar = True
                    drains.append(d)
                    glob.append(d)
                for e in ENGS:
                    w = Op(e, lambda h: h.nop())
                    w.isbar = True
                    w.deps = [(d, RAW) for d in drains if d.eng != e]
                    w.dma_deps = "all"
                    glob.append(w)
            glob.extend(self._schedule(segs[si]))
        self.eng_ops = {e: [] for e in ENGS}
        for op in glob:
            op.idx = len(self.eng_ops[op.eng])
            self.eng_ops[op.eng].append(op)
        chan_cnt = {c: 0 for c in self.chans}
        dma_wait_vals = {}
        for op in glob:
            isdma = op.chan is not None
            dw = {}
            if op.dma_deps == "all":
                for c in self.chans:
                    if chan_cnt[c] > 0:
                        dw[c] = chan_cnt[c]
            for d, kind in op.deps:
                if d.chan is not None:
                    dw[d.chan] = chan_cnt[d.chan]
                    continue
                if d.eng == op.eng and not isdma and kind != RAW and not self.strict:
                    continue
                if d.eng == op.eng and d.eng == "pe":
                    continue
                cur = op.waits.get(d.eng)
                if cur is None or cur.idx < d.idx:
                    op.waits[d.eng] = d
            if isdma:
                chan_cnt[op.chan] += 1
            dma_wait_vals[op] = dw
        for op in glob:
            for p in op.waits.values():
                p.signal = True
        for e in ENGS:
            c = 0
            for op in self.eng_ops[e]:
                if op.signal:
                    c += 1
                op.sigcount = c
        final_cnt = {c: chan_cnt[c] for c in final_chans}

        def run_engine(e, h):
            seen = {}
            for op in self.eng_ops[e]:
                for pe_, p in op.waits.items():
                    need = p.sigcount
                    if seen.get(pe_, 0) < need:
                        h.wait_ge(self.sems[pe_], need)
                        seen[pe_] = need
                for c, cnt in dma_wait_vals[op].items():
                    if seen.get(c, 0) < cnt:
                        h.wait_ge(c.sem, 16 * cnt)
                        seen[c] = cnt
                ins = op.fn(h)
                if op.chan is not None:
                    ins.then_inc(op.chan.sem, 16)
                elif op.signal:
                    ins.then_inc(self.sems[e], 1)
            if e == "sp":
                for c, cnt in final_cnt.items():
                    h.wait_ge(c.sem, 16 * cnt)

        with nc.Block() as block:
            @block.tensor
            def _(h):
                run_engine("pe", h)

            @block.scalar
            def _(h):
                run_engine("act", h)

            @block.vector
            def _(h):
                run_engine("dve", h)

            @block.gpsimd
            def _(h):
                run_engine("pool", h)

            @block.sync
            def _(h):
                run_engine("sp", h)


class Arena:
    def __init__(self, nc, nbytes):
        self.cap = nbytes // 2
        self.t = nc.alloc_sbuf_tensor("arena", [128, self.cap], BF16).ap()
        self.top = 0
        self.paged = False
        self.keymap = {}

    def pages(self, off, ne):
        return ["pg%d" % i for i in range(off // 256, (off + ne - 1) // 256 + 1)]

    def alloc(self, free_shape, dtype, key=None):
        n = 1
        for v in free_shape:
            n *= v
        ne = n * (2 if dtype == F32 else 1)
        ne_al = (ne + 255) // 256 * 256 if self.paged else (ne + 31) // 32 * 32
        off = self.top
        self.top += ne_al
        assert self.top <= self.cap, ("arena overflow", self.top * 2, self.cap * 2)
        self.last = (off, ne)
        if key is not None and self.paged:
            for k in ([key] if isinstance(key, str) else key):
                self.keymap[k] = self.pages(off, ne)
        v = self.t[:, off:off + ne]
        if dtype == F32:
            v = v.bitcast(F32)
        if len(free_shape) > 1:
            names = " ".join("a%d" % i for i in range(len(free_shape)))
            kw = {"a%d" % i: free_shape[i] for i in range(len(free_shape))}
            v = v.rearrange("p (%s) -> p %s" % (names, names), **kw)
        return v


class Builder:
    def __init__(self, nseq, nlayers, layer0=0, dbg=False):
        self.nseq = nseq
        self.nlayers = nlayers
        self.dbg = dbg
        nc = bass.Bass("TRN2", target_bir_lowering=False)
        self.nc = nc
        P = Prog(nc)
        self.P = P
        L = nlayers
        dt = nc.dram_tensor
        self.x = dt("x", [nseq * S, D], F32, kind="ExternalInput").ap()
        self.out = dt("out", [nseq * S, D], F32, kind="ExternalOutput").ap()
        self.w_in = dt("w_in", [L, D, N_IN], F32, kind="ExternalInput").ap()
        self.w_uq = dt("w_uq", [L, 256, 768], F32, kind="ExternalInput").ap()
        self.w_uq_sw = dt("w_uq_sw", [L, 256, 768], F32, kind="ExternalInput").ap()
        self.w_ukv_k = dt("w_ukv_k", [L, 128, 512], F32, kind="ExternalInput").ap()
        self.w_ukv_v = dt("w_ukv_v", [L, 128, 512], F32, kind="ExternalInput").ap()
        self.w_kpe = dt("w_kpe", [L, D, 96], F32, kind="ExternalInput").ap()
        self.w_kpe_sw = dt("w_kpe_sw", [L, D, 96], F32, kind="ExternalInput").ap()
        self.w_out = [dt("w_out_%s" % n, [L, 512, D], F32, kind="ExternalInput").ap() for n in "abc"]
        self.w_o = dt("w_o", [L, D, D], F32, kind="ExternalInput").ap()
        self.colv_d = dt("colv", [L, 128, NCV], F32, kind="ExternalInput").ap()
        self.gbc_d = dt("gbc", [L, 128, D], F32, kind="ExternalInput").ap()
        self.ident_d = dt("ident", [128, 128], F32, kind="ExternalInput").ap()
        self.ind2_d = dt("ind2", [128, 128], F32, kind="ExternalInput").ap()
        self.emask_d = dt("emask", [128, 25 * 256], F32, kind="ExternalInput").ap()
        self.ct_d = dt("ctab", [96, S], F32, kind="ExternalInput").ap()
        self.st_d = dt("stab", [96, S], F32, kind="ExternalInput").ap()
        if nlayers > 1:
            self.scr = dt("scr", [nseq * S, D], F32).ap()
        self.bsc = dt("bsc", [4, S], F32).ap()
        self.bsi = 0
        if dbg:
            self.dbg_h = dt("dbg_h", [128, 8 * S], BF16, kind="ExternalOutput").ap()
            self.dbg_y = dt("dbg_y", [128, 12 * S], BF16, kind="ExternalOutput").ap()
        self.ps = [nc.alloc_psum_tensor("ps%d" % i, [128, 512], F32).ap() for i in range(8)]
        A = Arena(nc, 212480)
        self.A = A
        self.hT = A.alloc([8, S], BF16)
        self.yT = A.alloc([12, S], BF16)
        self.ident = A.alloc([128], BF16)
        self.ones = A.alloc([128], BF16)
        self.ind2 = A.alloc([128], BF16)
        self.onesf = A.alloc([64], F32)
        self.emask = A.alloc([25, 256], BF16)
        self.colv = A.alloc([NCV], F32)
        self.epsc = A.alloc([1], F32)
        self.NWR = 8
        self.wr = [A.alloc([8, 128], BF16) for _ in range(self.NWR)]
        self.wrc = [P.chan("wr%d" % i) for i in range(self.NWR)]
        self.wri = 0
        A.top = (A.top + 255) // 256 * 256
        self.stage_base = A.top
        A.paged = True
        A.keymap = P.keymap
        self.cst = P.chan("const")
        self.cout = P.chan("out")
        self.cmisc = {}
        self.build()

    def misc_chan(self, key):
        if key not in self.cmisc:
            self.cmisc[key] = self.P.chan("m%d" % len(self.cmisc))
        return self.cmisc[key]

    def dma(self, q, out, in_, key_w=None, key_r=None, chan=None):
        if chan is None:
            chan = self.misc_chan(key_w if key_w is not None else key_r)
        nbytes = out.free_size() * out.partition_size() * 4
        self.P.add(q, lambda h: h.dma_start(out=out, in_=in_),
                   reads=[key_r] if key_r else [], writes=[key_w] if key_w else [], chan=chan, dur=nbytes / 150e3)

    def mm(self, out, lhsT, rhs, start, stop, r, w):
        self.P.add("pe", lambda h: h.matmul(out, lhsT=lhsT, rhs=rhs, start=start, stop=stop, skip_group_check=True), r, w,
                   dur=(0.03 + out.free_size() * 0.00043) * (4.0 if lhsT.dtype == F32 else 1.0))

    def tr(self, out, in_, r, w):
        ident = self.ident
        self.P.add("pe", lambda h: h.transpose(out, in_, ident), list(r) + ["const"], w, dur=0.1)

    def act(self, out, in_, func, r, w, scale=None, bias=None, accum=None):
        kw = {}
        if scale is not None:
            kw["scale"] = scale
        if bias is not None:
            kw["bias"] = bias
        if accum is not None:
            kw["accum_out"] = accum
        self.P.add("act", lambda h: h.activation(out=out, in_=in_, func=func, **kw), r, w,
                   dur=0.2 + out.free_size() * 0.0009, tset=ACT_SET.get(func.name))

    def _dd(self, out, eng="dve"):
        if eng == "pool":
            return 0.25 + out.free_size() * (0.0021 if out.dtype == F32 else 0.0009)
        return 0.12 + out.free_size() * (0.0015 if out.dtype == F32 else 0.0009)

    def tt(self, out, in0, in1, op, r, w, eng="dve"):
        self.P.add(eng, lambda h: h.tensor_tensor(out=out, in0=in0, in1=in1, op=op), r, w, dur=self._dd(out, eng))

    def stt(self, out, in0, scalar, in1, op0, op1, r, w, eng="dve"):
        self.P.add(eng, lambda h: h.scalar_tensor_tensor(out=out, in0=in0, scalar=scalar, in1=in1, op0=op0, op1=op1), r, w,
                   dur=self._dd(out, eng))

    def ts(self, out, in0, s1, s2, op0, op1, r, w, eng="dve"):
        if s2 is None:
            self.P.add(eng, lambda h: h.tensor_scalar(out=out, in0=in0, scalar1=s1, scalar2=None, op0=op0), r, w,
                       dur=self._dd(out, eng))
        else:
            self.P.add(eng, lambda h: h.tensor_scalar(out=out, in0=in0, scalar1=s1, scalar2=s2, op0=op0, op1=op1), r, w,
                       dur=self._dd(out, eng))

    def recip(self, out, in_, r, w):
        self.P.add("dve", lambda h: h.reciprocal(out=out, in_=in_), r, w, dur=self._dd(out))

    def cpy(self, out, in_, r, w, eng="dve"):
        self.P.add(eng, lambda h: h.tensor_copy(out=out, in_=in_), r, w, dur=self._dd(out))

    def memset(self, ap, val, w, eng="dve"):
        self.P.add(eng, lambda h: h.memset(ap, val), [], w, dur=self._dd(ap))

    def bcast_rows(self, src_row, key_src, dsts):
        from concourse.ap import AP
        i = self.bsi % 4
        self.bsi += 1
        n = src_row.free_size()
        self.dma("sp", self.bsc[i:i + 1, 0:n], src_row, key_w="bsc%d" % i, key_r=key_src)
        for (dst, kdst, c0, cn) in dsts:
            row = self.bsc[i:i + 1, c0:c0 + cn]
            src = AP(row.tensor, row.offset, [[0, 64], [1, cn]])
            self.dma("sp", dst, src, key_w=kdst, key_r="bsc%d" % i)

    def ldw(self, src):
        i = self.wri % self.NWR
        self.wri += 1
        self.dma("pool", self.wr[i], src, key_w="wr%d" % i, chan=self.wrc[i])
        return self.wr[i], "wr%d" % i

    def win_cols(self, l, c0, n):
        return self.w_in[l, :, c0:c0 + n].rearrange("(k p) n -> p k n", p=128)

    def rstd_from(self, ssq_ps, pkey, rows, inv_n, sd, rs):
        self.act(sd[0:rows], ssq_ps[0:rows], AF.Sqrt, ["const"], ["sd", pkey], scale=inv_n, bias=self.epsc[0:rows, 0:1])
        self.recip(rs[0:rows], sd[0:rows], ["sd"], ["rs"])

    def build(self):
        P = self.P
        self.dma("pool", self.ident, self.ident_d, key_w="const", chan=self.cst)
        self.dma("pool", self.ind2, self.ind2_d, key_w="const", chan=self.cst)
        self.dma("pool", self.emask, self.emask_d.rearrange("p (a b) -> p a b", a=25), key_w="const", chan=self.cst)
        self.memset(self.ones, 1.0, ["const"])
        self.memset(self.onesf, 1.0, ["const"])
        self.memset(self.epsc, EPS, ["const"])
        for l in range(self.nlayers):
            src = self.x if l == 0 else self.scr
            dst = self.out if l == self.nlayers - 1 else self.scr
            self.dma("sp", self.colv, self.colv_d[l], key_w="colv")
            for seq in range(self.nseq):
                self.A.top = self.stage_base
                self.stage0(l, seq, src)
                if self.dbg and l == 0 and seq == 0:
                    self.dma("sp", self.dbg_h.rearrange("p (a b) -> p a b", a=8), self.hT, key_r="hT3", chan=self.cout)
                self.A.top = self.stage_base
                self.stage_mla(l, seq)
                self.A.top = self.stage_base
                self.stage_dil(l, seq)
                self.A.top = self.stage_base
                self.stage_conv(l, seq)
                if self.dbg and l == 0 and seq == 0:
                    self.P.add("sp", lambda h: h.dma_start(out=self.dbg_y.rearrange("p (a b) -> p a b", a=12), in_=self.yT), reads=["Y%d_%d" % (c_, t_) for c_ in range(12) for t_ in range(4)], chan=self.cout)
                self.A.top = self.stage_base
                self.stage_out(l, seq, src, dst, final=(l == self.nlayers - 1))
        P.emit(final_chans=[self.cout] + ([self.cmisc["scrw"]] if "scrw" in self.cmisc else []))

    def stage0(self, l, seq, src):
        A = self.A
        NBUF = 4
        xb = [A.alloc([D], F32, key="xb%d" % i) for i in range(NBUF)]
        hn = [A.alloc([D], BF16, key="hn%d" % i) for i in range(NBUF)]
        junk = A.alloc([D], BF16, key="junk")
        gbc = A.alloc([D], F32, key="s0gbc")
        r_ss = self.ring(NBUF, [1], F32, "ss")
        r_sd = self.ring(NBUF, [1], F32, "sd1")
        r_rs = self.ring(NBUF, [1], F32, "rs1")
        self.dma("sp", gbc, self.gbc_d[l], key_w="s0gbc")
        for b in range(NB):
            i = b % NBUF
            r0 = seq * S + b * 128
            ss, kss = r_ss()
            sd1, ksd = r_sd()
            rs1, krs = r_rs()
            self.dma("sp", xb[i], src[r0:r0 + 128, :], key_w="xb%d" % i, key_r=("scr" if l > 0 else None))
            self.memset(ss, 0.0, [kss])
            self.act(junk, xb[i], AF.Square, ["xb%d" % i], ["junk", kss], accum=ss)
            self.act(sd1, ss, AF.Ln, [kss, "const"], [ksd], scale=1.0 / D, bias=self.epsc[:, 0:1])
            self.act(rs1, sd1, AF.Exp, [ksd], [krs], scale=-0.5)
            self.stt(hn[i], xb[i], rs1[:, 0:1], gbc, ALU.mult, ALU.mult, ["xb%d" % i, krs, "s0gbc"], ["hn%d" % i])
            pb = b % 8
            pst = self.ps[pb].bitcast(BF16)
            for k in range(8):
                self.tr(pst[:, k * 128:(k + 1) * 128], hn[i][:, k * 128:(k + 1) * 128], ["hn%d" % i], ["ps%d" % pb])
            if b % 2 == 0:
                self.act(self.hT[:, :, b * 128:(b + 1) * 128], pst.rearrange("p (k n) -> p k n", k=8), AF.Copy,
                         [], ["ps%d" % pb, "hT%d" % (b // 4)])
            else:
                self.cpy(self.hT[:, :, b * 128:(b + 1) * 128], pst.rearrange("p (k n) -> p k n", k=8), [], ["ps%d" % pb, "hT%d" % (b // 4)])

    def ring(self, n, shape, dtype, name):
        aps = [self.A.alloc(shape, dtype, key="%s%d" % (name, i)) for i in range(n)]
        st = {"i": 0}

        def nxt():
            i = st["i"] % n
            st["i"] += 1
            return aps[i], "%s%d" % (name, i)
        return nxt

    def psring(self, banks):
        st = {"i": 0}

        def nxt():
            b = banks[st["i"] % len(banks)]
            st["i"] += 1
            return self.ps[b], "ps%d" % b
        return nxt

    def stage_mla(self, l, seq):
        A = self.A
        cv = self.colv
        Kc = self.yT[:, 4:12, :]
        KK = ["yT_c", "yT_a"]
        Va = A.alloc([NB, 8, 65], BF16, key="Va")
        for t_ in range(NT):
            self.P.keymap["Va%d" % t_] = A.pages(A.last[0] + t_ * 4 * 8 * 65, 4 * 8 * 65)
        qT = A.alloc([8, TT], BF16, key=["qT%d" % h_ for h_ in range(8)])
        cqn = A.alloc([2, TT], BF16, key="cqn")
        ckvn = A.alloc([TT], BF16, key="ckvn")
        sqpe = A.alloc([TT], BF16, key="sqpe")
        kr = A.alloc([TT], F32, key="kr")
        ctt = A.alloc([TT], F32, key="ctt")
        stt_ = A.alloc([TT], F32, key="stt")
        cqf = A.alloc([TT], F32, key="cqf")
        sqf = A.alloc([TT], F32, key="sqf")
        ckf = A.alloc([TT], F32, key="ckf")
        skf = A.alloc([TT], F32, key="skf")
        sbz = A.alloc([4, TT], BF16, key="sbz")
        r_sq = self.ring(2, [TT], BF16, "sq")
        r_sd = self.ring(2, [TT], F32, "sd")
        r_rs = self.ring(2, [TT], F32, "rs")
        r_ta = self.ring(2, [TT], F32, "ta")
        r_tb = self.ring(2, [TT], F32, "tb")
        r_pt = self.ring(3, [TT], BF16, "PT")
        r_osb = self.ring(2, [TT], F32, "osb")
        r_rb = self.ring(2, [TT], F32, "rb")
        Wlat = A.alloc([8, 384], BF16, key="Wlat")
        Wkpe = A.alloc([8, 96], BF16, key="Wkpe")
        Wkpes = A.alloc([8, 96], BF16, key="Wkpes")
        Wuq = A.alloc([2, 768], BF16, key="Wuq")
        Wuqs = A.alloc([2, 768], BF16, key="Wuqs")
        Wk = A.alloc([512], BF16, key="Wk")
        Wv = A.alloc([512], BF16, key="Wv")
        cfg = os.environ.get("KMLA", "a")
        if cfg == "a":
            pP = self.psring([0, 1, 2])
            pSS = self.psring([3, 4])
            pS_ = self.psring([6, 7])
            pO_ = self.psring([5])
        elif cfg == "b":
            pP = self.psring([0, 1])
            pSS = self.psring([2, 3])
            pS_ = self.psring([4, 6, 7])
            pO_ = self.psring([5])
        else:
            pP = self.psring([0, 1])
            pSS = self.psring([2])
            pS_ = self.psring([3, 6, 7])
            pO_ = self.psring([4, 5])
        hT = self.hT
        self.dma("pool", Wlat, self.win_cols(l, C_CQ, 384), key_w="Wlat")
        self.dma("pool", Wkpe, self.w_kpe[l].rearrange("(k p) n -> p k n", p=128), key_w="Wkpe")
        self.dma("pool", Wkpes, self.w_kpe_sw[l].rearrange("(k p) n -> p k n", p=128), key_w="Wkpes")
        self.dma("pool", Wuq, self.w_uq[l].rearrange("(k p) n -> p k n", p=128), key_w="Wuq")
        self.dma("pool", Wuqs, self.w_uq_sw[l].rearrange("(k p) n -> p k n", p=128), key_w="Wuqs")
        self.dma("pool", Wk, self.w_ukv_k[l], key_w="Wk")
        self.dma("pool", Wv, self.w_ukv_v[l], key_w="Wv")
        self.memset(Va[:, :, :, 64:65], 1.0, ["Va"])
        sc_mla = 96.0 ** -0.5
        R = slice(64, 96)

        def rstd(ssq, kss, rows, inv_n):
            sd, ksd = r_sd()
            rs, krs = r_rs()
            self.act(sd[0:rows], ssq[0:rows], AF.Ln, ["const"], [ksd, kss], scale=inv_n, bias=self.epsc[0:rows, 0:1])
            self.act(rs[0:rows], sd[0:rows], AF.Exp, [ksd], [krs], scale=-0.5)
            return rs, krs

        for t in range(NT):
            T0 = t * TT
            hs = lambda k: hT[:, k, T0:T0 + TT]
            wbz = [self.ldw(self.win_cols(l, C_BZ + c * 128, 128)) for c in range(4)]
            self.dma("sp", ctt[0:96], self.ct_d[:, T0:T0 + TT], key_w="ctt")
            self.dma("sp", stt_[0:96], self.st_d[:, T0:T0 + TT], key_w="stt")
            self.ts(cqf[0:96], ctt[0:96], cv[0:96, CV_GQM:CV_GQM + 1], None, ALU.mult, None, ["ctt", "colv"], ["cqf"])
            self.ts(sqf[0:96], stt_[0:96], cv[0:96, CV_GQMS:CV_GQMS + 1], None, ALU.mult, None, ["stt", "colv"], ["sqf"])
            self.ts(ckf[0:96], ctt[0:96], cv[0:96, CV_GKM:CV_GKM + 1], None, ALU.mult, None, ["ctt", "colv"], ["ckf"])
            self.ts(skf[0:96], stt_[0:96], cv[0:96, CV_GKMS:CV_GKMS + 1], None, ALU.mult, None, ["stt", "colv"], ["skf"])
            pcq = [pP(), pP()]
            sqs = []
            for j in range(2):
                p_, k_ = pcq[j]
                for k in range(8):
                    self.mm(p_, Wlat[:, k, j * 128:(j + 1) * 128], hs(k), k == 0, k == 7, ["Wlat", "hT%d" % t], [k_])
                sq, ksq = r_sq()
                self.act(sq, p_, AF.Square, [], [ksq, k_])
                sqs.append((sq, ksq))
            ss, kss = pSS()
            self.mm(ss, self.ones, sqs[0][0], True, False, ["const", sqs[0][1]], [kss])
            self.mm(ss, self.ones, sqs[1][0], False, True, ["const", sqs[1][1]], [kss])
            rs, krs = rstd(ss, kss, 128, 1.0 / 256)
            for j in range(2):
                p_, k_ = pcq[j]
                self.stt(cqn[:, j, :], p_, cv[:, CV_GQ + j:CV_GQ + j + 1], rs, ALU.mult, ALU.mult, ["colv", krs], ["cqn", k_])
            p_, k_ = pP()
            for k in range(8):
                self.mm(p_, Wlat[:, k, 256:384], hs(k), k == 0, k == 7, ["Wlat", "hT%d" % t], [k_])
            sq, ksq = r_sq()
            self.act(sq, p_, AF.Square, [], [ksq, k_])
            ss, kss = pSS()
            self.mm(ss, self.ones, sq, True, True, ["const", ksq], [kss])
            rs, krs = rstd(ss, kss, 128, 1.0 / 128)
            self.stt(ckvn, p_, cv[:, CV_GKV:CV_GKV + 1], rs, ALU.mult, ALU.mult, ["colv", krs], ["ckvn", k_])
            p3, k3 = pP()
            for k in range(8):
                self.mm(p3[0:96], Wkpe[:, k, :], hs(k), k == 0, k == 7, ["Wkpe", "hT%d" % t], [k3])
            self.act(sqpe[R], p3[R], AF.Square, [], ["sqpe", k3])
            self.tt(kr[R], p3[R], ckf[R], ALU.mult, ["ckf"], ["kr", k3])
            p4, k4 = pP()
            for k in range(8):
                self.mm(p4[0:96], Wkpes[:, k, :], hs(k), k == 0, k == 7, ["Wkpes", "hT%d" % t], [k4])
            tb, ktb = r_tb()
            self.tt(tb[R], p4[R], skf[R], ALU.mult, ["skf"], [ktb, k4])
            self.tt(kr[R], kr[R], tb[R], ALU.add, [ktb, "kr"], ["kr"], eng="pool")
            for bb in range(4):
                blk = t * 4 + bb
                pv, pk = pP()
                self.mm(pv, ckvn[:, bb * 128:(bb + 1) * 128], Wv, True, True, ["ckvn", "Wv"], [pk])
                self.cpy(Va[:, blk, :, 0:64], pv.rearrange("p (h c) -> p h c", h=8), ["Va"], ["Va%d" % t, pk])
            for h in range(8):
                pK, pk = pP()
                self.mm(pK[0:64], Wk[:, h * 64:(h + 1) * 64], ckvn, True, True, ["Wk", "ckvn"], [pk])
                sq, ksq = r_sq()
                self.act(sq[0:64], pK[0:64], AF.Square, [], [ksq, pk])
                ss, kss = pSS()
                self.mm(ss[0:96], self.ones[0:64, 0:96], sq[0:64], True, False, ["const", ksq], [kss])
                self.mm(ss[0:96], self.ones[64:96, 0:96], sqpe[R], False, True, ["const", "sqpe"], [kss])
                rs, krs = rstd(ss, kss, 96, 1.0 / 96)
                self.stt(Kc[0:64, h, T0:T0 + TT], pK[0:64], cv[0:64, CV_GKM:CV_GKM + 1], rs[0:64], ALU.mult, ALU.mult,
                         ["colv", krs], ["Y%d_%d" % (4 + h, t), pk])
                self.tt(Kc[R, h, T0:T0 + TT], kr[R], rs[R], ALU.mult, ["kr", krs], ["Y%d_%d" % (4 + h, t)])
            for c in range(4):
                wt, wk = wbz[c]
                pb, pk = pP()
                for k in range(8):
                    self.mm(pb, wt[:, k, :], hs(k), k == 0, k == 7, [wk, "hT%d" % t], [pk])
                self.act(sbz[:, c, :], pb, AF.Silu, [], ["sbz", pk])
            for h in range(8):
                pQ, kq = pP()
                for j in range(2):
                    self.mm(pQ[0:96], Wuq[:, j, h * 96:(h + 1) * 96], cqn[:, j, :], j == 0, j == 1, ["Wuq", "cqn"], [kq])
                pQs, kqs = pP()
                for j in range(2):
                    self.mm(pQs[0:96], Wuqs[:, j, h * 96:(h + 1) * 96], cqn[:, j, :], j == 0, j == 1, ["Wuqs", "cqn"], [kqs])
                sq, ksq = r_sq()
                self.act(sq[0:96], pQ[0:96], AF.Square, [], [ksq, kq])
                ss, kss = pSS()
                self.mm(ss[0:96], self.ones[0:96, 0:96], sq[0:96], True, True, ["const", ksq], [kss])
                rs, krs = rstd(ss, kss, 96, 1.0 / 96)
                ta, kta = r_ta()
                tb, ktb = r_tb()
                self.tt(ta[0:96], pQ[0:96], cqf[0:96], ALU.mult, ["cqf"], [kta, kq])
                self.tt(tb[R], pQs[R], sqf[R], ALU.mult, ["sqf"], [ktb, kqs])
                self.tt(ta[R], ta[R], tb[R], ALU.add, [ktb, kta], [kta], eng="pool")
                self.tt(qT[0:96, h, :], ta[0:96], rs[0:96], ALU.mult, [kta, krs], ["qT%d" % h])
            for h in range(8):
                nkb = 4 * t + 4
                pO, ko = pO_()
                for kb in range(nkb):
                    c0 = 0 if kb < 4 * t else (kb - 4 * t) * 128
                    pS, ks = pS_()
                    pt, kpt = r_pt()
                    self.mm(pS[:, c0:TT], Kc[0:96, h, kb * 128:(kb + 1) * 128], qT[0:96, h, c0:TT], True, True,
                            ["Y%d_%d" % (4 + h, kb // 4), "qT%d" % h], [ks])
                    self.act(pt[:, c0:TT], pS[:, c0:TT], AF.Exp, [], [kpt, ks], scale=sc_mla)
                    if kb >= 4 * t:
                        self.tt(pt[:, c0:c0 + 128], pt[:, c0:c0 + 128], self.emask[:, 24, 0:128], ALU.mult, ["const", kpt], [kpt], eng="pool")
                    self.mm(pO[0:65, c0:TT], Va[:, kb, h, :], pt[:, c0:TT], kb == 0, kb == nkb - 1, ["Va", "Va%d" % (kb // 4), kpt], [ko])
                osb, kos = r_osb()
                self.cpy(osb[0:65], pO[0:65], [], [kos, ko])
                self.act(osb[64:65], osb[64:65], AF.Ln, [kos], [kos])
                self.act(osb[64:65], osb[64:65], AF.Exp, [kos], [kos], scale=-1.0)
                rb, krb = r_rb()
                self.bcast_rows(osb[64:65, :], kos, [(rb[0:64], krb, 0, TT)])
                ta, kta = r_ta()
                Rh = slice(0, 64) if h % 2 == 0 else slice(64, 128)
                self.tt(ta[Rh], osb[0:64], rb[0:64], ALU.mult, [kos, krb], [kta])
                self.tt(self.yT[Rh, h // 2, T0:T0 + TT], ta[Rh], sbz[Rh, h // 2, :], ALU.mult, [kta, "sbz"], ["Y%d_%d" % (h // 2, t)])

    def stage_dil(self, l, seq):
        A = self.A
        cv = self.colv
        hT = self.hT
        r_dq = self.ring(2, [S], BF16, "dqT")
        r_dk = self.ring(2, [S], BF16, "dkT")
        r_dv = self.ring(2, [S], BF16, "dvT")
        r_dva = self.ring(2, [NB, 2, 65], BF16, "dva")
        r_acc = self.ring(2, [2, S], F32, "acc")
        r_scz = self.ring(2, [S], BF16, "scz")
        r_sq = self.ring(2, [TT], BF16, "sq")
        r_sd = self.ring(2, [TT], F32, "sd")
        r_rs = self.ring(2, [TT], F32, "rs")
        r_ta = self.ring(2, [TT], F32, "ta")
        r_pt = self.ring(4, [256], BF16, "dPT")
        r_rb = self.ring(4, [TT], F32, "rb")
        cfg = os.environ.get("KDIL", "a")
        if cfg == "a":
            pP = self.psring([2, 3, 4, 5])
            pS_ = self.psring([6, 7])
            pO_ = self.psring([0, 1])
        else:
            pP = self.psring([3, 4, 5])
            pS_ = self.psring([2, 6, 7])
            pO_ = self.psring([0, 1])
        for _ in range(2):
            dva, kdva = r_dva()
            self.memset(dva[:, :, :, 64:65], 1.0, [kdva])

        def rstd(ssq, kss, rows, inv_n):
            sd, ksd = r_sd()
            rs, krs = r_rs()
            self.act(sd[0:rows], ssq[0:rows], AF.Ln, ["const"], [ksd, kss], scale=inv_n, bias=self.epsc[0:rows, 0:1])
            self.act(rs[0:rows], sd[0:rows], AF.Exp, [ksd], [krs], scale=-0.5)
            return rs, krs

        def perm_out(buf, g, t):
            if g == 0:
                return buf[:, t * TT:(t + 1) * TT], None
            if g == 1:
                return buf.rearrange("p (r m i) -> p r m i", r=4, m=4)[:, :, t, :], ("p (i r) -> p r i", 4)
            return buf.rearrange("p (r i) -> p r i", r=16)[:, :, 32 * t:32 * t + 32], ("p (i r) -> p r i", 16)

        def segments(g):
            segs = []
            for s_ in range(4):
                u = []
                if g == 0:
                    if s_ > 0:
                        u.append(((4 * s_ - 1) * 128, 4 * s_ - 1, 4 * s_ * 128, 128, 128))
                    for kb in range(4 * s_, 4 * s_ + 4):
                        u.append((kb * 128, kb, kb * 128, 256 if kb < 4 * s_ + 3 else 128, 0))
                elif g == 1:
                    for m in range(4):
                        blk = 4 * s_ + m
                        u.append((blk * 128, blk, blk * 128, 256 if m < 3 else 128, 0))
                else:
                    for c in range(4):
                        blk = 4 * s_ + c
                        u.append((blk * 128, blk, blk * 128, 128, 0))
                segs.append(u)
            return segs

        for j in range(4):
            scz, kscz = r_scz()
            acc, kacc = r_acc()
            wcz, kcz = self.ldw(self.win_cols(l, C_CZ + j * 128, 128))
            for t in range(NT):
                pc, kpc = pP()
                for k in range(8):
                    self.mm(pc, wcz[:, k, :], hT[:, k, t * TT:(t + 1) * TT], k == 0, k == 7, [kcz, "hT%d" % t], [kpc])
                self.act(scz[:, t * TT:(t + 1) * TT], pc, AF.Silu, [], [kscz, kpc])
            for g in range(3):
                c_off = g * 512 + j * 128
                wq, kwq = self.ldw(self.win_cols(l, C_DQ + c_off, 128))
                wk_, kwk = self.ldw(self.win_cols(l, C_DK + c_off, 128))
                wv, kwv = self.ldw(self.win_cols(l, C_DV + c_off, 128))
                dqT, kdq = r_dq()
                dkT, kdk = r_dk()
                dvT, kdv = r_dv()
                dva, kdva = r_dva()
                for t in range(NT):
                    hs = lambda k: hT[:, k, t * TT:(t + 1) * TT]
                    for (wt, wkey, dst, dkey, gcol) in ((wq, kwq, dqT, kdq, CV_GDQ + g), (wk_, kwk, dkT, kdk, CV_GDK + g)):
                        pp, pk = pP()
                        for k in range(8):
                            self.mm(pp, wt[:, k, :], hs(k), k == 0, k == 7, [wkey, "hT%d" % t], [pk])
                        sq, ksq = r_sq()
                        self.act(sq, pp, AF.Square, [], [ksq, pk])
                        ss, kss = pP()
                        self.mm(ss, self.ind2, sq, True, True, ["const", ksq], [kss])
                        rs, krs = rstd(ss, kss, 128, 1.0 / 64)
                        ov, pr = perm_out(dst, g, t)
                        if pr is None:
                            self.stt(ov, pp, cv[:, gcol:gcol + 1], rs, ALU.mult, ALU.mult, ["colv", krs], [dkey, pk])
                        else:
                            self.stt(ov, pp.rearrange(pr[0], r=pr[1]), cv[:, gcol:gcol + 1], rs.rearrange(pr[0], r=pr[1]),
                                     ALU.mult, ALU.mult, ["colv", krs], [dkey, pk])
                    pp, pk = pP()
                    for k in range(8):
                        self.mm(pp, wv[:, k, :], hs(k), k == 0, k == 7, [kwv, "hT%d" % t], [pk])
                    ov, pr = perm_out(dvT, g, t)
                    if pr is None:
                        self.cpy(ov, pp, [], [kdv, pk])
                    else:
                        self.cpy(ov, pp.rearrange(pr[0], r=pr[1]), [], [kdv, pk])
                for half in range(2):
                    pp, pk = pP()
                    pst = pp.bitcast(BF16)
                    for bb in range(8):
                        blk = half * 8 + bb
                        self.tr(pst[:, bb * 128:(bb + 1) * 128], dvT[:, blk * 128:(blk + 1) * 128], [kdv], [pk])
                    self.act(dva[:, half * 8:(half + 1) * 8, :, 0:64],
                             pst.rearrange("p (b h c) -> p b h c", b=8, h=2), AF.Copy, [], [kdva, pk])
                segs = segments(g)
                for hh in range(2):
                    Rr = slice(hh * 64, hh * 64 + 64)
                    em = self.emask[:, g * 8 + 2 * j + hh, :]
                    for b, units in enumerate(segs):
                        pO, ko = pO_()
                        first = True
                        for (K0, blk, q0, nq, m0) in units:
                            pS, ks = pS_()
                            pt, kpt = r_pt()
                            self.mm(pS[:, 0:nq], dkT[Rr, K0:K0 + 128], dqT[Rr, q0:q0 + nq], True, True, [kdk, kdq], [ks])
                            self.act(pt[:, 0:nq], pS[:, 0:nq], AF.Exp, [], [kpt, ks], scale=0.125)
                            self.tt(pt[:, 0:nq], pt[:, 0:nq], em[:, m0:m0 + nq], ALU.mult, ["const", kpt], [kpt], eng=("pool" if g == 2 else "dve"))
                            self.mm(pO[0:65, q0 - b * 512:q0 - b * 512 + nq], dva[:, blk, hh, :], pt[:, 0:nq], first, False,
                                    [kdva, kpt], [ko])
                            first = False
                        if g == 0:
                            self.act(acc[0:65, hh, b * 512:(b + 1) * 512], pO[0:65], AF.Copy, [], [kacc, ko])
                        elif g == 1:
                            av = acc[0:65, hh, :].rearrange("p (m i r) -> p r m i", m=4, r=4)[:, b]
                            self.tt(av, av, pO[0:65].rearrange("p (m i) -> p m i", m=4), ALU.add, [kacc], [kacc, ko])
                        else:
                            av = acc[0:65, hh, :].rearrange("p (i r) -> p r i", r=16)[:, 4 * b:4 * b + 4, :]
                            self.tt(av, av, pO[0:65].rearrange("p (r i) -> p r i", r=4), ALU.add, [kacc], [kacc, ko])
            for hh in range(2):
                Rh = slice(hh * 64, hh * 64 + 64)
                self.act(acc[64:65, hh, :], acc[64:65, hh, :], AF.Ln, [kacc], [kacc])
                self.act(acc[64:65, hh, :], acc[64:65, hh, :], AF.Exp, [kacc], [kacc], scale=-1.0)
                rbs = [r_rb() for _ in range(NT)]
                self.bcast_rows(acc[64:65, hh, :], kacc, [(rbs[t][0][0:64], rbs[t][1], t * TT, TT) for t in range(NT)])
                for t in range(NT):
                    cs = slice(t * TT, (t + 1) * TT)
                    rb, krb = rbs[t]
                    ta, kta = r_ta()
                    self.tt(ta[Rh], acc[0:64, hh, cs], rb[0:64], ALU.mult, [kacc, krb], [kta])
                    self.tt(self.yT[Rh, 4 + j, cs], ta[Rh], scz[Rh, cs], ALU.mult, [kta, kscz], ["Y%d_%d" % (4 + j, t)])

    def stage_conv(self, l, seq):
        A = self.A
        ps = self.ps
        cv = self.colv
        hT = self.hT
        r_up = self.ring(2, [S + 2], F32, "upad")
        r_acs = self.ring(2, [TT], F32, "acs")
        r_sz = self.ring(2, [TT], F32, "sz")
        r_c1 = self.ring(2, [TT], F32, "c1")
        r_c2 = self.ring(2, [TT], F32, "c2")
        r_t1 = self.ring(3, [TT], F32, "t1")
        n = 0
        for c in range(4):
            ws = [self.ldw(self.win_cols(l, base + c * 128, 128)) for base in (C_AB, C_AC, C_AX, C_AZ)]
            upad, kup = r_up()
            self.memset(upad[:, 0:2], 0.0, [kup])
            for t in range(NT):
                T0 = t * TT
                o = 4 * (n % 2)
                n += 1
                for i in range(4):
                    wt, wk = ws[i]
                    for k in range(8):
                        self.mm(ps[o + i], wt[:, k, :], hT[:, k, T0:T0 + TT], k == 0, k == 7, [wk, "hT%d" % t], ["ps%d" % (o + i)])
                kb_, kc_, kx_, kz_ = ["ps%d" % (o + i) for i in range(4)]
                acs, kacs = r_acs()
                sz, ksz = r_sz()
                c1, kc1 = r_c1()
                c2, kc2 = r_c2()
                self.act(acs, ps[o + 1], AF.Copy, [], [kacs, kc_])
                self.tt(upad[:, 2 + T0:2 + T0 + TT], acs, ps[o + 2], ALU.mult, [kacs], [kup, kx_])
                self.act(sz, ps[o + 3], AF.Silu, [], [ksz, kz_])
                t1, kt1 = r_t1()
                self.act(c1, upad[:, T0:T0 + TT], AF.Identity, [kup, "colv"], [kc1], scale=cv[:, CV_CW + c:CV_CW + c + 1],
                         bias=cv[:, CV_CB + c:CV_CB + c + 1])
                self.act(t1, upad[:, T0 + 1:T0 + 1 + TT], AF.Identity, [kup, "colv"], [kt1], scale=cv[:, CV_CW + 4 + c:CV_CW + 5 + c])
                self.tt(c1, c1, t1, ALU.add, [kc1, kt1], [kc1], eng="pool")
                t1, kt1 = r_t1()
                self.act(t1, upad[:, T0 + 2:T0 + 2 + TT], AF.Identity, [kup, "colv"], [kt1], scale=cv[:, CV_CW + 8 + c:CV_CW + 9 + c])
                self.tt(c1, c1, t1, ALU.add, [kc1, kt1], [kc1], eng="pool")
                self.tt(c2, c1, ps[o + 0], ALU.mult, [kc1], [kc2, kb_])
                self.tt(self.yT[:, 8 + c, T0:T0 + TT], c2, sz, ALU.mult, [kc2, ksz], ["Y%d_%d" % (8 + c, t)])

    def stage_out(self, l, seq, src, dst, final):
        A = self.A
        ps = self.ps
        cv = self.colv
        hT = self.hT
        yT = self.yT
        G = A.alloc([24, TT], BF16)
        for jj in range(24):
            self.P.keymap["G%d" % jj] = A.pages(A.last[0] + jj * TT, TT)
        r_m1 = self.ring(2, [TT], F32, "m1")
        r_m2 = self.ring(2, [TT], F32, "m2")
        mT = A.alloc([8, TT], BF16)
        for jj in range(8):
            self.P.keymap["mT%d" % jj] = A.pages(A.last[0] + jj * TT, TT)
        xb = [A.alloc([D], F32, key="oxb%d" % i) for i in range(2)]
        ob = [A.alloc([D], F32, key="ob%d" % i) for i in range(2)]
        Wo3 = [A.alloc([4, D], BF16, key="Wo3_%d" % i) for i in range(3)]
        Wo = A.alloc([8, D], BF16, key="Wo")
        for i in range(3):
            self.dma("pool", Wo3[i], self.w_out[i][l].rearrange("(k p) n -> p k n", p=128), key_w="Wo3_%d" % i)
        self.dma("pool", Wo, self.w_o[l].rearrange("(k p) n -> p k n", p=128), key_w="Wo")
        och = self.cout if final else self.misc_chan("scrw")
        pG = self.psring([0, 1])
        for t in range(NT):
            T0 = t * TT
            DEPTHW = 4
            wq = [self.ldw(self.win_cols(l, C_GATE + jj * 128, 128)) for jj in range(DEPTHW)]
            for jj in range(24):
                wt, wk = wq[jj]
                pg, kg = pG()
                for k in range(8):
                    self.mm(pg, wt[:, k, :], hT[:, k, T0:T0 + TT], k == 0, k == 7, [wk, "hT%d" % t], [kg])
                self.act(G[:, jj, :], pg, AF.Sigmoid, ["colv"], ["G%d" % jj, kg], bias=cv[:, CV_BG + jj:CV_BG + jj + 1])
                if jj + DEPTHW < 24:
                    wq.append(self.ldw(self.win_cols(l, C_GATE + (jj + DEPTHW) * 128, 128)))
            for j in range(8):
                o = 2 + 3 * (j % 2)
                for i in range(3):
                    for k in range(4):
                        self.mm(ps[o + i], Wo3[i][:, k, j * 128:(j + 1) * 128], yT[:, (8, 0, 4)[i] + k, T0:T0 + TT], k == 0, k == 3,
                                ["Wo3_%d" % i, "Y%d_%d" % ((8, 0, 4)[i] + k, t)], ["ps%d" % (o + i)])
                m1, k1 = r_m1()
                m2, k2 = r_m2()
                self.tt(m1, ps[o], G[:, j, :], ALU.mult, ["G%d" % j], [k1, "ps%d" % o])
                self.tt(m2, ps[o + 1], G[:, 8 + j, :], ALU.mult, ["G%d" % (8 + j)], [k2, "ps%d" % (o + 1)])
                self.tt(m1, m1, m2, ALU.add, [k1, k2], [k1], eng="pool")
                self.tt(m2, ps[o + 2], G[:, 16 + j, :], ALU.mult, ["G%d" % (16 + j)], [k2, "ps%d" % (o + 2)])
                self.tt(mT[:, j, :], m1, m2, ALU.add, [k1, k2], ["mT%d" % j], eng="pool")
            for bb in range(4):
                b = t * 4 + bb
                i = b % 2
                r0 = seq * S + b * 128
                self.dma("sp", xb[i], src[r0:r0 + 128, :], key_w="oxb%d" % i, key_r=("scr" if l > 0 else None))
                for half in range(2):
                    pg, kg = pG()
                    for k in range(8):
                        self.mm(pg, mT[:, k, bb * 128:(bb + 1) * 128], Wo[:, k, half * 512:(half + 1) * 512], k == 0, k == 7,
                                ["mT%d" % k, "Wo"], [kg])
                    self.tt(ob[i][:, half * 512:(half + 1) * 512], pg, xb[i][:, half * 512:(half + 1) * 512], ALU.add,
                            ["oxb%d" % i], ["ob%d" % i, kg])
                self.dma("sp", dst[r0:r0 + 128, :], ob[i], key_r="ob%d" % i, key_w=(None if final else "scr"), chan=och)


def _rope_tables():
    inv = (10000.0 ** (-np.arange(0, 32, 2, dtype=np.float32) / 32)).astype(np.float32)
    ang = np.arange(S, dtype=np.float32)[:, None] * inv[None, :]
    cos, sin = np.cos(ang).astype(np.float32), np.sin(ang).astype(np.float32)
    ct = np.ones((96, S), np.float32)
    st = np.zeros((96, S), np.float32)
    ct[64:80] = cos.T
    ct[80:96] = cos.T
    st[64:80] = -sin.T
    st[80:96] = sin.T
    return ct, st


def _emask():
    n = 24
    slopes = (2.0 ** (-8.0 * np.arange(1, n + 1, dtype=np.float32) / n)).reshape(3, 8)
    k = np.arange(128)[:, None].astype(np.float32)
    q = np.arange(128)[None, :].astype(np.float32)
    em = np.zeros((128, 25, 256), np.float32)
    for g in range(3):
        for h in range(8):
            sl = slopes[g, h] * DIL[g]
            em[:, g * 8 + h, 0:128] = np.where(k <= q, np.exp(-sl * np.maximum(q - k, 0.0)), 0.0)
            if g < 2:
                em[:, g * 8 + h, 128:256] = np.where(k >= q, np.exp(-sl * np.maximum(128.0 + q - k, 0.0)), 0.0)
    em[:, 24, 0:128] = (k <= q).astype(np.float32)
    return em.reshape(128, 25 * 256)


def _host_layout(inp, layers):
    f = lambda a: np.ascontiguousarray(a, dtype=np.float32)
    L = len(layers)
    w_uq = f(inp["w_uq"][layers])
    w_uq_sw = np.zeros_like(w_uq)
    for h in range(8):
        b = h * 96
        w_uq_sw[:, :, b + 64:b + 80] = w_uq[:, :, b + 80:b + 96]
        w_uq_sw[:, :, b + 80:b + 96] = w_uq[:, :, b + 64:b + 80]
    w_ukv = f(inp["w_ukv"][layers]).reshape(L, 128, 8, 128)
    w_ukv_k = f(w_ukv[:, :, :, 0:64].reshape(L, 128, 512))
    w_ukv_v = f(w_ukv[:, :, :, 64:128].reshape(L, 128, 512))
    w_in = inp["w_in"]
    w_kpe = np.zeros((L, D, 96), np.float32)
    w_kpe_sw = np.zeros((L, D, 96), np.float32)
    for i, l in enumerate(layers):
        kp = w_in[l][:, C_KPE:C_KPE + 32]
        w_kpe[i, :, 64:96] = kp
        w_kpe_sw[i, :, 64:80] = kp[:, 16:32]
        w_kpe_sw[i, :, 80:96] = kp[:, 0:16]
    colv = np.zeros((L, 128, NCV), np.float32)
    gbc = np.zeros((L, 128, D), np.float32)
    for i, l in enumerate(layers):
        colv[i, :, CV_BG:CV_BG + 24] = inp["b_gate"][l].reshape(24, 128).T
        for tap in range(3):
            colv[i, :, CV_CW + tap * 4:CV_CW + tap * 4 + 4] = inp["conv_w"][l][tap].reshape(4, 128).T
        colv[i, :, CV_CB:CV_CB + 4] = inp["conv_b"][l].reshape(4, 128).T
        colv[i, :, CV_GQ:CV_GQ + 2] = inp["q_a_norm_g"][l].reshape(2, 128).T
        colv[i, :, CV_GKV] = inp["kv_a_norm_g"][l]
        for (col, src) in ((CV_GQM, inp["mla_q_norm_g"][l]), (CV_GKM, inp["mla_k_norm_g"][l])):
            colv[i, 0:96, col] = src
            colv[i, 64:80, col + 1] = src[80:96]
            colv[i, 80:96, col + 1] = src[64:80]
        for g in range(3):
            colv[i, :, CV_GDQ + g] = np.tile(inp["dil_q_norm_g"][l][g], 2)
            colv[i, :, CV_GDK + g] = np.tile(inp["dil_k_norm_g"][l][g], 2)
        gbc[i] = np.broadcast_to(inp["norm_g"][l][None, :], (128, D))
    ind2 = np.zeros((128, 128), np.float32)
    ind2[0:64, 0:64] = 1.0
    ind2[64:128, 64:128] = 1.0
    ct, st = _rope_tables()
    shared = {
        "w_in": f(w_in[layers]), "w_uq": w_uq, "w_uq_sw": w_uq_sw, "w_ukv_k": w_ukv_k, "w_ukv_v": w_ukv_v,
        "w_kpe": w_kpe, "w_kpe_sw": w_kpe_sw,
        "w_out_a": f(inp["w_out_a"][layers]), "w_out_b": f(inp["w_out_b"][layers]), "w_out_c": f(inp["w_out_c"][layers]),
        "w_o": f(inp["w_o"][layers]), "colv": colv, "gbc": gbc,
        "ident": np.eye(128, dtype=np.float32), "ind2": ind2, "emask": _emask(), "ctab": ct, "stab": st,
    }
    return shared


_CACHE = {}


def _get_builder(nseq, nlayers, dbg=False):
    key = (nseq, nlayers, dbg)
    if key not in _CACHE:
        _CACHE[key] = Builder(nseq, nlayers, dbg=dbg)
    return _CACHE[key]


def kernel(**inputs):
    inp = {k: np.asarray(v) for k, v in inputs.items()}
    x = np.ascontiguousarray(inp["x"], dtype=np.float32)
    B = x.shape[0]
    ncores = 8
    nseq = B // ncores
    shared = _host_layout(inp, list(range(DEPTH)))
    bld = Builder(nseq, DEPTH)
    in_maps = []
    for c in range(ncores):
        m = dict(shared)
        m["x"] = x[c * nseq:(c + 1) * nseq].reshape(nseq * S, D)
        in_maps.append(m)
    res = run_bass_kernel_spmd(bld.nc, in_maps, core_ids=list(range(ncores)))
    outs = [np.asarray(r["out"]).reshape(nseq, S, D) for r in res.results]
    return np.concatenate(outs, axis=0).astype(np.float32)
```
